# Optimizing a Trainium2 kernel written in Bass

```python
import math
import jax
import jax.numpy as jnp
from jax import lax
import numpy as np

D_MODEL = 1024
BATCH = 8
SEQ = 4096
DEPTH = 2

GRID_W = 64
CTX_LEN = 256
EPS = 1e-6
N_MOD = 6

NA_HEADS = 8
NA_HEAD_DIM = 64
NA_WIDTH = NA_HEADS * NA_HEAD_DIM
NA_KH_MAX = 8
NA_KW = 16

HY_WIDTH = 256
HY_SHORT = 3
HY_BANDS = 16
HY_EMB = 1 + 2 * HY_BANDS
HY_HIDDEN = 64
HY_FAST_DECAY = 0.3
HY_SLOW_DECAY = 1.5
HY_TARGET = 1e-2

RG_WIDTH = 256
RG_BLOCKS = 4
RG_BLOCK_DIM = RG_WIDTH // RG_BLOCKS
RG_CONV = 4
RG_C = 8.0

N_BRANCH = 3
HY_OFF = 3 * NA_WIDTH
RG_OFF = HY_OFF + 3 * HY_WIDTH
IN_COLS = RG_OFF + 2 * RG_WIDTH

PEER_HEADS = 8
PEER_DK = 256
PEER_N_KEYS = 128
PEER_TOPK = 16
PEER_N_EXPERTS = PEER_N_KEYS * PEER_N_KEYS
PEER_CHUNK = 128

kernel_name = 'hybrid_na_hyena_rglru_peer_dit'


def rmsnorm(x, g):
    xf = x.astype(jnp.float32)
    y = xf * lax.rsqrt(jnp.mean(xf * xf, axis=-1, keepdims=True) + EPS)
    return (y * g.astype(jnp.float32)).astype(x.dtype)


def dwconv(x, w, b, pad):
    y = lax.conv_general_dilated(x, w[:, None, :].astype(x.dtype), window_strides=(1,), padding=[pad],
                                 dimension_numbers=('NWC', 'WIO', 'NWC'), feature_group_count=x.shape[-1])
    return y + b.astype(x.dtype)


def neighbourhood_attention(q, k, v, kc, vc, rpb):
    B, T, H, hd = q.shape
    rows = T // GRID_W
    kh = min(NA_KH_MAX, rows)
    scale = hd ** -0.5
    qg = q.reshape(B, rows, GRID_W, H, hd)
    kg = k.reshape(B, rows, GRID_W, H, hd)
    vg = v.reshape(B, rows, GRID_W, H, hd)
    row_start = jnp.clip(jnp.arange(rows) - kh // 2, 0, rows - kh)
    cols = jnp.arange(GRID_W)
    col_keys = jnp.clip(cols - NA_KW // 2, 0, GRID_W - NA_KW)[:, None] + jnp.arange(NA_KW)[None, :]
    dc_idx = col_keys - cols[:, None] + (NA_KW - 1)

    def row_block(r):
        rs = row_start[r]
        q_r = lax.dynamic_index_in_dim(qg, r, axis=1, keepdims=False)
        k_win = lax.dynamic_slice_in_dim(kg, rs, kh, axis=1)[:, :, col_keys]
        v_win = lax.dynamic_slice_in_dim(vg, rs, kh, axis=1)[:, :, col_keys]
        dr_idx = rs + jnp.arange(kh) - r + (NA_KH_MAX - 1)
        bias = rpb[:, dr_idx[:, None, None], dc_idx[None]].transpose(0, 2, 1, 3)
        s_loc = jnp.einsum('bwhd,bawkhd->bhwak', q_r, k_win) * scale + bias
        s_loc = s_loc.reshape(B, H, GRID_W, kh * NA_KW)
        s_ctx = jnp.einsum('bwhd,bchd->bhwc', q_r, kc) * scale
        s = jnp.concatenate([s_loc.astype(jnp.float32), s_ctx.astype(jnp.float32)], axis=-1)
        p = jax.nn.softmax(s, axis=-1).astype(v.dtype)
        p_loc = p[..., :kh * NA_KW].reshape(B, H, GRID_W, kh, NA_KW)
        p_ctx = p[..., kh * NA_KW:]
        return (jnp.einsum('bhwak,bawkhd->bwhd', p_loc, v_win)
                + jnp.einsum('bhwc,bchd->bwhd', p_ctx, vc))

    out = lax.map(row_block, jnp.arange(rows))
    return out.transpose(1, 0, 2, 3, 4).reshape(B, T, H * hd)


def context_attention(q, k, v):
    scale = q.shape[-1] ** -0.5
    s = jnp.einsum('bqhd,bkhd->bhqk', q, k).astype(jnp.float32) * scale
    p = jax.nn.softmax(s, axis=-1).astype(v.dtype)
    o = jnp.einsum('bhqk,bkhd->bqhd', p, v)
    return o.reshape(q.shape[0], q.shape[1], -1)


def hyena_filters(L, f1_w, f1_b, f2_w, f2_b, f3_w, freq):
    f32 = jnp.float32
    t = jnp.linspace(0.0, 1.0, L, dtype=f32)[:, None]
    w = (2.0 * math.pi / L) * jnp.arange(L, dtype=f32)[:, None]
    bands = jnp.linspace(1e-4, HY_BANDS - 1, HY_BANDS, dtype=f32)[None, :]
    z = jnp.concatenate([t, jnp.cos(bands * w), -jnp.sin(bands * w)], axis=-1)
    h = jnp.sin(freq[0].astype(f32) * (z @ f1_w.astype(f32) + f1_b.astype(f32)))
    h = jnp.sin(freq[1].astype(f32) * (h @ f2_w.astype(f32) + f2_b.astype(f32)))
    h = h @ f3_w.astype(f32)
    deltas = jnp.linspace(math.log(HY_TARGET) / HY_SLOW_DECAY, math.log(HY_TARGET) / HY_FAST_DECAY,
                          HY_WIDTH, dtype=f32)
    decay = jnp.exp(-t * jnp.abs(deltas)[None, :])
    h = h.reshape(L, 2, HY_WIDTH) * decay[:, None, :]
    return h[:, 0], h[:, 1]


def bidirectional_long_conv(u, h_fwd, h_bwd):
    L = u.shape[1]
    k = jnp.concatenate([h_fwd, jnp.zeros_like(h_fwd[:1]), jnp.flip(h_bwd[1:], axis=0)], axis=0)
    U = jnp.fft.rfft(u.astype(jnp.float32), n=2 * L, axis=1)
    K = jnp.fft.rfft(k, n=2 * L, axis=0)
    return jnp.fft.irfft(U * K[None], n=2 * L, axis=1)[:, :L]


def hyena(p, short_w, short_b, f1_w, f1_b, f2_w, f2_b, f3_w, freq, d_skip):
    L = p.shape[1]
    pc = dwconv(p, short_w, short_b, (HY_SHORT // 2, HY_SHORT // 2))
    x0, x1, v = jnp.split(pc, 3, axis=-1)
    u = (v * x1).astype(jnp.float32)
    h_fwd, h_bwd = hyena_filters(L, f1_w, f1_b, f2_w, f2_b, f3_w, freq)
    y = bidirectional_long_conv(u, h_fwd, h_bwd) + u * d_skip.astype(jnp.float32)
    return (x0.astype(jnp.float32) * y).astype(p.dtype)


def _lru_combine(e1, e2):
    a1, b1 = e1
    a2, b2 = e2
    return a1 * a2, a2 * b1 + b2


def rglru_coeffs(xc, wa, ba, wx, bx, lam):
    B, L, _ = xc.shape
    f32 = jnp.float32
    xb = xc.reshape(B, L, RG_BLOCKS, RG_BLOCK_DIM)
    r = jax.nn.sigmoid(jnp.einsum('blnd,nde->blne', xb, wa.astype(f32)).reshape(B, L, RG_WIDTH) + ba.astype(f32))
    i = jax.nn.sigmoid(jnp.einsum('blnd,nde->blne', xb, wx.astype(f32)).reshape(B, L, RG_WIDTH) + bx.astype(f32))
    log_a = -RG_C * r * jax.nn.softplus(-lam.astype(f32))
    a = jnp.exp(log_a)
    return a, jnp.sqrt(-jnp.expm1(2.0 * log_a)) * (i * xc)


def linear_scan(a, b, h0):
    b = b.at[:, 0].add(a[:, 0] * h0)
    return lax.associative_scan(_lru_combine, (a, b), axis=1)[1]


def rglru_bidirectional(x, conv_w, conv_b, wa, ba, wx, bx, lam, h0_fwd, h0_bwd):
    xc = dwconv(x, conv_w, conv_b, (RG_CONV // 2, RG_CONV - 1 - RG_CONV // 2)).astype(jnp.float32)
    a_f, b_f = rglru_coeffs(xc, wa[0], ba[0], wx[0], bx[0], lam[0])
    a_b, b_b = rglru_coeffs(xc, wa[1], ba[1], wx[1], bx[1], lam[1])
    h_fwd = linear_scan(a_f, b_f, h0_fwd)
    h_bwd = jnp.flip(linear_scan(jnp.flip(a_b, 1), jnp.flip(b_b, 1), h0_bwd), 1)
    return h_fwd, h_bwd


def gated_merge(xn, ya, yb, yc, w_gate, b_gate, w_br_a, w_br_b, w_br_c, w_out):
    g = jax.nn.sigmoid((xn @ w_gate + b_gate).astype(jnp.float32)).astype(xn.dtype)
    g = g.reshape(xn.shape[:-1] + (N_BRANCH, D_MODEL))
    m = g[..., 0, :] * (ya @ w_br_a) + g[..., 1, :] * (yb @ w_br_b) + g[..., 2, :] * (yc @ w_br_c)
    return m @ w_out


def peer(xt, wq, keys, u, v):
    N, D = xt.shape

    def chunk(xb):
        q = (xb @ wq).reshape(PEER_CHUNK, PEER_HEADS, 2, PEER_DK // 2)
        s = jnp.einsum('thpd,hpkd->thpk', q, keys).astype(jnp.float32)
        s_top, i_top = lax.top_k(s, PEER_TOPK)
        cand = (s_top[:, :, 0, :, None] + s_top[:, :, 1, None, :]).reshape(PEER_CHUNK, PEER_HEADS, PEER_TOPK * PEER_TOPK)
        cand_idx = (i_top[:, :, 0, :, None] * PEER_N_KEYS + i_top[:, :, 1, None, :]).reshape(PEER_CHUNK, PEER_HEADS, PEER_TOPK * PEER_TOPK)
        best, pos = lax.top_k(cand, PEER_TOPK)
        idx = jnp.take_along_axis(cand_idx, pos, axis=-1)
        g = jax.nn.softmax(best, axis=-1)
        u_sel = jnp.take(u, idx, axis=0)
        v_sel = jnp.take(v, idx, axis=0)
        act = jax.nn.gelu(jnp.einsum('thkd,td->thk', u_sel, xb).astype(jnp.float32)) * g
        return jnp.einsum('thk,thkd->td', act.astype(xb.dtype), v_sel)

    return lax.map(chunk, xt.reshape(N // PEER_CHUNK, PEER_CHUNK, D)).reshape(N, D)


def setup_inputs(seed: int = 0) -> dict:
    key = jax.random.key(seed)
    ks = iter(jax.random.split(key, 48))
    f32 = jnp.float32
    D = D_MODEL

    def nrm(shape, scale):
        return jax.random.normal(next(ks), shape, f32) * scale

    a8 = jax.random.uniform(next(ks), (DEPTH, 2, RG_WIDTH), f32, 0.9, 0.999)
    a_base = a8 ** (1.0 / RG_C)
    return {
        'x': nrm((BATCH, SEQ, D), 1.0),
        'c': nrm((BATCH, D), 1.0),
        'ctx': nrm((BATCH, CTX_LEN, D), 1.0),
        'c_ctx': nrm((D,), 1.0),
        'norm_mix_g': 1.0 + nrm((DEPTH, D), 0.02),
        'norm_ffn_g': 1.0 + nrm((DEPTH, D), 0.02),
        'w_ada': nrm((DEPTH, D, N_MOD * D), 0.5 * D ** -0.5),
        'b_ada': nrm((DEPTH, N_MOD * D), 0.02),
        'w_in': nrm((DEPTH, D, IN_COLS), D ** -0.5),
        'na_rpb': nrm((DEPTH, NA_HEADS, 2 * NA_KH_MAX - 1, 2 * NA_KW - 1), 0.5),
        'hy_short_w': nrm((DEPTH, HY_SHORT, 3 * HY_WIDTH), HY_SHORT ** -0.5),
        'hy_short_b': nrm((DEPTH, 3 * HY_WIDTH), 0.02),
        'hy_f1_w': nrm((DEPTH, HY_EMB, HY_HIDDEN), HY_EMB ** -0.5),
        'hy_f1_b': nrm((DEPTH, HY_HIDDEN), 0.1),
        'hy_f2_w': nrm((DEPTH, HY_HIDDEN, HY_HIDDEN), HY_HIDDEN ** -0.5),
        'hy_f2_b': nrm((DEPTH, HY_HIDDEN), 0.1),
        'hy_f3_w': nrm((DEPTH, HY_HIDDEN, 2 * HY_WIDTH), 0.01),
        'hy_freq': 1.0 + nrm((DEPTH, 2, HY_HIDDEN), 0.1),
        'hy_bias': nrm((DEPTH, HY_WIDTH), 0.1),
        'rg_conv_w': nrm((DEPTH, RG_CONV, RG_WIDTH), RG_CONV ** -0.5),
        'rg_conv_b': nrm((DEPTH, RG_WIDTH), 0.02),
        'rg_wa': nrm((DEPTH, 2, RG_BLOCKS, RG_BLOCK_DIM, RG_BLOCK_DIM), RG_BLOCK_DIM ** -0.5),
        'rg_ba': nrm((DEPTH, 2, RG_WIDTH), 0.1),
        'rg_wx': nrm((DEPTH, 2, RG_BLOCKS, RG_BLOCK_DIM, RG_BLOCK_DIM), RG_BLOCK_DIM ** -0.5),
        'rg_bx': nrm((DEPTH, 2, RG_WIDTH), 0.1),
        'rg_lambda': jnp.log(a_base) - jnp.log1p(-a_base),
        'w_gate': nrm((DEPTH, D, N_BRANCH * D), D ** -0.5),
        'b_gate': nrm((DEPTH, N_BRANCH * D), 0.1),
        'w_br_a': nrm((DEPTH, NA_WIDTH, D), NA_WIDTH ** -0.5),
        'w_br_b': nrm((DEPTH, HY_WIDTH, D), HY_WIDTH ** -0.5),
        'w_br_c': nrm((DEPTH, RG_WIDTH, D), RG_WIDTH ** -0.5),
        'w_out': nrm((DEPTH, D, D), D ** -0.5),
        'peer_wq': nrm((DEPTH, D, PEER_HEADS * PEER_DK), D ** -0.5),
        'peer_keys': nrm((DEPTH, PEER_HEADS, 2, PEER_N_KEYS, PEER_DK // 2), (PEER_DK // 2) ** -0.5),
        'peer_u': nrm((DEPTH, PEER_N_EXPERTS, D), D ** -0.5),
        'peer_v': nrm((DEPTH, PEER_N_EXPERTS, D), 0.5),
        'final_g': 1.0 + nrm((D,), 0.02),
    }


def reference(x, c, ctx, c_ctx, norm_mix_g, norm_ffn_g, w_ada, b_ada, w_in, na_rpb,
              hy_short_w, hy_short_b, hy_f1_w, hy_f1_b, hy_f2_w, hy_f2_b, hy_f3_w, hy_freq, hy_bias,
              rg_conv_w, rg_conv_b, rg_wa, rg_ba, rg_wx, rg_bx, rg_lambda,
              w_gate, b_gate, w_br_a, w_br_b, w_br_c, w_out,
              peer_wq, peer_keys, peer_u, peer_v, final_g):
    B, T, D = x.shape
    C = ctx.shape[1]
    H, hd = NA_HEADS, NA_HEAD_DIM
    for l in range(DEPTH):
        last = l == DEPTH - 1
        mod = (jax.nn.silu(c) @ w_ada[l] + b_ada[l]).reshape(B, 1, N_MOD, D)
        mod_c = (jax.nn.silu(c_ctx) @ w_ada[l] + b_ada[l]).reshape(1, 1, N_MOD, D)
        hy_args = (hy_short_w[l], hy_short_b[l], hy_f1_w[l], hy_f1_b[l], hy_f2_w[l], hy_f2_b[l],
                   hy_f3_w[l], hy_freq[l], hy_bias[l])
        rg_args = (rg_conv_w[l], rg_conv_b[l], rg_wa[l], rg_ba[l], rg_wx[l], rg_bx[l], rg_lambda[l])
        merge_args = (w_gate[l], b_gate[l], w_br_a[l], w_br_b[l], w_br_c[l], w_out[l])

        xn = rmsnorm(x, norm_mix_g[l]) * (1.0 + mod[:, :, 1]) + mod[:, :, 0]
        cn = rmsnorm(ctx, norm_mix_g[l]) * (1.0 + mod_c[:, :, 1]) + mod_c[:, :, 0]
        p = xn @ w_in[l]
        pc = cn @ w_in[l]
        q = p[..., 0:NA_WIDTH].reshape(B, T, H, hd)
        k = p[..., NA_WIDTH:2 * NA_WIDTH].reshape(B, T, H, hd)
        v = p[..., 2 * NA_WIDTH:3 * NA_WIDTH].reshape(B, T, H, hd)
        qc = pc[..., 0:NA_WIDTH].reshape(B, C, H, hd)
        kc = pc[..., NA_WIDTH:2 * NA_WIDTH].reshape(B, C, H, hd)
        vc = pc[..., 2 * NA_WIDTH:3 * NA_WIDTH].reshape(B, C, H, hd)
        ya = neighbourhood_attention(q, k, v, kc, vc, na_rpb[l])
        yb = hyena(p[..., HY_OFF:RG_OFF], *hy_args)
        h0 = jnp.zeros((B, RG_WIDTH), jnp.float32)
        hf_c, hb_c = rglru_bidirectional(pc[..., RG_OFF:RG_OFF + RG_WIDTH], *rg_args, h0, h0)
        hf, hb = rglru_bidirectional(p[..., RG_OFF:RG_OFF + RG_WIDTH], *rg_args, hf_c[:, -1], hb_c[:, 0])
        yc = ((hf + hb) * jax.nn.gelu(p[..., RG_OFF + RG_WIDTH:IN_COLS].astype(jnp.float32))).astype(x.dtype)
        x = x + (mod[:, :, 2] * gated_merge(xn, ya, yb, yc, *merge_args)).astype(x.dtype)
        if not last:
            ya_c = context_attention(qc, kc, vc)
            yb_c = hyena(pc[..., HY_OFF:RG_OFF], *hy_args)
            yc_c = ((hf_c + hb_c) * jax.nn.gelu(pc[..., RG_OFF + RG_WIDTH:IN_COLS].astype(jnp.float32))).astype(ctx.dtype)
            ctx = ctx + (mod_c[:, :, 2] * gated_merge(cn, ya_c, yb_c, yc_c, *merge_args)).astype(ctx.dtype)

        xn = rmsnorm(x, norm_ffn_g[l]) * (1.0 + mod[:, :, 4]) + mod[:, :, 3]
        y = peer(xn.reshape(B * T, D), peer_wq[l], peer_keys[l], peer_u[l], peer_v[l]).reshape(B, T, D)
        x = x + (mod[:, :, 5] * y).astype(x.dtype)
        if not last:
            cn = rmsnorm(ctx, norm_ffn_g[l]) * (1.0 + mod_c[:, :, 4]) + mod_c[:, :, 3]
            yc_f = peer(cn.reshape(B * C, D), peer_wq[l], peer_keys[l], peer_u[l], peer_v[l]).reshape(B, C, D)
            ctx = ctx + (mod_c[:, :, 5] * yc_f).astype(ctx.dtype)
    return rmsnorm(x, final_g)
```

```python
import numpy as np
import concourse.bass as bass
import concourse.mybir as mybir
from concourse.bass_utils import run_bass_kernel_spmd
from contextlib import ExitStack

F32 = mybir.dt.float32
BF16 = mybir.dt.bfloat16
I32 = mybir.dt.int32
U32 = mybir.dt.uint32
U16 = mybir.dt.uint16
AF = mybir.ActivationFunctionType
ALU = mybir.AluOpType
AX = mybir.AxisListType

ENGS = ['pe', 'dve', 'act', 'pool', 'sp']
EPOCH = 30000
N_DMA_SEMS = 12


class Res:
    __slots__ = ('w', 'r')

    def __init__(self):
        self.w = {}
        self.r = {}


class Buf:
    def __init__(self, t):
        self.t = t
        self.res = Res()
        self.parts = {}

    def part(self, key):
        r = self.parts.get(key)
        if r is None:
            r = Res()
            self.parts[key] = r
        return r


def _res(x):
    return x.res if isinstance(x, Buf) else x


class Sched:
    def __init__(self, nc, es):
        self.nc = nc
        self.es = es
        self.ops = {e: [] for e in ENGS}
        self.sem = {}
        self.cnt = {}
        self.known = {e: {} for e in ENGS}
        self.nsem = 0
        for e in ENGS:
            self._new_epoch(e)
        self.dma_sems = [self._mksem("dq%d" % i) for i in range(N_DMA_SEMS)]
        self.dma_cnt = [0] * N_DMA_SEMS
        self.dma_rr = 0
        self.all_sems = {}
        self.n_instr = 0
        self.stage_es = None

    def _mksem(self, name):
        self.nsem += 1
        return self.es.enter_context(self.nc.semaphore("%s_%d" % (name, self.nsem)))

    def _new_epoch(self, e):
        self.sem[e] = self._mksem("e" + e)
        self.cnt[e] = 0

    def sbuf(self, name, shape, dtype):
        self.nbuf = getattr(self, 'nbuf', 0) + 1
        es = self.stage_es if self.stage_es is not None else self.es
        return Buf(es.enter_context(self.nc.sbuf_tensor("%s_%d" % (name, self.nbuf), shape, dtype)))

    def psum(self, name, shape, dtype):
        return Buf(self.es.enter_context(self.nc.psum_tensor(name, shape, dtype)))

    def barrier(self):
        for e in ENGS:
            kn = self.known[e]
            for sid, (sem, val) in self.all_sems.items():
                if kn.get(sid, 0) < val:
                    kn[sid] = val
                    self.ops[e].append(('wait', sem, val))
            for o in ENGS:
                if o == e or o == 'sp' or self.cnt[o] == 0:
                    continue
                sem = self.sem[o]
                if kn.get(id(sem), 0) < self.cnt[o]:
                    kn[id(sem)] = self.cnt[o]
                    self.ops[e].append(('wait', sem, self.cnt[o]))

    def dram(self, name, shape, dtype):
        return Buf(self.nc.dram_tensor(name, shape, dtype, kind="Internal"))

    def _collect(self, e, reads, writes):
        need = {}

        def mrg(d):
            for s, v in d.items():
                if need.get(s, (None, 0))[1] < v[1]:
                    need[s] = v
        for r in reads:
            mrg(_res(r).w)
        for w in writes:
            w = _res(w)
            mrg(w.w)
            mrg(w.r)
        kn = self.known[e]
        for sid, (sem, val, eng) in need.items():
            if eng == 'pe' and e == 'pe':
                continue
            if kn.get(sid, 0) < val:
                kn[sid] = val
                self.ops[e].append(('wait', sem, val))

    def _publish(self, key, reads, writes):
        sid = id(key[0])
        for w in writes:
            w = _res(w)
            w.w = {sid: key}
            w.r = {}
        for r in reads:
            _res(r).r[sid] = key

    def op(self, e, fn, reads=(), writes=()):
        self._collect(e, reads, writes)
        if self.cnt[e] >= EPOCH:
            self._new_epoch(e)
        self.cnt[e] += 1
        sem = self.sem[e]
        self.ops[e].append(('op', fn, sem))
        self._publish((sem, self.cnt[e], e), reads, writes)
        self.n_instr += 1

    def dma(self, e, out, in_, reads=(), writes=(), fn=None, **kw):
        self._collect(e, reads, writes)
        i = self.dma_rr
        self.dma_rr = (i + 1) % N_DMA_SEMS
        self.dma_cnt[i] += 16
        if self.dma_cnt[i] >= EPOCH:
            self.dma_sems[i] = self._mksem("dq")
            self.dma_cnt[i] = 16
        sem = self.dma_sems[i]
        if fn is None:
            def fn(g, out=out, in_=in_, kw=kw):
                return g.dma_start(out=out, in_=in_, **kw)
        self.ops[e].append(('dma', fn, sem))
        self._publish((sem, self.dma_cnt[i], 'dma'), reads, writes)
        self.all_sems[id(sem)] = (sem, self.dma_cnt[i])
        self.n_instr += 1

    def finish(self):
        sp = self.ops['sp']
        for sid, (sem, val) in self.all_sems.items():
            sp.append(('wait', sem, val))
        for e in ENGS:
            if e != 'sp' and self.cnt[e] > 0:
                sp.append(('wait', self.sem[e], self.cnt[e]))
        self.flush()

    def flush(self):
        nc = self.nc
        ops = self.ops
        self.ops = {e: [] for e in ENGS}

        def emit(eng, lst):
            for it in lst:
                if it[0] == 'wait':
                    eng.wait_ge(it[1], it[2])
                elif it[0] == 'op':
                    it[1](eng).then_inc(it[2], 1)
                else:
                    it[1](eng).then_inc(it[2], 16)

        with nc.Block() as block:
            @block.tensor
            def _(g):
                emit(g, ops['pe'])

            @block.vector
            def _(g):
                emit(g, ops['dve'])

            @block.scalar
            def _(g):
                emit(g, ops['act'])

            @block.gpsimd
            def _(g):
                emit(g, ops['pool'])

            @block.sync
            def _(g):
                emit(g, ops['sp'])


import math

D = 1024
T = 4096
C = 256
NTOK = T + C
DEPTH = 2
EPS = 1e-6
GW = 64
NEG = -30000.0
DBG = False
STAGES = None


def _groups():
    g = [(0, C)]
    for i in range(T // 512):
        g.append((C + 512 * i, 512))
    return g


def build(dbg=False, stop_after=None):
    nc = bass.Bass("TRN2", target_bir_lowering=False)

    def din(name, shape, dt=F32):
        return nc.dram_tensor(name, list(shape), dt, kind="ExternalInput").ap()

    kind_s = "ExternalOutput" if dbg else "Internal"

    def dscr(name, shape, dt):
        return nc.dram_tensor(name, list(shape), dt, kind=kind_s).ap()

    x_in = din("x", [T, D])
    c_in = din("ctx", [C, D])
    cvec = din("cvec", [128, 8, 2])
    gmix = din("gmix", [DEPTH, 128, 8])
    gffn = din("gffn", [DEPTH, 128, 8])
    gfin = din("gfin", [128, D])
    w_ada = din("w_ada", [DEPTH, D, 6 * D])
    b_adaT = din("b_adaT", [DEPTH, 128, 48])
    w_in = din("w_in", [DEPTH, D, 2816])
    tab_in = din("tab", [DEPTH, 128, 8, 22 * 64])
    hy_swT = din("hy_swT", [DEPTH, 128, 6, 3])
    hy_sbT = din("hy_sbT", [DEPTH, 128, 6])
    hy_f1w = din("hy_f1w", [DEPTH, 33, 64])
    hy_f1b = din("hy_f1b", [DEPTH, 64, 1])
    hy_f2w = din("hy_f2w", [DEPTH, 64, 64])
    hy_f2b = din("hy_f2b", [DEPTH, 64, 1])
    hy_f3w = din("hy_f3w", [DEPTH, 64, 512])
    hy_fq = din("hy_fq", [DEPTH, 64, 2])
    hy_dsk = din("hy_dsk", [DEPTH, 128, 2])
    zT_m = din("zT_m", [33, T])
    zT_c = din("zT_c", [33, C])
    dec_m = din("dec_m", [128, 2, T])
    dec_c = din("dec_c", [128, 2, C])
    rg_cw = din("rg_cw", [DEPTH, 128, 2, 4])
    rg_cb = din("rg_cb", [DEPTH, 128, 2])
    rg_wa = din("rg_wa", [DEPTH, 2, 4, 64, 64])
    rg_wx = din("rg_wx", [DEPTH, 2, 4, 64, 64])
    rg_baT = din("rg_baT", [DEPTH, 128, 2, 2])
    rg_bxT = din("rg_bxT", [DEPTH, 128, 2, 2])
    rg_lamT = din("rg_lamT", [DEPTH, 128, 2, 2])
    w_gate = din("w_gate", [DEPTH, D, 3 * D])
    b_gateT = din("b_gateT", [DEPTH, 128, 24])
    w_bra = din("w_br_a", [DEPTH, 512, D])
    w_brb = din("w_br_b", [DEPTH, 256, D])
    w_brc = din("w_br_c", [DEPTH, 256, D])
    w_out = din("w_out", [DEPTH, D, D])
    p_wq = din("peer_wq", [DEPTH, D, 2048])
    p_keys = din("peer_keys", [DEPTH, 8, 2, 128, 128])
    p_u = din("peer_u", [DEPTH, 16384, D])
    p_v = din("peer_v", [DEPTH, 16384, D])
    ident_in = din("ident", [128, 128])
    iota_in = din("iota16", [128, 16])
    osel_in = din("onesel", [128, 2, 128])
    y_out = nc.dram_tensor("y", [T, D], F32, kind="ExternalOutput").ap()

    xres = dscr("xres", [T, D], F32)
    cres = dscr("cres", [C, D], F32)
    bcd = dscr("bcd", [10, 128, D], F32)
    xnT_d = dscr("xnT_d", [128, 8, NTOK], BF16)
    xntm_d = dscr("xntm_d", [NTOK, D], BF16)
    qkT_d = dscr("qkT_d", [8, 128, NTOK], BF16)
    vp_d = dscr("vp_d", [4, NTOK, 256], BF16)
    phT_d = dscr("phT_d", [6, 128, NTOK], F32)
    prT_d = dscr("prT_d", [4, 128, NTOK], F32)
    yaT_d = dscr("yaT_d", [4, 128, NTOK], BF16)
    ybT_d = dscr("ybT_d", [2, 128, NTOK], BF16)
    ycT_d = dscr("ycT_d", [2, 128, NTOK], BF16)
    ed_m = dscr("ed_m", [256, 2 * T], BF16)
    ed_c = dscr("ed_c", [256, 2 * C], BF16)
    ub_m = dscr("ub_m", [128, 2 * 128 * 32], BF16)
    ub_c = dscr("ub_c", [128, 2 * 128 * 2], BF16)
    x0_d = dscr("x0_d", [2, 128, NTOK], BF16)
    dbg_h = dscr("dbg_h", [2, 64, T], F32) if dbg else None

    ges = ExitStack()
    with ges:
        S = Sched(nc, ges)
        PS = [S.psum("ps%d" % i, [128, 512], F32) for i in range(6)]
        PSB = [S.psum("psb%d" % i, [128, 1024], BF16) for i in range(2)]
        ident_f = S.sbuf("ident_f", [128, 128], F32)
        ident_b = S.sbuf("ident_b", [128, 128], BF16)
        ones_f = S.sbuf("ones_f", [128, 128], F32)
        iota16 = S.sbuf("iota16", [128, 16], F32)
        osel = S.sbuf("osel", [128, 2, 128], BF16)
        osel_f = S.sbuf("osel_f", [128, 2, 128], F32)
        modT = S.sbuf("modT", [128, 48, 2], F32)
        Gs = S.sbuf("Gs", [128, 2, 8, 2], F32)
        S.dma('sp', ident_f.t[:], ident_in, writes=[ident_f])
        S.dma('sp', iota16.t[:], iota_in, writes=[iota16])
        S.dma('sp', osel_f.t[:], osel_in, writes=[osel_f])
        S.op('dve', lambda g: g.tensor_copy(out=ident_b.t[:], in_=ident_f.t[:]), reads=[ident_f], writes=[ident_b])
        S.op('dve', lambda g: g.tensor_copy(out=osel.t[:], in_=osel_f.t[:]), reads=[osel_f], writes=[osel])
        S.op('dve', lambda g: g.memset(ones_f.t[:], 1.0), writes=[ones_f])
        rr = {'n': 0}

        def alt(engs=('dve', 'act')):
            rr['n'] += 1
            return engs[rr['n'] % len(engs)]

        def copy_op(e, out, in_, reads, writes):
            if e == 'act':
                S.op('act', lambda g: g.activation(out=out, in_=in_, func=AF.Copy), reads=reads, writes=writes)
            else:
                S.op(e, lambda g: g.tensor_copy(out=out, in_=in_), reads=reads, writes=writes)

        class Stage:
            def __init__(self, name):
                self.name = name

            def __enter__(self):
                self.es = ExitStack()
                self.es.__enter__()
                S.stage_es = self.es
                return self

            def __exit__(self, *a):
                S.barrier()
                S.flush()
                S.stage_es = None
                self.es.__exit__(None, None, None)
                return False

        GROUPS = _groups()

        def stage_mod(l):
            with Stage("mod"):
                sc = S.sbuf("sc", [128, 8, 2], F32)
                ba = S.sbuf("ba", [128, 48], F32)
                gm = S.sbuf("gm", [128, 2, 8], F32)
                wb = [S.sbuf("wada", [128, 6144], F32) for _ in range(2)]
                diag = [S.sbuf("diag", [128, 128], F32) for _ in range(2)]
                bct = [S.sbuf("bct", [128, D], F32) for _ in range(2)]
                S.dma('sp', sc.t[:], cvec, writes=[sc])
                S.dma('sp', ba.t[:], b_adaT[l], writes=[ba])
                S.dma('sp', gm.t[:, 0, :], gmix[l], writes=[gm])
                S.dma('sp', gm.t[:, 1, :], gffn[l], writes=[gm])
                S.op('act', lambda g: g.activation(out=sc.t[:], in_=sc.t[:], func=AF.Silu), reads=[sc], writes=[sc])
                ps = PS[0]
                S.op('dve', lambda g: g.memset(ps.t[:, 0:96], 0.0), writes=[ps])
                for kc in range(8):
                    w = wb[kc % 2]
                    for hh in range(2):
                        S.dma('sp', w.t[:, hh * 3072:(hh + 1) * 3072], w_ada[l, kc * 128:(kc + 1) * 128, hh * 3072:(hh + 1) * 3072], writes=[w])
                    for j in range(48):
                        S.op('pe', lambda g, w=w, j=j, kc=kc: g.matmul(ps.t[:, 2 * j:2 * j + 2], lhsT=w.t[:, j * 128:(j + 1) * 128], rhs=sc.t[:, kc, :], start=False, stop=False, skip_group_check=True), reads=[w, sc], writes=[ps])
                S.op('dve', lambda g: g.tensor_tensor(out=modT.t[:], in0=ps.t[:, 0:96].rearrange("p (j s) -> p j s", s=2), in1=ba.t[:].unsqueeze(2).broadcast_to([128, 48, 2]), op=ALU.add), reads=[ps, ba], writes=[modT])
                for wh, off in ((0, 8), (1, 32)):
                    S.op('dve', lambda g, wh=wh, off=off: g.tensor_scalar(out=Gs.t[:, wh], in0=modT.t[:, off:off + 8, :], scalar1=1.0, scalar2=None, op0=ALU.add), reads=[modT], writes=[Gs])
                    S.op('dve', lambda g, wh=wh: g.tensor_tensor(out=Gs.t[:, wh], in0=Gs.t[:, wh], in1=gm.t[:, wh, :].unsqueeze(2).broadcast_to([128, 8, 2]), op=ALU.mult), reads=[Gs, gm], writes=[Gs])
                def srcs(idx):
                    if idx < 8:
                        wh, kind, s = idx // 4, (idx // 2) % 2, idx % 2
                        if kind == 0:
                            return lambda kc: Gs.t[:, wh, kc, s:s + 1]
                        off = 0 if wh == 0 else 24
                        return lambda kc: modT.t[:, off + kc, s:s + 1]
                    s = idx - 8
                    return lambda kc: modT.t[:, 40 + kc, s:s + 1]
                n = 0
                for idx in range(10):
                    f = srcs(idx)
                    bt = bct[idx % 2]
                    for half in range(2):
                        pb = PS[1 + (n % 2)]
                        n += 1
                        for k4 in range(4):
                            kc = half * 4 + k4
                            dg = diag[kc % 2]
                            S.op('dve', lambda g, dg=dg, f=f, kc=kc: g.tensor_scalar(out=dg.t[:], in0=ident_f.t[:], scalar1=f(kc), scalar2=None, op0=ALU.mult), reads=[ident_f, Gs, modT], writes=[dg])
                            S.op('pe', lambda g, pb=pb, dg=dg, k4=k4: g.matmul(pb.t[:, k4 * 128:(k4 + 1) * 128], lhsT=ones_f.t[:], rhs=dg.t[:], start=True, stop=True, skip_group_check=True), reads=[ones_f, dg], writes=[pb])
                        copy_op('act', bt.t[:, half * 512:(half + 1) * 512], pb.t[:], [pb], [bt])
                    S.dma('sp', bcd[idx], bt.t[:], reads=[bt])

        def stage_norm(l, wh, first):
            with Stage("norm"):
                Gb = [S.sbuf("Gb", [128, D], F32) for _ in range(2)]
                Sb = [S.sbuf("Sb", [128, D], F32) for _ in range(2)]
                for s in range(2):
                    S.dma('sp', Gb[s].t[:], bcd[wh * 4 + 0 + s], writes=[Gb[s]])
                    S.dma('sp', Sb[s].t[:], bcd[wh * 4 + 2 + s], writes=[Sb[s]])
                xnT = S.sbuf("xnT", [128, 8, NTOK], BF16)
                xin = [S.sbuf("xin", [128, D], F32) for _ in range(3)]
                junk = S.sbuf("junk", [128, D], BF16)
                tmp = [S.sbuf("tmpn", [128, D], F32) for _ in range(2)]
                xnb = [S.sbuf("xnb", [128, D], BF16) for _ in range(2)]
                ss = S.sbuf("ss", [128, 34], F32)
                t1 = S.sbuf("t1", [128, 34], F32)
                rstd = S.sbuf("rstd", [128, 34], F32)
                for tt in range(34):
                    s = 1 if tt < 2 else 0
                    if tt < 2:
                        src = (c_in if first else cres)[tt * 128:(tt + 1) * 128, :]
                    else:
                        src = (x_in if first else xres)[(tt - 2) * 128:(tt - 1) * 128, :]
                    xt = xin[tt % 3]
                    S.dma('sp', xt.t[:], src, writes=[xt])
                    S.op('act', lambda g, xt=xt, tt=tt: g.activation(out=junk.t[:], in_=xt.t[:], func=AF.Square, accum_out=ss.t[:, tt:tt + 1]), reads=[xt], writes=[junk, ss.part(tt)])
                    S.op('dve', lambda g, tt=tt: g.tensor_scalar(out=t1.t[:, tt:tt + 1], in0=ss.t[:, tt:tt + 1], scalar1=1.0 / D, scalar2=EPS, op0=ALU.mult, op1=ALU.add), reads=[ss.part(tt)], writes=[t1.part(tt)])
                    S.op('act', lambda g, tt=tt: g.activation(out=t1.t[:, tt:tt + 1], in_=t1.t[:, tt:tt + 1], func=AF.Sqrt), reads=[t1.part(tt)], writes=[t1.part(tt)])
                    S.op('dve', lambda g, tt=tt: g.reciprocal(out=rstd.t[:, tt:tt + 1], in_=t1.t[:, tt:tt + 1]), reads=[t1.part(tt)], writes=[rstd.part(tt)])
                    tm = tmp[tt % 2]
                    xb = xnb[tt % 2]
                    S.op('dve', lambda g, tm=tm, xt=xt, tt=tt, s=s: g.scalar_tensor_tensor(out=tm.t[:], in0=xt.t[:], scalar=rstd.t[:, tt:tt + 1], in1=Gb[s].t[:], op0=ALU.mult, op1=ALU.mult), reads=[xt, rstd.part(tt), Gb[s]], writes=[tm])
                    S.op('pool', lambda g, tm=tm, xb=xb, s=s: g.tensor_tensor(out=xb.t[:], in0=tm.t[:], in1=Sb[s].t[:], op=ALU.add), reads=[tm, Sb[s]], writes=[xb])
                    S.dma('sp', xntm_d[tt * 128:(tt + 1) * 128, :], xb.t[:], reads=[xb])
                    pb = PSB[tt % 2]
                    for kc in range(8):
                        S.op('pe', lambda g, pb=pb, xb=xb, kc=kc: g.transpose(out=pb.t[:, kc * 128:(kc + 1) * 128], in_=xb.t[:, kc * 128:(kc + 1) * 128], identity=ident_b.t[:]), reads=[xb, ident_b], writes=[pb])
                    copy_op('act', xnT.t[:, :, tt * 128:(tt + 1) * 128], pb.t[:].rearrange("p (k t) -> p k t", k=8), [pb], [xnT.part(tt)])
                for kc in range(8):
                    S.dma('sp', xnT_d[:, kc, :], xnT.t[:, kc, :], reads=[xnT.part(tt) for tt in range(34)])

        def stage_proj(l):
            with Stage("proj"):
                xnT = S.sbuf("xnT", [128, 8, NTOK], BF16)
                for kc in range(8):
                    S.dma('sp', xnT.t[:, kc, :], xnT_d[:, kc, :], writes=[xnT])
                wst = [S.sbuf("wst", [128, 2816], F32) for _ in range(2)]
                wbf = S.sbuf("wbf", [128, 8, 2816], BF16)
                for kc in range(8):
                    ws = wst[kc % 2]
                    S.dma('sp', ws.t[:], w_in[l, kc * 128:(kc + 1) * 128, :], writes=[ws])
                    copy_op(alt(('dve', 'pool')), wbf.t[:, kc, :], ws.t[:], [ws], [wbf.part(kc)])
                wparts = [wbf.part(kc) for kc in range(8)]
                ob = [S.sbuf("ob", [128, 512], BF16) for _ in range(3)]
                of = [S.sbuf("of", [128, 512], F32) for _ in range(3)]
                vpt = [S.sbuf("vpt", [128, 8, 128], BF16) for _ in range(2)]
                for v in vpt:
                    S.op('pool', lambda g, v=v: g.memset(v.t[:], 0.0), writes=[v])
                n = 0
                for (c0, nn) in GROUPS:
                    for ch in list(range(8)) + list(range(12, 22)):
                        ps = PS[n % 4]
                        for kc in range(8):
                            S.op('pe', lambda g, ps=ps, kc=kc, ch=ch, c0=c0, nn=nn: g.matmul(ps.t[:, 0:nn], lhsT=wbf.t[:, kc, ch * 128:(ch + 1) * 128], rhs=xnT.t[:, kc, c0:c0 + nn], start=(kc == 0), stop=(kc == 7)), reads=[xnT] + wparts, writes=[ps])
                        if ch < 8:
                            o = ob[n % 3]
                            if ch < 4:
                                S.op('act', lambda g, o=o, ps=ps, nn=nn: g.activation(out=o.t[:, 0:nn], in_=ps.t[:, 0:nn], func=AF.Copy, scale=0.125), reads=[ps], writes=[o])
                            else:
                                copy_op('dve', o.t[:, 0:nn], ps.t[:, 0:nn], [ps], [o])
                            S.dma('sp', qkT_d[ch, :, c0:c0 + nn], o.t[:, 0:nn], reads=[o])
                        else:
                            o = of[n % 3]
                            copy_op(alt(), o.t[:, 0:nn], ps.t[:, 0:nn], [ps], [o])
                            dst = phT_d[ch - 12] if ch < 18 else prT_d[ch - 18]
                            S.dma('sp', dst[:, c0:c0 + nn], o.t[:, 0:nn], reads=[o])
                        n += 1
                    for t4 in range(nn // 128):
                        tc0 = c0 + t4 * 128
                        ps = PS[4 + (n % 2)]
                        vt = vpt[n % 2]
                        n += 1
                        for kc in range(8):
                            S.op('pe', lambda g, ps=ps, kc=kc, tc0=tc0: g.matmul(ps.t[:], lhsT=xnT.t[:, kc, tc0:tc0 + 128], rhs=wbf.t[:, kc, 1024:1536], start=(kc == 0), stop=(kc == 7)), reads=[xnT] + wparts, writes=[ps])
                        S.op('dve', lambda g, ps=ps, vt=vt: g.tensor_copy(out=bass.AP(vt.t[:].tensor, vt.t[:].offset, [[1024, 128], [256, 4], [192, 2], [1, 64]]), in_=ps.t[:].rearrange("p (j e d) -> p j e d", j=4, e=2)), reads=[ps], writes=[vt])
                        for j in range(4):
                            S.dma('sp', vp_d[j, tc0:tc0 + 128, :], vt.t[:, 2 * j:2 * j + 2, :].rearrange("p h c -> p (h c)"), reads=[vt])

        def attn_ranges(i):
            r0 = 8 * i
            rows = list(range(r0, r0 + 8))
            rs = lambda r: min(max(r - 4, 0), GW - 8)
            amin = min(rs(r) for r in rows)
            amax = max(rs(r) for r in rows) + 7
            a0s = list(range(amin - (amin % 2), amax + 1, 2))
            out = []
            for a0 in a0s:
                hal = []
                for a in (a0, a0 + 1):
                    v = [r for r in rows if rs(r) <= a <= rs(r) + 7] if a < GW else []
                    hal.append((v[0] - r0, v[-1] - r0 + 1) if v else None)
                lo = min(h[0] for h in hal if h)
                hi = max(h[1] for h in hal if h)
                out.append((a0, hal, (lo, hi)))
            return out

        def stage_attn(l):
            with Stage("attn"):
                tab = S.sbuf("tab", [128, 8, 22 * 64], BF16)
                tst = [S.sbuf("tst", [128, 22 * 64], F32) for _ in range(2)]
                for h in range(8):
                    S.dma('sp', tst[h % 2].t[:], tab_in[l, :, h, :], writes=[tst[h % 2]])
                    copy_op(alt(('dve', 'pool')), tab.t[:, h, :], tst[h % 2].t[:], [tst[h % 2]], [tab.part(h)])
                qTs = [S.sbuf("qTs", [128, NTOK], BF16) for _ in range(2)]
                kTs = [S.sbuf("kTs", [128, NTOK], BF16) for _ in range(2)]
                vps = [S.sbuf("vps", [128, 34, 256], BF16) for _ in range(2)]
                pts = [S.sbuf("pt", [128, 512], BF16) for _ in range(3)]
                rec = [S.sbuf("rec", [128, 512], F32) for _ in range(2)]
                yab = [S.sbuf("yab", [128, 512], BF16) for _ in range(2)]
                n = {'s': 0, 'o': 0}
                for j in range(4):
                    qT, kT, vp = qTs[j % 2], kTs[j % 2], vps[j % 2]
                    S.dma('sp', qT.t[:], qkT_d[j], writes=[qT])
                    S.dma('sp', kT.t[:], qkT_d[4 + j], writes=[kT])
                    for q4 in range(2):
                        S.dma('sp', vp.t[:, q4 * 17:(q4 + 1) * 17, :], vp_d[j, q4 * 17 * 128:(q4 + 1) * 17 * 128, :].rearrange("(t p) c -> p t c", p=128), writes=[vp])
                    for i in range(-1, 8):
                        if i < 0:
                            qc0, nq = 0, C
                            klist = [('c', 0, None), ('c', 1, None)]
                        else:
                            qc0, nq = C + 512 * i, 512
                            klist = [('c', 0, None), ('c', 1, None)] + [('l', a0, (hal, un)) for (a0, hal, un) in attn_ranges(i)]
                        O = PS[3 + (n['o'] % 2) * 1]
                        Dn = PS[5] if (n['o'] % 2) else PS[4]
                        O = PS[2] if (n['o'] % 2) else PS[3]
                        n['o'] += 1
                        first = True
                        for e in range(2):
                            h = 2 * j + e
                            pb = e * 64
                            for (kind, a0, info) in klist:
                                Sp = PS[n['s'] % 2]
                                pt = pts[n['s'] % 3]
                                n['s'] += 1
                                if kind == 'c':
                                    kc0 = a0 * 128
                                    lo, hi = 0, nq
                                    S.op('pe', lambda g, Sp=Sp, pb=pb, kc0=kc0, qc0=qc0, nq=nq, qT=qT, kT=kT: g.matmul(Sp.t[:, 0:nq], lhsT=kT.t[pb:pb + 64, kc0:kc0 + 128], rhs=qT.t[pb:pb + 64, qc0:qc0 + nq], start=True, stop=True), reads=[qT, kT], writes=[Sp])
                                    S.op('act', lambda g, Sp=Sp, pt=pt, nq=nq: g.activation(out=pt.t[:, 0:nq], in_=Sp.t[:, 0:nq], func=AF.Exp), reads=[Sp], writes=[pt])
                                    vtile = a0
                                else:
                                    hal, (ulo, uhi) = info
                                    kc0 = C + a0 * GW
                                    lo, hi = ulo * GW, uhi * GW
                                    r0 = 8 * i
                                    e0 = (r0 + ulo) - a0 + 10
                                    assert 0 <= e0 and e0 + (uhi - ulo) <= 22, (i, a0, e0)
                                    S.op('pe', lambda g, Sp=Sp, pb=pb, kc0=kc0, qc0=qc0, lo=lo, hi=hi, qT=qT, kT=kT: g.matmul(Sp.t[:, lo:hi], lhsT=kT.t[pb:pb + 64, kc0:kc0 + 128], rhs=qT.t[pb:pb + 64, qc0 + lo:qc0 + hi], start=True, stop=False), reads=[qT, kT], writes=[Sp])
                                    S.op('pe', lambda g, Sp=Sp, lo=lo, hi=hi, h=h, e0=e0: g.matmul(Sp.t[:, lo:hi], lhsT=ident_b.t[:], rhs=tab.t[:, h, e0 * 64:e0 * 64 + (hi - lo)], start=False, stop=True), reads=[tab.part(h), ident_b], writes=[Sp])
                                    for hf_, rng in enumerate(hal):
                                        p0 = hf_ * 64
                                        if rng is None:
                                            S.op('pool', lambda g, pt=pt, p0=p0, lo=lo, hi=hi: g.memset(pt.t[p0:p0 + 64, lo:hi], 0.0), writes=[pt])
                                            continue
                                        vlo, vhi = rng[0] * GW, rng[1] * GW
                                        S.op('act', lambda g, Sp=Sp, pt=pt, p0=p0, vlo=vlo, vhi=vhi: g.activation(out=pt.t[p0:p0 + 64, vlo:vhi], in_=Sp.t[p0:p0 + 64, vlo:vhi], func=AF.Exp), reads=[Sp], writes=[pt])
                                        if vlo > lo:
                                            S.op('pool', lambda g, pt=pt, p0=p0, lo=lo, vlo=vlo: g.memset(pt.t[p0:p0 + 64, lo:vlo], 0.0), writes=[pt])
                                        if vhi < hi:
                                            S.op('pool', lambda g, pt=pt, p0=p0, hi=hi, vhi=vhi: g.memset(pt.t[p0:p0 + 64, vhi:hi], 0.0), writes=[pt])
                                    vtile = 2 + a0 // 2
                                S.op('pe', lambda g, O=O, vp=vp, vtile=vtile, e=e, pt=pt, lo=lo, hi=hi, first=first: g.matmul(O.t[:, lo:hi], lhsT=vp.t[:, vtile, e * 128:(e + 1) * 128], rhs=pt.t[:, lo:hi], start=first, stop=False, skip_group_check=True), reads=[vp, pt], writes=[O])
                                S.op('pe', lambda g, Dn=Dn, e=e, pt=pt, lo=lo, hi=hi, first=first: g.matmul(Dn.t[:, lo:hi], lhsT=osel.t[:, e, :], rhs=pt.t[:, lo:hi], start=first, stop=False, skip_group_check=True), reads=[osel, pt], writes=[Dn])
                                first = False
                        rc = rec[n['o'] % 2]
                        yb_ = yab[n['o'] % 2]
                        S.op('dve', lambda g, rc=rc, Dn=Dn, nq=nq: g.reciprocal(out=rc.t[:, 0:nq], in_=Dn.t[:, 0:nq]), reads=[Dn], writes=[rc])
                        S.op('dve', lambda g, rc=rc, O=O, yb_=yb_, nq=nq: g.tensor_tensor(out=yb_.t[:, 0:nq], in0=O.t[:, 0:nq], in1=rc.t[:, 0:nq], op=ALU.mult), reads=[O, rc], writes=[yb_])
                        S.dma('sp', yaT_d[j, :, qc0:qc0 + nq], yb_.t[:, 0:nq], reads=[yb_])

        def sin_layer(ps, n, bias, fq, tmp, tmp2, out, outbuf, xr):
            S.op('dve', lambda g: g.tensor_scalar(out=tmp.t[0:64, 0:n], in0=ps.t[0:64, 0:n], scalar1=bias, scalar2=fq, op0=ALU.add, op1=ALU.mult), reads=[ps] + xr, writes=[tmp])
            MAGIC = 12582912.0
            S.op('dve', lambda g: g.tensor_scalar(out=tmp2.t[0:64, 0:n], in0=tmp.t[0:64, 0:n], scalar1=1.0 / (2 * math.pi), scalar2=MAGIC, op0=ALU.mult, op1=ALU.add), reads=[tmp], writes=[tmp2])
            S.op('dve', lambda g: g.tensor_scalar(out=tmp2.t[0:64, 0:n], in0=tmp2.t[0:64, 0:n], scalar1=MAGIC, scalar2=-2 * math.pi, op0=ALU.subtract, op1=ALU.mult), reads=[tmp2], writes=[tmp2])
            S.op('dve', lambda g: g.tensor_tensor(out=tmp.t[0:64, 0:n], in0=tmp.t[0:64, 0:n], in1=tmp2.t[0:64, 0:n], op=ALU.add), reads=[tmp, tmp2], writes=[tmp])
            S.op('dve', lambda g: g.tensor_scalar(out=tmp.t[0:64, 0:n], in0=tmp.t[0:64, 0:n], scalar1=-3.1415925, scalar2=3.1415925, op0=ALU.max, op1=ALU.min), reads=[tmp], writes=[tmp])
            S.op('act', lambda g: g.activation(out=out, in_=tmp.t[0:64, 0:n], func=AF.Sin), reads=[tmp], writes=[outbuf])

        def stage_hy_filt(l, L, zT, dec, ed):
            with Stage("hyfilt"):
                z = S.sbuf("z", [33, L], F32)
                dc = S.sbuf("dc", [128, 2, L], F32)
                f1w = S.sbuf("f1w", [33, 64], F32)
                f2w = S.sbuf("f2w", [64, 64], F32)
                f3w = S.sbuf("f3w", [64, 512], F32)
                fb = S.sbuf("fb", [64, 2], F32)
                fq = S.sbuf("fq", [64, 2], F32)
                dsk = S.sbuf("dsk", [128, 2], F32)
                h1 = S.sbuf("h1", [64, L], F32)
                h2 = S.sbuf("h2", [64, L], F32)
                hT = [S.sbuf("hT", [128, L], F32) for _ in range(4)]
                tmp = [S.sbuf("stmp", [64, 512], F32) for _ in range(4)]
                et = [S.sbuf("et", [128, 2 * L], BF16) for _ in range(2)]
                S.dma('sp', z.t[:], zT, writes=[z])
                S.dma('sp', dc.t[:], dec, writes=[dc])
                S.dma('sp', f1w.t[:], hy_f1w[l], writes=[f1w])
                S.dma('sp', f2w.t[:], hy_f2w[l], writes=[f2w])
                S.dma('sp', f3w.t[:], hy_f3w[l], writes=[f3w])
                S.dma('sp', fb.t[:, 0:1], hy_f1b[l], writes=[fb])
                S.dma('sp', fb.t[:, 1:2], hy_f2b[l], writes=[fb])
                S.dma('sp', fq.t[:], hy_fq[l], writes=[fq])
                S.dma('sp', dsk.t[:], hy_dsk[l], writes=[dsk])
                n = 0
                for c0 in range(0, L, 512):
                    nn = min(512, L - c0)
                    ps = PS[n % 2]
                    S.op('pe', lambda g, ps=ps, c0=c0, nn=nn: g.matmul(ps.t[0:64, 0:nn], lhsT=f1w.t[:], rhs=z.t[:, c0:c0 + nn], start=True, stop=True), reads=[f1w, z], writes=[ps])
                    sin_layer(ps, nn, fb.t[:, 0:1], fq.t[:, 0:1], tmp[0], tmp[2], h1.t[:, c0:c0 + nn], h1, [fb, fq])
                    ps2 = PS[2 + n % 2]
                    S.op('pe', lambda g, ps2=ps2, c0=c0, nn=nn: g.matmul(ps2.t[0:64, 0:nn], lhsT=f2w.t[:], rhs=h1.t[:, c0:c0 + nn], start=True, stop=True), reads=[f2w, h1], writes=[ps2])
                    sin_layer(ps2, nn, fb.t[:, 1:2], fq.t[:, 1:2], tmp[1], tmp[3], h2.t[:, c0:c0 + nn], h2, [fb, fq])
                    for c4 in range(4):
                        ps3 = PS[4 + c4 % 2]
                        S.op('pe', lambda g, ps3=ps3, c4=c4, c0=c0, nn=nn: g.matmul(ps3.t[:, 0:nn], lhsT=f3w.t[:, c4 * 128:(c4 + 1) * 128], rhs=h2.t[:, c0:c0 + nn], start=True, stop=True), reads=[f3w, h2], writes=[ps3])
                        S.op('dve', lambda g, ps3=ps3, c4=c4, c0=c0, nn=nn: g.tensor_tensor(out=hT[c4].t[:, c0:c0 + nn], in0=ps3.t[:, 0:nn], in1=dc.t[:, c4 % 2, c0:c0 + nn], op=ALU.mult), reads=[ps3, dc], writes=[hT[c4]])
                    n += 1
                if dbg_h is not None and L == T:
                    S.dma('sp', dbg_h[0], h1.t[:], reads=[h1])
                    S.dma('sp', dbg_h[1], h2.t[:], reads=[h2])
                for cc in range(2):
                    e_ = et[cc]
                    S.op('pool', lambda g, e_=e_: g.memset(e_.t[:, 0:1], 0.0), writes=[e_])
                    copy_op('act', e_.t[:, L:2 * L], hT[cc].t[:, :], [hT[cc]], [e_])
                    S.op('dve', lambda g, e_=e_, cc=cc: g.tensor_scalar(out=e_.t[:, L:L + 1], in0=hT[cc].t[:, 0:1], scalar1=dsk.t[:, cc:cc + 1], scalar2=None, op0=ALU.add), reads=[hT[cc], dsk, e_], writes=[e_])
                    S.op('dve', lambda g, e_=e_, cc=cc: g.tensor_copy(out=e_.t[:, 1:L], in_=hT[2 + cc].t[:, L - 1:0:-1]), reads=[hT[2 + cc], e_], writes=[e_])
                    S.dma('sp', ed[cc * 128:(cc + 1) * 128, :], e_.t[:], reads=[e_])

        def stage_hy_sc(l, L, c0, ub):
            nb = L // 128
            with Stage("hysc"):
                sw = S.sbuf("sw", [128, 6, 3], F32)
                sb = S.sbuf("sb", [128, 6], F32)
                S.dma('sp', sw.t[:], hy_swT[l], writes=[sw])
                S.dma('sp', sb.t[:], hy_sbT[l], writes=[sb])
                pin = [S.sbuf("pin", [128, L], F32) for _ in range(2)]
                ta = S.sbuf("ta", [128, L], F32)
                tb = S.sbuf("tb", [128, L], F32)
                vv = S.sbuf("vv", [128, L], F32)
                x0b = [S.sbuf("x0b", [128, L], BF16) for _ in range(2)]
                utr = [S.sbuf("utr", [128, L], BF16) for _ in range(2)]
                ubs = S.sbuf("ubs", [128, 2, 128, nb], BF16)
                n = {'p': 0}

                def conv(c6, out_buf, out_ap_full, final_writes):
                    p = pin[n['p'] % 2]
                    n['p'] += 1
                    S.dma('sp', p.t[:], phT_d[c6, :, c0:c0 + L], writes=[p])
                    S.op('dve', lambda g: g.tensor_scalar(out=ta.t[:], in0=p.t[:], scalar1=sw.t[:, c6, 1:2], scalar2=sb.t[:, c6:c6 + 1], op0=ALU.mult, op1=ALU.add), reads=[p, sw, sb], writes=[ta])
                    S.op('dve', lambda g: g.scalar_tensor_tensor(out=ta.t[:, 1:L], in0=p.t[:, 0:L - 1], scalar=sw.t[:, c6, 0:1], in1=ta.t[:, 1:L], op0=ALU.mult, op1=ALU.add), reads=[p, sw, ta], writes=[ta])
                    S.op('dve', lambda g: g.scalar_tensor_tensor(out=out_ap_full(0, L - 1), in0=p.t[:, 1:L], scalar=sw.t[:, c6, 2:3], in1=ta.t[:, 0:L - 1], op0=ALU.mult, op1=ALU.add), reads=[p, sw, ta], writes=[out_buf])
                    copy_op('dve', out_ap_full(L - 1, L), ta.t[:, L - 1:L], [ta, out_buf], [out_buf])

                for cc in range(2):
                    conv(cc, x0b[cc], lambda a, b, cc=cc: x0b[cc].t[:, a:b], None)
                    S.dma('sp', x0_d[cc, :, c0:c0 + L], x0b[cc].t[:], reads=[x0b[cc]])
                for cc in range(2):
                    conv(2 + cc, tb, lambda a, b: tb.t[:, a:b], None)
                    conv(4 + cc, vv, lambda a, b, vv=vv: vv.t[:, a:b], None)
                    S.op('dve', lambda g, cc=cc, vv=vv: g.tensor_tensor(out=utr[cc].t[:, ::-1], in0=vv.t[:], in1=tb.t[:], op=ALU.mult), reads=[vv, tb], writes=[utr[cc]])
                    for jb in range(0, nb, 8):
                        k = min(8, nb - jb)
                        pb = PSB[(jb // 8) % 2]
                        for q in range(k):
                            S.op('pe', lambda g, pb=pb, q=q, jb=jb, cc=cc: g.transpose(out=pb.t[:, q * 128:(q + 1) * 128], in_=utr[cc].t[:, (jb + q) * 128:(jb + q + 1) * 128], identity=ident_b.t[:]), reads=[utr[cc], ident_b], writes=[pb])
                        base = ubs.t[:, cc, :, :]
                        off = base.offset + (nb - 1 - jb)
                        outap = bass.AP(ubs.t[:].tensor, off, [[2 * 128 * nb, 128], [-1, k], [nb, 128]])
                        S.op('dve', lambda g, pb=pb, k=k, outap=outap: g.tensor_copy(out=outap, in_=pb.t[:, 0:k * 128].rearrange("p (q c) -> p q c", q=k)), reads=[pb], writes=[ubs])
                S.dma('sp', ub, ubs.t[:].rearrange("p a c j -> p (a c j)"), reads=[ubs])

        def stage_hy_toep(l, L, c0, ed, ub):
            nb = L // 128
            W = 2 * L - 127
            with Stage("hytoep"):
                ubs = S.sbuf("ubs", [128, 2, 128, nb], BF16)
                S.dma('sp', ubs.t[:].rearrange("p a c j -> p (a c j)"), ub, writes=[ubs])
                kts = [S.sbuf("kt", [128, W], BF16) for _ in range(3)]
                ysb = S.sbuf("ysb", [128, 128, nb], F32)
                x0b = S.sbuf("x0b", [128, L], BF16)
                ybo = S.sbuf("ybo", [128, L], BF16)
                per_bank = min(512 // nb, 128)
                for cc in range(2):
                    S.dma('sp', x0b.t[:], x0_d[cc, :, c0:c0 + L], writes=[x0b])
                    for c in range(128):
                        ch = cc * 128 + c
                        kt = kts[ch % 3]
                        S.dma('sp', kt.t[:], bass.AP(ed.tensor, ed.offset + ch * 2 * L, [[1, 128], [1, W]]), writes=[kt])
                        bank = PS[(c // per_bank) % 2]
                        col = (c % per_bank) * nb
                        ds = [0] + [d for d in range(-(nb - 1), nb) if d != 0]
                        for d in ds:
                            j0, j1 = max(0, -d), min(nb, nb - d)
                            xo = L - 127 + 128 * d
                            S.op('pe', lambda g, bank=bank, col=col, kt=kt, xo=xo, cc=cc, c=c, j0=j0, j1=j1, d=d: g.matmul(bank.t[:, col + j0 + d:col + j1 + d], lhsT=kt.t[:, xo:xo + 128], rhs=ubs.t[:, cc, c, j0:j1], start=(d == 0), stop=False, skip_group_check=True), reads=[kt, ubs], writes=[bank])
                        if c % per_bank == per_bank - 1:
                            cb = c - per_bank + 1
                            copy_op(alt(), ysb.t[:, cb:c + 1, :], bank.t[:, 0:per_bank * nb].rearrange("p (c j) -> p c j", j=nb), [bank], [ysb])
                    for I0 in range(0, nb, 4):
                        k = min(4, nb - I0)
                        pt_ = PS[2 + (I0 // 4) % 2]
                        for q in range(k):
                            S.op('pe', lambda g, pt_=pt_, q=q, I0=I0: g.transpose(out=pt_.t[:, q * 128:(q + 1) * 128], in_=ysb.t[:, :, I0 + q], identity=ident_f.t[:]), reads=[ysb, ident_f], writes=[pt_])
                        S.op('dve', lambda g, pt_=pt_, k=k, I0=I0: g.tensor_tensor(out=ybo.t[:, I0 * 128:(I0 + k) * 128], in0=pt_.t[:, 0:k * 128], in1=x0b.t[:, I0 * 128:(I0 + k) * 128], op=ALU.mult), reads=[pt_, x0b], writes=[ybo])
                    S.dma('sp', ybT_d[cc, :, c0:c0 + L], ybo.t[:], reads=[ybo])

        def stage_rglru(l):
            with Stage("rglru"):
                cw = S.sbuf("cw", [128, 2, 4], F32)
                cb = S.sbuf("cb", [128, 2], F32)
                baT = S.sbuf("baT", [128, 2, 2], F32)
                bxT = S.sbuf("bxT", [128, 2, 2], F32)
                lam = S.sbuf("lam", [128, 2, 2], F32)
                m8 = S.sbuf("m8", [128, 2, 2], F32)
                m16 = S.sbuf("m16", [128, 2, 2], F32)
                h0 = S.sbuf("h0", [128, 2, 2], F32)
                bdf = S.sbuf("bdf", [128, 8, 128], F32)
                bd = S.sbuf("bd", [128, 8, 128], BF16)
                for (t_, src) in ((cw, rg_cw[l]), (cb, rg_cb[l]), (baT, rg_baT[l]), (bxT, rg_bxT[l]), (lam, rg_lamT[l])):
                    S.dma('sp', t_.t[:], src, writes=[t_])
                S.op('pool', lambda g: g.memset(bdf.t[:], 0.0), writes=[bdf])
                for cc in range(2):
                    for dr in range(2):
                        for ax, wsrc in ((0, rg_wa), (1, rg_wx)):
                            idx = (cc * 2 + dr) * 2 + ax
                            for hb_ in range(2):
                                S.dma('sp', bdf.t[hb_ * 64:(hb_ + 1) * 64, idx, hb_ * 64:(hb_ + 1) * 64], wsrc[l, dr, 2 * cc + hb_], reads=[bdf], writes=[bdf])
                copy_op('dve', bd.t[:], bdf.t[:], [bdf], [bd])
                S.op('act', lambda g: g.activation(out=lam.t[:], in_=lam.t[:], func=AF.Exp, scale=-1.0), reads=[lam], writes=[lam])
                S.op('act', lambda g: g.activation(out=lam.t[:], in_=lam.t[:], func=AF.Ln, bias=1.0), reads=[lam], writes=[lam])
                S.op('dve', lambda g: g.tensor_scalar(out=m8.t[:], in0=lam.t[:], scalar1=-8.0, scalar2=None, op0=ALU.mult), reads=[lam], writes=[m8])
                S.op('dve', lambda g: g.tensor_scalar(out=m16.t[:], in0=lam.t[:], scalar1=-16.0, scalar2=None, op0=ALU.mult), reads=[lam], writes=[m16])
                LM = T
                xin = S.sbuf("rxin", [128, LM], F32)
                xc = S.sbuf("rxc", [128, LM], F32)
                xcb = S.sbuf("rxcb", [128, LM], BF16)
                rb = S.sbuf("rr", [128, LM], F32)
                ib = S.sbuf("ri", [128, LM], F32)
                ab = S.sbuf("ra", [128, LM], F32)
                hh = [S.sbuf("rh", [128, LM], F32) for _ in range(2)]
                yo = S.sbuf("ryo", [128, LM], BF16)
                n = 0
                for (L, c0, isctx) in ((C, 0, True), (T, C, False)):
                    for cc in range(2):
                        S.dma('sp', xin.t[:, 0:L], prT_d[cc, :, c0:c0 + L], writes=[xin])
                        S.op('dve', lambda g, L=L, cc=cc: g.tensor_scalar(out=xc.t[:, 0:L], in0=xin.t[:, 0:L], scalar1=cw.t[:, cc, 2:3], scalar2=cb.t[:, cc:cc + 1], op0=ALU.mult, op1=ALU.add), reads=[xin, cw, cb], writes=[xc])
                        for (k, sh) in ((0, -2), (1, -1), (3, 1)):
                            if sh < 0:
                                oa, ia = (-sh, L), (0, L + sh)
                            else:
                                oa, ia = (0, L - sh), (sh, L)
                            S.op('dve', lambda g, oa=oa, ia=ia, k=k, cc=cc: g.scalar_tensor_tensor(out=xc.t[:, oa[0]:oa[1]], in0=xin.t[:, ia[0]:ia[1]], scalar=cw.t[:, cc, k:k + 1], in1=xc.t[:, oa[0]:oa[1]], op0=ALU.mult, op1=ALU.add), reads=[xin, cw, xc], writes=[xc])
                        copy_op('act', xcb.t[:, 0:L], xc.t[:, 0:L], [xc], [xcb])
                        for dr in range(2):
                            for g0 in range(0, L, 512):
                                nn = min(512, L - g0)
                                for ax, dst, bias in ((0, rb, baT), (1, ib, bxT)):
                                    ps = PS[n % 4]
                                    n += 1
                                    idx = (cc * 2 + dr) * 2 + ax
                                    S.op('pe', lambda g, ps=ps, idx=idx, g0=g0, nn=nn: g.matmul(ps.t[:, 0:nn], lhsT=bd.t[:, idx, :], rhs=xcb.t[:, g0:g0 + nn], start=True, stop=True), reads=[bd, xcb], writes=[ps])
                                    S.op('act', lambda g, ps=ps, dst=dst, bias=bias, g0=g0, nn=nn, cc=cc, dr=dr: g.activation(out=dst.t[:, g0:g0 + nn], in_=ps.t[:, 0:nn], func=AF.Sigmoid, bias=bias.t[:, cc, dr:dr + 1]), reads=[ps, bias], writes=[dst])
                            S.op('act', lambda g, L=L, cc=cc, dr=dr: g.activation(out=ab.t[:, 0:L], in_=rb.t[:, 0:L], func=AF.Exp, scale=m8.t[:, cc, dr:dr + 1]), reads=[rb, m8], writes=[ab])
                            S.op('act', lambda g, L=L, cc=cc, dr=dr: g.activation(out=rb.t[:, 0:L], in_=rb.t[:, 0:L], func=AF.Exp, scale=m16.t[:, cc, dr:dr + 1]), reads=[rb, m16], writes=[rb])
                            S.op('act', lambda g, L=L: g.activation(out=rb.t[:, 0:L], in_=rb.t[:, 0:L], func=AF.Sqrt, scale=-1.0, bias=1.0), reads=[rb], writes=[rb])
                            S.op('dve', lambda g, L=L: g.tensor_tensor(out=ib.t[:, 0:L], in0=ib.t[:, 0:L], in1=xc.t[:, 0:L], op=ALU.mult), reads=[ib, xc], writes=[ib])
                            S.op('dve', lambda g, L=L: g.tensor_tensor(out=ib.t[:, 0:L], in0=ib.t[:, 0:L], in1=rb.t[:, 0:L], op=ALU.mult), reads=[ib, rb], writes=[ib])
                            init = 0.0 if isctx else h0.t[:, cc, dr:dr + 1]
                            ho = hh[dr]
                            if dr == 0:
                                S.op('dve', lambda g, L=L, init=init, ho=ho: g.tensor_tensor_scan(out=ho.t[:, 0:L], data0=ab.t[:, 0:L], data1=ib.t[:, 0:L], initial=init, op0=ALU.mult, op1=ALU.add), reads=[ab, ib, h0], writes=[ho])
                            else:
                                S.op('dve', lambda g, L=L, init=init, ho=ho: g.tensor_tensor_scan(out=ho.t[:, 0:L][:, ::-1], data0=ab.t[:, 0:L][:, ::-1], data1=ib.t[:, 0:L][:, ::-1], initial=init, op0=ALU.mult, op1=ALU.add), reads=[ab, ib, h0], writes=[ho])
                        if isctx:
                            copy_op('dve', h0.t[:, cc, 0:1], hh[0].t[:, L - 1:L], [hh[0], h0], [h0])
                            copy_op('dve', h0.t[:, cc, 1:2], hh[1].t[:, 0:1], [hh[1], h0], [h0])
                        S.dma('sp', xin.t[:, 0:L], prT_d[2 + cc, :, c0:c0 + L], writes=[xin])
                        S.op('act', lambda g, L=L: g.activation(out=xin.t[:, 0:L], in_=xin.t[:, 0:L], func=AF.Gelu), reads=[xin], writes=[xin])
                        S.op('dve', lambda g, L=L: g.tensor_tensor(out=hh[0].t[:, 0:L], in0=hh[0].t[:, 0:L], in1=hh[1].t[:, 0:L], op=ALU.add), reads=[hh[0], hh[1]], writes=[hh[0]])
                        S.op('dve', lambda g, L=L: g.tensor_tensor(out=yo.t[:, 0:L], in0=hh[0].t[:, 0:L], in1=xin.t[:, 0:L], op=ALU.mult), reads=[hh[0], xin], writes=[yo])
                        S.dma('sp', ycT_d[cc, :, c0:c0 + L], yo.t[:, 0:L], reads=[yo])

        def load_w_bf(dst, src_rows_fn, nk, ncols, stg):
            for kc in range(nk):
                st = stg[kc % len(stg)]
                S.dma('sp', st.t[:, 0:ncols], src_rows_fn(kc), writes=[st])
                copy_op(alt(('dve', 'pool')), dst.t[:, kc, :], st.t[:, 0:ncols], [st], [dst])

        def stage_merge(l, first):
            with Stage("merge"):
                stg = [S.sbuf("mstg", [128, 3072], F32)]
                wg = S.sbuf("wg", [128, 8, 3072], BF16)
                wa_ = S.sbuf("wa", [128, 4, D], BF16)
                wb_ = S.sbuf("wb", [128, 2, D], BF16)
                wc_ = S.sbuf("wc", [128, 2, D], BF16)
                wo_ = S.sbuf("wo", [128, 8, D], BF16)
                bg = S.sbuf("bg", [128, 24], F32)
                S.dma('sp', bg.t[:], b_gateT[l], writes=[bg])
                load_w_bf(wg, lambda kc: w_gate[l, kc * 128:(kc + 1) * 128, :], 8, 3072, stg)
                load_w_bf(wa_, lambda kc: w_bra[l, kc * 128:(kc + 1) * 128, :], 4, D, stg)
                load_w_bf(wb_, lambda kc: w_brb[l, kc * 128:(kc + 1) * 128, :], 2, D, stg)
                load_w_bf(wc_, lambda kc: w_brc[l, kc * 128:(kc + 1) * 128, :], 2, D, stg)
                load_w_bf(wo_, lambda kc: w_out[l, kc * 128:(kc + 1) * 128, :], 8, D, stg)
                xg = S.sbuf("xg", [128, 8, 512], BF16)
                ya = S.sbuf("mya", [128, 4, 512], BF16)
                yb = S.sbuf("myb", [128, 2, 512], BF16)
                yc = S.sbuf("myc", [128, 2, 512], BF16)
                mT = S.sbuf("mT", [128, 8, 512], BF16)
                gt = [S.sbuf("gt", [128, 512], BF16) for _ in range(3)]
                t1 = S.sbuf("mt1", [128, 512], F32)
                t2 = S.sbuf("mt2", [128, 512], F32)
                oT = S.sbuf("oT", [128, 8, 512], F32)
                xt = [S.sbuf("mxt", [128, D], F32) for _ in range(2)]
                for (c0, nn) in GROUPS:
                    s = 1 if c0 == 0 else 0
                    for kc in range(8):
                        S.dma('sp', xg.t[:, kc, 0:nn], xnT_d[:, kc, c0:c0 + nn], writes=[xg])
                    for kc in range(4):
                        S.dma('sp', ya.t[:, kc, 0:nn], yaT_d[kc, :, c0:c0 + nn], writes=[ya])
                    for kc in range(2):
                        S.dma('sp', yb.t[:, kc, 0:nn], ybT_d[kc, :, c0:c0 + nn], writes=[yb])
                        S.dma('sp', yc.t[:, kc, 0:nn], ycT_d[kc, :, c0:c0 + nn], writes=[yc])
                    for mc in range(8):
                        brs = ((wa_, ya, 4), (wb_, yb, 2), (wc_, yc, 2))
                        for bi in range(3):
                            pg = PS[bi]
                            for kc in range(8):
                                S.op('pe', lambda g, pg=pg, kc=kc, bi=bi, mc=mc, nn=nn: g.matmul(pg.t[:, 0:nn], lhsT=wg.t[:, kc, bi * D + mc * 128:bi * D + (mc + 1) * 128], rhs=xg.t[:, kc, 0:nn], start=(kc == 0), stop=(kc == 7)), reads=[wg, xg], writes=[pg])
                            S.op('act', lambda g, pg=pg, bi=bi, mc=mc, nn=nn: g.activation(out=gt[bi].t[:, 0:nn], in_=pg.t[:, 0:nn], func=AF.Sigmoid, bias=bg.t[:, bi * 8 + mc:bi * 8 + mc + 1]), reads=[pg, bg], writes=[gt[bi]])
                            pbr = PS[3 + bi]
                            w_, y_, nk = brs[bi]
                            for kc in range(nk):
                                S.op('pe', lambda g, pbr=pbr, kc=kc, w_=w_, y_=y_, nk=nk, mc=mc, nn=nn: g.matmul(pbr.t[:, 0:nn], lhsT=w_.t[:, kc, mc * 128:(mc + 1) * 128], rhs=y_.t[:, kc, 0:nn], start=(kc == 0), stop=(kc == nk - 1)), reads=[w_, y_], writes=[pbr])
                        S.op('dve', lambda g, nn=nn: g.tensor_tensor(out=t1.t[:, 0:nn], in0=PS[3].t[:, 0:nn], in1=gt[0].t[:, 0:nn], op=ALU.mult), reads=[PS[3], gt[0]], writes=[t1])
                        S.op('dve', lambda g, nn=nn: g.tensor_tensor(out=t2.t[:, 0:nn], in0=PS[4].t[:, 0:nn], in1=gt[1].t[:, 0:nn], op=ALU.mult), reads=[PS[4], gt[1]], writes=[t2])
                        S.op('pool', lambda g, nn=nn: g.tensor_tensor(out=t1.t[:, 0:nn], in0=t1.t[:, 0:nn], in1=t2.t[:, 0:nn], op=ALU.add), reads=[t1, t2], writes=[t1])
                        S.op('dve', lambda g, nn=nn: g.tensor_tensor(out=t2.t[:, 0:nn], in0=PS[5].t[:, 0:nn], in1=gt[2].t[:, 0:nn], op=ALU.mult), reads=[PS[5], gt[2]], writes=[t2])
                        S.op('pool', lambda g, nn=nn, mc=mc: g.tensor_tensor(out=mT.t[:, mc, 0:nn], in0=t1.t[:, 0:nn], in1=t2.t[:, 0:nn], op=ALU.add), reads=[t1, t2], writes=[mT])
                    for oc in range(8):
                        po = PS[oc % 2]
                        for mc in range(8):
                            S.op('pe', lambda g, po=po, mc=mc, oc=oc, nn=nn: g.matmul(po.t[:, 0:nn], lhsT=wo_.t[:, mc, oc * 128:(oc + 1) * 128], rhs=mT.t[:, mc, 0:nn], start=(mc == 0), stop=(mc == 7)), reads=[wo_, mT], writes=[po])
                        S.op('act', lambda g, po=po, oc=oc, nn=nn, s=s: g.activation(out=oT.t[:, oc, 0:nn], in_=po.t[:, 0:nn], func=AF.Copy, scale=modT.t[:, 16 + oc, s:s + 1]), reads=[po, modT], writes=[oT])
                    for t4 in range(nn // 128):
                        tok0 = c0 + t4 * 128
                        if c0 == 0:
                            src = (c_in if first else cres)[tok0:tok0 + 128, :]
                            dst = cres[tok0:tok0 + 128, :]
                        else:
                            src = (x_in if first else xres)[tok0 - C:tok0 - C + 128, :]
                            dst = xres[tok0 - C:tok0 - C + 128, :]
                        x_ = xt[t4 % 2]
                        S.dma('sp', x_.t[:], src, writes=[x_])
                        for half in range(2):
                            pt_ = PS[2 + half]
                            for q in range(4):
                                oc = half * 4 + q
                                S.op('pe', lambda g, pt_=pt_, q=q, oc=oc, t4=t4: g.transpose(out=pt_.t[:, q * 128:(q + 1) * 128], in_=oT.t[:, oc, t4 * 128:(t4 + 1) * 128], identity=ident_f.t[:]), reads=[oT, ident_f], writes=[pt_])
                            S.op('dve', lambda g, pt_=pt_, x_=x_, half=half: g.tensor_tensor(out=x_.t[:, half * 512:(half + 1) * 512], in0=pt_.t[:], in1=x_.t[:, half * 512:(half + 1) * 512], op=ALU.add), reads=[pt_, x_], writes=[x_])
                        S.dma('sp', dst, x_.t[:], reads=[x_])

        def top16(src_ap, src_reads, vals, idxs, scr, vparts, iparts):
            S.op('dve', lambda g: g.max(out=vals[:, 0:8], in_=src_ap), reads=src_reads, writes=vparts)
            S.op('dve', lambda g: g.max_index(out=idxs[:, 0:8], in_max=vals[:, 0:8], in_values=src_ap), reads=src_reads + vparts, writes=iparts)
            S.op('dve', lambda g: g.match_replace(out=scr.t[:, 0:src_ap.shape[1]], in_to_replace=vals[:, 0:8], in_values=src_ap, imm_value=-1e30), reads=src_reads + vparts, writes=[scr])
            S.op('dve', lambda g: g.max(out=vals[:, 8:16], in_=scr.t[:, 0:src_ap.shape[1]]), reads=[scr], writes=vparts)
            S.op('dve', lambda g: g.max_index(out=idxs[:, 8:16], in_max=vals[:, 8:16], in_values=scr.t[:, 0:src_ap.shape[1]]), reads=[scr] + vparts, writes=iparts)

        def stage_peer(l):
            with Stage("peer"):
                stg = [S.sbuf("pstg", [128, 2048], F32)]
                wq = S.sbuf("wq", [128, 8, 2048], BF16)
                load_w_bf(wq, lambda kc: p_wq[l, kc * 128:(kc + 1) * 128, :], 8, 2048, stg)
                keysT = S.sbuf("keysT", [128, 16, 128], BF16)
                kst = [S.sbuf("kst", [128, 128], F32) for _ in range(2)]
                for hp in range(16):
                    ks = kst[hp % 2]
                    S.dma('sp', ks.t[:], p_keys[l, hp // 2, hp % 2], writes=[ks])
                    pk = PS[hp % 2]
                    S.op('pe', lambda g, pk=pk, ks=ks: g.transpose(out=pk.t[:, 0:128], in_=ks.t[:], identity=ident_f.t[:]), reads=[ks, ident_f], writes=[pk])
                    copy_op('dve', keysT.t[:, hp, :], pk.t[:, 0:128], [pk], [keysT])
                g5 = [S.sbuf("g5", [128, D], F32) for _ in range(2)]
                for s in range(2):
                    S.dma('sp', g5[s].t[:], bcd[8 + s], writes=[g5[s]])
                xg = S.sbuf("pxg", [128, 8, 512], BF16)
                qT = S.sbuf("pqT", [128, 16, 512], BF16)
                top = S.sbuf("ptop", [128, 16, 16], F32)
                it = S.sbuf("pit", [128, 16, 16], U32)
                itf = S.sbuf("pitf", [128, 16, 16], F32)
                scr = S.sbuf("pscr", [128, 256], F32)
                cand = S.sbuf("pcand", [128, 8, 256], F32)
                best = S.sbuf("pbest", [128, 8, 16], F32)
                pos = S.sbuf("ppos", [128, 8, 16], U32)
                pa = S.sbuf("ppa", [128, 8, 16], U32)
                paf = S.sbuf("ppaf", [128, 2, 8, 16], F32)
                eq = S.sbuf("peq", [128, 8, 16, 16], F32)
                isel = S.sbuf("pisel", [128, 2, 8, 16], F32)
                idxf = S.sbuf("pidxf", [128, 128], F32)
                idxu = [S.sbuf("pidxu", [128, 128], U32) for _ in range(2)]
                gw = S.sbuf("pgw", [128, 8, 16], F32)
                zs = S.sbuf("pzs", [128, 8], F32)
                dots = S.sbuf("pdots", [128, 128], F32)
                actv = [S.sbuf("pact", [128, 128], F32) for _ in range(2)]
                NR = 6
                ug = [S.sbuf("ug", [128, D], F32) for _ in range(NR)]
                vg = [S.sbuf("vg", [128, D], F32) for _ in range(NR)]
                dgs = [S.sbuf("pdg", [128, 128], F32) for _ in range(4)]
                xn = [S.sbuf("pxn", [128, D], BF16) for _ in range(2)]
                xt = [S.sbuf("pxt", [128, D], F32) for _ in range(2)]
                junk = S.sbuf("pjunk", [128, D], F32)
                ytmp = S.sbuf("pytmp", [128, D], F32)
                tile_i = 0
                for (c0, nn) in GROUPS:
                    s = 1 if c0 == 0 else 0
                    for kc in range(8):
                        S.dma('sp', xg.t[:, kc, 0:nn], xnT_d[:, kc, c0:c0 + nn], writes=[xg])
                    for hp in range(16):
                        ps = PS[hp % 2]
                        for kc in range(8):
                            S.op('pe', lambda g, ps=ps, kc=kc, hp=hp, nn=nn: g.matmul(ps.t[:, 0:nn], lhsT=wq.t[:, kc, hp * 128:(hp + 1) * 128], rhs=xg.t[:, kc, 0:nn], start=(kc == 0), stop=(kc == 7)), reads=[wq, xg], writes=[ps])
                        copy_op(alt(), qT.t[:, hp, 0:nn], ps.t[:, 0:nn], [ps], [qT])
                    for t4 in range(nn // 128):
                        tok0 = c0 + t4 * 128
                        ti = tile_i
                        tile_i += 1
                        xn_ = xn[ti % 2]
                        x_ = xt[ti % 2]
                        S.dma('sp', xn_.t[:], xntm_d[tok0:tok0 + 128, :], writes=[xn_])
                        xr = (cres[tok0:tok0 + 128, :] if c0 == 0 else xres[tok0 - C:tok0 - C + 128, :])
                        S.dma('sp', x_.t[:], xr, writes=[x_])
                        for hp in range(16):
                            ps = PS[2 + hp % 2]
                            S.op('pe', lambda g, ps=ps, hp=hp, t4=t4: g.matmul(ps.t[:, 0:128], lhsT=qT.t[:, hp, t4 * 128:(t4 + 1) * 128], rhs=keysT.t[:, hp, :], start=True, stop=True), reads=[qT, keysT], writes=[ps])
                            top16(ps.t[:, 0:128], [ps], top.t[:, hp, :], it.t[:, hp, :], scr, [top], [it])
                        copy_op('dve', itf.t[:], it.t[:], [it], [itf])
                        tv = top.t[:].rearrange("p (h q) k -> p h q k", q=2)
                        S.op('dve', lambda g, tv=tv: g.tensor_tensor(out=cand.t[:].rearrange("p h (a b) -> p h a b", a=16), in0=tv[:, :, 0, :].unsqueeze(3).broadcast_to([128, 8, 16, 16]), in1=tv[:, :, 1, :].unsqueeze(2).broadcast_to([128, 8, 16, 16]), op=ALU.add), reads=[top], writes=[cand])
                        for h in range(8):
                            top16(cand.t[:, h, :], [cand], best.t[:, h, :], pos.t[:, h, :], scr, [best], [pos])
                        S.op('dve', lambda g: g.tensor_tensor(out=gw.t[:], in0=best.t[:], in1=best.t[:, :, 0:1].broadcast_to([128, 8, 16]), op=ALU.subtract), reads=[best], writes=[gw])
                        S.op('act', lambda g: g.activation(out=gw.t[:], in_=gw.t[:], func=AF.Exp), reads=[gw], writes=[gw])
                        S.op('dve', lambda g: g.tensor_reduce(out=zs.t[:], in_=gw.t[:], axis=AX.X, op=ALU.add), reads=[gw], writes=[zs])
                        S.op('dve', lambda g: g.reciprocal(out=zs.t[:], in_=zs.t[:]), reads=[zs], writes=[zs])
                        S.op('dve', lambda g: g.tensor_tensor(out=gw.t[:], in0=gw.t[:], in1=zs.t[:].unsqueeze(2).broadcast_to([128, 8, 16]), op=ALU.mult), reads=[gw, zs], writes=[gw])
                        S.op('dve', lambda g: g.tensor_single_scalar(out=pa.t[:], in_=pos.t[:], scalar=4, op=ALU.logical_shift_right), reads=[pos], writes=[pa])
                        copy_op('dve', paf.t[:, 0], pa.t[:], [pa], [paf])
                        S.op('dve', lambda g: g.tensor_single_scalar(out=pa.t[:], in_=pos.t[:], scalar=15, op=ALU.bitwise_and), reads=[pos, paf], writes=[pa])
                        copy_op('dve', paf.t[:, 1], pa.t[:], [pa], [paf])
                        itv = itf.t[:].rearrange("p (h q) k -> p h q k", q=2)
                        for q in range(2):
                            S.op('dve', lambda g, q=q: g.tensor_tensor(out=eq.t[:], in0=paf.t[:, q].unsqueeze(3).broadcast_to([128, 8, 16, 16]), in1=iota16.t[:].unsqueeze(1).unsqueeze(1).broadcast_to([128, 8, 16, 16]), op=ALU.is_equal), reads=[paf, iota16], writes=[eq])
                            S.op('dve', lambda g, q=q, itv=itv: g.tensor_tensor(out=eq.t[:], in0=eq.t[:], in1=itv[:, :, q, :].unsqueeze(2).broadcast_to([128, 8, 16, 16]), op=ALU.mult), reads=[eq, itf], writes=[eq])
                            S.op('dve', lambda g, q=q: g.tensor_reduce(out=isel.t[:, q], in_=eq.t[:], axis=AX.X, op=ALU.add), reads=[eq], writes=[isel])
                        S.op('dve', lambda g: g.scalar_tensor_tensor(out=idxf.t[:].rearrange("p (h k) -> p h k", h=8), in0=isel.t[:, 0], scalar=128.0, in1=isel.t[:, 1], op0=ALU.mult, op1=ALU.add), reads=[isel], writes=[idxf])
                        iu = idxu[ti % 2]
                        copy_op('dve', iu.t[:], idxf.t[:], [idxf], [iu])
                        for hk in range(128):
                            u_ = ug[hk % NR]
                            S.dma('pool', None, None, reads=[iu], writes=[u_], fn=lambda g, u_=u_, iu=iu, hk=hk: g.indirect_dma_start(out=u_.t[:], out_offset=None, in_=p_u.rearrange('l e d -> (l e) d'), in_offset=bass.IndirectOffsetOnAxis(ap=iu.t[:, hk:hk + 1], axis=0), element_offset=l * 16384 * D))
                            S.op('dve', lambda g, u_=u_, xn_=xn_, hk=hk: g.scalar_tensor_tensor(out=junk.t[:], in0=u_.t[:], scalar=1.0, in1=xn_.t[:], op0=ALU.mult, op1=ALU.mult, accum_out=dots.t[:, hk:hk + 1]), reads=[u_, xn_], writes=[dots.part(hk)])
                        av = actv[ti % 2]
                        S.op('act', lambda g, av=av: g.activation(out=av.t[:], in_=dots.t[:], func=AF.Gelu), reads=[dots.part(hk) for hk in range(128)], writes=[av])
                        S.op('dve', lambda g, av=av: g.tensor_tensor(out=av.t[:], in0=av.t[:], in1=gw.t[:].rearrange("p h k -> p (h k)"), op=ALU.mult), reads=[av, gw], writes=[av])
                        py = [PS[4], PS[5]]
                        for hk in range(128):
                            v_ = vg[hk % NR]
                            S.dma('pool', None, None, reads=[iu], writes=[v_], fn=lambda g, v_=v_, iu=iu, hk=hk: g.indirect_dma_start(out=v_.t[:], out_offset=None, in_=p_v.rearrange('l e d -> (l e) d'), in_offset=bass.IndirectOffsetOnAxis(ap=iu.t[:, hk:hk + 1], axis=0), element_offset=l * 16384 * D))
                            dg = dgs[hk % 4]
                            S.op('pool', lambda g, dg=dg, av=av, hk=hk: g.tensor_scalar(out=dg.t[:], in0=ident_f.t[:], scalar1=av.t[:, hk:hk + 1], scalar2=None, op0=ALU.mult), reads=[ident_f, av], writes=[dg])
                            for half in range(2):
                                S.op('pe', lambda g, dg=dg, v_=v_, half=half, hk=hk: g.matmul(py[half].t[:], lhsT=dg.t[:], rhs=v_.t[:, half * 512:(half + 1) * 512], start=(hk == 0), stop=(hk == 127)), reads=[dg, v_], writes=[py[half]])
                        for half in range(2):
                            S.op('dve', lambda g, half=half, s=s: g.tensor_tensor(out=ytmp.t[:, half * 512:(half + 1) * 512], in0=py[half].t[:], in1=g5[s].t[:, half * 512:(half + 1) * 512], op=ALU.mult), reads=[py[half], g5[s]], writes=[ytmp])
                        S.op('dve', lambda g, x_=x_: g.tensor_tensor(out=x_.t[:], in0=x_.t[:], in1=ytmp.t[:], op=ALU.add), reads=[x_, ytmp], writes=[x_])
                        S.dma('sp', xr, x_.t[:], reads=[x_])

        def stage_final():
            with Stage("final"):
                gf = S.sbuf("gf", [128, D], F32)
                S.dma('sp', gf.t[:], gfin, writes=[gf])
                xin = [S.sbuf("fx", [128, D], F32) for _ in range(3)]
                junk = S.sbuf("fj", [128, D], BF16)
                yo = [S.sbuf("fy", [128, D], F32) for _ in range(2)]
                ss = S.sbuf("fss", [128, 32], F32)
                t1 = S.sbuf("ft1", [128, 32], F32)
                for tt in range(32):
                    xt = xin[tt % 3]
                    S.dma('sp', xt.t[:], xres[tt * 128:(tt + 1) * 128, :], writes=[xt])
                    S.op('act', lambda g, xt=xt, tt=tt: g.activation(out=junk.t[:], in_=xt.t[:], func=AF.Square, accum_out=ss.t[:, tt:tt + 1]), reads=[xt], writes=[junk, ss.part(tt)])
                    S.op('dve', lambda g, tt=tt: g.tensor_scalar(out=t1.t[:, tt:tt + 1], in0=ss.t[:, tt:tt + 1], scalar1=1.0 / D, scalar2=EPS, op0=ALU.mult, op1=ALU.add), reads=[ss.part(tt)], writes=[t1.part(tt)])
                    S.op('act', lambda g, tt=tt: g.activation(out=t1.t[:, tt:tt + 1], in_=t1.t[:, tt:tt + 1], func=AF.Sqrt), reads=[t1.part(tt)], writes=[t1.part(tt)])
                    S.op('dve', lambda g, tt=tt: g.reciprocal(out=t1.t[:, tt:tt + 1], in_=t1.t[:, tt:tt + 1]), reads=[t1.part(tt)], writes=[t1.part(tt)])
                    y_ = yo[tt % 2]
                    S.op('dve', lambda g, y_=y_, xt=xt, tt=tt: g.scalar_tensor_tensor(out=y_.t[:], in0=xt.t[:], scalar=t1.t[:, tt:tt + 1], in1=gf.t[:], op0=ALU.mult, op1=ALU.mult), reads=[xt, t1.part(tt), gf], writes=[y_])
                    S.dma('sp', y_out[tt * 128:(tt + 1) * 128, :], y_.t[:], reads=[y_])

        S.barrier()
        S.flush()
        plan = []
        for l in range(DEPTH):
            first = (l == 0)
            plan += [("mod", lambda l=l: stage_mod(l)),
                     ("norm1", lambda l=l, first=first: stage_norm(l, 0, first)),
                     ("proj", lambda l=l: stage_proj(l)),
                     ("attn", lambda l=l: stage_attn(l)),
                     ("hyfc", lambda l=l: stage_hy_filt(l, C, zT_c, dec_c, ed_c)),
                     ("hysc_c", lambda l=l: stage_hy_sc(l, C, 0, ub_c)),
                     ("hytc", lambda l=l: stage_hy_toep(l, C, 0, ed_c, ub_c)),
                     ("hyfm", lambda l=l: stage_hy_filt(l, T, zT_m, dec_m, ed_m)),
                     ("hysc_m", lambda l=l: stage_hy_sc(l, T, C, ub_m)),
                     ("hytm", lambda l=l: stage_hy_toep(l, T, C, ed_m, ub_m)),
                     ("rglru", lambda l=l: stage_rglru(l)),
                     ("merge", lambda l=l, first=first: stage_merge(l, first)),
                     ("norm2", lambda l=l: stage_norm(l, 1, False)),
                     ("peer", lambda l=l: stage_peer(l))]
        plan.append(("final", stage_final))
        for i, (nm, f) in enumerate(plan):
            f()
            if stop_after is not None and i + 1 >= stop_after:
                break
        S.finish()
        build.n_instr = S.n_instr
    return nc


def _chunkT(v, n):
    return np.ascontiguousarray(np.asarray(v, np.float32).reshape(n, 128).T)


def _consts():
    f32 = np.float32
    out = {}
    for nm, L in (("m", T), ("c", C)):
        t = np.linspace(0.0, 1.0, L, dtype=f32)[:, None]
        w = (f32(2.0 * math.pi / L) * np.arange(L, dtype=f32))[:, None]
        bands = np.linspace(1e-4, 15, 16, dtype=f32)[None, :]
        z = np.concatenate([t, np.cos(bands * w), -np.sin(bands * w)], axis=-1).astype(f32)
        deltas = np.linspace(math.log(1e-2) / 1.5, math.log(1e-2) / 0.3, 256, dtype=f32)
        decay = np.exp(-t * np.abs(deltas)[None, :]).astype(f32)
        out["zT_" + nm] = np.ascontiguousarray(z.T)
        out["dec_" + nm] = np.ascontiguousarray(decay.T.reshape(2, 128, L).transpose(1, 0, 2))
    out["ident"] = np.eye(128, dtype=f32)
    out["iota16"] = np.ascontiguousarray(np.broadcast_to(np.arange(16, dtype=f32)[None, :], (128, 16)))
    osel = np.zeros((128, 2, 128), f32)
    osel[:, 0, 0:64] = 1.0
    osel[:, 1, 64:128] = 1.0
    out["onesel"] = osel
    return out


def _rpb_table(rpb):
    wq = np.arange(64)
    start = np.clip(wq - 8, 0, 48)
    wk = np.arange(64)
    colok = (wk[:, None] >= start[None, :]) & (wk[:, None] < start[None, :] + 16)
    dc = np.clip(wk[:, None] - wq[None, :] + 15, 0, 30)
    tab = np.full((DEPTH, 128, 8, 22, 64), NEG, np.float32)
    for half in range(2):
        for ep in range(22):
            e = ep - 3 - half
            if e < 0 or e > 14:
                continue
            dr = 14 - e
            g = rpb[:, :, dr][:, :, dc]
            g = np.where(colok[None, None], g, np.float32(NEG))
            tab[:, half * 64:(half + 1) * 64, :, ep, :] = g.transpose(0, 2, 1, 3)
    return np.ascontiguousarray(tab.reshape(DEPTH, 128, 8, 22 * 64))


_NC_CACHE = {}


def kernel(**inp):
    f32 = np.float32
    g = lambda k: np.asarray(inp[k], f32)
    shared = dict(_consts())
    shared["gmix"] = np.stack([_chunkT(g("norm_mix_g")[l], 8) for l in range(DEPTH)])
    shared["gffn"] = np.stack([_chunkT(g("norm_ffn_g")[l], 8) for l in range(DEPTH)])
    shared["gfin"] = np.ascontiguousarray(np.broadcast_to(g("final_g")[None, :], (128, D)))
    shared["w_ada"] = g("w_ada")
    shared["b_adaT"] = np.stack([_chunkT(g("b_ada")[l], 48) for l in range(DEPTH)])
    shared["w_in"] = g("w_in")
    shared["tab"] = _rpb_table(g("na_rpb"))
    shared["hy_swT"] = np.ascontiguousarray(g("hy_short_w").reshape(DEPTH, 3, 6, 128).transpose(0, 3, 2, 1))
    shared["hy_sbT"] = np.stack([_chunkT(g("hy_short_b")[l], 6) for l in range(DEPTH)])
    shared["hy_f1w"] = g("hy_f1_w")
    shared["hy_f1b"] = g("hy_f1_b")[:, :, None].copy()
    shared["hy_f2w"] = g("hy_f2_w")
    shared["hy_f2b"] = g("hy_f2_b")[:, :, None].copy()
    shared["hy_f3w"] = g("hy_f3_w")
    shared["hy_fq"] = np.ascontiguousarray(g("hy_freq").transpose(0, 2, 1))
    shared["hy_dsk"] = np.stack([_chunkT(g("hy_bias")[l], 2) for l in range(DEPTH)])
    shared["rg_cw"] = np.ascontiguousarray(g("rg_conv_w").reshape(DEPTH, 4, 2, 128).transpose(0, 3, 2, 1))
    shared["rg_cb"] = np.stack([_chunkT(g("rg_conv_b")[l], 2) for l in range(DEPTH)])
    shared["rg_wa"] = g("rg_wa")
    shared["rg_wx"] = g("rg_wx")
    for nm, k in (("rg_baT", "rg_ba"), ("rg_bxT", "rg_bx"), ("rg_lamT", "rg_lambda")):
        shared[nm] = np.ascontiguousarray(g(k).reshape(DEPTH, 2, 2, 128).transpose(0, 3, 2, 1))
    shared["w_gate"] = g("w_gate")
    shared["b_gateT"] = np.stack([_chunkT(g("b_gate")[l], 24) for l in range(DEPTH)])
    shared["w_br_a"] = g("w_br_a")
    shared["w_br_b"] = g("w_br_b")
    shared["w_br_c"] = g("w_br_c")
    shared["w_out"] = g("w_out")
    shared["peer_wq"] = g("peer_wq")
    shared["peer_keys"] = g("peer_keys")
    shared["peer_u"] = g("peer_u")
    shared["peer_v"] = g("peer_v")
    x = g("x")
    ctx = g("ctx")
    c = g("c")
    cc = g("c_ctx")
    nb = x.shape[0]
    in_maps = []
    for b in range(nb):
        m = dict(shared)
        m["x"] = np.ascontiguousarray(x[b])
        m["ctx"] = np.ascontiguousarray(ctx[b])
        m["cvec"] = np.ascontiguousarray(np.stack([_chunkT(c[b], 8), _chunkT(cc, 8)], axis=-1))
        in_maps.append(m)
    if "nc" not in _NC_CACHE:
        _NC_CACHE["nc"] = build()
    nc = _NC_CACHE["nc"]
    res = run_bass_kernel_spmd(nc, in_maps, core_ids=list(range(nb)))
    return np.stack([np.asarray(r["y"], f32) for r in res.results], axis=0)
```

```python
import numpy as np
import concourse.bass as bass
import concourse.mybir as mybir
from concourse.bass_utils import run_bass_kernel_spmd
from contextlib import ExitStack

F32 = mybir.dt.float32
BF16 = mybir.dt.bfloat16
I32 = mybir.dt.int32
U32 = mybir.dt.uint32
U16 = mybir.dt.uint16
AF = mybir.ActivationFunctionType
ALU = mybir.AluOpType
AX = mybir.AxisListType

ENGS = ['pe', 'dve', 'act', 'pool', 'sp']
EPOCH = 30000
N_DMA_SEMS = 12


class Res:
    __slots__ = ('w', 'r')

    def __init__(self):
        self.w = {}
        self.r = {}


class Buf:
    def __init__(self, t):
        self.t = t
        self.res = Res()
        self.parts = {}

    def part(self, key):
        r = self.parts.get(key)
        if r is None:
            r = Res()
            self.parts[key] = r
        return r


def _res(x):
    return x.res if isinstance(x, Buf) else x


class Sched:
    def __init__(self, nc, es):
        self.nc = nc
        self.es = es
        self.ops = {e: [] for e in ENGS}
        self.sem = {}
        self.cnt = {}
        self.known = {e: {} for e in ENGS}
        self.nsem = 0
        for e in ENGS:
            self._new_epoch(e)
        self.dma_sems = [self._mksem("dq%d" % i) for i in range(N_DMA_SEMS)]
        self.dma_cnt = [0] * N_DMA_SEMS
        self.dma_rr = 0
        self.all_sems = {}
        self.n_instr = 0
        self.stage_es = None

    def _mksem(self, name):
        self.nsem += 1
        return self.es.enter_context(self.nc.semaphore("%s_%d" % (name, self.nsem)))

    def _new_epoch(self, e):
        self.sem[e] = self._mksem("e" + e)
        self.cnt[e] = 0

    def sbuf(self, name, shape, dtype):
        self.nbuf = getattr(self, 'nbuf', 0) + 1
        es = self.stage_es if self.stage_es is not None else self.es
        return Buf(es.enter_context(self.nc.sbuf_tensor("%s_%d" % (name, self.nbuf), shape, dtype)))

    def psum(self, name, shape, dtype):
        return Buf(self.es.enter_context(self.nc.psum_tensor(name, shape, dtype)))

    def barrier(self):
        for e in ENGS:
            kn = self.known[e]
            for sid, (sem, val) in self.all_sems.items():
                if kn.get(sid, 0) < val:
                    kn[sid] = val
                    self.ops[e].append(('wait', sem, val))
            for o in ENGS:
                if o == e or o == 'sp' or self.cnt[o] == 0:
                    continue
                sem = self.sem[o]
                if kn.get(id(sem), 0) < self.cnt[o]:
                    kn[id(sem)] = self.cnt[o]
                    self.ops[e].append(('wait', sem, self.cnt[o]))

    def dram(self, name, shape, dtype):
        return Buf(self.nc.dram_tensor(name, shape, dtype, kind="Internal"))

    def _collect(self, e, reads, writes):
        need = {}

        def mrg(d):
            for s, v in d.items():
                if need.get(s, (None, 0))[1] < v[1]:
                    need[s] = v
        for r in reads:
            mrg(_res(r).w)
        for w in writes:
            w = _res(w)
            mrg(w.w)
            mrg(w.r)
        kn = self.known[e]
        for sid, (sem, val, eng) in need.items():
            if eng == 'pe' and e == 'pe':
                continue
            if kn.get(sid, 0) < val:
                kn[sid] = val
                self.ops[e].append(('wait', sem, val))

    def _publish(self, key, reads, writes):
        sid = id(key[0])
        for w in writes:
            w = _res(w)
            w.w = {sid: key}
            w.r = {}
        for r in reads:
            _res(r).r[sid] = key

    def op(self, e, fn, reads=(), writes=()):
        self._collect(e, reads, writes)
        if self.cnt[e] >= EPOCH:
            self._new_epoch(e)
        self.cnt[e] += 1
        sem = self.sem[e]
        self.ops[e].append(('op', fn, sem))
        self._publish((sem, self.cnt[e], e), reads, writes)
        self.n_instr += 1

    def dma(self, e, out, in_, reads=(), writes=(), fn=None, **kw):
        self._collect(e, reads, writes)
        i = self.dma_rr
        self.dma_rr = (i + 1) % N_DMA_SEMS
        self.dma_cnt[i] += 16
        if self.dma_cnt[i] >= EPOCH:
            self.dma_sems[i] = self._mksem("dq")
            self.dma_cnt[i] = 16
        sem = self.dma_sems[i]
        if fn is None:
            def fn(g, out=out, in_=in_, kw=kw):
                return g.dma_start(out=out, in_=in_, **kw)
        self.ops[e].append(('dma', fn, sem))
        self._publish((sem, self.dma_cnt[i], 'dma'), reads, writes)
        self.all_sems[id(sem)] = (sem, self.dma_cnt[i])
        self.n_instr += 1

    def finish(self):
        sp = self.ops['sp']
        for sid, (sem, val) in self.all_sems.items():
            sp.append(('wait', sem, val))
        for e in ENGS:
            if e != 'sp' and self.cnt[e] > 0:
                sp.append(('wait', self.sem[e], self.cnt[e]))
        self.flush()

    def flush(self):
        nc = self.nc
        ops = self.ops
        self.ops = {e: [] for e in ENGS}

        def emit(eng, lst):
            for it in lst:
                if it[0] == 'wait':
                    eng.wait_ge(it[1], it[2])
                elif it[0] == 'op':
                    it[1](eng).then_inc(it[2], 1)
                else:
                    it[1](eng).then_inc(it[2], 16)

        with nc.Block() as block:
            @block.tensor
            def _(g):
                emit(g, ops['pe'])

            @block.vector
            def _(g):
                emit(g, ops['dve'])

            @block.scalar
            def _(g):
                emit(g, ops['act'])

            @block.gpsimd
            def _(g):
                emit(g, ops['pool'])

            @block.sync
            def _(g):
                emit(g, ops['sp'])


import math

D = 1024
T = 4096
C = 256
NTOK = T + C
DEPTH = 2
EPS = 1e-6
GW = 64
NEG = -30000.0
DBG = False
STAGES = None


def _groups():
    g = [(0, C)]
    for i in range(T // 512):
        g.append((C + 512 * i, 512))
    return g


def build(dbg=False, stop_after=None):
    nc = bass.Bass("TRN2", target_bir_lowering=False)

    def din(name, shape, dt=F32):
        return nc.dram_tensor(name, list(shape), dt, kind="ExternalInput").ap()

    kind_s = "ExternalOutput" if dbg else "Internal"

    def dscr(name, shape, dt):
        return nc.dram_tensor(name, list(shape), dt, kind=kind_s).ap()

    x_in = din("x", [T, D])
    c_in = din("ctx", [C, D])
    cvec = din("cvec", [128, 8, 2])
    gmix = din("gmix", [DEPTH, 128, 8])
    gffn = din("gffn", [DEPTH, 128, 8])
    gfin = din("gfin", [128, D])
    w_ada = din("w_ada", [DEPTH, D, 6 * D])
    b_adaT = din("b_adaT", [DEPTH, 128, 48])
    w_in = din("w_in", [DEPTH, D, 2816])
    tab_in = din("tab", [DEPTH, 128, 8, 22 * 64])
    hy_swT = din("hy_swT", [DEPTH, 128, 6, 3])
    hy_sbT = din("hy_sbT", [DEPTH, 128, 6])
    hy_f1w = din("hy_f1w", [DEPTH, 33, 64])
    hy_f1b = din("hy_f1b", [DEPTH, 64, 1])
    hy_f2w = din("hy_f2w", [DEPTH, 64, 64])
    hy_f2b = din("hy_f2b", [DEPTH, 64, 1])
    hy_f3w = din("hy_f3w", [DEPTH, 64, 512])
    hy_fq = din("hy_fq", [DEPTH, 64, 2])
    hy_dsk = din("hy_dsk", [DEPTH, 128, 2])
    zT_m = din("zT_m", [33, T])
    zT_c = din("zT_c", [33, C])
    dec_m = din("dec_m", [128, 2, T])
    dec_c = din("dec_c", [128, 2, C])
    rg_cw = din("rg_cw", [DEPTH, 128, 2, 4])
    rg_cb = din("rg_cb", [DEPTH, 128, 2])
    rg_wa = din("rg_wa", [DEPTH, 2, 4, 64, 64])
    rg_wx = din("rg_wx", [DEPTH, 2, 4, 64, 64])
    rg_baT = din("rg_baT", [DEPTH, 128, 2, 2])
    rg_bxT = din("rg_bxT", [DEPTH, 128, 2, 2])
    rg_lamT = din("rg_lamT", [DEPTH, 128, 2, 2])
    w_gate = din("w_gate", [DEPTH, D, 3 * D])
    b_gateT = din("b_gateT", [DEPTH, 128, 24])
    w_bra = din("w_br_a", [DEPTH, 512, D])
    w_brb = din("w_br_b", [DEPTH, 256, D])
    w_brc = din("w_br_c", [DEPTH, 256, D])
    w_out = din("w_out", [DEPTH, D, D])
    p_wq = din("peer_wq", [DEPTH, D, 2048])
    p_keys = din("peer_keys", [DEPTH, 8, 2, 128, 128])
    p_u = din("peer_u", [DEPTH, 16384, D])
    p_v = din("peer_v", [DEPTH, 16384, D])
    ident_in = din("ident", [128, 128])
    iota_in = din("iota16", [128, 16])
    osel_in = din("onesel", [128, 2, 128])
    y_out = nc.dram_tensor("y", [T, D], F32, kind="ExternalOutput").ap()

    xres = dscr("xres", [T, D], F32)
    cres = dscr("cres", [C, D], F32)
    bcd = dscr("bcd", [10, 128, D], F32)
    xnT_d = dscr("xnT_d", [128, 8, NTOK], BF16)
    xntm_d = dscr("xntm_d", [NTOK, D], BF16)
    qkT_d = dscr("qkT_d", [8, 128, NTOK], BF16)
    vp_d = dscr("vp_d", [4, NTOK, 256], BF16)
    phT_d = dscr("phT_d", [6, 128, NTOK], F32)
    prT_d = dscr("prT_d", [4, 128, NTOK], F32)
    yaT_d = dscr("yaT_d", [4, 128, NTOK], BF16)
    ybT_d = dscr("ybT_d", [2, 128, NTOK], BF16)
    ycT_d = dscr("ycT_d", [2, 128, NTOK], BF16)
    ed_m = dscr("ed_m", [256, 2 * T], BF16)
    ed_c = dscr("ed_c", [256, 2 * C], BF16)
    ub_m = dscr("ub_m", [128, 2 * 128 * 32], BF16)
    ub_c = dscr("ub_c", [128, 2 * 128 * 2], BF16)
    x0_d = dscr("x0_d", [2, 128, NTOK], BF16)
    uv_d = nc.dram_tensor("uv_d", [16384, 2 * D], BF16, kind="Internal").ap()
    dbg_h = dscr("dbg_h", [2, 64, T], F32) if dbg else None

    ges = ExitStack()
    with ges:
        S = Sched(nc, ges)
        PS = [S.psum("ps%d" % i, [128, 512], F32) for i in range(6)]
        PSB = [S.psum("psb%d" % i, [128, 1024], BF16) for i in range(2)]
        ident_f = S.sbuf("ident_f", [128, 128], F32)
        ident_b = S.sbuf("ident_b", [128, 128], BF16)
        ones_f = S.sbuf("ones_f", [128, 128], F32)
        iota16 = S.sbuf("iota16", [128, 16], F32)
        osel = S.sbuf("osel", [128, 2, 128], BF16)
        osel_f = S.sbuf("osel_f", [128, 2, 128], F32)
        modT = S.sbuf("modT", [128, 48, 2], F32)
        Gs = S.sbuf("Gs", [128, 2, 8, 2], F32)
        S.dma('sp', ident_f.t[:], ident_in, writes=[ident_f])
        S.dma('sp', iota16.t[:], iota_in, writes=[iota16])
        S.dma('sp', osel_f.t[:], osel_in, writes=[osel_f])
        S.op('dve', lambda g: g.tensor_copy(out=ident_b.t[:], in_=ident_f.t[:]), reads=[ident_f], writes=[ident_b])
        S.op('dve', lambda g: g.tensor_copy(out=osel.t[:], in_=osel_f.t[:]), reads=[osel_f], writes=[osel])
        S.op('dve', lambda g: g.memset(ones_f.t[:], 1.0), writes=[ones_f])
        rr = {'n': 0}

        def alt(engs=('dve', 'act')):
            rr['n'] += 1
            return engs[rr['n'] % len(engs)]

        def copy_op(e, out, in_, reads, writes):
            if e == 'act':
                S.op('act', lambda g: g.activation(out=out, in_=in_, func=AF.Copy), reads=reads, writes=writes)
            else:
                S.op(e, lambda g: g.tensor_copy(out=out, in_=in_), reads=reads, writes=writes)

        class Stage:
            def __init__(self, name):
                self.name = name

            def __enter__(self):
                self.es = ExitStack()
                self.es.__enter__()
                S.stage_es = self.es
                return self

            def __exit__(self, *a):
                S.barrier()
                S.flush()
                S.stage_es = None
                self.es.__exit__(None, None, None)
                return False

        GROUPS = _groups()

        def stage_mod(l):
            with Stage("mod"):
                sc = S.sbuf("sc", [128, 8, 2], F32)
                ba = S.sbuf("ba", [128, 48], F32)
                gm = S.sbuf("gm", [128, 2, 8], F32)
                wb = [S.sbuf("wada", [128, 6144], F32) for _ in range(2)]
                diag = [S.sbuf("diag", [128, 128], F32) for _ in range(2)]
                bct = [S.sbuf("bct", [128, D], F32) for _ in range(2)]
                S.dma('sp', sc.t[:], cvec, writes=[sc])
                S.dma('sp', ba.t[:], b_adaT[l], writes=[ba])
                S.dma('sp', gm.t[:, 0, :], gmix[l], writes=[gm])
                S.dma('sp', gm.t[:, 1, :], gffn[l], writes=[gm])
                S.op('act', lambda g: g.activation(out=sc.t[:], in_=sc.t[:], func=AF.Silu), reads=[sc], writes=[sc])
                ps = PS[0]
                S.op('dve', lambda g: g.memset(ps.t[:, 0:96], 0.0), writes=[ps])
                for kc in range(8):
                    w = wb[kc % 2]
                    for hh in range(2):
                        S.dma('sp', w.t[:, hh * 3072:(hh + 1) * 3072], w_ada[l, kc * 128:(kc + 1) * 128, hh * 3072:(hh + 1) * 3072], writes=[w])
                    for j in range(48):
                        S.op('pe', lambda g, w=w, j=j, kc=kc: g.matmul(ps.t[:, 2 * j:2 * j + 2], lhsT=w.t[:, j * 128:(j + 1) * 128], rhs=sc.t[:, kc, :], start=False, stop=False, skip_group_check=True), reads=[w, sc], writes=[ps])
                S.op('dve', lambda g: g.tensor_tensor(out=modT.t[:], in0=ps.t[:, 0:96].rearrange("p (j s) -> p j s", s=2), in1=ba.t[:].unsqueeze(2).broadcast_to([128, 48, 2]), op=ALU.add), reads=[ps, ba], writes=[modT])
                for wh, off in ((0, 8), (1, 32)):
                    S.op('dve', lambda g, wh=wh, off=off: g.tensor_scalar(out=Gs.t[:, wh], in0=modT.t[:, off:off + 8, :], scalar1=1.0, scalar2=None, op0=ALU.add), reads=[modT], writes=[Gs])
                    S.op('dve', lambda g, wh=wh: g.tensor_tensor(out=Gs.t[:, wh], in0=Gs.t[:, wh], in1=gm.t[:, wh, :].unsqueeze(2).broadcast_to([128, 8, 2]), op=ALU.mult), reads=[Gs, gm], writes=[Gs])
                def srcs(idx):
                    if idx < 8:
                        wh, kind, s = idx // 4, (idx // 2) % 2, idx % 2
                        if kind == 0:
                            return lambda kc: Gs.t[:, wh, kc, s:s + 1]
                        off = 0 if wh == 0 else 24
                        return lambda kc: modT.t[:, off + kc, s:s + 1]
                    s = idx - 8
                    return lambda kc: modT.t[:, 40 + kc, s:s + 1]
                n = 0
                for idx in range(10):
                    f = srcs(idx)
                    bt = bct[idx % 2]
                    for half in range(2):
                        pb = PS[1 + (n % 2)]
                        n += 1
                        for k4 in range(4):
                            kc = half * 4 + k4
                            dg = diag[kc % 2]
                            S.op('dve', lambda g, dg=dg, f=f, kc=kc: g.tensor_scalar(out=dg.t[:], in0=ident_f.t[:], scalar1=f(kc), scalar2=None, op0=ALU.mult), reads=[ident_f, Gs, modT], writes=[dg])
                            S.op('pe', lambda g, pb=pb, dg=dg, k4=k4: g.matmul(pb.t[:, k4 * 128:(k4 + 1) * 128], lhsT=ones_f.t[:], rhs=dg.t[:], start=True, stop=True, skip_group_check=True), reads=[ones_f, dg], writes=[pb])
                        copy_op('act', bt.t[:, half * 512:(half + 1) * 512], pb.t[:], [pb], [bt])
                    S.dma('sp', bcd[idx], bt.t[:], reads=[bt])

        def stage_norm(l, wh, first):
            with Stage("norm"):
                Gb = [S.sbuf("Gb", [128, D], F32) for _ in range(2)]
                Sb = [S.sbuf("Sb", [128, D], F32) for _ in range(2)]
                for s in range(2):
                    S.dma('sp', Gb[s].t[:], bcd[wh * 4 + 0 + s], writes=[Gb[s]])
                    S.dma('sp', Sb[s].t[:], bcd[wh * 4 + 2 + s], writes=[Sb[s]])
                xnT = S.sbuf("xnT", [128, 8, NTOK], BF16)
                xin = [S.sbuf("xin", [128, D], F32) for _ in range(3)]
                junk = S.sbuf("junk", [128, D], BF16)
                tmp = [S.sbuf("tmpn", [128, D], F32) for _ in range(2)]
                xnb = [S.sbuf("xnb", [128, D], BF16) for _ in range(2)]
                ss = S.sbuf("ss", [128, 34], F32)
                t1 = S.sbuf("t1", [128, 34], F32)
                rstd = S.sbuf("rstd", [128, 34], F32)
                for tt in range(34):
                    s = 1 if tt < 2 else 0
                    if tt < 2:
                        src = (c_in if first else cres)[tt * 128:(tt + 1) * 128, :]
                    else:
                        src = (x_in if first else xres)[(tt - 2) * 128:(tt - 1) * 128, :]
                    xt = xin[tt % 3]
                    S.dma('sp', xt.t[:], src, writes=[xt])
                    S.op('act', lambda g, xt=xt, tt=tt: g.activation(out=junk.t[:], in_=xt.t[:], func=AF.Square, accum_out=ss.t[:, tt:tt + 1]), reads=[xt], writes=[junk, ss.part(tt)])
                    S.op('dve', lambda g, tt=tt: g.tensor_scalar(out=t1.t[:, tt:tt + 1], in0=ss.t[:, tt:tt + 1], scalar1=1.0 / D, scalar2=EPS, op0=ALU.mult, op1=ALU.add), reads=[ss.part(tt)], writes=[t1.part(tt)])
                    S.op('act', lambda g, tt=tt: g.activation(out=t1.t[:, tt:tt + 1], in_=t1.t[:, tt:tt + 1], func=AF.Sqrt), reads=[t1.part(tt)], writes=[t1.part(tt)])
                    S.op('dve', lambda g, tt=tt: g.reciprocal(out=rstd.t[:, tt:tt + 1], in_=t1.t[:, tt:tt + 1]), reads=[t1.part(tt)], writes=[rstd.part(tt)])
                    tm = tmp[tt % 2]
                    xb = xnb[tt % 2]
                    S.op('dve', lambda g, tm=tm, xt=xt, tt=tt, s=s: g.scalar_tensor_tensor(out=tm.t[:], in0=xt.t[:], scalar=rstd.t[:, tt:tt + 1], in1=Gb[s].t[:], op0=ALU.mult, op1=ALU.mult), reads=[xt, rstd.part(tt), Gb[s]], writes=[tm])
                    S.op('pool', lambda g, tm=tm, xb=xb, s=s: g.tensor_tensor(out=xb.t[:], in0=tm.t[:], in1=Sb[s].t[:], op=ALU.add), reads=[tm, Sb[s]], writes=[xb])
                    S.dma('sp', xntm_d[tt * 128:(tt + 1) * 128, :], xb.t[:], reads=[xb])
                    pb = PSB[tt % 2]
                    for kc in range(8):
                        S.op('pe', lambda g, pb=pb, xb=xb, kc=kc: g.transpose(out=pb.t[:, kc * 128:(kc + 1) * 128], in_=xb.t[:, kc * 128:(kc + 1) * 128], identity=ident_b.t[:]), reads=[xb, ident_b], writes=[pb])
                    copy_op('act', xnT.t[:, :, tt * 128:(tt + 1) * 128], pb.t[:].rearrange("p (k t) -> p k t", k=8), [pb], [xnT.part(tt)])
                for kc in range(8):
                    S.dma('sp', xnT_d[:, kc, :], xnT.t[:, kc, :], reads=[xnT.part(tt) for tt in range(34)])

        def stage_proj(l):
            with Stage("proj"):
                xnT = S.sbuf("xnT", [128, 8, NTOK], BF16)
                for kc in range(8):
                    S.dma('sp', xnT.t[:, kc, :], xnT_d[:, kc, :], writes=[xnT])
                wst = [S.sbuf("wst", [128, 2816], F32) for _ in range(2)]
                wbf = S.sbuf("wbf", [128, 8, 2816], BF16)
                for kc in range(8):
                    ws = wst[kc % 2]
                    S.dma('sp', ws.t[:], w_in[l, kc * 128:(kc + 1) * 128, :], writes=[ws])
                    copy_op(alt(('dve', 'pool')), wbf.t[:, kc, :], ws.t[:], [ws], [wbf.part(kc)])
                wparts = [wbf.part(kc) for kc in range(8)]
                ob = [S.sbuf("ob", [128, 512], BF16) for _ in range(3)]
                of = [S.sbuf("of", [128, 512], F32) for _ in range(3)]
                vpt = [S.sbuf("vpt", [128, 8, 128], BF16) for _ in range(2)]
                for v in vpt:
                    S.op('pool', lambda g, v=v: g.memset(v.t[:], 0.0), writes=[v])
                n = 0
                for (c0, nn) in GROUPS:
                    for ch in list(range(8)) + list(range(12, 22)):
                        ps = PS[n % 4]
                        for kc in range(8):
                            S.op('pe', lambda g, ps=ps, kc=kc, ch=ch, c0=c0, nn=nn: g.matmul(ps.t[:, 0:nn], lhsT=wbf.t[:, kc, ch * 128:(ch + 1) * 128], rhs=xnT.t[:, kc, c0:c0 + nn], start=(kc == 0), stop=(kc == 7)), reads=[xnT] + wparts, writes=[ps])
                        if ch < 8:
                            o = ob[n % 3]
                            if ch < 4:
                                S.op('act', lambda g, o=o, ps=ps, nn=nn: g.activation(out=o.t[:, 0:nn], in_=ps.t[:, 0:nn], func=AF.Copy, scale=0.125), reads=[ps], writes=[o])
                            else:
                                copy_op('dve', o.t[:, 0:nn], ps.t[:, 0:nn], [ps], [o])
                            S.dma('sp', qkT_d[ch, :, c0:c0 + nn], o.t[:, 0:nn], reads=[o])
                        else:
                            o = of[n % 3]
                            copy_op(alt(), o.t[:, 0:nn], ps.t[:, 0:nn], [ps], [o])
                            dst = phT_d[ch - 12] if ch < 18 else prT_d[ch - 18]
                            S.dma('sp', dst[:, c0:c0 + nn], o.t[:, 0:nn], reads=[o])
                        n += 1
                    for t4 in range(nn // 128):
                        tc0 = c0 + t4 * 128
                        ps = PS[4 + (n % 2)]
                        vt = vpt[n % 2]
                        n += 1
                        for kc in range(8):
                            S.op('pe', lambda g, ps=ps, kc=kc, tc0=tc0: g.matmul(ps.t[:], lhsT=xnT.t[:, kc, tc0:tc0 + 128], rhs=wbf.t[:, kc, 1024:1536], start=(kc == 0), stop=(kc == 7)), reads=[xnT] + wparts, writes=[ps])
                        S.op('dve', lambda g, ps=ps, vt=vt: g.tensor_copy(out=bass.AP(vt.t[:].tensor, vt.t[:].offset, [[1024, 128], [256, 4], [192, 2], [1, 64]]), in_=ps.t[:].rearrange("p (j e d) -> p j e d", j=4, e=2)), reads=[ps], writes=[vt])
                        for j in range(4):
                            S.dma('sp', vp_d[j, tc0:tc0 + 128, :], vt.t[:, 2 * j:2 * j + 2, :].rearrange("p h c -> p (h c)"), reads=[vt])

        def attn_ranges(i):
            r0 = 8 * i
            rows = list(range(r0, r0 + 8))
            rs = lambda r: min(max(r - 4, 0), GW - 8)
            amin = min(rs(r) for r in rows)
            amax = max(rs(r) for r in rows) + 7
            a0s = list(range(amin - (amin % 2), amax + 1, 2))
            out = []
            for a0 in a0s:
                hal = []
                for a in (a0, a0 + 1):
                    v = [r for r in rows if rs(r) <= a <= rs(r) + 7] if a < GW else []
                    hal.append((v[0] - r0, v[-1] - r0 + 1) if v else None)
                lo = min(h[0] for h in hal if h)
                hi = max(h[1] for h in hal if h)
                out.append((a0, hal, (lo, hi)))
            return out

        def stage_attn(l):
            with Stage("attn"):
                tab = S.sbuf("tab", [128, 8, 22 * 64], BF16)
                tst = [S.sbuf("tst", [128, 22 * 64], F32) for _ in range(2)]
                for h in range(8):
                    S.dma('sp', tst[h % 2].t[:], tab_in[l, :, h, :], writes=[tst[h % 2]])
                    copy_op(alt(('dve', 'pool')), tab.t[:, h, :], tst[h % 2].t[:], [tst[h % 2]], [tab.part(h)])
                qTs = [S.sbuf("qTs", [128, NTOK], BF16) for _ in range(2)]
                kTs = [S.sbuf("kTs", [128, NTOK], BF16) for _ in range(2)]
                vps = [S.sbuf("vps", [128, 34, 256], BF16) for _ in range(2)]
                pts = [S.sbuf("pt", [128, 512], BF16) for _ in range(3)]
                rec = [S.sbuf("rec", [128, 512], F32) for _ in range(2)]
                yab = [S.sbuf("yab", [128, 512], BF16) for _ in range(2)]
                n = {'s': 0, 'o': 0}
                for j in range(4):
                    qT, kT, vp = qTs[j % 2], kTs[j % 2], vps[j % 2]
                    S.dma('sp', qT.t[:], qkT_d[j], writes=[qT])
                    S.dma('sp', kT.t[:], qkT_d[4 + j], writes=[kT])
                    for q4 in range(2):
                        S.dma('sp', vp.t[:, q4 * 17:(q4 + 1) * 17, :], vp_d[j, q4 * 17 * 128:(q4 + 1) * 17 * 128, :].rearrange("(t p) c -> p t c", p=128), writes=[vp])
                    for i in range(-1, 8):
                        if i < 0:
                            qc0, nq = 0, C
                            klist = [('c', 0, None), ('c', 1, None)]
                        else:
                            qc0, nq = C + 512 * i, 512
                            klist = [('c', 0, None), ('c', 1, None)] + [('l', a0, (hal, un)) for (a0, hal, un) in attn_ranges(i)]
                        O = PS[3 + (n['o'] % 2) * 1]
                        Dn = PS[5] if (n['o'] % 2) else PS[4]
                        O = PS[2] if (n['o'] % 2) else PS[3]
                        n['o'] += 1
                        first = True
                        for e in range(2):
                            h = 2 * j + e
                            pb = e * 64
                            for (kind, a0, info) in klist:
                                Sp = PS[n['s'] % 2]
                                pt = pts[n['s'] % 3]
                                n['s'] += 1
                                if kind == 'c':
                                    kc0 = a0 * 128
                                    lo, hi = 0, nq
                                    S.op('pe', lambda g, Sp=Sp, pb=pb, kc0=kc0, qc0=qc0, nq=nq, qT=qT, kT=kT: g.matmul(Sp.t[:, 0:nq], lhsT=kT.t[pb:pb + 64, kc0:kc0 + 128], rhs=qT.t[pb:pb + 64, qc0:qc0 + nq], start=True, stop=True), reads=[qT, kT], writes=[Sp])
                                    S.op('act', lambda g, Sp=Sp, pt=pt, nq=nq: g.activation(out=pt.t[:, 0:nq], in_=Sp.t[:, 0:nq], func=AF.Exp), reads=[Sp], writes=[pt])
                                    vtile = a0
                                else:
                                    hal, (ulo, uhi) = info
                                    kc0 = C + a0 * GW
                                    lo, hi = ulo * GW, uhi * GW
                                    r0 = 8 * i
                                    e0 = (r0 + ulo) - a0 + 10
                                    assert 0 <= e0 and e0 + (uhi - ulo) <= 22, (i, a0, e0)
                                    S.op('pe', lambda g, Sp=Sp, pb=pb, kc0=kc0, qc0=qc0, lo=lo, hi=hi, qT=qT, kT=kT: g.matmul(Sp.t[:, lo:hi], lhsT=kT.t[pb:pb + 64, kc0:kc0 + 128], rhs=qT.t[pb:pb + 64, qc0 + lo:qc0 + hi], start=True, stop=False), reads=[qT, kT], writes=[Sp])
                                    S.op('pe', lambda g, Sp=Sp, lo=lo, hi=hi, h=h, e0=e0: g.matmul(Sp.t[:, lo:hi], lhsT=ident_b.t[:], rhs=tab.t[:, h, e0 * 64:e0 * 64 + (hi - lo)], start=False, stop=True), reads=[tab.part(h), ident_b], writes=[Sp])
                                    for hf_, rng in enumerate(hal):
                                        p0 = hf_ * 64
                                        if rng is None:
                                            S.op('pool', lambda g, pt=pt, p0=p0, lo=lo, hi=hi: g.memset(pt.t[p0:p0 + 64, lo:hi], 0.0), writes=[pt])
                                            continue
                                        vlo, vhi = rng[0] * GW, rng[1] * GW
                                        S.op('act', lambda g, Sp=Sp, pt=pt, p0=p0, vlo=vlo, vhi=vhi: g.activation(out=pt.t[p0:p0 + 64, vlo:vhi], in_=Sp.t[p0:p0 + 64, vlo:vhi], func=AF.Exp), reads=[Sp], writes=[pt])
                                        if vlo > lo:
                                            S.op('pool', lambda g, pt=pt, p0=p0, lo=lo, vlo=vlo: g.memset(pt.t[p0:p0 + 64, lo:vlo], 0.0), writes=[pt])
                                        if vhi < hi:
                                            S.op('pool', lambda g, pt=pt, p0=p0, hi=hi, vhi=vhi: g.memset(pt.t[p0:p0 + 64, vhi:hi], 0.0), writes=[pt])
                                    vtile = 2 + a0 // 2
                                S.op('pe', lambda g, O=O, vp=vp, vtile=vtile, e=e, pt=pt, lo=lo, hi=hi, first=first: g.matmul(O.t[:, lo:hi], lhsT=vp.t[:, vtile, e * 128:(e + 1) * 128], rhs=pt.t[:, lo:hi], start=first, stop=False, skip_group_check=True), reads=[vp, pt], writes=[O])
                                S.op('pe', lambda g, Dn=Dn, e=e, pt=pt, lo=lo, hi=hi, first=first: g.matmul(Dn.t[:, lo:hi], lhsT=osel.t[:, e, :], rhs=pt.t[:, lo:hi], start=first, stop=False, skip_group_check=True), reads=[osel, pt], writes=[Dn])
                                first = False
                        rc = rec[n['o'] % 2]
                        yb_ = yab[n['o'] % 2]
                        S.op('dve', lambda g, rc=rc, Dn=Dn, nq=nq: g.reciprocal(out=rc.t[:, 0:nq], in_=Dn.t[:, 0:nq]), reads=[Dn], writes=[rc])
                        S.op('dve', lambda g, rc=rc, O=O, yb_=yb_, nq=nq: g.tensor_tensor(out=yb_.t[:, 0:nq], in0=O.t[:, 0:nq], in1=rc.t[:, 0:nq], op=ALU.mult), reads=[O, rc], writes=[yb_])
                        S.dma('sp', yaT_d[j, :, qc0:qc0 + nq], yb_.t[:, 0:nq], reads=[yb_])

        def sin_layer(ps, n, bias, fq, tmp, tmp2, out, outbuf, xr):
            S.op('dve', lambda g: g.tensor_scalar(out=tmp.t[0:64, 0:n], in0=ps.t[0:64, 0:n], scalar1=bias, scalar2=fq, op0=ALU.add, op1=ALU.mult), reads=[ps] + xr, writes=[tmp])
            MAGIC = 12582912.0
            S.op('dve', lambda g: g.tensor_scalar(out=tmp2.t[0:64, 0:n], in0=tmp.t[0:64, 0:n], scalar1=1.0 / (2 * math.pi), scalar2=MAGIC, op0=ALU.mult, op1=ALU.add), reads=[tmp], writes=[tmp2])
            S.op('dve', lambda g: g.tensor_scalar(out=tmp2.t[0:64, 0:n], in0=tmp2.t[0:64, 0:n], scalar1=MAGIC, scalar2=-2 * math.pi, op0=ALU.subtract, op1=ALU.mult), reads=[tmp2], writes=[tmp2])
            S.op('dve', lambda g: g.tensor_tensor(out=tmp.t[0:64, 0:n], in0=tmp.t[0:64, 0:n], in1=tmp2.t[0:64, 0:n], op=ALU.add), reads=[tmp, tmp2], writes=[tmp])
            S.op('dve', lambda g: g.tensor_scalar(out=tmp.t[0:64, 0:n], in0=tmp.t[0:64, 0:n], scalar1=-3.1415925, scalar2=3.1415925, op0=ALU.max, op1=ALU.min), reads=[tmp], writes=[tmp])
            S.op('act', lambda g: g.activation(out=out, in_=tmp.t[0:64, 0:n], func=AF.Sin), reads=[tmp], writes=[outbuf])

        def stage_hy_filt(l, L, zT, dec, ed):
            with Stage("hyfilt"):
                z = S.sbuf("z", [33, L], F32)
                dc = S.sbuf("dc", [128, 2, L], F32)
                f1w = S.sbuf("f1w", [33, 64], F32)
                f2w = S.sbuf("f2w", [64, 64], F32)
                f3w = S.sbuf("f3w", [64, 512], F32)
                fb = S.sbuf("fb", [64, 2], F32)
                fq = S.sbuf("fq", [64, 2], F32)
                dsk = S.sbuf("dsk", [128, 2], F32)
                h1 = S.sbuf("h1", [64, L], F32)
                h2 = S.sbuf("h2", [64, L], F32)
                hT = [S.sbuf("hT", [128, L], F32) for _ in range(4)]
                tmp = [S.sbuf("stmp", [64, 512], F32) for _ in range(4)]
                et = [S.sbuf("et", [128, 2 * L], BF16) for _ in range(2)]
                S.dma('sp', z.t[:], zT, writes=[z])
                S.dma('sp', dc.t[:], dec, writes=[dc])
                S.dma('sp', f1w.t[:], hy_f1w[l], writes=[f1w])
                S.dma('sp', f2w.t[:], hy_f2w[l], writes=[f2w])
                S.dma('sp', f3w.t[:], hy_f3w[l], writes=[f3w])
                S.dma('sp', fb.t[:, 0:1], hy_f1b[l], writes=[fb])
                S.dma('sp', fb.t[:, 1:2], hy_f2b[l], writes=[fb])
                S.dma('sp', fq.t[:], hy_fq[l], writes=[fq])
                S.dma('sp', dsk.t[:], hy_dsk[l], writes=[dsk])
                n = 0
                for c0 in range(0, L, 512):
                    nn = min(512, L - c0)
                    ps = PS[n % 2]
                    S.op('pe', lambda g, ps=ps, c0=c0, nn=nn: g.matmul(ps.t[0:64, 0:nn], lhsT=f1w.t[:], rhs=z.t[:, c0:c0 + nn], start=True, stop=True), reads=[f1w, z], writes=[ps])
                    sin_layer(ps, nn, fb.t[:, 0:1], fq.t[:, 0:1], tmp[0], tmp[2], h1.t[:, c0:c0 + nn], h1, [fb, fq])
                    ps2 = PS[2 + n % 2]
                    S.op('pe', lambda g, ps2=ps2, c0=c0, nn=nn: g.matmul(ps2.t[0:64, 0:nn], lhsT=f2w.t[:], rhs=h1.t[:, c0:c0 + nn], start=True, stop=True), reads=[f2w, h1], writes=[ps2])
                    sin_layer(ps2, nn, fb.t[:, 1:2], fq.t[:, 1:2], tmp[1], tmp[3], h2.t[:, c0:c0 + nn], h2, [fb, fq])
                    for c4 in range(4):
                        ps3 = PS[4 + c4 % 2]
                        S.op('pe', lambda g, ps3=ps3, c4=c4, c0=c0, nn=nn: g.matmul(ps3.t[:, 0:nn], lhsT=f3w.t[:, c4 * 128:(c4 + 1) * 128], rhs=h2.t[:, c0:c0 + nn], start=True, stop=True), reads=[f3w, h2], writes=[ps3])
                        S.op('dve', lambda g, ps3=ps3, c4=c4, c0=c0, nn=nn: g.tensor_tensor(out=hT[c4].t[:, c0:c0 + nn], in0=ps3.t[:, 0:nn], in1=dc.t[:, c4 % 2, c0:c0 + nn], op=ALU.mult), reads=[ps3, dc], writes=[hT[c4]])
                    n += 1
                if dbg_h is not None and L == T:
                    S.dma('sp', dbg_h[0], h1.t[:], reads=[h1])
                    S.dma('sp', dbg_h[1], h2.t[:], reads=[h2])
                for cc in range(2):
                    e_ = et[cc]
                    S.op('pool', lambda g, e_=e_: g.memset(e_.t[:, 0:1], 0.0), writes=[e_])
                    copy_op('act', e_.t[:, L:2 * L], hT[cc].t[:, :], [hT[cc]], [e_])
                    S.op('dve', lambda g, e_=e_, cc=cc: g.tensor_scalar(out=e_.t[:, L:L + 1], in0=hT[cc].t[:, 0:1], scalar1=dsk.t[:, cc:cc + 1], scalar2=None, op0=ALU.add), reads=[hT[cc], dsk, e_], writes=[e_])
                    S.op('dve', lambda g, e_=e_, cc=cc: g.tensor_copy(out=e_.t[:, 1:L], in_=hT[2 + cc].t[:, L - 1:0:-1]), reads=[hT[2 + cc], e_], writes=[e_])
                    S.dma('sp', ed[cc * 128:(cc + 1) * 128, :], e_.t[:], reads=[e_])

        def stage_hy_sc(l, L, c0, ub):
            nb = L // 128
            with Stage("hysc"):
                sw = S.sbuf("sw", [128, 6, 3], F32)
                sb = S.sbuf("sb", [128, 6], F32)
                S.dma('sp', sw.t[:], hy_swT[l], writes=[sw])
                S.dma('sp', sb.t[:], hy_sbT[l], writes=[sb])
                pin = [S.sbuf("pin", [128, L], F32) for _ in range(2)]
                ta = S.sbuf("ta", [128, L], F32)
                tb = S.sbuf("tb", [128, L], F32)
                vv = S.sbuf("vv", [128, L], F32)
                x0b = [S.sbuf("x0b", [128, L], BF16) for _ in range(2)]
                utr = [S.sbuf("utr", [128, L], BF16) for _ in range(2)]
                ubs = S.sbuf("ubs", [128, 2, 128, nb], BF16)
                n = {'p': 0}

                def conv(c6, out_buf, out_ap_full, final_writes):
                    p = pin[n['p'] % 2]
                    n['p'] += 1
                    S.dma('sp', p.t[:], phT_d[c6, :, c0:c0 + L], writes=[p])
                    S.op('dve', lambda g: g.tensor_scalar(out=ta.t[:], in0=p.t[:], scalar1=sw.t[:, c6, 1:2], scalar2=sb.t[:, c6:c6 + 1], op0=ALU.mult, op1=ALU.add), reads=[p, sw, sb], writes=[ta])
                    S.op('dve', lambda g: g.scalar_tensor_tensor(out=ta.t[:, 1:L], in0=p.t[:, 0:L - 1], scalar=sw.t[:, c6, 0:1], in1=ta.t[:, 1:L], op0=ALU.mult, op1=ALU.add), reads=[p, sw, ta], writes=[ta])
                    S.op('dve', lambda g: g.scalar_tensor_tensor(out=out_ap_full(0, L - 1), in0=p.t[:, 1:L], scalar=sw.t[:, c6, 2:3], in1=ta.t[:, 0:L - 1], op0=ALU.mult, op1=ALU.add), reads=[p, sw, ta], writes=[out_buf])
                    copy_op('dve', out_ap_full(L - 1, L), ta.t[:, L - 1:L], [ta, out_buf], [out_buf])

                for cc in range(2):
                    conv(cc, x0b[cc], lambda a, b, cc=cc: x0b[cc].t[:, a:b], None)
                    S.dma('sp', x0_d[cc, :, c0:c0 + L], x0b[cc].t[:], reads=[x0b[cc]])
                for cc in range(2):
                    conv(2 + cc, tb, lambda a, b: tb.t[:, a:b], None)
                    conv(4 + cc, vv, lambda a, b, vv=vv: vv.t[:, a:b], None)
                    S.op('dve', lambda g, cc=cc, vv=vv: g.tensor_tensor(out=utr[cc].t[:, ::-1], in0=vv.t[:], in1=tb.t[:], op=ALU.mult), reads=[vv, tb], writes=[utr[cc]])
                    for jb in range(0, nb, 8):
                        k = min(8, nb - jb)
                        pb = PSB[(jb // 8) % 2]
                        for q in range(k):
                            S.op('pe', lambda g, pb=pb, q=q, jb=jb, cc=cc: g.transpose(out=pb.t[:, q * 128:(q + 1) * 128], in_=utr[cc].t[:, (jb + q) * 128:(jb + q + 1) * 128], identity=ident_b.t[:]), reads=[utr[cc], ident_b], writes=[pb])
                        base = ubs.t[:, cc, :, :]
                        off = base.offset + (nb - 1 - jb)
                        outap = bass.AP(ubs.t[:].tensor, off, [[2 * 128 * nb, 128], [-1, k], [nb, 128]])
                        S.op('dve', lambda g, pb=pb, k=k, outap=outap: g.tensor_copy(out=outap, in_=pb.t[:, 0:k * 128].rearrange("p (q c) -> p q c", q=k)), reads=[pb], writes=[ubs])
                S.dma('sp', ub, ubs.t[:].rearrange("p a c j -> p (a c j)"), reads=[ubs])

        def stage_hy_toep(l, L, c0, ed, ub):
            nb = L // 128
            W = 2 * L - 127
            with Stage("hytoep"):
                ubs = S.sbuf("ubs", [128, 2, 128, nb], BF16)
                S.dma('sp', ubs.t[:].rearrange("p a c j -> p (a c j)"), ub, writes=[ubs])
                kts = [S.sbuf("kt", [128, W], BF16) for _ in range(3)]
                ysb = S.sbuf("ysb", [128, 128, nb], F32)
                x0b = S.sbuf("x0b", [128, L], BF16)
                ybo = S.sbuf("ybo", [128, L], BF16)
                per_bank = min(512 // nb, 128)
                for cc in range(2):
                    S.dma('sp', x0b.t[:], x0_d[cc, :, c0:c0 + L], writes=[x0b])
                    for c in range(128):
                        ch = cc * 128 + c
                        kt = kts[ch % 3]
                        S.dma('sp', kt.t[:], bass.AP(ed.tensor, ed.offset + ch * 2 * L, [[1, 128], [1, W]]), writes=[kt])
                        bank = PS[(c // per_bank) % 2]
                        col = (c % per_bank) * nb
                        ds = [0] + [d for d in range(-(nb - 1), nb) if d != 0]
                        for d in ds:
                            j0, j1 = max(0, -d), min(nb, nb - d)
                            xo = L - 127 + 128 * d
                            S.op('pe', lambda g, bank=bank, col=col, kt=kt, xo=xo, cc=cc, c=c, j0=j0, j1=j1, d=d: g.matmul(bank.t[:, col + j0 + d:col + j1 + d], lhsT=kt.t[:, xo:xo + 128], rhs=ubs.t[:, cc, c, j0:j1], start=(d == 0), stop=False, skip_group_check=True), reads=[kt, ubs], writes=[bank])
                        if c % per_bank == per_bank - 1:
                            cb = c - per_bank + 1
                            copy_op(alt(), ysb.t[:, cb:c + 1, :], bank.t[:, 0:per_bank * nb].rearrange("p (c j) -> p c j", j=nb), [bank], [ysb])
                    for I0 in range(0, nb, 4):
                        k = min(4, nb - I0)
                        pt_ = PS[2 + (I0 // 4) % 2]
                        for q in range(k):
                            S.op('pe', lambda g, pt_=pt_, q=q, I0=I0: g.transpose(out=pt_.t[:, q * 128:(q + 1) * 128], in_=ysb.t[:, :, I0 + q], identity=ident_f.t[:]), reads=[ysb, ident_f], writes=[pt_])
                        S.op('dve', lambda g, pt_=pt_, k=k, I0=I0: g.tensor_tensor(out=ybo.t[:, I0 * 128:(I0 + k) * 128], in0=pt_.t[:, 0:k * 128], in1=x0b.t[:, I0 * 128:(I0 + k) * 128], op=ALU.mult), reads=[pt_, x0b], writes=[ybo])
                    S.dma('sp', ybT_d[cc, :, c0:c0 + L], ybo.t[:], reads=[ybo])

        def stage_rglru(l):
            with Stage("rglru"):
                cw = S.sbuf("cw", [128, 2, 4], F32)
                cb = S.sbuf("cb", [128, 2], F32)
                baT = S.sbuf("baT", [128, 2, 2], F32)
                bxT = S.sbuf("bxT", [128, 2, 2], F32)
                lam = S.sbuf("lam", [128, 2, 2], F32)
                m8 = S.sbuf("m8", [128, 2, 2], F32)
                m16 = S.sbuf("m16", [128, 2, 2], F32)
                h0 = S.sbuf("h0", [128, 2, 2], F32)
                bdf = S.sbuf("bdf", [128, 8, 128], F32)
                bd = S.sbuf("bd", [128, 8, 128], BF16)
                for (t_, src) in ((cw, rg_cw[l]), (cb, rg_cb[l]), (baT, rg_baT[l]), (bxT, rg_bxT[l]), (lam, rg_lamT[l])):
                    S.dma('sp', t_.t[:], src, writes=[t_])
                S.op('pool', lambda g: g.memset(bdf.t[:], 0.0), writes=[bdf])
                for cc in range(2):
                    for dr in range(2):
                        for ax, wsrc in ((0, rg_wa), (1, rg_wx)):
                            idx = (cc * 2 + dr) * 2 + ax
                            for hb_ in range(2):
                                S.dma('sp', bdf.t[hb_ * 64:(hb_ + 1) * 64, idx, hb_ * 64:(hb_ + 1) * 64], wsrc[l, dr, 2 * cc + hb_], reads=[bdf], writes=[bdf])
                copy_op('dve', bd.t[:], bdf.t[:], [bdf], [bd])
                S.op('act', lambda g: g.activation(out=lam.t[:], in_=lam.t[:], func=AF.Exp, scale=-1.0), reads=[lam], writes=[lam])
                S.op('act', lambda g: g.activation(out=lam.t[:], in_=lam.t[:], func=AF.Ln, bias=1.0), reads=[lam], writes=[lam])
                S.op('dve', lambda g: g.tensor_scalar(out=m8.t[:], in0=lam.t[:], scalar1=-8.0, scalar2=None, op0=ALU.mult), reads=[lam], writes=[m8])
                S.op('dve', lambda g: g.tensor_scalar(out=m16.t[:], in0=lam.t[:], scalar1=-16.0, scalar2=None, op0=ALU.mult), reads=[lam], writes=[m16])
                LM = T
                xin = S.sbuf("rxin", [128, LM], F32)
                xc = S.sbuf("rxc", [128, LM], F32)
                xcb = S.sbuf("rxcb", [128, LM], BF16)
                rb = S.sbuf("rr", [128, LM], F32)
                ib = S.sbuf("ri", [128, LM], F32)
                ab = S.sbuf("ra", [128, LM], F32)
                hh = [S.sbuf("rh", [128, LM], F32) for _ in range(2)]
                yo = S.sbuf("ryo", [128, LM], BF16)
                n = 0
                for (L, c0, isctx) in ((C, 0, True), (T, C, False)):
                    for cc in range(2):
                        S.dma('sp', xin.t[:, 0:L], prT_d[cc, :, c0:c0 + L], writes=[xin])
                        S.op('dve', lambda g, L=L, cc=cc: g.tensor_scalar(out=xc.t[:, 0:L], in0=xin.t[:, 0:L], scalar1=cw.t[:, cc, 2:3], scalar2=cb.t[:, cc:cc + 1], op0=ALU.mult, op1=ALU.add), reads=[xin, cw, cb], writes=[xc])
                        for (k, sh) in ((0, -2), (1, -1), (3, 1)):
                            if sh < 0:
                                oa, ia = (-sh, L), (0, L + sh)
                            else:
                                oa, ia = (0, L - sh), (sh, L)
                            S.op('dve', lambda g, oa=oa, ia=ia, k=k, cc=cc: g.scalar_tensor_tensor(out=xc.t[:, oa[0]:oa[1]], in0=xin.t[:, ia[0]:ia[1]], scalar=cw.t[:, cc, k:k + 1], in1=xc.t[:, oa[0]:oa[1]], op0=ALU.mult, op1=ALU.add), reads=[xin, cw, xc], writes=[xc])
                        copy_op('act', xcb.t[:, 0:L], xc.t[:, 0:L], [xc], [xcb])
                        for dr in range(2):
                            for g0 in range(0, L, 512):
                                nn = min(512, L - g0)
                                for ax, dst, bias in ((0, rb, baT), (1, ib, bxT)):
                                    ps = PS[n % 4]
                                    n += 1
                                    idx = (cc * 2 + dr) * 2 + ax
                                    S.op('pe', lambda g, ps=ps, idx=idx, g0=g0, nn=nn: g.matmul(ps.t[:, 0:nn], lhsT=bd.t[:, idx, :], rhs=xcb.t[:, g0:g0 + nn], start=True, stop=True), reads=[bd, xcb], writes=[ps])
                                    S.op('act', lambda g, ps=ps, dst=dst, bias=bias, g0=g0, nn=nn, cc=cc, dr=dr: g.activation(out=dst.t[:, g0:g0 + nn], in_=ps.t[:, 0:nn], func=AF.Sigmoid, bias=bias.t[:, cc, dr:dr + 1]), reads=[ps, bias], writes=[dst])
                            S.op('act', lambda g, L=L, cc=cc, dr=dr: g.activation(out=ab.t[:, 0:L], in_=rb.t[:, 0:L], func=AF.Exp, scale=m8.t[:, cc, dr:dr + 1]), reads=[rb, m8], writes=[ab])
                            S.op('act', lambda g, L=L, cc=cc, dr=dr: g.activation(out=rb.t[:, 0:L], in_=rb.t[:, 0:L], func=AF.Exp, scale=m16.t[:, cc, dr:dr + 1]), reads=[rb, m16], writes=[rb])
                            S.op('act', lambda g, L=L: g.activation(out=rb.t[:, 0:L], in_=rb.t[:, 0:L], func=AF.Sqrt, scale=-1.0, bias=1.0), reads=[rb], writes=[rb])
                            S.op('dve', lambda g, L=L: g.tensor_tensor(out=ib.t[:, 0:L], in0=ib.t[:, 0:L], in1=xc.t[:, 0:L], op=ALU.mult), reads=[ib, xc], writes=[ib])
                            S.op('dve', lambda g, L=L: g.tensor_tensor(out=ib.t[:, 0:L], in0=ib.t[:, 0:L], in1=rb.t[:, 0:L], op=ALU.mult), reads=[ib, rb], writes=[ib])
                            init = 0.0 if isctx else h0.t[:, cc, dr:dr + 1]
                            ho = hh[dr]
                            if dr == 0:
                                S.op('dve', lambda g, L=L, init=init, ho=ho: g.tensor_tensor_scan(out=ho.t[:, 0:L], data0=ab.t[:, 0:L], data1=ib.t[:, 0:L], initial=init, op0=ALU.mult, op1=ALU.add), reads=[ab, ib, h0], writes=[ho])
                            else:
                                S.op('dve', lambda g, L=L, init=init, ho=ho: g.tensor_tensor_scan(out=ho.t[:, 0:L][:, ::-1], data0=ab.t[:, 0:L][:, ::-1], data1=ib.t[:, 0:L][:, ::-1], initial=init, op0=ALU.mult, op1=ALU.add), reads=[ab, ib, h0], writes=[ho])
                        if isctx:
                            copy_op('dve', h0.t[:, cc, 0:1], hh[0].t[:, L - 1:L], [hh[0], h0], [h0])
                            copy_op('dve', h0.t[:, cc, 1:2], hh[1].t[:, 0:1], [hh[1], h0], [h0])
                        S.dma('sp', xin.t[:, 0:L], prT_d[2 + cc, :, c0:c0 + L], writes=[xin])
                        S.op('act', lambda g, L=L: g.activation(out=xin.t[:, 0:L], in_=xin.t[:, 0:L], func=AF.Gelu), reads=[xin], writes=[xin])
                        S.op('dve', lambda g, L=L: g.tensor_tensor(out=hh[0].t[:, 0:L], in0=hh[0].t[:, 0:L], in1=hh[1].t[:, 0:L], op=ALU.add), reads=[hh[0], hh[1]], writes=[hh[0]])
                        S.op('dve', lambda g, L=L: g.tensor_tensor(out=yo.t[:, 0:L], in0=hh[0].t[:, 0:L], in1=xin.t[:, 0:L], op=ALU.mult), reads=[hh[0], xin], writes=[yo])
                        S.dma('sp', ycT_d[cc, :, c0:c0 + L], yo.t[:, 0:L], reads=[yo])

        def load_w_bf(dst, src_rows_fn, nk, ncols, stg):
            for kc in range(nk):
                st = stg[kc % len(stg)]
                S.dma('sp', st.t[:, 0:ncols], src_rows_fn(kc), writes=[st])
                copy_op(alt(('dve', 'pool')), dst.t[:, kc, :], st.t[:, 0:ncols], [st], [dst])

        def stage_merge(l, first):
            with Stage("merge"):
                stg = [S.sbuf("mstg", [128, 3072], F32)]
                wg = S.sbuf("wg", [128, 8, 3072], BF16)
                wa_ = S.sbuf("wa", [128, 4, D], BF16)
                wb_ = S.sbuf("wb", [128, 2, D], BF16)
                wc_ = S.sbuf("wc", [128, 2, D], BF16)
                wo_ = S.sbuf("wo", [128, 8, D], BF16)
                bg = S.sbuf("bg", [128, 24], F32)
                S.dma('sp', bg.t[:], b_gateT[l], writes=[bg])
                load_w_bf(wg, lambda kc: w_gate[l, kc * 128:(kc + 1) * 128, :], 8, 3072, stg)
                load_w_bf(wa_, lambda kc: w_bra[l, kc * 128:(kc + 1) * 128, :], 4, D, stg)
                load_w_bf(wb_, lambda kc: w_brb[l, kc * 128:(kc + 1) * 128, :], 2, D, stg)
                load_w_bf(wc_, lambda kc: w_brc[l, kc * 128:(kc + 1) * 128, :], 2, D, stg)
                load_w_bf(wo_, lambda kc: w_out[l, kc * 128:(kc + 1) * 128, :], 8, D, stg)
                xg = S.sbuf("xg", [128, 8, 512], BF16)
                ya = S.sbuf("mya", [128, 4, 512], BF16)
                yb = S.sbuf("myb", [128, 2, 512], BF16)
                yc = S.sbuf("myc", [128, 2, 512], BF16)
                mT = S.sbuf("mT", [128, 8, 512], BF16)
                gt = [S.sbuf("gt", [128, 512], BF16) for _ in range(3)]
                t1 = S.sbuf("mt1", [128, 512], F32)
                t2 = S.sbuf("mt2", [128, 512], F32)
                oT = S.sbuf("oT", [128, 8, 512], F32)
                xt = [S.sbuf("mxt", [128, D], F32) for _ in range(2)]
                for (c0, nn) in GROUPS:
                    s = 1 if c0 == 0 else 0
                    for kc in range(8):
                        S.dma('sp', xg.t[:, kc, 0:nn], xnT_d[:, kc, c0:c0 + nn], writes=[xg])
                    for kc in range(4):
                        S.dma('sp', ya.t[:, kc, 0:nn], yaT_d[kc, :, c0:c0 + nn], writes=[ya])
                    for kc in range(2):
                        S.dma('sp', yb.t[:, kc, 0:nn], ybT_d[kc, :, c0:c0 + nn], writes=[yb])
                        S.dma('sp', yc.t[:, kc, 0:nn], ycT_d[kc, :, c0:c0 + nn], writes=[yc])
                    for mc in range(8):
                        brs = ((wa_, ya, 4), (wb_, yb, 2), (wc_, yc, 2))
                        for bi in range(3):
                            pg = PS[bi]
                            for kc in range(8):
                                S.op('pe', lambda g, pg=pg, kc=kc, bi=bi, mc=mc, nn=nn: g.matmul(pg.t[:, 0:nn], lhsT=wg.t[:, kc, bi * D + mc * 128:bi * D + (mc + 1) * 128], rhs=xg.t[:, kc, 0:nn], start=(kc == 0), stop=(kc == 7)), reads=[wg, xg], writes=[pg])
                            S.op('act', lambda g, pg=pg, bi=bi, mc=mc, nn=nn: g.activation(out=gt[bi].t[:, 0:nn], in_=pg.t[:, 0:nn], func=AF.Sigmoid, bias=bg.t[:, bi * 8 + mc:bi * 8 + mc + 1]), reads=[pg, bg], writes=[gt[bi]])
                            pbr = PS[3 + bi]
                            w_, y_, nk = brs[bi]
                            for kc in range(nk):
                                S.op('pe', lambda g, pbr=pbr, kc=kc, w_=w_, y_=y_, nk=nk, mc=mc, nn=nn: g.matmul(pbr.t[:, 0:nn], lhsT=w_.t[:, kc, mc * 128:(mc + 1) * 128], rhs=y_.t[:, kc, 0:nn], start=(kc == 0), stop=(kc == nk - 1)), reads=[w_, y_], writes=[pbr])
                        S.op('dve', lambda g, nn=nn: g.tensor_tensor(out=t1.t[:, 0:nn], in0=PS[3].t[:, 0:nn], in1=gt[0].t[:, 0:nn], op=ALU.mult), reads=[PS[3], gt[0]], writes=[t1])
                        S.op('dve', lambda g, nn=nn: g.tensor_tensor(out=t2.t[:, 0:nn], in0=PS[4].t[:, 0:nn], in1=gt[1].t[:, 0:nn], op=ALU.mult), reads=[PS[4], gt[1]], writes=[t2])
                        S.op('pool', lambda g, nn=nn: g.tensor_tensor(out=t1.t[:, 0:nn], in0=t1.t[:, 0:nn], in1=t2.t[:, 0:nn], op=ALU.add), reads=[t1, t2], writes=[t1])
                        S.op('dve', lambda g, nn=nn: g.tensor_tensor(out=t2.t[:, 0:nn], in0=PS[5].t[:, 0:nn], in1=gt[2].t[:, 0:nn], op=ALU.mult), reads=[PS[5], gt[2]], writes=[t2])
                        S.op('pool', lambda g, nn=nn, mc=mc: g.tensor_tensor(out=mT.t[:, mc, 0:nn], in0=t1.t[:, 0:nn], in1=t2.t[:, 0:nn], op=ALU.add), reads=[t1, t2], writes=[mT])
                    for oc in range(8):
                        po = PS[oc % 2]
                        for mc in range(8):
                            S.op('pe', lambda g, po=po, mc=mc, oc=oc, nn=nn: g.matmul(po.t[:, 0:nn], lhsT=wo_.t[:, mc, oc * 128:(oc + 1) * 128], rhs=mT.t[:, mc, 0:nn], start=(mc == 0), stop=(mc == 7)), reads=[wo_, mT], writes=[po])
                        S.op('act', lambda g, po=po, oc=oc, nn=nn, s=s: g.activation(out=oT.t[:, oc, 0:nn], in_=po.t[:, 0:nn], func=AF.Copy, scale=modT.t[:, 16 + oc, s:s + 1]), reads=[po, modT], writes=[oT])
                    for t4 in range(nn // 128):
                        tok0 = c0 + t4 * 128
                        if c0 == 0:
                            src = (c_in if first else cres)[tok0:tok0 + 128, :]
                            dst = cres[tok0:tok0 + 128, :]
                        else:
                            src = (x_in if first else xres)[tok0 - C:tok0 - C + 128, :]
                            dst = xres[tok0 - C:tok0 - C + 128, :]
                        x_ = xt[t4 % 2]
                        S.dma('sp', x_.t[:], src, writes=[x_])
                        for half in range(2):
                            pt_ = PS[2 + half]
                            for q in range(4):
                                oc = half * 4 + q
                                S.op('pe', lambda g, pt_=pt_, q=q, oc=oc, t4=t4: g.transpose(out=pt_.t[:, q * 128:(q + 1) * 128], in_=oT.t[:, oc, t4 * 128:(t4 + 1) * 128], identity=ident_f.t[:]), reads=[oT, ident_f], writes=[pt_])
                            S.op('dve', lambda g, pt_=pt_, x_=x_, half=half: g.tensor_tensor(out=x_.t[:, half * 512:(half + 1) * 512], in0=pt_.t[:], in1=x_.t[:, half * 512:(half + 1) * 512], op=ALU.add), reads=[pt_, x_], writes=[x_])
                        S.dma('sp', dst, x_.t[:], reads=[x_])

        def top16(src_ap, src_reads, vals, idxs, scr, vparts, iparts):
            S.op('dve', lambda g: g.max(out=vals[:, 0:8], in_=src_ap), reads=src_reads, writes=vparts)
            S.op('dve', lambda g: g.max_index(out=idxs[:, 0:8], in_max=vals[:, 0:8], in_values=src_ap), reads=src_reads + vparts, writes=iparts)
            S.op('dve', lambda g: g.match_replace(out=scr.t[:, 0:src_ap.shape[1]], in_to_replace=vals[:, 0:8], in_values=src_ap, imm_value=-1e30), reads=src_reads + vparts, writes=[scr])
            S.op('dve', lambda g: g.max(out=vals[:, 8:16], in_=scr.t[:, 0:src_ap.shape[1]]), reads=[scr], writes=vparts)
            S.op('dve', lambda g: g.max_index(out=idxs[:, 8:16], in_max=vals[:, 8:16], in_values=scr.t[:, 0:src_ap.shape[1]]), reads=[scr] + vparts, writes=iparts)

        def stage_peer_prep(l):
            with Stage("pprep"):
                R = 4
                uin = [S.sbuf("uin", [128, R, D], F32) for _ in range(2)]
                vin = [S.sbuf("vin", [128, R, D], F32) for _ in range(2)]
                uvo = [S.sbuf("uvo", [128, R, 2 * D], BF16) for _ in range(2)]
                uview = p_u[l].rearrange("(p r) d -> p r d", p=128)
                vview = p_v[l].rearrange("(p r) d -> p r d", p=128)
                oview = uv_d.rearrange("(p r) d -> p r d", p=128)
                nch = 128 // R

                def loads(c):
                    S.dma('sp', uin[c % 2].t[:], uview[:, c * R:(c + 1) * R, :], writes=[uin[c % 2]])
                    S.dma('sp', vin[c % 2].t[:], vview[:, c * R:(c + 1) * R, :], writes=[vin[c % 2]])
                loads(0)
                for c in range(nch):
                    if c + 1 < nch:
                        loads(c + 1)
                    a, b, o = uin[c % 2], vin[c % 2], uvo[c % 2]
                    S.op('dve', lambda g, a=a, o=o: g.tensor_copy(out=o.t[:, :, 0:D], in_=a.t[:]), reads=[a], writes=[o.part(0)])
                    S.op('act', lambda g, b=b, o=o: g.activation(out=o.t[:, :, D:2 * D], in_=b.t[:], func=AF.Copy), reads=[b], writes=[o.part(1)])
                    S.dma('pool', oview[:, c * R:(c + 1) * R, :], o.t[:], reads=[o.part(0), o.part(1)])

        def top16g(src_ap, src_reads, vals, idxs, scr, vparts, iparts):
            n = src_ap.shape[1]
            S.op('dve', lambda g: g.max(out=vals[:, 0:8], in_=src_ap), reads=src_reads, writes=vparts)
            yield
            S.op('dve', lambda g: g.max_index(out=idxs[:, 0:8], in_max=vals[:, 0:8], in_values=src_ap), reads=src_reads + vparts, writes=iparts)
            yield
            S.op('dve', lambda g: g.match_replace(out=scr.t[:, 0:n], in_to_replace=vals[:, 0:8], in_values=src_ap, imm_value=-1e30), reads=src_reads + vparts, writes=[scr])
            yield
            S.op('dve', lambda g: g.max(out=vals[:, 8:16], in_=scr.t[:, 0:n]), reads=[scr], writes=vparts)
            yield
            S.op('dve', lambda g: g.max_index(out=idxs[:, 8:16], in_max=vals[:, 8:16], in_values=scr.t[:, 0:n]), reads=[scr] + vparts, writes=iparts)
            yield

        def stage_peer(l):
            with Stage("peer"):
                stg = [S.sbuf("pstg", [128, 2048], F32)]
                wq = S.sbuf("wq", [128, 8, 2048], BF16)
                load_w_bf(wq, lambda kc: p_wq[l, kc * 128:(kc + 1) * 128, :], 8, 2048, stg)
                keysT = S.sbuf("keysT", [128, 16, 128], BF16)
                kst = [S.sbuf("kst", [128, 128], F32) for _ in range(2)]
                for hp in range(16):
                    ks = kst[hp % 2]
                    S.dma('sp', ks.t[:], p_keys[l, hp // 2, hp % 2], writes=[ks])
                    pk = PS[hp % 2]
                    S.op('pe', lambda g, pk=pk, ks=ks: g.transpose(out=pk.t[:, 0:128], in_=ks.t[:], identity=ident_f.t[:]), reads=[ks, ident_f], writes=[pk])
                    copy_op('dve', keysT.t[:, hp, :], pk.t[:, 0:128], [pk], [keysT])
                g5 = [S.sbuf("g5", [128, D], F32) for _ in range(2)]
                for s in range(2):
                    S.dma('sp', g5[s].t[:], bcd[8 + s], writes=[g5[s]])
                xg = S.sbuf("pxg", [128, 8, 512], BF16)
                qTs = [S.sbuf("pqT", [128, 16, 512], BF16) for _ in range(2)]
                top = S.sbuf("ptop", [128, 16, 16], F32)
                it = S.sbuf("pit", [128, 16, 16], U32)
                itf = S.sbuf("pitf", [128, 16, 16], F32)
                scr = S.sbuf("pscr", [128, 256], F32)
                cand = S.sbuf("pcand", [128, 8, 256], F32)
                best = S.sbuf("pbest", [128, 8, 16], F32)
                pos = S.sbuf("ppos", [128, 8, 16], U32)
                pa = S.sbuf("ppa", [128, 8, 16], U32)
                paf = S.sbuf("ppaf", [128, 2, 8, 16], F32)
                eq = S.sbuf("peq", [128, 8, 16, 16], F32)
                isel = S.sbuf("pisel", [128, 2, 8, 16], F32)
                idxf = S.sbuf("pidxf", [128, 128], F32)
                idxu = [S.sbuf("pidxu", [128, 128], U32) for _ in range(2)]
                gws = [S.sbuf("pgw", [128, 8, 16], F32) for _ in range(2)]
                zs = S.sbuf("pzs", [128, 8], F32)
                dotb = [S.sbuf("pdots", [128, 128], F32) for _ in range(2)]
                actv = [S.sbuf("pact", [128, 128], F32) for _ in range(2)]
                NR = 10
                uvr = [S.sbuf("uvr", [128, 2 * D], BF16) for _ in range(NR)]
                dgs = [S.sbuf("pdg", [128, 128], BF16) for _ in range(8)]
                xn = [S.sbuf("pxn", [128, D], BF16) for _ in range(2)]
                xt = [S.sbuf("pxt", [128, D], F32) for _ in range(2)]
                junk = S.sbuf("pjunk", [128, D], BF16)
                ytmp = S.sbuf("pytmp", [128, D], F32)
                uvflat = uv_d

                tiles = []
                for gi, (c0, nn) in enumerate(GROUPS):
                    for t4 in range(nn // 128):
                        tiles.append((gi, c0, nn, t4))

                def emit_group_q(gi):
                    c0, nn = GROUPS[gi]
                    qT = qTs[gi % 2]
                    for kc in range(8):
                        S.dma('sp', xg.t[:, kc, 0:nn], xnT_d[:, kc, c0:c0 + nn], writes=[xg])
                    for hp in range(16):
                        ps = PS[hp % 2]
                        for kc in range(8):
                            S.op('pe', lambda g, ps=ps, kc=kc, hp=hp, nn=nn: g.matmul(ps.t[:, 0:nn], lhsT=wq.t[:, kc, hp * 128:(hp + 1) * 128], rhs=xg.t[:, kc, 0:nn], start=(kc == 0), stop=(kc == 7)), reads=[wq, xg], writes=[ps])
                        copy_op('act', qT.t[:, hp, 0:nn], ps.t[:, 0:nn], [ps], [qT])

                def topk_gen(ti):
                    gi, c0, nn, t4 = tiles[ti]
                    qT = qTs[gi % 2]
                    gw = gws[ti % 2]
                    iu = idxu[ti % 2]
                    for hp in range(16):
                        ps = PS[2 + hp % 2]
                        S.op('pe', lambda g, ps=ps, hp=hp, t4=t4, qT=qT: g.matmul(ps.t[:, 0:128], lhsT=qT.t[:, hp, t4 * 128:(t4 + 1) * 128], rhs=keysT.t[:, hp, :], start=True, stop=True), reads=[qT, keysT], writes=[ps])
                        yield from top16g(ps.t[:, 0:128], [ps], top.t[:, hp, :], it.t[:, hp, :], scr, [top], [it])
                    copy_op('pool', itf.t[:], it.t[:], [it], [itf])
                    tv = top.t[:].rearrange("p (h q) k -> p h q k", q=2)
                    S.op('dve', lambda g, tv=tv: g.tensor_tensor(out=cand.t[:].rearrange("p h (a b) -> p h a b", a=16), in0=tv[:, :, 0, :].unsqueeze(3).broadcast_to([128, 8, 16, 16]), in1=tv[:, :, 1, :].unsqueeze(2).broadcast_to([128, 8, 16, 16]), op=ALU.add), reads=[top], writes=[cand])
                    yield
                    for h in range(8):
                        yield from top16g(cand.t[:, h, :], [cand], best.t[:, h, :], pos.t[:, h, :], scr, [best], [pos])
                    S.op('dve', lambda g: g.tensor_tensor(out=gw.t[:], in0=best.t[:], in1=best.t[:, :, 0:1].broadcast_to([128, 8, 16]), op=ALU.subtract), reads=[best], writes=[gw])
                    yield
                    S.op('act', lambda g: g.activation(out=gw.t[:], in_=gw.t[:], func=AF.Exp), reads=[gw], writes=[gw])
                    S.op('dve', lambda g: g.tensor_reduce(out=zs.t[:], in_=gw.t[:], axis=AX.X, op=ALU.add), reads=[gw], writes=[zs])
                    yield
                    S.op('dve', lambda g: g.reciprocal(out=zs.t[:], in_=zs.t[:]), reads=[zs], writes=[zs])
                    yield
                    S.op('dve', lambda g: g.tensor_tensor(out=gw.t[:], in0=gw.t[:], in1=zs.t[:].unsqueeze(2).broadcast_to([128, 8, 16]), op=ALU.mult), reads=[gw, zs], writes=[gw])
                    yield
                    S.op('dve', lambda g: g.tensor_single_scalar(out=pa.t[:], in_=pos.t[:], scalar=4, op=ALU.logical_shift_right), reads=[pos], writes=[pa])
                    yield
                    copy_op('dve', paf.t[:, 0], pa.t[:], [pa], [paf])
                    yield
                    S.op('dve', lambda g: g.tensor_single_scalar(out=pa.t[:], in_=pos.t[:], scalar=15, op=ALU.bitwise_and), reads=[pos, paf], writes=[pa])
                    yield
                    copy_op('dve', paf.t[:, 1], pa.t[:], [pa], [paf])
                    yield
                    itv = itf.t[:].rearrange("p (h q) k -> p h q k", q=2)
                    for q in range(2):
                        S.op('dve', lambda g, q=q: g.tensor_tensor(out=eq.t[:], in0=paf.t[:, q].unsqueeze(3).broadcast_to([128, 8, 16, 16]), in1=iota16.t[:].unsqueeze(1).unsqueeze(1).broadcast_to([128, 8, 16, 16]), op=ALU.is_equal), reads=[paf, iota16], writes=[eq])
                        yield
                        S.op('dve', lambda g, q=q, itv=itv: g.tensor_tensor(out=eq.t[:], in0=eq.t[:], in1=itv[:, :, q, :].unsqueeze(2).broadcast_to([128, 8, 16, 16]), op=ALU.mult), reads=[eq, itf], writes=[eq])
                        yield
                        S.op('dve', lambda g, q=q: g.tensor_reduce(out=isel.t[:, q], in_=eq.t[:], axis=AX.X, op=ALU.add), reads=[eq], writes=[isel])
                        yield
                    S.op('dve', lambda g: g.scalar_tensor_tensor(out=idxf.t[:].rearrange("p (h k) -> p h k", h=8), in0=isel.t[:, 0], scalar=128.0, in1=isel.t[:, 1], op0=ALU.mult, op1=ALU.add), reads=[isel], writes=[idxf])
                    yield
                    copy_op('dve', iu.t[:], idxf.t[:], [idxf], [iu])
                    yield

                emit_group_q(0)
                for _ in topk_gen(0):
                    pass
                py = [PS[4], PS[5]]
                for ti, (gi, c0, nn, t4) in enumerate(tiles):
                    s = 1 if c0 == 0 else 0
                    tok0 = c0 + t4 * 128
                    nxt = None
                    if ti + 1 < len(tiles):
                        if tiles[ti + 1][0] != gi:
                            emit_group_q(gi + 1)
                        nxt = topk_gen(ti + 1)
                    xn_ = xn[ti % 2]
                    x_ = xt[ti % 2]
                    iu = idxu[ti % 2]
                    gw = gws[ti % 2]
                    dots = dotb[ti % 2]
                    av = actv[ti % 2]
                    S.dma('sp', xn_.t[:], xntm_d[tok0:tok0 + 128, :], writes=[xn_])
                    xr = (cres[tok0:tok0 + 128, :] if c0 == 0 else xres[tok0 - C:tok0 - C + 128, :])
                    S.dma('sp', x_.t[:], xr, writes=[x_])
                    gwf = gw.t[:].rearrange("p h k -> p (h k)")
                    for hk in range(128):
                        r_ = uvr[hk % NR]
                        S.dma('pool', None, None, reads=[iu], writes=[r_], fn=lambda g, r_=r_, iu=iu, hk=hk: g.indirect_dma_start(out=r_.t[:], out_offset=None, in_=uvflat, in_offset=bass.IndirectOffsetOnAxis(ap=iu.t[:, hk:hk + 1], axis=0)))
                        S.op('dve', lambda g, r_=r_, xn_=xn_, hk=hk, dots=dots: g.scalar_tensor_tensor(out=junk.t[:], in0=r_.t[:, 0:D], scalar=1.0, in1=xn_.t[:], op0=ALU.mult, op1=ALU.mult, accum_out=dots.t[:, hk:hk + 1]), reads=[r_, xn_], writes=[dots.part(hk // 8)])
                        if nxt is not None:
                            for _ in range(2):
                                next(nxt, None)
                        if hk % 8 == 7:
                            b0 = hk - 7
                            S.op('act', lambda g, av=av, dots=dots, b0=b0: g.activation(out=av.t[:, b0:b0 + 8], in_=dots.t[:, b0:b0 + 8], func=AF.Gelu), reads=[dots.part(b0 // 8)], writes=[av.part(b0 // 8)])
                            S.op('dve', lambda g, av=av, b0=b0, gwf=gwf: g.tensor_tensor(out=av.t[:, b0:b0 + 8], in0=av.t[:, b0:b0 + 8], in1=gwf[:, b0:b0 + 8], op=ALU.mult), reads=[av.part(b0 // 8), gw], writes=[av.part(b0 // 8)])
                            for k in range(b0, hk + 1):
                                dg = dgs[k % 8]
                                rk = uvr[k % NR]
                                S.op('act', lambda g, dg=dg, av=av, k=k: g.activation(out=dg.t[:], in_=ident_b.t[:], func=AF.Copy, scale=av.t[:, k:k + 1]), reads=[ident_b, av.part(k // 8)], writes=[dg])
                                for half in range(2):
                                    S.op('pe', lambda g, dg=dg, rk=rk, half=half, k=k: g.matmul(py[half].t[:], lhsT=dg.t[:], rhs=rk.t[:, D + half * 512:D + (half + 1) * 512], start=(k == 0), stop=(k == 127)), reads=[dg, rk], writes=[py[half]])
                    if nxt is not None:
                        for _ in nxt:
                            pass
                    for half in range(2):
                        S.op('dve', lambda g, half=half, s=s: g.tensor_tensor(out=ytmp.t[:, half * 512:(half + 1) * 512], in0=py[half].t[:], in1=g5[s].t[:, half * 512:(half + 1) * 512], op=ALU.mult), reads=[py[half], g5[s]], writes=[ytmp])
                    S.op('dve', lambda g, x_=x_: g.tensor_tensor(out=x_.t[:], in0=x_.t[:], in1=ytmp.t[:], op=ALU.add), reads=[x_, ytmp], writes=[x_])
                    S.dma('sp', xr, x_.t[:], reads=[x_])

        def stage_final():
            with Stage("final"):
                gf = S.sbuf("gf", [128, D], F32)
                S.dma('sp', gf.t[:], gfin, writes=[gf])
                xin = [S.sbuf("fx", [128, D], F32) for _ in range(3)]
                junk = S.sbuf("fj", [128, D], BF16)
                yo = [S.sbuf("fy", [128, D], F32) for _ in range(2)]
                ss = S.sbuf("fss", [128, 32], F32)
                t1 = S.sbuf("ft1", [128, 32], F32)
                for tt in range(32):
                    xt = xin[tt % 3]
                    S.dma('sp', xt.t[:], xres[tt * 128:(tt + 1) * 128, :], writes=[xt])
                    S.op('act', lambda g, xt=xt, tt=tt: g.activation(out=junk.t[:], in_=xt.t[:], func=AF.Square, accum_out=ss.t[:, tt:tt + 1]), reads=[xt], writes=[junk, ss.part(tt)])
                    S.op('dve', lambda g, tt=tt: g.tensor_scalar(out=t1.t[:, tt:tt + 1], in0=ss.t[:, tt:tt + 1], scalar1=1.0 / D, scalar2=EPS, op0=ALU.mult, op1=ALU.add), reads=[ss.part(tt)], writes=[t1.part(tt)])
                    S.op('act', lambda g, tt=tt: g.activation(out=t1.t[:, tt:tt + 1], in_=t1.t[:, tt:tt + 1], func=AF.Sqrt), reads=[t1.part(tt)], writes=[t1.part(tt)])
                    S.op('dve', lambda g, tt=tt: g.reciprocal(out=t1.t[:, tt:tt + 1], in_=t1.t[:, tt:tt + 1]), reads=[t1.part(tt)], writes=[t1.part(tt)])
                    y_ = yo[tt % 2]
                    S.op('dve', lambda g, y_=y_, xt=xt, tt=tt: g.scalar_tensor_tensor(out=y_.t[:], in0=xt.t[:], scalar=t1.t[:, tt:tt + 1], in1=gf.t[:], op0=ALU.mult, op1=ALU.mult), reads=[xt, t1.part(tt), gf], writes=[y_])
                    S.dma('sp', y_out[tt * 128:(tt + 1) * 128, :], y_.t[:], reads=[y_])

        S.barrier()
        S.flush()
        plan = []
        for l in range(DEPTH):
            first = (l == 0)
            plan += [("mod", lambda l=l: stage_mod(l)),
                     ("norm1", lambda l=l, first=first: stage_norm(l, 0, first)),
                     ("proj", lambda l=l: stage_proj(l)),
                     ("attn", lambda l=l: stage_attn(l)),
                     ("hyfc", lambda l=l: stage_hy_filt(l, C, zT_c, dec_c, ed_c)),
                     ("hysc_c", lambda l=l: stage_hy_sc(l, C, 0, ub_c)),
                     ("hytc", lambda l=l: stage_hy_toep(l, C, 0, ed_c, ub_c)),
                     ("hyfm", lambda l=l: stage_hy_filt(l, T, zT_m, dec_m, ed_m)),
                     ("hysc_m", lambda l=l: stage_hy_sc(l, T, C, ub_m)),
                     ("hytm", lambda l=l: stage_hy_toep(l, T, C, ed_m, ub_m)),
                     ("rglru", lambda l=l: stage_rglru(l)),
                     ("merge", lambda l=l, first=first: stage_merge(l, first)),
                     ("norm2", lambda l=l: stage_norm(l, 1, False)),
                     ("pprep", lambda l=l: stage_peer_prep(l)),
                     ("peer", lambda l=l: stage_peer(l))]
        plan.append(("final", stage_final))
        for i, (nm, f) in enumerate(plan):
            f()
            if stop_after is not None and i + 1 >= stop_after:
                break
        S.finish()
        build.n_instr = S.n_instr
    return nc


def _chunkT(v, n):
    return np.ascontiguousarray(np.asarray(v, np.float32).reshape(n, 128).T)


def _consts():
    f32 = np.float32
    out = {}
    for nm, L in (("m", T), ("c", C)):
        t = np.linspace(0.0, 1.0, L, dtype=f32)[:, None]
        w = (f32(2.0 * math.pi / L) * np.arange(L, dtype=f32))[:, None]
        bands = np.linspace(1e-4, 15, 16, dtype=f32)[None, :]
        z = np.concatenate([t, np.cos(bands * w), -np.sin(bands * w)], axis=-1).astype(f32)
        deltas = np.linspace(math.log(1e-2) / 1.5, math.log(1e-2) / 0.3, 256, dtype=f32)
        decay = np.exp(-t * np.abs(deltas)[None, :]).astype(f32)
        out["zT_" + nm] = np.ascontiguousarray(z.T)
        out["dec_" + nm] = np.ascontiguousarray(decay.T.reshape(2, 128, L).transpose(1, 0, 2))
    out["ident"] = np.eye(128, dtype=f32)
    out["iota16"] = np.ascontiguousarray(np.broadcast_to(np.arange(16, dtype=f32)[None, :], (128, 16)))
    osel = np.zeros((128, 2, 128), f32)
    osel[:, 0, 0:64] = 1.0
    osel[:, 1, 64:128] = 1.0
    out["onesel"] = osel
    return out


def _rpb_table(rpb):
    wq = np.arange(64)
    start = np.clip(wq - 8, 0, 48)
    wk = np.arange(64)
    colok = (wk[:, None] >= start[None, :]) & (wk[:, None] < start[None, :] + 16)
    dc = np.clip(wk[:, None] - wq[None, :] + 15, 0, 30)
    tab = np.full((DEPTH, 128, 8, 22, 64), NEG, np.float32)
    for half in range(2):
        for ep in range(22):
            e = ep - 3 - half
            if e < 0 or e > 14:
                continue
            dr = 14 - e
            g = rpb[:, :, dr][:, :, dc]
            g = np.where(colok[None, None], g, np.float32(NEG))
            tab[:, half * 64:(half + 1) * 64, :, ep, :] = g.transpose(0, 2, 1, 3)
    return np.ascontiguousarray(tab.reshape(DEPTH, 128, 8, 22 * 64))


_NC_CACHE = {}


def kernel(**inp):
    f32 = np.float32
    g = lambda k: np.asarray(inp[k], f32)
    shared = dict(_consts())
    shared["gmix"] = np.stack([_chunkT(g("norm_mix_g")[l], 8) for l in range(DEPTH)])
    shared["gffn"] = np.stack([_chunkT(g("norm_ffn_g")[l], 8) for l in range(DEPTH)])
    shared["gfin"] = np.ascontiguousarray(np.broadcast_to(g("final_g")[None, :], (128, D)))
    shared["w_ada"] = g("w_ada")
    shared["b_adaT"] = np.stack([_chunkT(g("b_ada")[l], 48) for l in range(DEPTH)])
    shared["w_in"] = g("w_in")
    shared["tab"] = _rpb_table(g("na_rpb"))
    shared["hy_swT"] = np.ascontiguousarray(g("hy_short_w").reshape(DEPTH, 3, 6, 128).transpose(0, 3, 2, 1))
    shared["hy_sbT"] = np.stack([_chunkT(g("hy_short_b")[l], 6) for l in range(DEPTH)])
    shared["hy_f1w"] = g("hy_f1_w")
    shared["hy_f1b"] = g("hy_f1_b")[:, :, None].copy()
    shared["hy_f2w"] = g("hy_f2_w")
    shared["hy_f2b"] = g("hy_f2_b")[:, :, None].copy()
    shared["hy_f3w"] = g("hy_f3_w")
    shared["hy_fq"] = np.ascontiguousarray(g("hy_freq").transpose(0, 2, 1))
    shared["hy_dsk"] = np.stack([_chunkT(g("hy_bias")[l], 2) for l in range(DEPTH)])
    shared["rg_cw"] = np.ascontiguousarray(g("rg_conv_w").reshape(DEPTH, 4, 2, 128).transpose(0, 3, 2, 1))
    shared["rg_cb"] = np.stack([_chunkT(g("rg_conv_b")[l], 2) for l in range(DEPTH)])
    shared["rg_wa"] = g("rg_wa")
    shared["rg_wx"] = g("rg_wx")
    for nm, k in (("rg_baT", "rg_ba"), ("rg_bxT", "rg_bx"), ("rg_lamT", "rg_lambda")):
        shared[nm] = np.ascontiguousarray(g(k).reshape(DEPTH, 2, 2, 128).transpose(0, 3, 2, 1))
    shared["w_gate"] = g("w_gate")
    shared["b_gateT"] = np.stack([_chunkT(g("b_gate")[l], 24) for l in range(DEPTH)])
    shared["w_br_a"] = g("w_br_a")
    shared["w_br_b"] = g("w_br_b")
    shared["w_br_c"] = g("w_br_c")
    shared["w_out"] = g("w_out")
    shared["peer_wq"] = g("peer_wq")
    shared["peer_keys"] = g("peer_keys")
    shared["peer_u"] = g("peer_u")
    shared["peer_v"] = g("peer_v")
    x = g("x")
    ctx = g("ctx")
    c = g("c")
    cc = g("c_ctx")
    nb = x.shape[0]
    in_maps = []
    for b in range(nb):
        m = dict(shared)
        m["x"] = np.ascontiguousarray(x[b])
        m["ctx"] = np.ascontiguousarray(ctx[b])
        m["cvec"] = np.ascontiguousarray(np.stack([_chunkT(c[b], 8), _chunkT(cc, 8)], axis=-1))
        in_maps.append(m)
    if "nc" not in _NC_CACHE:
        _NC_CACHE["nc"] = build()
    nc = _NC_CACHE["nc"]
    res = run_bass_kernel_spmd(nc, in_maps, core_ids=list(range(nb)))
    return np.stack([np.asarray(r["y"], f32) for r in res.results], axis=0)
```

```python
import numpy as np
import concourse.bass as bass
import concourse.mybir as mybir
from concourse.bass_utils import run_bass_kernel_spmd
from contextlib import ExitStack

F32 = mybir.dt.float32
BF16 = mybir.dt.bfloat16
I32 = mybir.dt.int32
U32 = mybir.dt.uint32
U16 = mybir.dt.uint16
AF = mybir.ActivationFunctionType
ALU = mybir.AluOpType
AX = mybir.AxisListType

ENGS = ['pe', 'dve', 'act', 'pool', 'sp']
EPOCH = 30000
N_DMA_SEMS = 8


class Res:
    __slots__ = ('w', 'r')

    def __init__(self):
        self.w = {}
        self.r = {}


class Buf:
    def __init__(self, t):
        self.t = t
        self.res = Res()
        self.parts = {}

    def part(self, key):
        r = self.parts.get(key)
        if r is None:
            r = Res()
            self.parts[key] = r
        return r


def _res(x):
    return x.res if isinstance(x, Buf) else x


class Sched:
    def __init__(self, nc, es):
        self.nc = nc
        self.es = es
        self.ops = {e: [] for e in ENGS}
        self.sem = {}
        self.cnt = {}
        self.known = {e: {} for e in ENGS}
        self.nsem = 0
        for e in ENGS:
            self._new_epoch(e)
        self.dma_sems = [self._mksem("dq%d" % i) for i in range(N_DMA_SEMS)]
        self.dma_cnt = [0] * N_DMA_SEMS
        self.dma_rr = 0
        self.all_sems = {}
        self.n_instr = 0
        self.stage_es = None
        self.load_sems = []
        self.load_cnt = []
        self.buf_sem = {}
        self.next_load = 0

    def _mksem(self, name):
        self.nsem += 1
        return self.es.enter_context(self.nc.semaphore("%s_%d" % (name, self.nsem)))

    def _new_epoch(self, e):
        self.sem[e] = self._mksem("e" + e)
        self.cnt[e] = 0

    def sbuf(self, name, shape, dtype):
        self.nbuf = getattr(self, 'nbuf', 0) + 1
        es = self.stage_es if self.stage_es is not None else self.es
        return Buf(es.enter_context(self.nc.sbuf_tensor("%s_%d" % (name, self.nbuf), shape, dtype)))

    def psum(self, name, shape, dtype):
        return Buf(self.es.enter_context(self.nc.psum_tensor(name, shape, dtype)))

    def barrier(self):
        for e in ENGS:
            kn = self.known[e]
            for sid, (sem, val) in self.all_sems.items():
                if kn.get(sid, 0) < val:
                    kn[sid] = val
                    self.ops[e].append(('wait', sem, val))
            for o in ENGS:
                if o == e or o == 'sp' or self.cnt[o] == 0:
                    continue
                sem = self.sem[o]
                if kn.get(id(sem), 0) < self.cnt[o]:
                    kn[id(sem)] = self.cnt[o]
                    self.ops[e].append(('wait', sem, self.cnt[o]))

    def dram(self, name, shape, dtype):
        return Buf(self.nc.dram_tensor(name, shape, dtype, kind="Internal"))

    def _collect(self, e, reads, writes):
        need = {}

        def mrg(d):
            for s, v in d.items():
                if need.get(s, (None, 0))[1] < v[1]:
                    need[s] = v
        for r in reads:
            mrg(_res(r).w)
        for w in writes:
            w = _res(w)
            mrg(w.w)
            mrg(w.r)
        kn = self.known[e]
        for sid, (sem, val, eng) in need.items():
            if eng == 'pe' and e == 'pe':
                continue
            if kn.get(sid, 0) < val:
                kn[sid] = val
                self.ops[e].append(('wait', sem, val))

    def _publish(self, key, reads, writes):
        sid = id(key[0])
        for w in writes:
            w = _res(w)
            w.w = {sid: key}
            w.r = {}
        for r in reads:
            _res(r).r[sid] = key

    def op(self, e, fn, reads=(), writes=()):
        self._collect(e, reads, writes)
        if self.cnt[e] >= EPOCH:
            self._new_epoch(e)
        self.cnt[e] += 1
        sem = self.sem[e]
        self.ops[e].append(('op', fn, sem))
        self._publish((sem, self.cnt[e], e), reads, writes)
        self.n_instr += 1

    def dma(self, e, out, in_, reads=(), writes=(), fn=None, **kw):
        self._collect(e, reads, writes)
        if writes:
            key = id(_res(writes[0]))
            i = self.buf_sem.get(key)
            if i is None:
                i = self.next_load
                self.next_load += 1
                if i >= len(self.load_sems):
                    self.load_sems.append(self._mksem("dl"))
                    self.load_cnt.append(0)
                self.buf_sem[key] = i
            self.load_cnt[i] += 16
            if self.load_cnt[i] >= EPOCH:
                self.load_sems[i] = self._mksem("dl")
                self.load_cnt[i] = 16
            sem, cnt = self.load_sems[i], self.load_cnt[i]
        else:
            i = self.dma_rr
            self.dma_rr = (i + 1) % N_DMA_SEMS
            self.dma_cnt[i] += 16
            if self.dma_cnt[i] >= EPOCH:
                self.dma_sems[i] = self._mksem("dq")
                self.dma_cnt[i] = 16
            sem, cnt = self.dma_sems[i], self.dma_cnt[i]
        if fn is None:
            def fn(g, out=out, in_=in_, kw=kw):
                return g.dma_start(out=out, in_=in_, **kw)
        self.ops[e].append(('dma', fn, sem))
        self._publish((sem, cnt, 'dma'), reads, writes)
        self.all_sems[id(sem)] = (sem, cnt)
        self.n_instr += 1

    def finish(self):
        sp = self.ops['sp']
        for sid, (sem, val) in self.all_sems.items():
            sp.append(('wait', sem, val))
        for e in ENGS:
            if e != 'sp' and self.cnt[e] > 0:
                sp.append(('wait', self.sem[e], self.cnt[e]))
        self.flush()

    def flush(self):
        nc = self.nc
        ops = self.ops
        self.buf_sem = {}
        self.next_load = 0
        self.ops = {e: [] for e in ENGS}

        def emit(eng, lst):
            for it in lst:
                if it[0] == 'wait':
                    eng.wait_ge(it[1], it[2])
                elif it[0] == 'op':
                    it[1](eng).then_inc(it[2], 1)
                else:
                    it[1](eng).then_inc(it[2], 16)

        with nc.Block() as block:
            @block.tensor
            def _(g):
                emit(g, ops['pe'])

            @block.vector
            def _(g):
                emit(g, ops['dve'])

            @block.scalar
            def _(g):
                emit(g, ops['act'])

            @block.gpsimd
            def _(g):
                emit(g, ops['pool'])

            @block.sync
            def _(g):
                emit(g, ops['sp'])

import math

D = 1024
T = 4096
C = 256
NTOK = T + C
DEPTH = 2
EPS = 1e-6
GW = 64
NEG = -30000.0
DBG = False
STAGES = None


def _groups():
    g = [(0, C)]
    for i in range(T // 512):
        g.append((C + 512 * i, 512))
    return g


def build(dbg=False, stop_after=None):
    nc = bass.Bass("TRN2", target_bir_lowering=False)

    def din(name, shape, dt=F32):
        return nc.dram_tensor(name, list(shape), dt, kind="ExternalInput").ap()

    kind_s = "ExternalOutput" if dbg else "Internal"

    def dscr(name, shape, dt):
        return nc.dram_tensor(name, list(shape), dt, kind=kind_s).ap()

    x_in = din("x", [T, D])
    c_in = din("ctx", [C, D])
    cvec = din("cvec", [128, 8, 2])
    gmix = din("gmix", [DEPTH, 128, 8])
    gffn = din("gffn", [DEPTH, 128, 8])
    gfin = din("gfin", [128, D])
    w_ada = din("w_ada", [DEPTH, D, 6 * D])
    b_adaT = din("b_adaT", [DEPTH, 128, 48])
    w_in = din("w_in", [DEPTH, D, 2816])
    tab_in = din("tab", [DEPTH, 128, 8, 22 * 64])
    hy_swT = din("hy_swT", [DEPTH, 128, 6, 3])
    hy_sbT = din("hy_sbT", [DEPTH, 128, 6])
    hy_f1w = din("hy_f1w", [DEPTH, 33, 64])
    hy_f1b = din("hy_f1b", [DEPTH, 64, 1])
    hy_f2w = din("hy_f2w", [DEPTH, 64, 64])
    hy_f2b = din("hy_f2b", [DEPTH, 64, 1])
    hy_f3w = din("hy_f3w", [DEPTH, 64, 512])
    hy_fq = din("hy_fq", [DEPTH, 64, 2])
    hy_dsk = din("hy_dsk", [DEPTH, 128, 2])
    zT_m = din("zT_m", [33, T])
    zT_c = din("zT_c", [33, C])
    dec_m = din("dec_m", [128, 2, T])
    dec_c = din("dec_c", [128, 2, C])
    rg_cw = din("rg_cw", [DEPTH, 128, 2, 4])
    rg_cb = din("rg_cb", [DEPTH, 128, 2])
    rg_wa = din("rg_wa", [DEPTH, 2, 4, 64, 64])
    rg_wx = din("rg_wx", [DEPTH, 2, 4, 64, 64])
    rg_baT = din("rg_baT", [DEPTH, 128, 2, 2])
    rg_bxT = din("rg_bxT", [DEPTH, 128, 2, 2])
    rg_lamT = din("rg_lamT", [DEPTH, 128, 2, 2])
    w_gate = din("w_gate", [DEPTH, D, 3 * D])
    b_gateT = din("b_gateT", [DEPTH, 128, 24])
    w_bra = din("w_br_a", [DEPTH, 512, D])
    w_brb = din("w_br_b", [DEPTH, 256, D])
    w_brc = din("w_br_c", [DEPTH, 256, D])
    w_out = din("w_out", [DEPTH, D, D])
    p_wq = din("peer_wq", [DEPTH, D, 2048])
    p_keys = din("peer_keys", [DEPTH, 8, 2, 128, 128])
    p_u = din("peer_u", [DEPTH, 16384, D])
    p_v = din("peer_v", [DEPTH, 16384, D])
    ident_in = din("ident", [128, 128])
    iota_in = din("iota16", [128, 16])
    osel_in = din("onesel", [128, 2, 128])
    y_out = nc.dram_tensor("y", [T, D], F32, kind="ExternalOutput").ap()

    xres = dscr("xres", [T, D], F32)
    cres = dscr("cres", [C, D], F32)
    bcd = dscr("bcd", [10, 128, D], F32)
    xnT_d = dscr("xnT_d", [128, 8, NTOK], BF16)
    xntm_d = dscr("xntm_d", [NTOK, D], BF16)
    qkT_d = dscr("qkT_d", [8, 128, NTOK], BF16)
    vp_d = dscr("vp_d", [4, NTOK, 256], BF16)
    phT_d = dscr("phT_d", [6, 128, NTOK], F32)
    prT_d = dscr("prT_d", [4, 128, NTOK], F32)
    yaT_d = dscr("yaT_d", [4, 128, NTOK], BF16)
    ybT_d = dscr("ybT_d", [2, 128, NTOK], BF16)
    ycT_d = dscr("ycT_d", [2, 128, NTOK], BF16)
    ed_m = dscr("ed_m", [256, 2 * T], BF16)
    ed_c = dscr("ed_c", [256, 2 * C], BF16)
    ub_m = dscr("ub_m", [128, 2 * 128 * 32], BF16)
    ub_c = dscr("ub_c", [128, 2 * 128 * 2], BF16)
    x0_d = dscr("x0_d", [2, 128, NTOK], BF16)
    uv_d = nc.dram_tensor("uv_d", [16384, 2 * D], BF16, kind="Internal").ap()
    qT_d = nc.dram_tensor("qT_d", [128, 16, NTOK], BF16, kind="Internal").ap()
    dbg_h = dscr("dbg_h", [2, 64, T], F32) if dbg else None

    ges = ExitStack()
    with ges:
        S = Sched(nc, ges)
        PS = [S.psum("ps%d" % i, [128, 512], F32) for i in range(6)]
        PSB = [S.psum("psb%d" % i, [128, 1024], BF16) for i in range(2)]
        ident_f = S.sbuf("ident_f", [128, 128], F32)
        ident_b = S.sbuf("ident_b", [128, 128], BF16)
        ones_f = S.sbuf("ones_f", [128, 128], F32)
        iota16 = S.sbuf("iota16", [128, 16], F32)
        osel = S.sbuf("osel", [128, 2, 128], BF16)
        osel_f = S.sbuf("osel_f", [128, 2, 128], F32)
        modT = S.sbuf("modT", [128, 48, 2], F32)
        Gs = S.sbuf("Gs", [128, 2, 8, 2], F32)
        S.dma('sp', ident_f.t[:], ident_in, writes=[ident_f])
        S.dma('sp', iota16.t[:], iota_in, writes=[iota16])
        S.dma('sp', osel_f.t[:], osel_in, writes=[osel_f])
        S.op('dve', lambda g: g.tensor_copy(out=ident_b.t[:], in_=ident_f.t[:]), reads=[ident_f], writes=[ident_b])
        S.op('dve', lambda g: g.tensor_copy(out=osel.t[:], in_=osel_f.t[:]), reads=[osel_f], writes=[osel])
        S.op('dve', lambda g: g.memset(ones_f.t[:], 1.0), writes=[ones_f])
        rr = {'n': 0}

        def alt(engs=('dve', 'act')):
            rr['n'] += 1
            return engs[rr['n'] % len(engs)]

        def copy_op(e, out, in_, reads, writes):
            if e == 'act':
                S.op('act', lambda g: g.activation(out=out, in_=in_, func=AF.Copy), reads=reads, writes=writes)
            else:
                S.op(e, lambda g: g.tensor_copy(out=out, in_=in_), reads=reads, writes=writes)

        class Stage:
            def __init__(self, name):
                self.name = name

            def __enter__(self):
                self.es = ExitStack()
                self.es.__enter__()
                S.stage_es = self.es
                return self

            def __exit__(self, *a):
                S.barrier()
                S.flush()
                S.stage_es = None
                self.es.__exit__(None, None, None)
                return False

        GROUPS = _groups()

        def stage_mod(l):
            with Stage("mod"):
                sc = S.sbuf("sc", [128, 8, 2], F32)
                ba = S.sbuf("ba", [128, 48], F32)
                gm = S.sbuf("gm", [128, 2, 8], F32)
                wb = [S.sbuf("wada", [128, 6144], F32) for _ in range(2)]
                diag = [S.sbuf("diag", [128, 128], F32) for _ in range(2)]
                bct = [S.sbuf("bct", [128, D], F32) for _ in range(2)]
                S.dma('sp', sc.t[:], cvec, writes=[sc])
                S.dma('sp', ba.t[:], b_adaT[l], writes=[ba])
                S.dma('sp', gm.t[:, 0, :], gmix[l], writes=[gm])
                S.dma('sp', gm.t[:, 1, :], gffn[l], writes=[gm])
                S.op('act', lambda g: g.activation(out=sc.t[:], in_=sc.t[:], func=AF.Silu), reads=[sc], writes=[sc])
                ps = PS[0]
                S.op('dve', lambda g: g.memset(ps.t[:, 0:96], 0.0), writes=[ps])
                for kc in range(8):
                    w = wb[kc % 2]
                    for hh in range(2):
                        S.dma('sp', w.t[:, hh * 3072:(hh + 1) * 3072], w_ada[l, kc * 128:(kc + 1) * 128, hh * 3072:(hh + 1) * 3072], writes=[w])
                    for j in range(48):
                        S.op('pe', lambda g, w=w, j=j, kc=kc: g.matmul(ps.t[:, 2 * j:2 * j + 2], lhsT=w.t[:, j * 128:(j + 1) * 128], rhs=sc.t[:, kc, :], start=False, stop=False, skip_group_check=True), reads=[w, sc], writes=[ps])
                S.op('dve', lambda g: g.tensor_tensor(out=modT.t[:], in0=ps.t[:, 0:96].rearrange("p (j s) -> p j s", s=2), in1=ba.t[:].unsqueeze(2).broadcast_to([128, 48, 2]), op=ALU.add), reads=[ps, ba], writes=[modT])
                for wh, off in ((0, 8), (1, 32)):
                    S.op('dve', lambda g, wh=wh, off=off: g.tensor_scalar(out=Gs.t[:, wh], in0=modT.t[:, off:off + 8, :], scalar1=1.0, scalar2=None, op0=ALU.add), reads=[modT], writes=[Gs])
                    S.op('dve', lambda g, wh=wh: g.tensor_tensor(out=Gs.t[:, wh], in0=Gs.t[:, wh], in1=gm.t[:, wh, :].unsqueeze(2).broadcast_to([128, 8, 2]), op=ALU.mult), reads=[Gs, gm], writes=[Gs])
                def srcs(idx):
                    if idx < 8:
                        wh, kind, s = idx // 4, (idx // 2) % 2, idx % 2
                        if kind == 0:
                            return lambda kc: Gs.t[:, wh, kc, s:s + 1]
                        off = 0 if wh == 0 else 24
                        return lambda kc: modT.t[:, off + kc, s:s + 1]
                    s = idx - 8
                    return lambda kc: modT.t[:, 40 + kc, s:s + 1]
                n = 0
                for idx in range(10):
                    f = srcs(idx)
                    bt = bct[idx % 2]
                    for half in range(2):
                        pb = PS[1 + (n % 2)]
                        n += 1
                        for k4 in range(4):
                            kc = half * 4 + k4
                            dg = diag[kc % 2]
                            S.op('dve', lambda g, dg=dg, f=f, kc=kc: g.tensor_scalar(out=dg.t[:], in0=ident_f.t[:], scalar1=f(kc), scalar2=None, op0=ALU.mult), reads=[ident_f, Gs, modT], writes=[dg])
                            S.op('pe', lambda g, pb=pb, dg=dg, k4=k4: g.matmul(pb.t[:, k4 * 128:(k4 + 1) * 128], lhsT=ones_f.t[:], rhs=dg.t[:], start=True, stop=True, skip_group_check=True), reads=[ones_f, dg], writes=[pb])
                        copy_op('act', bt.t[:, half * 512:(half + 1) * 512], pb.t[:], [pb], [bt])
                    S.dma('sp', bcd[idx], bt.t[:], reads=[bt])

        def stage_norm(l, wh, first):
            with Stage("norm"):
                Gb = [S.sbuf("Gb", [128, D], F32) for _ in range(2)]
                Sb = [S.sbuf("Sb", [128, D], F32) for _ in range(2)]
                for s in range(2):
                    S.dma('sp', Gb[s].t[:], bcd[wh * 4 + 0 + s], writes=[Gb[s]])
                    S.dma('sp', Sb[s].t[:], bcd[wh * 4 + 2 + s], writes=[Sb[s]])
                xnT = S.sbuf("xnT", [128, 8, NTOK], BF16)
                xin = [S.sbuf("xin", [128, D], F32) for _ in range(3)]
                junk = S.sbuf("junk", [128, D], BF16)
                tmp = [S.sbuf("tmpn", [128, D], F32) for _ in range(2)]
                xnb = [S.sbuf("xnb", [128, D], BF16) for _ in range(2)]
                ss = S.sbuf("ss", [128, 34], F32)
                t1 = S.sbuf("t1", [128, 34], F32)
                rstd = S.sbuf("rstd", [128, 34], F32)
                for tt in range(34):
                    s = 1 if tt < 2 else 0
                    if tt < 2:
                        src = (c_in if first else cres)[tt * 128:(tt + 1) * 128, :]
                    else:
                        src = (x_in if first else xres)[(tt - 2) * 128:(tt - 1) * 128, :]
                    xt = xin[tt % 3]
                    S.dma('sp', xt.t[:], src, writes=[xt])
                    S.op('act', lambda g, xt=xt, tt=tt: g.activation(out=junk.t[:], in_=xt.t[:], func=AF.Square, accum_out=ss.t[:, tt:tt + 1]), reads=[xt], writes=[junk, ss.part(tt)])
                    S.op('dve', lambda g, tt=tt: g.tensor_scalar(out=t1.t[:, tt:tt + 1], in0=ss.t[:, tt:tt + 1], scalar1=1.0 / D, scalar2=EPS, op0=ALU.mult, op1=ALU.add), reads=[ss.part(tt)], writes=[t1.part(tt)])
                    S.op('act', lambda g, tt=tt: g.activation(out=t1.t[:, tt:tt + 1], in_=t1.t[:, tt:tt + 1], func=AF.Sqrt), reads=[t1.part(tt)], writes=[t1.part(tt)])
                    S.op('dve', lambda g, tt=tt: g.reciprocal(out=rstd.t[:, tt:tt + 1], in_=t1.t[:, tt:tt + 1]), reads=[t1.part(tt)], writes=[rstd.part(tt)])
                    tm = tmp[tt % 2]
                    xb = xnb[tt % 2]
                    S.op('dve', lambda g, tm=tm, xt=xt, tt=tt, s=s: g.scalar_tensor_tensor(out=tm.t[:], in0=xt.t[:], scalar=rstd.t[:, tt:tt + 1], in1=Gb[s].t[:], op0=ALU.mult, op1=ALU.mult), reads=[xt, rstd.part(tt), Gb[s]], writes=[tm])
                    S.op('pool', lambda g, tm=tm, xb=xb, s=s: g.tensor_tensor(out=xb.t[:], in0=tm.t[:], in1=Sb[s].t[:], op=ALU.add), reads=[tm, Sb[s]], writes=[xb])
                    S.dma('sp', xntm_d[tt * 128:(tt + 1) * 128, :], xb.t[:], reads=[xb])
                    pb = PSB[tt % 2]
                    for kc in range(8):
                        S.op('pe', lambda g, pb=pb, xb=xb, kc=kc: g.transpose(out=pb.t[:, kc * 128:(kc + 1) * 128], in_=xb.t[:, kc * 128:(kc + 1) * 128], identity=ident_b.t[:]), reads=[xb, ident_b], writes=[pb])
                    copy_op('act', xnT.t[:, :, tt * 128:(tt + 1) * 128], pb.t[:].rearrange("p (k t) -> p k t", k=8), [pb], [xnT.part(tt)])
                for kc in range(8):
                    S.dma('sp', xnT_d[:, kc, :], xnT.t[:, kc, :], reads=[xnT.part(tt) for tt in range(34)])

        def stage_proj(l):
            with Stage("proj"):
                xnT = S.sbuf("xnT", [128, 8, NTOK], BF16)
                for kc in range(8):
                    S.dma('sp', xnT.t[:, kc, :], xnT_d[:, kc, :], writes=[xnT])
                wst = [S.sbuf("wst", [128, 2816], F32) for _ in range(2)]
                wbf = S.sbuf("wbf", [128, 8, 2816], BF16)
                for kc in range(8):
                    ws = wst[kc % 2]
                    S.dma('sp', ws.t[:], w_in[l, kc * 128:(kc + 1) * 128, :], writes=[ws])
                    copy_op(alt(('dve', 'pool')), wbf.t[:, kc, :], ws.t[:], [ws], [wbf.part(kc)])
                wparts = [wbf.part(kc) for kc in range(8)]
                ob = [S.sbuf("ob", [128, 512], BF16) for _ in range(3)]
                of = [S.sbuf("of", [128, 512], F32) for _ in range(3)]
                vpt = [S.sbuf("vpt", [128, 8, 128], BF16) for _ in range(2)]
                for v in vpt:
                    S.op('pool', lambda g, v=v: g.memset(v.t[:], 0.0), writes=[v])
                n = 0
                for (c0, nn) in GROUPS:
                    for ch in list(range(8)) + list(range(12, 22)):
                        ps = PS[n % 4]
                        for kc in range(8):
                            S.op('pe', lambda g, ps=ps, kc=kc, ch=ch, c0=c0, nn=nn: g.matmul(ps.t[:, 0:nn], lhsT=wbf.t[:, kc, ch * 128:(ch + 1) * 128], rhs=xnT.t[:, kc, c0:c0 + nn], start=(kc == 0), stop=(kc == 7)), reads=[xnT] + wparts, writes=[ps])
                        if ch < 8:
                            o = ob[n % 3]
                            if ch < 4:
                                S.op('act', lambda g, o=o, ps=ps, nn=nn: g.activation(out=o.t[:, 0:nn], in_=ps.t[:, 0:nn], func=AF.Copy, scale=0.125), reads=[ps], writes=[o])
                            else:
                                copy_op('dve', o.t[:, 0:nn], ps.t[:, 0:nn], [ps], [o])
                            S.dma('sp', qkT_d[ch, :, c0:c0 + nn], o.t[:, 0:nn], reads=[o])
                        else:
                            o = of[n % 3]
                            copy_op(alt(), o.t[:, 0:nn], ps.t[:, 0:nn], [ps], [o])
                            dst = phT_d[ch - 12] if ch < 18 else prT_d[ch - 18]
                            S.dma('sp', dst[:, c0:c0 + nn], o.t[:, 0:nn], reads=[o])
                        n += 1
                    for t4 in range(nn // 128):
                        tc0 = c0 + t4 * 128
                        ps = PS[4 + (n % 2)]
                        vt = vpt[n % 2]
                        n += 1
                        for kc in range(8):
                            S.op('pe', lambda g, ps=ps, kc=kc, tc0=tc0: g.matmul(ps.t[:], lhsT=xnT.t[:, kc, tc0:tc0 + 128], rhs=wbf.t[:, kc, 1024:1536], start=(kc == 0), stop=(kc == 7)), reads=[xnT] + wparts, writes=[ps])
                        S.op('dve', lambda g, ps=ps, vt=vt: g.tensor_copy(out=bass.AP(vt.t[:].tensor, vt.t[:].offset, [[1024, 128], [256, 4], [192, 2], [1, 64]]), in_=ps.t[:].rearrange("p (j e d) -> p j e d", j=4, e=2)), reads=[ps], writes=[vt])
                        for j in range(4):
                            S.dma('sp', vp_d[j, tc0:tc0 + 128, :], vt.t[:, 2 * j:2 * j + 2, :].rearrange("p h c -> p (h c)"), reads=[vt])

        def attn_ranges(i):
            r0 = 8 * i
            rows = list(range(r0, r0 + 8))
            rs = lambda r: min(max(r - 4, 0), GW - 8)
            amin = min(rs(r) for r in rows)
            amax = max(rs(r) for r in rows) + 7
            a0s = list(range(amin - (amin % 2), amax + 1, 2))
            out = []
            for a0 in a0s:
                hal = []
                for a in (a0, a0 + 1):
                    v = [r for r in rows if rs(r) <= a <= rs(r) + 7] if a < GW else []
                    hal.append((v[0] - r0, v[-1] - r0 + 1) if v else None)
                lo = min(h[0] for h in hal if h)
                hi = max(h[1] for h in hal if h)
                out.append((a0, hal, (lo, hi)))
            return out

        def stage_attn(l):
            with Stage("attn"):
                tab = S.sbuf("tab", [128, 8, 22 * 64], BF16)
                tst = [S.sbuf("tst", [128, 22 * 64], F32) for _ in range(2)]
                for h in range(8):
                    S.dma('sp', tst[h % 2].t[:], tab_in[l, :, h, :], writes=[tst[h % 2]])
                    copy_op(alt(('dve', 'pool')), tab.t[:, h, :], tst[h % 2].t[:], [tst[h % 2]], [tab.part(h)])
                qTs = [S.sbuf("qTs", [128, NTOK], BF16) for _ in range(2)]
                kTs = [S.sbuf("kTs", [128, NTOK], BF16) for _ in range(2)]
                vps = [S.sbuf("vps", [128, 34, 256], BF16) for _ in range(2)]
                pts = [S.sbuf("pt", [128, 512], BF16) for _ in range(3)]
                rec = [S.sbuf("rec", [128, 512], F32) for _ in range(2)]
                yab = [S.sbuf("yab", [128, 512], BF16) for _ in range(2)]
                n = {'s': 0, 'o': 0}
                for j in range(4):
                    qT, kT, vp = qTs[j % 2], kTs[j % 2], vps[j % 2]
                    S.dma('sp', qT.t[:], qkT_d[j], writes=[qT])
                    S.dma('sp', kT.t[:], qkT_d[4 + j], writes=[kT])
                    for q4 in range(2):
                        S.dma('sp', vp.t[:, q4 * 17:(q4 + 1) * 17, :], vp_d[j, q4 * 17 * 128:(q4 + 1) * 17 * 128, :].rearrange("(t p) c -> p t c", p=128), writes=[vp])
                    for i in range(-1, 8):
                        if i < 0:
                            qc0, nq = 0, C
                            klist = [('c', 0, None), ('c', 1, None)]
                        else:
                            qc0, nq = C + 512 * i, 512
                            klist = [('c', 0, None), ('c', 1, None)] + [('l', a0, (hal, un)) for (a0, hal, un) in attn_ranges(i)]
                        O = PS[3 + (n['o'] % 2) * 1]
                        Dn = PS[5] if (n['o'] % 2) else PS[4]
                        O = PS[2] if (n['o'] % 2) else PS[3]
                        n['o'] += 1
                        first = True
                        for e in range(2):
                            h = 2 * j + e
                            pb = e * 64
                            for (kind, a0, info) in klist:
                                Sp = PS[n['s'] % 2]
                                pt = pts[n['s'] % 3]
                                n['s'] += 1
                                if kind == 'c':
                                    kc0 = a0 * 128
                                    lo, hi = 0, nq
                                    S.op('pe', lambda g, Sp=Sp, pb=pb, kc0=kc0, qc0=qc0, nq=nq, qT=qT, kT=kT: g.matmul(Sp.t[:, 0:nq], lhsT=kT.t[pb:pb + 64, kc0:kc0 + 128], rhs=qT.t[pb:pb + 64, qc0:qc0 + nq], start=True, stop=True), reads=[qT, kT], writes=[Sp])
                                    S.op('act', lambda g, Sp=Sp, pt=pt, nq=nq: g.activation(out=pt.t[:, 0:nq], in_=Sp.t[:, 0:nq], func=AF.Exp), reads=[Sp], writes=[pt])
                                    vtile = a0
                                else:
                                    hal, (ulo, uhi) = info
                                    kc0 = C + a0 * GW
                                    lo, hi = ulo * GW, uhi * GW
                                    r0 = 8 * i
                                    e0 = (r0 + ulo) - a0 + 10
                                    assert 0 <= e0 and e0 + (uhi - ulo) <= 22, (i, a0, e0)
                                    S.op('pe', lambda g, Sp=Sp, pb=pb, kc0=kc0, qc0=qc0, lo=lo, hi=hi, qT=qT, kT=kT: g.matmul(Sp.t[:, lo:hi], lhsT=kT.t[pb:pb + 64, kc0:kc0 + 128], rhs=qT.t[pb:pb + 64, qc0 + lo:qc0 + hi], start=True, stop=False), reads=[qT, kT], writes=[Sp])
                                    S.op('pe', lambda g, Sp=Sp, lo=lo, hi=hi, h=h, e0=e0: g.matmul(Sp.t[:, lo:hi], lhsT=ident_b.t[:], rhs=tab.t[:, h, e0 * 64:e0 * 64 + (hi - lo)], start=False, stop=True), reads=[tab.part(h), ident_b], writes=[Sp])
                                    for hf_, rng in enumerate(hal):
                                        p0 = hf_ * 64
                                        if rng is None:
                                            S.op('pool', lambda g, pt=pt, p0=p0, lo=lo, hi=hi: g.memset(pt.t[p0:p0 + 64, lo:hi], 0.0), writes=[pt])
                                            continue
                                        vlo, vhi = rng[0] * GW, rng[1] * GW
                                        S.op('act', lambda g, Sp=Sp, pt=pt, p0=p0, vlo=vlo, vhi=vhi: g.activation(out=pt.t[p0:p0 + 64, vlo:vhi], in_=Sp.t[p0:p0 + 64, vlo:vhi], func=AF.Exp), reads=[Sp], writes=[pt])
                                        if vlo > lo:
                                            S.op('pool', lambda g, pt=pt, p0=p0, lo=lo, vlo=vlo: g.memset(pt.t[p0:p0 + 64, lo:vlo], 0.0), writes=[pt])
                                        if vhi < hi:
                                            S.op('pool', lambda g, pt=pt, p0=p0, hi=hi, vhi=vhi: g.memset(pt.t[p0:p0 + 64, vhi:hi], 0.0), writes=[pt])
                                    vtile = 2 + a0 // 2
                                S.op('pe', lambda g, O=O, vp=vp, vtile=vtile, e=e, pt=pt, lo=lo, hi=hi, first=first: g.matmul(O.t[:, lo:hi], lhsT=vp.t[:, vtile, e * 128:(e + 1) * 128], rhs=pt.t[:, lo:hi], start=first, stop=False, skip_group_check=True), reads=[vp, pt], writes=[O])
                                S.op('pe', lambda g, Dn=Dn, e=e, pt=pt, lo=lo, hi=hi, first=first: g.matmul(Dn.t[:, lo:hi], lhsT=osel.t[:, e, :], rhs=pt.t[:, lo:hi], start=first, stop=False, skip_group_check=True), reads=[osel, pt], writes=[Dn])
                                first = False
                        rc = rec[n['o'] % 2]
                        yb_ = yab[n['o'] % 2]
                        S.op('dve', lambda g, rc=rc, Dn=Dn, nq=nq: g.reciprocal(out=rc.t[:, 0:nq], in_=Dn.t[:, 0:nq]), reads=[Dn], writes=[rc])
                        S.op('dve', lambda g, rc=rc, O=O, yb_=yb_, nq=nq: g.tensor_tensor(out=yb_.t[:, 0:nq], in0=O.t[:, 0:nq], in1=rc.t[:, 0:nq], op=ALU.mult), reads=[O, rc], writes=[yb_])
                        S.dma('sp', yaT_d[j, :, qc0:qc0 + nq], yb_.t[:, 0:nq], reads=[yb_])

        def sin_layer(ps, n, bias, fq, tmp, tmp2, out, outbuf, xr):
            S.op('dve', lambda g: g.tensor_scalar(out=tmp.t[0:64, 0:n], in0=ps.t[0:64, 0:n], scalar1=bias, scalar2=fq, op0=ALU.add, op1=ALU.mult), reads=[ps] + xr, writes=[tmp])
            MAGIC = 12582912.0
            S.op('dve', lambda g: g.tensor_scalar(out=tmp2.t[0:64, 0:n], in0=tmp.t[0:64, 0:n], scalar1=1.0 / (2 * math.pi), scalar2=MAGIC, op0=ALU.mult, op1=ALU.add), reads=[tmp], writes=[tmp2])
            S.op('dve', lambda g: g.tensor_scalar(out=tmp2.t[0:64, 0:n], in0=tmp2.t[0:64, 0:n], scalar1=MAGIC, scalar2=-2 * math.pi, op0=ALU.subtract, op1=ALU.mult), reads=[tmp2], writes=[tmp2])
            S.op('dve', lambda g: g.tensor_tensor(out=tmp.t[0:64, 0:n], in0=tmp.t[0:64, 0:n], in1=tmp2.t[0:64, 0:n], op=ALU.add), reads=[tmp, tmp2], writes=[tmp])
            S.op('dve', lambda g: g.tensor_scalar(out=tmp.t[0:64, 0:n], in0=tmp.t[0:64, 0:n], scalar1=-3.1415925, scalar2=3.1415925, op0=ALU.max, op1=ALU.min), reads=[tmp], writes=[tmp])
            S.op('act', lambda g: g.activation(out=out, in_=tmp.t[0:64, 0:n], func=AF.Sin), reads=[tmp], writes=[outbuf])

        def stage_hy_filt(l, L, zT, dec, ed):
            with Stage("hyfilt"):
                z = S.sbuf("z", [33, L], F32)
                dc = S.sbuf("dc", [128, 2, L], F32)
                f1w = S.sbuf("f1w", [33, 64], F32)
                f2w = S.sbuf("f2w", [64, 64], F32)
                f3w = S.sbuf("f3w", [64, 512], F32)
                fb = S.sbuf("fb", [64, 2], F32)
                fq = S.sbuf("fq", [64, 2], F32)
                dsk = S.sbuf("dsk", [128, 2], F32)
                h1 = S.sbuf("h1", [64, L], F32)
                h2 = S.sbuf("h2", [64, L], F32)
                hT = [S.sbuf("hT", [128, L], F32) for _ in range(4)]
                tmp = [S.sbuf("stmp", [64, 512], F32) for _ in range(4)]
                et = [S.sbuf("et", [128, 2 * L], BF16) for _ in range(2)]
                S.dma('sp', z.t[:], zT, writes=[z])
                S.dma('sp', dc.t[:], dec, writes=[dc])
                S.dma('sp', f1w.t[:], hy_f1w[l], writes=[f1w])
                S.dma('sp', f2w.t[:], hy_f2w[l], writes=[f2w])
                S.dma('sp', f3w.t[:], hy_f3w[l], writes=[f3w])
                S.dma('sp', fb.t[:, 0:1], hy_f1b[l], writes=[fb])
                S.dma('sp', fb.t[:, 1:2], hy_f2b[l], writes=[fb])
                S.dma('sp', fq.t[:], hy_fq[l], writes=[fq])
                S.dma('sp', dsk.t[:], hy_dsk[l], writes=[dsk])
                n = 0
                for c0 in range(0, L, 512):
                    nn = min(512, L - c0)
                    ps = PS[n % 2]
                    S.op('pe', lambda g, ps=ps, c0=c0, nn=nn: g.matmul(ps.t[0:64, 0:nn], lhsT=f1w.t[:], rhs=z.t[:, c0:c0 + nn], start=True, stop=True), reads=[f1w, z], writes=[ps])
                    sin_layer(ps, nn, fb.t[:, 0:1], fq.t[:, 0:1], tmp[0], tmp[2], h1.t[:, c0:c0 + nn], h1, [fb, fq])
                    ps2 = PS[2 + n % 2]
                    S.op('pe', lambda g, ps2=ps2, c0=c0, nn=nn: g.matmul(ps2.t[0:64, 0:nn], lhsT=f2w.t[:], rhs=h1.t[:, c0:c0 + nn], start=True, stop=True), reads=[f2w, h1], writes=[ps2])
                    sin_layer(ps2, nn, fb.t[:, 1:2], fq.t[:, 1:2], tmp[1], tmp[3], h2.t[:, c0:c0 + nn], h2, [fb, fq])
                    for c4 in range(4):
                        ps3 = PS[4 + c4 % 2]
                        S.op('pe', lambda g, ps3=ps3, c4=c4, c0=c0, nn=nn: g.matmul(ps3.t[:, 0:nn], lhsT=f3w.t[:, c4 * 128:(c4 + 1) * 128], rhs=h2.t[:, c0:c0 + nn], start=True, stop=True), reads=[f3w, h2], writes=[ps3])
                        S.op('dve', lambda g, ps3=ps3, c4=c4, c0=c0, nn=nn: g.tensor_tensor(out=hT[c4].t[:, c0:c0 + nn], in0=ps3.t[:, 0:nn], in1=dc.t[:, c4 % 2, c0:c0 + nn], op=ALU.mult), reads=[ps3, dc], writes=[hT[c4]])
                    n += 1
                if dbg_h is not None and L == T:
                    S.dma('sp', dbg_h[0], h1.t[:], reads=[h1])
                    S.dma('sp', dbg_h[1], h2.t[:], reads=[h2])
                for cc in range(2):
                    e_ = et[cc]
                    S.op('pool', lambda g, e_=e_: g.memset(e_.t[:, 0:1], 0.0), writes=[e_])
                    copy_op('act', e_.t[:, L:2 * L], hT[cc].t[:, :], [hT[cc]], [e_])
                    S.op('dve', lambda g, e_=e_, cc=cc: g.tensor_scalar(out=e_.t[:, L:L + 1], in0=hT[cc].t[:, 0:1], scalar1=dsk.t[:, cc:cc + 1], scalar2=None, op0=ALU.add), reads=[hT[cc], dsk, e_], writes=[e_])
                    S.op('dve', lambda g, e_=e_, cc=cc: g.tensor_copy(out=e_.t[:, 1:L], in_=hT[2 + cc].t[:, L - 1:0:-1]), reads=[hT[2 + cc], e_], writes=[e_])
                    S.dma('sp', ed[cc * 128:(cc + 1) * 128, :], e_.t[:], reads=[e_])

        def stage_hy_sc(l, L, c0, ub):
            nb = L // 128
            with Stage("hysc"):
                sw = S.sbuf("sw", [128, 6, 3], F32)
                sb = S.sbuf("sb", [128, 6], F32)
                S.dma('sp', sw.t[:], hy_swT[l], writes=[sw])
                S.dma('sp', sb.t[:], hy_sbT[l], writes=[sb])
                pin = [S.sbuf("pin", [128, L], F32) for _ in range(2)]
                ta = S.sbuf("ta", [128, L], F32)
                tb = S.sbuf("tb", [128, L], F32)
                vv = S.sbuf("vv", [128, L], F32)
                x0b = [S.sbuf("x0b", [128, L], BF16) for _ in range(2)]
                utr = [S.sbuf("utr", [128, L], BF16) for _ in range(2)]
                ubs = S.sbuf("ubs", [128, 2, 128, nb], BF16)
                n = {'p': 0}

                def conv(c6, out_buf, out_ap_full, final_writes):
                    p = pin[n['p'] % 2]
                    n['p'] += 1
                    S.dma('sp', p.t[:], phT_d[c6, :, c0:c0 + L], writes=[p])
                    S.op('dve', lambda g: g.tensor_scalar(out=ta.t[:], in0=p.t[:], scalar1=sw.t[:, c6, 1:2], scalar2=sb.t[:, c6:c6 + 1], op0=ALU.mult, op1=ALU.add), reads=[p, sw, sb], writes=[ta])
                    S.op('dve', lambda g: g.scalar_tensor_tensor(out=ta.t[:, 1:L], in0=p.t[:, 0:L - 1], scalar=sw.t[:, c6, 0:1], in1=ta.t[:, 1:L], op0=ALU.mult, op1=ALU.add), reads=[p, sw, ta], writes=[ta])
                    S.op('dve', lambda g: g.scalar_tensor_tensor(out=out_ap_full(0, L - 1), in0=p.t[:, 1:L], scalar=sw.t[:, c6, 2:3], in1=ta.t[:, 0:L - 1], op0=ALU.mult, op1=ALU.add), reads=[p, sw, ta], writes=[out_buf])
                    copy_op('dve', out_ap_full(L - 1, L), ta.t[:, L - 1:L], [ta, out_buf], [out_buf])

                for cc in range(2):
                    conv(cc, x0b[cc], lambda a, b, cc=cc: x0b[cc].t[:, a:b], None)
                    S.dma('sp', x0_d[cc, :, c0:c0 + L], x0b[cc].t[:], reads=[x0b[cc]])
                for cc in range(2):
                    conv(2 + cc, tb, lambda a, b: tb.t[:, a:b], None)
                    conv(4 + cc, vv, lambda a, b, vv=vv: vv.t[:, a:b], None)
                    S.op('dve', lambda g, cc=cc, vv=vv: g.tensor_tensor(out=utr[cc].t[:, ::-1], in0=vv.t[:], in1=tb.t[:], op=ALU.mult), reads=[vv, tb], writes=[utr[cc]])
                    for jb in range(0, nb, 8):
                        k = min(8, nb - jb)
                        pb = PSB[(jb // 8) % 2]
                        for q in range(k):
                            S.op('pe', lambda g, pb=pb, q=q, jb=jb, cc=cc: g.transpose(out=pb.t[:, q * 128:(q + 1) * 128], in_=utr[cc].t[:, (jb + q) * 128:(jb + q + 1) * 128], identity=ident_b.t[:]), reads=[utr[cc], ident_b], writes=[pb])
                        base = ubs.t[:, cc, :, :]
                        off = base.offset + (nb - 1 - jb)
                        outap = bass.AP(ubs.t[:].tensor, off, [[2 * 128 * nb, 128], [-1, k], [nb, 128]])
                        S.op('dve', lambda g, pb=pb, k=k, outap=outap: g.tensor_copy(out=outap, in_=pb.t[:, 0:k * 128].rearrange("p (q c) -> p q c", q=k)), reads=[pb], writes=[ubs])
                S.dma('sp', ub, ubs.t[:].rearrange("p a c j -> p (a c j)"), reads=[ubs])

        def stage_hy_toep(l, L, c0, ed, ub):
            nb = L // 128
            W = 2 * L - 127
            with Stage("hytoep"):
                ubs = S.sbuf("ubs", [128, 2, 128, nb], BF16)
                S.dma('sp', ubs.t[:].rearrange("p a c j -> p (a c j)"), ub, writes=[ubs])
                kts = [S.sbuf("kt", [128, W], BF16) for _ in range(3)]
                ysb = S.sbuf("ysb", [128, 128, nb], F32)
                x0b = S.sbuf("x0b", [128, L], BF16)
                ybo = S.sbuf("ybo", [128, L], BF16)
                per_bank = min(512 // nb, 128)
                for cc in range(2):
                    S.dma('sp', x0b.t[:], x0_d[cc, :, c0:c0 + L], writes=[x0b])
                    for c in range(128):
                        ch = cc * 128 + c
                        kt = kts[ch % 3]
                        S.dma('sp', kt.t[:], bass.AP(ed.tensor, ed.offset + ch * 2 * L, [[1, 128], [1, W]]), writes=[kt])
                        bank = PS[(c // per_bank) % 2]
                        col = (c % per_bank) * nb
                        ds = [0] + [d for d in range(-(nb - 1), nb) if d != 0]
                        for d in ds:
                            j0, j1 = max(0, -d), min(nb, nb - d)
                            xo = L - 127 + 128 * d
                            S.op('pe', lambda g, bank=bank, col=col, kt=kt, xo=xo, cc=cc, c=c, j0=j0, j1=j1, d=d: g.matmul(bank.t[:, col + j0 + d:col + j1 + d], lhsT=kt.t[:, xo:xo + 128], rhs=ubs.t[:, cc, c, j0:j1], start=(d == 0), stop=False, skip_group_check=True), reads=[kt, ubs], writes=[bank])
                        if c % per_bank == per_bank - 1:
                            cb = c - per_bank + 1
                            copy_op(alt(), ysb.t[:, cb:c + 1, :], bank.t[:, 0:per_bank * nb].rearrange("p (c j) -> p c j", j=nb), [bank], [ysb])
                    for I0 in range(0, nb, 4):
                        k = min(4, nb - I0)
                        pt_ = PS[2 + (I0 // 4) % 2]
                        for q in range(k):
                            S.op('pe', lambda g, pt_=pt_, q=q, I0=I0: g.transpose(out=pt_.t[:, q * 128:(q + 1) * 128], in_=ysb.t[:, :, I0 + q], identity=ident_f.t[:]), reads=[ysb, ident_f], writes=[pt_])
                        S.op('dve', lambda g, pt_=pt_, k=k, I0=I0: g.tensor_tensor(out=ybo.t[:, I0 * 128:(I0 + k) * 128], in0=pt_.t[:, 0:k * 128], in1=x0b.t[:, I0 * 128:(I0 + k) * 128], op=ALU.mult), reads=[pt_, x0b], writes=[ybo])
                    S.dma('sp', ybT_d[cc, :, c0:c0 + L], ybo.t[:], reads=[ybo])

        def stage_rglru(l):
            with Stage("rglru"):
                cw = S.sbuf("cw", [128, 2, 4], F32)
                cb = S.sbuf("cb", [128, 2], F32)
                baT = S.sbuf("baT", [128, 2, 2], F32)
                bxT = S.sbuf("bxT", [128, 2, 2], F32)
                lam = S.sbuf("lam", [128, 2, 2], F32)
                m8 = S.sbuf("m8", [128, 2, 2], F32)
                m16 = S.sbuf("m16", [128, 2, 2], F32)
                h0 = S.sbuf("h0", [128, 2, 2], F32)
                bdf = S.sbuf("bdf", [128, 8, 128], F32)
                bd = S.sbuf("bd", [128, 8, 128], BF16)
                for (t_, src) in ((cw, rg_cw[l]), (cb, rg_cb[l]), (baT, rg_baT[l]), (bxT, rg_bxT[l]), (lam, rg_lamT[l])):
                    S.dma('sp', t_.t[:], src, writes=[t_])
                S.op('pool', lambda g: g.memset(bdf.t[:], 0.0), writes=[bdf])
                for cc in range(2):
                    for dr in range(2):
                        for ax, wsrc in ((0, rg_wa), (1, rg_wx)):
                            idx = (cc * 2 + dr) * 2 + ax
                            for hb_ in range(2):
                                S.dma('sp', bdf.t[hb_ * 64:(hb_ + 1) * 64, idx, hb_ * 64:(hb_ + 1) * 64], wsrc[l, dr, 2 * cc + hb_], reads=[bdf], writes=[bdf])
                copy_op('dve', bd.t[:], bdf.t[:], [bdf], [bd])
                S.op('act', lambda g: g.activation(out=lam.t[:], in_=lam.t[:], func=AF.Exp, scale=-1.0), reads=[lam], writes=[lam])
                S.op('act', lambda g: g.activation(out=lam.t[:], in_=lam.t[:], func=AF.Ln, bias=1.0), reads=[lam], writes=[lam])
                S.op('dve', lambda g: g.tensor_scalar(out=m8.t[:], in0=lam.t[:], scalar1=-8.0, scalar2=None, op0=ALU.mult), reads=[lam], writes=[m8])
                S.op('dve', lambda g: g.tensor_scalar(out=m16.t[:], in0=lam.t[:], scalar1=-16.0, scalar2=None, op0=ALU.mult), reads=[lam], writes=[m16])
                LM = T
                xin = S.sbuf("rxin", [128, LM], F32)
                xc = S.sbuf("rxc", [128, LM], F32)
                xcb = S.sbuf("rxcb", [128, LM], BF16)
                rb = S.sbuf("rr", [128, LM], F32)
                ib = S.sbuf("ri", [128, LM], F32)
                ab = S.sbuf("ra", [128, LM], F32)
                hh = [S.sbuf("rh", [128, LM], F32) for _ in range(2)]
                yo = S.sbuf("ryo", [128, LM], BF16)
                n = 0
                for (L, c0, isctx) in ((C, 0, True), (T, C, False)):
                    for cc in range(2):
                        S.dma('sp', xin.t[:, 0:L], prT_d[cc, :, c0:c0 + L], writes=[xin])
                        S.op('dve', lambda g, L=L, cc=cc: g.tensor_scalar(out=xc.t[:, 0:L], in0=xin.t[:, 0:L], scalar1=cw.t[:, cc, 2:3], scalar2=cb.t[:, cc:cc + 1], op0=ALU.mult, op1=ALU.add), reads=[xin, cw, cb], writes=[xc])
                        for (k, sh) in ((0, -2), (1, -1), (3, 1)):
                            if sh < 0:
                                oa, ia = (-sh, L), (0, L + sh)
                            else:
                                oa, ia = (0, L - sh), (sh, L)
                            S.op('dve', lambda g, oa=oa, ia=ia, k=k, cc=cc: g.scalar_tensor_tensor(out=xc.t[:, oa[0]:oa[1]], in0=xin.t[:, ia[0]:ia[1]], scalar=cw.t[:, cc, k:k + 1], in1=xc.t[:, oa[0]:oa[1]], op0=ALU.mult, op1=ALU.add), reads=[xin, cw, xc], writes=[xc])
                        copy_op('act', xcb.t[:, 0:L], xc.t[:, 0:L], [xc], [xcb])
                        for dr in range(2):
                            for g0 in range(0, L, 512):
                                nn = min(512, L - g0)
                                for ax, dst, bias in ((0, rb, baT), (1, ib, bxT)):
                                    ps = PS[n % 4]
                                    n += 1
                                    idx = (cc * 2 + dr) * 2 + ax
                                    S.op('pe', lambda g, ps=ps, idx=idx, g0=g0, nn=nn: g.matmul(ps.t[:, 0:nn], lhsT=bd.t[:, idx, :], rhs=xcb.t[:, g0:g0 + nn], start=True, stop=True), reads=[bd, xcb], writes=[ps])
                                    S.op('act', lambda g, ps=ps, dst=dst, bias=bias, g0=g0, nn=nn, cc=cc, dr=dr: g.activation(out=dst.t[:, g0:g0 + nn], in_=ps.t[:, 0:nn], func=AF.Sigmoid, bias=bias.t[:, cc, dr:dr + 1]), reads=[ps, bias], writes=[dst])
                            S.op('act', lambda g, L=L, cc=cc, dr=dr: g.activation(out=ab.t[:, 0:L], in_=rb.t[:, 0:L], func=AF.Exp, scale=m8.t[:, cc, dr:dr + 1]), reads=[rb, m8], writes=[ab])
                            S.op('act', lambda g, L=L, cc=cc, dr=dr: g.activation(out=rb.t[:, 0:L], in_=rb.t[:, 0:L], func=AF.Exp, scale=m16.t[:, cc, dr:dr + 1]), reads=[rb, m16], writes=[rb])
                            S.op('act', lambda g, L=L: g.activation(out=rb.t[:, 0:L], in_=rb.t[:, 0:L], func=AF.Sqrt, scale=-1.0, bias=1.0), reads=[rb], writes=[rb])
                            S.op('dve', lambda g, L=L: g.tensor_tensor(out=ib.t[:, 0:L], in0=ib.t[:, 0:L], in1=xc.t[:, 0:L], op=ALU.mult), reads=[ib, xc], writes=[ib])
                            S.op('dve', lambda g, L=L: g.tensor_tensor(out=ib.t[:, 0:L], in0=ib.t[:, 0:L], in1=rb.t[:, 0:L], op=ALU.mult), reads=[ib, rb], writes=[ib])
                            init = 0.0 if isctx else h0.t[:, cc, dr:dr + 1]
                            ho = hh[dr]
                            if dr == 0:
                                S.op('dve', lambda g, L=L, init=init, ho=ho: g.tensor_tensor_scan(out=ho.t[:, 0:L], data0=ab.t[:, 0:L], data1=ib.t[:, 0:L], initial=init, op0=ALU.mult, op1=ALU.add), reads=[ab, ib, h0], writes=[ho])
                            else:
                                S.op('dve', lambda g, L=L, init=init, ho=ho: g.tensor_tensor_scan(out=ho.t[:, 0:L][:, ::-1], data0=ab.t[:, 0:L][:, ::-1], data1=ib.t[:, 0:L][:, ::-1], initial=init, op0=ALU.mult, op1=ALU.add), reads=[ab, ib, h0], writes=[ho])
                        if isctx:
                            copy_op('dve', h0.t[:, cc, 0:1], hh[0].t[:, L - 1:L], [hh[0], h0], [h0])
                            copy_op('dve', h0.t[:, cc, 1:2], hh[1].t[:, 0:1], [hh[1], h0], [h0])
                        S.dma('sp', xin.t[:, 0:L], prT_d[2 + cc, :, c0:c0 + L], writes=[xin])
                        S.op('act', lambda g, L=L: g.activation(out=xin.t[:, 0:L], in_=xin.t[:, 0:L], func=AF.Gelu), reads=[xin], writes=[xin])
                        S.op('dve', lambda g, L=L: g.tensor_tensor(out=hh[0].t[:, 0:L], in0=hh[0].t[:, 0:L], in1=hh[1].t[:, 0:L], op=ALU.add), reads=[hh[0], hh[1]], writes=[hh[0]])
                        S.op('dve', lambda g, L=L: g.tensor_tensor(out=yo.t[:, 0:L], in0=hh[0].t[:, 0:L], in1=xin.t[:, 0:L], op=ALU.mult), reads=[hh[0], xin], writes=[yo])
                        S.dma('sp', ycT_d[cc, :, c0:c0 + L], yo.t[:, 0:L], reads=[yo])

        def load_w_bf(dst, src_rows_fn, nk, ncols, stg):
            for kc in range(nk):
                st = stg[kc % len(stg)]
                S.dma('sp', st.t[:, 0:ncols], src_rows_fn(kc), writes=[st])
                copy_op(alt(('dve', 'pool')), dst.t[:, kc, :], st.t[:, 0:ncols], [st], [dst])

        def stage_merge(l, first):
            with Stage("merge"):
                stg = [S.sbuf("mstg", [128, 3072], F32)]
                wg = S.sbuf("wg", [128, 8, 3072], BF16)
                wa_ = S.sbuf("wa", [128, 4, D], BF16)
                wb_ = S.sbuf("wb", [128, 2, D], BF16)
                wc_ = S.sbuf("wc", [128, 2, D], BF16)
                wo_ = S.sbuf("wo", [128, 8, D], BF16)
                bg = S.sbuf("bg", [128, 24], F32)
                S.dma('sp', bg.t[:], b_gateT[l], writes=[bg])
                load_w_bf(wg, lambda kc: w_gate[l, kc * 128:(kc + 1) * 128, :], 8, 3072, stg)
                load_w_bf(wa_, lambda kc: w_bra[l, kc * 128:(kc + 1) * 128, :], 4, D, stg)
                load_w_bf(wb_, lambda kc: w_brb[l, kc * 128:(kc + 1) * 128, :], 2, D, stg)
                load_w_bf(wc_, lambda kc: w_brc[l, kc * 128:(kc + 1) * 128, :], 2, D, stg)
                load_w_bf(wo_, lambda kc: w_out[l, kc * 128:(kc + 1) * 128, :], 8, D, stg)
                xg = S.sbuf("xg", [128, 8, 512], BF16)
                ya = S.sbuf("mya", [128, 4, 512], BF16)
                yb = S.sbuf("myb", [128, 2, 512], BF16)
                yc = S.sbuf("myc", [128, 2, 512], BF16)
                mT = S.sbuf("mT", [128, 8, 512], BF16)
                gt = [S.sbuf("gt", [128, 512], BF16) for _ in range(3)]
                t1 = S.sbuf("mt1", [128, 512], F32)
                t2 = S.sbuf("mt2", [128, 512], F32)
                oT = S.sbuf("oT", [128, 8, 512], F32)
                xt = [S.sbuf("mxt", [128, D], F32) for _ in range(2)]
                for (c0, nn) in GROUPS:
                    s = 1 if c0 == 0 else 0
                    for kc in range(8):
                        S.dma('sp', xg.t[:, kc, 0:nn], xnT_d[:, kc, c0:c0 + nn], writes=[xg])
                    for kc in range(4):
                        S.dma('sp', ya.t[:, kc, 0:nn], yaT_d[kc, :, c0:c0 + nn], writes=[ya])
                    for kc in range(2):
                        S.dma('sp', yb.t[:, kc, 0:nn], ybT_d[kc, :, c0:c0 + nn], writes=[yb])
                        S.dma('sp', yc.t[:, kc, 0:nn], ycT_d[kc, :, c0:c0 + nn], writes=[yc])
                    for mc in range(8):
                        brs = ((wa_, ya, 4), (wb_, yb, 2), (wc_, yc, 2))
                        for bi in range(3):
                            pg = PS[bi]
                            for kc in range(8):
                                S.op('pe', lambda g, pg=pg, kc=kc, bi=bi, mc=mc, nn=nn: g.matmul(pg.t[:, 0:nn], lhsT=wg.t[:, kc, bi * D + mc * 128:bi * D + (mc + 1) * 128], rhs=xg.t[:, kc, 0:nn], start=(kc == 0), stop=(kc == 7)), reads=[wg, xg], writes=[pg])
                            S.op('act', lambda g, pg=pg, bi=bi, mc=mc, nn=nn: g.activation(out=gt[bi].t[:, 0:nn], in_=pg.t[:, 0:nn], func=AF.Sigmoid, bias=bg.t[:, bi * 8 + mc:bi * 8 + mc + 1]), reads=[pg, bg], writes=[gt[bi]])
                            pbr = PS[3 + bi]
                            w_, y_, nk = brs[bi]
                            for kc in range(nk):
                                S.op('pe', lambda g, pbr=pbr, kc=kc, w_=w_, y_=y_, nk=nk, mc=mc, nn=nn: g.matmul(pbr.t[:, 0:nn], lhsT=w_.t[:, kc, mc * 128:(mc + 1) * 128], rhs=y_.t[:, kc, 0:nn], start=(kc == 0), stop=(kc == nk - 1)), reads=[w_, y_], writes=[pbr])
                        S.op('dve', lambda g, nn=nn: g.tensor_tensor(out=t1.t[:, 0:nn], in0=PS[3].t[:, 0:nn], in1=gt[0].t[:, 0:nn], op=ALU.mult), reads=[PS[3], gt[0]], writes=[t1])
                        S.op('dve', lambda g, nn=nn: g.tensor_tensor(out=t2.t[:, 0:nn], in0=PS[4].t[:, 0:nn], in1=gt[1].t[:, 0:nn], op=ALU.mult), reads=[PS[4], gt[1]], writes=[t2])
                        S.op('pool', lambda g, nn=nn: g.tensor_tensor(out=t1.t[:, 0:nn], in0=t1.t[:, 0:nn], in1=t2.t[:, 0:nn], op=ALU.add), reads=[t1, t2], writes=[t1])
                        S.op('dve', lambda g, nn=nn: g.tensor_tensor(out=t2.t[:, 0:nn], in0=PS[5].t[:, 0:nn], in1=gt[2].t[:, 0:nn], op=ALU.mult), reads=[PS[5], gt[2]], writes=[t2])
                        S.op('pool', lambda g, nn=nn, mc=mc: g.tensor_tensor(out=mT.t[:, mc, 0:nn], in0=t1.t[:, 0:nn], in1=t2.t[:, 0:nn], op=ALU.add), reads=[t1, t2], writes=[mT])
                    for oc in range(8):
                        po = PS[oc % 2]
                        for mc in range(8):
                            S.op('pe', lambda g, po=po, mc=mc, oc=oc, nn=nn: g.matmul(po.t[:, 0:nn], lhsT=wo_.t[:, mc, oc * 128:(oc + 1) * 128], rhs=mT.t[:, mc, 0:nn], start=(mc == 0), stop=(mc == 7)), reads=[wo_, mT], writes=[po])
                        S.op('act', lambda g, po=po, oc=oc, nn=nn, s=s: g.activation(out=oT.t[:, oc, 0:nn], in_=po.t[:, 0:nn], func=AF.Copy, scale=modT.t[:, 16 + oc, s:s + 1]), reads=[po, modT], writes=[oT])
                    for t4 in range(nn // 128):
                        tok0 = c0 + t4 * 128
                        if c0 == 0:
                            src = (c_in if first else cres)[tok0:tok0 + 128, :]
                            dst = cres[tok0:tok0 + 128, :]
                        else:
                            src = (x_in if first else xres)[tok0 - C:tok0 - C + 128, :]
                            dst = xres[tok0 - C:tok0 - C + 128, :]
                        x_ = xt[t4 % 2]
                        S.dma('sp', x_.t[:], src, writes=[x_])
                        for half in range(2):
                            pt_ = PS[2 + half]
                            for q in range(4):
                                oc = half * 4 + q
                                S.op('pe', lambda g, pt_=pt_, q=q, oc=oc, t4=t4: g.transpose(out=pt_.t[:, q * 128:(q + 1) * 128], in_=oT.t[:, oc, t4 * 128:(t4 + 1) * 128], identity=ident_f.t[:]), reads=[oT, ident_f], writes=[pt_])
                            S.op('dve', lambda g, pt_=pt_, x_=x_, half=half: g.tensor_tensor(out=x_.t[:, half * 512:(half + 1) * 512], in0=pt_.t[:], in1=x_.t[:, half * 512:(half + 1) * 512], op=ALU.add), reads=[pt_, x_], writes=[x_])
                        S.dma('sp', dst, x_.t[:], reads=[x_])

        def top16(src_ap, src_reads, vals, idxs, scr, vparts, iparts):
            S.op('dve', lambda g: g.max(out=vals[:, 0:8], in_=src_ap), reads=src_reads, writes=vparts)
            S.op('dve', lambda g: g.max_index(out=idxs[:, 0:8], in_max=vals[:, 0:8], in_values=src_ap), reads=src_reads + vparts, writes=iparts)
            S.op('dve', lambda g: g.match_replace(out=scr.t[:, 0:src_ap.shape[1]], in_to_replace=vals[:, 0:8], in_values=src_ap, imm_value=-1e30), reads=src_reads + vparts, writes=[scr])
            S.op('dve', lambda g: g.max(out=vals[:, 8:16], in_=scr.t[:, 0:src_ap.shape[1]]), reads=[scr], writes=vparts)
            S.op('dve', lambda g: g.max_index(out=idxs[:, 8:16], in_max=vals[:, 8:16], in_values=scr.t[:, 0:src_ap.shape[1]]), reads=[scr] + vparts, writes=iparts)

        def stage_peer_prep(l):
            with Stage("pprep"):
                R = 4
                uin = [S.sbuf("uin", [128, R, D], F32) for _ in range(2)]
                vin = [S.sbuf("vin", [128, R, D], F32) for _ in range(2)]
                uvo = [S.sbuf("uvo", [128, R, 2 * D], BF16) for _ in range(2)]
                uview = p_u[l].rearrange("(p r) d -> p r d", p=128)
                vview = p_v[l].rearrange("(p r) d -> p r d", p=128)
                oview = uv_d.rearrange("(p r) d -> p r d", p=128)
                nch = 128 // R

                def loads(c):
                    S.dma('sp', uin[c % 2].t[:], uview[:, c * R:(c + 1) * R, :], writes=[uin[c % 2]])
                    S.dma('sp', vin[c % 2].t[:], vview[:, c * R:(c + 1) * R, :], writes=[vin[c % 2]])
                loads(0)
                for c in range(nch):
                    if c + 1 < nch:
                        loads(c + 1)
                    a, b, o = uin[c % 2], vin[c % 2], uvo[c % 2]
                    S.op('dve', lambda g, a=a, o=o: g.tensor_copy(out=o.t[:, :, 0:D], in_=a.t[:]), reads=[a], writes=[o.part(0)])
                    S.op('act', lambda g, b=b, o=o: g.activation(out=o.t[:, :, D:2 * D], in_=b.t[:], func=AF.Copy), reads=[b], writes=[o.part(1)])
                    S.dma('pool', oview[:, c * R:(c + 1) * R, :], o.t[:], reads=[o.part(0), o.part(1)])

        def top16g(src_ap, src_reads, vals, idxs, scr, vparts, iparts):
            n = src_ap.shape[1]
            S.op('dve', lambda g: g.max(out=vals[:, 0:8], in_=src_ap), reads=src_reads, writes=vparts)
            yield
            S.op('dve', lambda g: g.max_index(out=idxs[:, 0:8], in_max=vals[:, 0:8], in_values=src_ap), reads=src_reads + vparts, writes=iparts)
            yield
            S.op('dve', lambda g: g.match_replace(out=scr.t[:, 0:n], in_to_replace=vals[:, 0:8], in_values=src_ap, imm_value=-1e30), reads=src_reads + vparts, writes=[scr])
            yield
            S.op('dve', lambda g: g.max(out=vals[:, 8:16], in_=scr.t[:, 0:n]), reads=[scr], writes=vparts)
            yield
            S.op('dve', lambda g: g.max_index(out=idxs[:, 8:16], in_max=vals[:, 8:16], in_values=scr.t[:, 0:n]), reads=[scr] + vparts, writes=iparts)
            yield

        def stage_peer_q(l):
            with Stage("peerq"):
                stg = [S.sbuf("pstg", [128, 2048], F32) for _ in range(2)]
                wq = S.sbuf("wq", [128, 8, 2048], BF16)
                load_w_bf(wq, lambda kc: p_wq[l, kc * 128:(kc + 1) * 128, :], 8, 2048, stg)
                xgs = [S.sbuf("pxg", [128, 8, 512], BF16) for _ in range(2)]
                qTs = [S.sbuf("pqT", [128, 16, 512], BF16) for _ in range(2)]
                n = 0
                for gi, (c0, nn) in enumerate(GROUPS):
                    xg = xgs[gi % 2]
                    qT = qTs[gi % 2]
                    for kc in range(8):
                        S.dma('sp', xg.t[:, kc, 0:nn], xnT_d[:, kc, c0:c0 + nn], writes=[xg])
                    for hp in range(16):
                        ps = PS[n % 4]
                        n += 1
                        for kc in range(8):
                            S.op('pe', lambda g, ps=ps, kc=kc, hp=hp, nn=nn, xg=xg: g.matmul(ps.t[:, 0:nn], lhsT=wq.t[:, kc, hp * 128:(hp + 1) * 128], rhs=xg.t[:, kc, 0:nn], start=(kc == 0), stop=(kc == 7)), reads=[wq, xg], writes=[ps])
                        copy_op(alt(), qT.t[:, hp, 0:nn], ps.t[:, 0:nn], [ps], [qT])
                    S.dma('pool', qT_d[:, :, c0:c0 + nn], qT.t[:, :, 0:nn], reads=[qT])

        def stage_peer(l):
            with Stage("peer"):
                keysT = S.sbuf("keysT", [128, 16, 128], BF16)
                kst = [S.sbuf("kst", [128, 128], F32) for _ in range(2)]
                for hp in range(16):
                    ks = kst[hp % 2]
                    S.dma('sp', ks.t[:], p_keys[l, hp // 2, hp % 2], writes=[ks])
                    pk = PS[hp % 2]
                    S.op('pe', lambda g, pk=pk, ks=ks: g.transpose(out=pk.t[:, 0:128], in_=ks.t[:], identity=ident_f.t[:]), reads=[ks, ident_f], writes=[pk])
                    copy_op('dve', keysT.t[:, hp, :], pk.t[:, 0:128], [pk], [keysT])
                g5 = [S.sbuf("g5", [128, D], F32) for _ in range(2)]
                for s in range(2):
                    S.dma('sp', g5[s].t[:], bcd[8 + s], writes=[g5[s]])
                qTs = [S.sbuf("pqT", [128, 16, 512], BF16) for _ in range(2)]
                top = S.sbuf("ptop", [128, 16, 16], F32)
                it = S.sbuf("pit", [128, 16, 16], U32)
                itf = S.sbuf("pitf", [128, 16, 16], F32)
                scr = S.sbuf("pscr", [128, 256], F32)
                cand = S.sbuf("pcand", [128, 8, 256], F32)
                eqb = S.sbuf("peq", [128, 8, 256], F32)
                eq = eqb
                best = S.sbuf("pbest", [128, 8, 16], F32)
                pos = S.sbuf("ppos", [128, 8, 16], U32)
                pa = S.sbuf("ppa", [128, 8, 16], U32)
                paf = S.sbuf("ppaf", [128, 2, 8, 16], F32)
                isel = S.sbuf("pisel", [128, 2, 8, 16], F32)
                idxf = S.sbuf("pidxf", [128, 128], F32)
                idxu = [S.sbuf("pidxu", [128, 128], U32) for _ in range(2)]
                gws = [S.sbuf("pgw", [128, 8, 16], F32) for _ in range(2)]
                zs = S.sbuf("pzs", [128, 8], F32)
                dotb = [S.sbuf("pdots", [128, 128], F32) for _ in range(2)]
                actv = [S.sbuf("pact", [128, 128], F32) for _ in range(2)]
                NR = 24
                BT = 4
                uvr = [S.sbuf("uvr", [128, 2 * D], BF16) for _ in range(NR)]
                dgs = [S.sbuf("pdg", [128, 128], BF16) for _ in range(8)]
                xn = [S.sbuf("pxn", [128, D], BF16) for _ in range(2)]
                xt = [S.sbuf("pxt", [128, D], F32) for _ in range(2)]
                junk = S.sbuf("pjunk", [128, D], BF16)
                prods = [S.sbuf("pprod", [128, D], BF16) for _ in range(3)]
                junk2 = S.sbuf("pjunk2", [128, D], BF16)
                ytmp = S.sbuf("pytmp", [128, D], F32)
                uvflat = uv_d

                tiles = []
                for gi, (c0, nn) in enumerate(GROUPS):
                    for t4 in range(nn // 128):
                        tiles.append((gi, c0, nn, t4))

                def emit_group_q(gi):
                    c0, nn = GROUPS[gi]
                    qT = qTs[gi % 2]
                    S.dma('sp', qT.t[:, :, 0:nn], qT_d[:, :, c0:c0 + nn], writes=[qT])

                def topk_gen(ti):
                    gi, c0, nn, t4 = tiles[ti]
                    qT = qTs[gi % 2]
                    gw = gws[ti % 2]
                    iu = idxu[ti % 2]
                    for hp in range(16):
                        ps = PS[2 + hp % 2]
                        S.op('pe', lambda g, ps=ps, hp=hp, t4=t4, qT=qT: g.matmul(ps.t[:, 0:128], lhsT=qT.t[:, hp, t4 * 128:(t4 + 1) * 128], rhs=keysT.t[:, hp, :], start=True, stop=True), reads=[qT, keysT], writes=[ps])
                        yield from top16g(ps.t[:, 0:128], [ps], top.t[:, hp, :], it.t[:, hp, :], scr, [top], [it])
                    copy_op('pool', itf.t[:], it.t[:], [it], [itf])
                    tv = top.t[:].rearrange("p (h q) k -> p h q k", q=2)
                    S.op('dve', lambda g, tv=tv: g.tensor_tensor(out=cand.t[:].rearrange("p h (a b) -> p h a b", a=16), in0=tv[:, :, 0, :].unsqueeze(3).broadcast_to([128, 8, 16, 16]), in1=tv[:, :, 1, :].unsqueeze(2).broadcast_to([128, 8, 16, 16]), op=ALU.add), reads=[top], writes=[cand])
                    yield
                    for h in range(8):
                        yield from top16g(cand.t[:, h, :], [cand], best.t[:, h, :], pos.t[:, h, :], scr, [best], [pos])
                    S.op('dve', lambda g: g.tensor_tensor(out=gw.t[:], in0=best.t[:], in1=best.t[:, :, 0:1].broadcast_to([128, 8, 16]), op=ALU.subtract), reads=[best], writes=[gw])
                    yield
                    S.op('act', lambda g: g.activation(out=gw.t[:], in_=gw.t[:], func=AF.Exp), reads=[gw], writes=[gw])
                    S.op('dve', lambda g: g.tensor_reduce(out=zs.t[:], in_=gw.t[:], axis=AX.X, op=ALU.add), reads=[gw], writes=[zs])
                    yield
                    S.op('dve', lambda g: g.reciprocal(out=zs.t[:], in_=zs.t[:]), reads=[zs], writes=[zs])
                    yield
                    S.op('dve', lambda g: g.tensor_tensor(out=gw.t[:], in0=gw.t[:], in1=zs.t[:].unsqueeze(2).broadcast_to([128, 8, 16]), op=ALU.mult), reads=[gw, zs], writes=[gw])
                    yield
                    S.op('dve', lambda g: g.tensor_single_scalar(out=pa.t[:], in_=pos.t[:], scalar=4, op=ALU.logical_shift_right), reads=[pos], writes=[pa])
                    yield
                    copy_op('dve', paf.t[:, 0], pa.t[:], [pa], [paf])
                    yield
                    S.op('dve', lambda g: g.tensor_single_scalar(out=pa.t[:], in_=pos.t[:], scalar=15, op=ALU.bitwise_and), reads=[pos, paf], writes=[pa])
                    yield
                    copy_op('dve', paf.t[:, 1], pa.t[:], [pa], [paf])
                    yield
                    itv = itf.t[:].rearrange("p (h q) k -> p h q k", q=2)
                    eqv = eqb.t[:].rearrange("p h (a b) -> p h a b", a=16)
                    for q in range(2):
                        S.op('dve', lambda g, q=q: g.tensor_tensor(out=eqv, in0=paf.t[:, q].unsqueeze(3).broadcast_to([128, 8, 16, 16]), in1=iota16.t[:].unsqueeze(1).unsqueeze(1).broadcast_to([128, 8, 16, 16]), op=ALU.is_equal), reads=[paf, iota16], writes=[eq])
                        yield
                        S.op('dve', lambda g, q=q, itv=itv: g.tensor_tensor(out=eqv, in0=eqv, in1=itv[:, :, q, :].unsqueeze(2).broadcast_to([128, 8, 16, 16]), op=ALU.mult), reads=[eq, itf], writes=[eq])
                        yield
                        S.op('dve', lambda g, q=q: g.tensor_reduce(out=isel.t[:, q], in_=eqv, axis=AX.X, op=ALU.add), reads=[eq], writes=[isel])
                        yield
                    S.op('dve', lambda g: g.scalar_tensor_tensor(out=idxf.t[:].rearrange("p (h k) -> p h k", h=8), in0=isel.t[:, 0], scalar=128.0, in1=isel.t[:, 1], op0=ALU.mult, op1=ALU.add), reads=[isel], writes=[idxf])
                    yield
                    copy_op('dve', iu.t[:], idxf.t[:], [idxf], [iu])
                    yield

                def finish_batch(b0, av, gw, gwf, py):
                    S.op('dve', lambda g: g.tensor_tensor(out=av.t[:, b0:b0 + BT], in0=av.t[:, b0:b0 + BT], in1=gwf[:, b0:b0 + BT], op=ALU.mult), reads=[av.part(b0 // BT), gw], writes=[av.part(b0 // BT)])
                    for k in range(b0, b0 + BT):
                        dg = dgs[k % 8]
                        rk = uvr[k % NR]
                        S.op('dve', lambda g, dg=dg, k=k: g.tensor_scalar(out=dg.t[:], in0=ident_b.t[:], scalar1=av.t[:, k:k + 1], scalar2=None, op0=ALU.mult), reads=[ident_b, av.part(k // BT)], writes=[dg])
                        for half in range(2):
                            S.op('pe', lambda g, dg=dg, rk=rk, half=half, k=k: g.matmul(py[half].t[:], lhsT=dg.t[:], rhs=rk.t[:, D + half * 512:D + (half + 1) * 512], start=(k == 0), stop=(k == 127)), reads=[dg, rk], writes=[py[half]])

                emit_group_q(0)
                for _ in topk_gen(0):
                    pass
                py = [PS[4], PS[5]]
                for ti, (gi, c0, nn, t4) in enumerate(tiles):
                    s = 1 if c0 == 0 else 0
                    tok0 = c0 + t4 * 128
                    nxt = None
                    if ti + 1 < len(tiles):
                        if tiles[ti + 1][0] != gi:
                            emit_group_q(gi + 1)
                        nxt = topk_gen(ti + 1)
                    xn_ = xn[ti % 2]
                    x_ = xt[ti % 2]
                    iu = idxu[ti % 2]
                    gw = gws[ti % 2]
                    dots = dotb[ti % 2]
                    av = actv[ti % 2]
                    S.dma('sp', xn_.t[:], xntm_d[tok0:tok0 + 128, :], writes=[xn_])
                    xr = (cres[tok0:tok0 + 128, :] if c0 == 0 else xres[tok0 - C:tok0 - C + 128, :])
                    S.dma('sp', x_.t[:], xr, writes=[x_])
                    gwf = gw.t[:].rearrange("p h k -> p (h k)")
                    for hk in range(128):
                        r_ = uvr[hk % NR]
                        S.dma('pool', None, None, reads=[iu], writes=[r_], fn=lambda g, r_=r_, iu=iu, hk=hk: g.indirect_dma_start(out=r_.t[:], out_offset=None, in_=uvflat, in_offset=bass.IndirectOffsetOnAxis(ap=iu.t[:, hk:hk + 1], axis=0)))
                        if hk % 6 == 5:
                            S.op('dve', lambda g, r_=r_, xn_=xn_, hk=hk, dots=dots: g.scalar_tensor_tensor(out=junk2.t[:], in0=r_.t[:, 0:D], scalar=1.0, in1=xn_.t[:], op0=ALU.mult, op1=ALU.mult, accum_out=dots.t[:, hk:hk + 1]), reads=[r_, xn_], writes=[dots.part(hk // BT)])
                        else:
                            pr_ = prods[hk % 3]
                            S.op('dve', lambda g, r_=r_, xn_=xn_, pr_=pr_: g.tensor_tensor(out=pr_.t[:], in0=r_.t[:, 0:D], in1=xn_.t[:], op=ALU.mult), reads=[r_, xn_], writes=[pr_])
                            S.op('act', lambda g, pr_=pr_, hk=hk, dots=dots: g.activation(out=junk.t[:], in_=pr_.t[:], func=AF.Copy, accum_out=dots.t[:, hk:hk + 1]), reads=[pr_], writes=[dots.part(hk // BT)])
                        if nxt is not None:
                            for _ in range(2):
                                next(nxt, None)
                        if hk % BT == BT - 1:
                            b0 = hk - (BT - 1)
                            if b0 >= BT:
                                finish_batch(b0 - BT, av, gw, gwf, py)
                            else:
                                for _ in range(2):
                                    S.op('act', lambda g: g.activation(out=junk.t[:, 0:8], in_=junk.t[:, 8:16], func=AF.Copy))
                            S.op('act', lambda g, av=av, dots=dots, b0=b0: g.activation(out=av.t[:, b0:b0 + BT], in_=dots.t[:, b0:b0 + BT], func=AF.Gelu), reads=[dots.part(b0 // BT)], writes=[av.part(b0 // BT)])
                    finish_batch(128 - BT, av, gw, gwf, py)
                    if nxt is not None:
                        for _ in nxt:
                            pass
                    for half in range(2):
                        S.op('dve', lambda g, half=half, s=s: g.tensor_tensor(out=ytmp.t[:, half * 512:(half + 1) * 512], in0=py[half].t[:], in1=g5[s].t[:, half * 512:(half + 1) * 512], op=ALU.mult), reads=[py[half], g5[s]], writes=[ytmp])
                    S.op('dve', lambda g, x_=x_: g.tensor_tensor(out=x_.t[:], in0=x_.t[:], in1=ytmp.t[:], op=ALU.add), reads=[x_, ytmp], writes=[x_])
                    S.dma('sp', xr, x_.t[:], reads=[x_])

        def stage_final():
            with Stage("final"):
                gf = S.sbuf("gf", [128, D], F32)
                S.dma('sp', gf.t[:], gfin, writes=[gf])
                xin = [S.sbuf("fx", [128, D], F32) for _ in range(3)]
                junk = S.sbuf("fj", [128, D], BF16)
                yo = [S.sbuf("fy", [128, D], F32) for _ in range(2)]
                ss = S.sbuf("fss", [128, 32], F32)
                t1 = S.sbuf("ft1", [128, 32], F32)
                for tt in range(32):
                    xt = xin[tt % 3]
                    S.dma('sp', xt.t[:], xres[tt * 128:(tt + 1) * 128, :], writes=[xt])
                    S.op('act', lambda g, xt=xt, tt=tt: g.activation(out=junk.t[:], in_=xt.t[:], func=AF.Square, accum_out=ss.t[:, tt:tt + 1]), reads=[xt], writes=[junk, ss.part(tt)])
                    S.op('dve', lambda g, tt=tt: g.tensor_scalar(out=t1.t[:, tt:tt + 1], in0=ss.t[:, tt:tt + 1], scalar1=1.0 / D, scalar2=EPS, op0=ALU.mult, op1=ALU.add), reads=[ss.part(tt)], writes=[t1.part(tt)])
                    S.op('act', lambda g, tt=tt: g.activation(out=t1.t[:, tt:tt + 1], in_=t1.t[:, tt:tt + 1], func=AF.Sqrt), reads=[t1.part(tt)], writes=[t1.part(tt)])
                    S.op('dve', lambda g, tt=tt: g.reciprocal(out=t1.t[:, tt:tt + 1], in_=t1.t[:, tt:tt + 1]), reads=[t1.part(tt)], writes=[t1.part(tt)])
                    y_ = yo[tt % 2]
                    S.op('dve', lambda g, y_=y_, xt=xt, tt=tt: g.scalar_tensor_tensor(out=y_.t[:], in0=xt.t[:], scalar=t1.t[:, tt:tt + 1], in1=gf.t[:], op0=ALU.mult, op1=ALU.mult), reads=[xt, t1.part(tt), gf], writes=[y_])
                    S.dma('sp', y_out[tt * 128:(tt + 1) * 128, :], y_.t[:], reads=[y_])

        S.barrier()
        S.flush()
        plan = []
        for l in range(DEPTH):
            first = (l == 0)
            plan += [("mod", lambda l=l: stage_mod(l)),
                     ("norm1", lambda l=l, first=first: stage_norm(l, 0, first)),
                     ("proj", lambda l=l: stage_proj(l)),
                     ("attn", lambda l=l: stage_attn(l)),
                     ("hyfc", lambda l=l: stage_hy_filt(l, C, zT_c, dec_c, ed_c)),
                     ("hysc_c", lambda l=l: stage_hy_sc(l, C, 0, ub_c)),
                     ("hytc", lambda l=l: stage_hy_toep(l, C, 0, ed_c, ub_c)),
                     ("hyfm", lambda l=l: stage_hy_filt(l, T, zT_m, dec_m, ed_m)),
                     ("hysc_m", lambda l=l: stage_hy_sc(l, T, C, ub_m)),
                     ("hytm", lambda l=l: stage_hy_toep(l, T, C, ed_m, ub_m)),
                     ("rglru", lambda l=l: stage_rglru(l)),
                     ("merge", lambda l=l, first=first: stage_merge(l, first)),
                     ("norm2", lambda l=l: stage_norm(l, 1, False)),
                     ("pprep", lambda l=l: stage_peer_prep(l)),
                     ("peerq", lambda l=l: stage_peer_q(l)),
                     ("peer", lambda l=l: stage_peer(l))]
        plan.append(("final", stage_final))
        for i, (nm, f) in enumerate(plan):
            f()
            if stop_after is not None and i + 1 >= stop_after:
                break
        S.finish()
        build.n_instr = S.n_instr
    return nc


def _chunkT(v, n):
    return np.ascontiguousarray(np.asarray(v, np.float32).reshape(n, 128).T)


def _consts():
    f32 = np.float32
    out = {}
    for nm, L in (("m", T), ("c", C)):
        t = np.linspace(0.0, 1.0, L, dtype=f32)[:, None]
        w = (f32(2.0 * math.pi / L) * np.arange(L, dtype=f32))[:, None]
        bands = np.linspace(1e-4, 15, 16, dtype=f32)[None, :]
        z = np.concatenate([t, np.cos(bands * w), -np.sin(bands * w)], axis=-1).astype(f32)
        deltas = np.linspace(math.log(1e-2) / 1.5, math.log(1e-2) / 0.3, 256, dtype=f32)
        decay = np.exp(-t * np.abs(deltas)[None, :]).astype(f32)
        out["zT_" + nm] = np.ascontiguousarray(z.T)
        out["dec_" + nm] = np.ascontiguousarray(decay.T.reshape(2, 128, L).transpose(1, 0, 2))
    out["ident"] = np.eye(128, dtype=f32)
    out["iota16"] = np.ascontiguousarray(np.broadcast_to(np.arange(16, dtype=f32)[None, :], (128, 16)))
    osel = np.zeros((128, 2, 128), f32)
    osel[:, 0, 0:64] = 1.0
    osel[:, 1, 64:128] = 1.0
    out["onesel"] = osel
    return out


def _rpb_table(rpb):
    wq = np.arange(64)
    start = np.clip(wq - 8, 0, 48)
    wk = np.arange(64)
    colok = (wk[:, None] >= start[None, :]) & (wk[:, None] < start[None, :] + 16)
    dc = np.clip(wk[:, None] - wq[None, :] + 15, 0, 30)
    tab = np.full((DEPTH, 128, 8, 22, 64), NEG, np.float32)
    for half in range(2):
        for ep in range(22):
            e = ep - 3 - half
            if e < 0 or e > 14:
                continue
            dr = 14 - e
            g = rpb[:, :, dr][:, :, dc]
            g = np.where(colok[None, None], g, np.float32(NEG))
            tab[:, half * 64:(half + 1) * 64, :, ep, :] = g.transpose(0, 2, 1, 3)
    return np.ascontiguousarray(tab.reshape(DEPTH, 128, 8, 22 * 64))


_NC_CACHE = {}


def kernel(**inp):
    f32 = np.float32
    g = lambda k: np.asarray(inp[k], f32)
    shared = dict(_consts())
    shared["gmix"] = np.stack([_chunkT(g("norm_mix_g")[l], 8) for l in range(DEPTH)])
    shared["gffn"] = np.stack([_chunkT(g("norm_ffn_g")[l], 8) for l in range(DEPTH)])
    shared["gfin"] = np.ascontiguousarray(np.broadcast_to(g("final_g")[None, :], (128, D)))
    shared["w_ada"] = g("w_ada")
    shared["b_adaT"] = np.stack([_chunkT(g("b_ada")[l], 48) for l in range(DEPTH)])
    shared["w_in"] = g("w_in")
    shared["tab"] = _rpb_table(g("na_rpb"))
    shared["hy_swT"] = np.ascontiguousarray(g("hy_short_w").reshape(DEPTH, 3, 6, 128).transpose(0, 3, 2, 1))
    shared["hy_sbT"] = np.stack([_chunkT(g("hy_short_b")[l], 6) for l in range(DEPTH)])
    shared["hy_f1w"] = g("hy_f1_w")
    shared["hy_f1b"] = g("hy_f1_b")[:, :, None].copy()
    shared["hy_f2w"] = g("hy_f2_w")
    shared["hy_f2b"] = g("hy_f2_b")[:, :, None].copy()
    shared["hy_f3w"] = g("hy_f3_w")
    shared["hy_fq"] = np.ascontiguousarray(g("hy_freq").transpose(0, 2, 1))
    shared["hy_dsk"] = np.stack([_chunkT(g("hy_bias")[l], 2) for l in range(DEPTH)])
    shared["rg_cw"] = np.ascontiguousarray(g("rg_conv_w").reshape(DEPTH, 4, 2, 128).transpose(0, 3, 2, 1))
    shared["rg_cb"] = np.stack([_chunkT(g("rg_conv_b")[l], 2) for l in range(DEPTH)])
    shared["rg_wa"] = g("rg_wa")
    shared["rg_wx"] = g("rg_wx")
    for nm, k in (("rg_baT", "rg_ba"), ("rg_bxT", "rg_bx"), ("rg_lamT", "rg_lambda")):
        shared[nm] = np.ascontiguousarray(g(k).reshape(DEPTH, 2, 2, 128).transpose(0, 3, 2, 1))
    shared["w_gate"] = g("w_gate")
    shared["b_gateT"] = np.stack([_chunkT(g("b_gate")[l], 24) for l in range(DEPTH)])
    shared["w_br_a"] = g("w_br_a")
    shared["w_br_b"] = g("w_br_b")
    shared["w_br_c"] = g("w_br_c")
    shared["w_out"] = g("w_out")
    shared["peer_wq"] = g("peer_wq")
    shared["peer_keys"] = g("peer_keys")
    shared["peer_u"] = g("peer_u")
    shared["peer_v"] = g("peer_v")
    x = g("x")
    ctx = g("ctx")
    c = g("c")
    cc = g("c_ctx")
    nb = x.shape[0]
    in_maps = []
    for b in range(nb):
        m = dict(shared)
        m["x"] = np.ascontiguousarray(x[b])
        m["ctx"] = np.ascontiguousarray(ctx[b])
        m["cvec"] = np.ascontiguousarray(np.stack([_chunkT(c[b], 8), _chunkT(cc, 8)], axis=-1))
        in_maps.append(m)
    if "nc" not in _NC_CACHE:
        _NC_CACHE["nc"] = build()
    nc = _NC_CACHE["nc"]
    res = run_bass_kernel_spmd(nc, in_maps, core_ids=list(range(nb)))
    return np.stack([np.asarray(r["y"], f32) for r in res.results], axis=0)
```

```python
import numpy as np
import concourse.bass as bass
import concourse.mybir as mybir
from concourse.bass_utils import run_bass_kernel_spmd
from contextlib import ExitStack

F32 = mybir.dt.float32
BF16 = mybir.dt.bfloat16
I32 = mybir.dt.int32
U32 = mybir.dt.uint32
U16 = mybir.dt.uint16
AF = mybir.ActivationFunctionType
ALU = mybir.AluOpType
AX = mybir.AxisListType

ENGS = ['pe', 'dve', 'act', 'pool', 'sp']
EPOCH = 30000
N_DMA_SEMS = 8


class Res:
    __slots__ = ('w', 'r')

    def __init__(self):
        self.w = {}
        self.r = {}


class Buf:
    def __init__(self, t):
        self.t = t
        self.res = Res()
        self.parts = {}

    def part(self, key):
        r = self.parts.get(key)
        if r is None:
            r = Res()
            self.parts[key] = r
        return r


def _res(x):
    return x.res if isinstance(x, Buf) else x


class Sched:
    def __init__(self, nc, es):
        self.nc = nc
        self.es = es
        self.ops = {e: [] for e in ENGS}
        self.sem = {}
        self.cnt = {}
        self.known = {e: {} for e in ENGS}
        self.nsem = 0
        for e in ENGS:
            self._new_epoch(e)
        self.dma_sems = [self._mksem("dq%d" % i) for i in range(N_DMA_SEMS)]
        self.dma_cnt = [0] * N_DMA_SEMS
        self.dma_rr = 0
        self.all_sems = {}
        self.n_instr = 0
        self.stage_es = None
        self.load_sems = []
        self.load_cnt = []
        self.buf_sem = {}
        self.next_load = 0

    def _mksem(self, name):
        self.nsem += 1
        return self.es.enter_context(self.nc.semaphore("%s_%d" % (name, self.nsem)))

    def _new_epoch(self, e):
        self.sem[e] = self._mksem("e" + e)
        self.cnt[e] = 0

    def sbuf(self, name, shape, dtype):
        self.nbuf = getattr(self, 'nbuf', 0) + 1
        es = self.stage_es if self.stage_es is not None else self.es
        return Buf(es.enter_context(self.nc.sbuf_tensor("%s_%d" % (name, self.nbuf), shape, dtype)))

    def psum(self, name, shape, dtype):
        return Buf(self.es.enter_context(self.nc.psum_tensor(name, shape, dtype)))

    def barrier(self):
        for e in ENGS:
            kn = self.known[e]
            for sid, (sem, val) in self.all_sems.items():
                if kn.get(sid, 0) < val:
                    kn[sid] = val
                    self.ops[e].append(('wait', sem, val))
            for o in ENGS:
                if o == e or o == 'sp' or self.cnt[o] == 0:
                    continue
                sem = self.sem[o]
                if kn.get(id(sem), 0) < self.cnt[o]:
                    kn[id(sem)] = self.cnt[o]
                    self.ops[e].append(('wait', sem, self.cnt[o]))

    def dram(self, name, shape, dtype):
        return Buf(self.nc.dram_tensor(name, shape, dtype, kind="Internal"))

    def _collect(self, e, reads, writes):
        need = {}

        def mrg(d):
            for s, v in d.items():
                if need.get(s, (None, 0))[1] < v[1]:
                    need[s] = v
        for r in reads:
            mrg(_res(r).w)
        for w in writes:
            w = _res(w)
            mrg(w.w)
            mrg(w.r)
        kn = self.known[e]
        for sid, (sem, val, eng) in need.items():
            if eng == 'pe' and e == 'pe':
                continue
            if kn.get(sid, 0) < val:
                kn[sid] = val
                self.ops[e].append(('wait', sem, val))

    def _publish(self, key, reads, writes):
        sid = id(key[0])
        for w in writes:
            w = _res(w)
            w.w = {sid: key}
            w.r = {}
        for r in reads:
            _res(r).r[sid] = key

    def op(self, e, fn, reads=(), writes=()):
        self._collect(e, reads, writes)
        if self.cnt[e] >= EPOCH:
            self._new_epoch(e)
        self.cnt[e] += 1
        sem = self.sem[e]
        self.ops[e].append(('op', fn, sem))
        self._publish((sem, self.cnt[e], e), reads, writes)
        self.n_instr += 1

    def dma(self, e, out, in_, reads=(), writes=(), fn=None, **kw):
        self._collect(e, reads, writes)
        if writes:
            key = id(_res(writes[0]))
            i = self.buf_sem.get(key)
            if i is None:
                i = self.next_load
                self.next_load += 1
                if i >= len(self.load_sems):
                    self.load_sems.append(self._mksem("dl"))
                    self.load_cnt.append(0)
                self.buf_sem[key] = i
            self.load_cnt[i] += 16
            if self.load_cnt[i] >= EPOCH:
                self.load_sems[i] = self._mksem("dl")
                self.load_cnt[i] = 16
            sem, cnt = self.load_sems[i], self.load_cnt[i]
        else:
            i = self.dma_rr
            self.dma_rr = (i + 1) % N_DMA_SEMS
            self.dma_cnt[i] += 16
            if self.dma_cnt[i] >= EPOCH:
                self.dma_sems[i] = self._mksem("dq")
                self.dma_cnt[i] = 16
            sem, cnt = self.dma_sems[i], self.dma_cnt[i]
        if fn is None:
            def fn(g, out=out, in_=in_, kw=kw):
                return g.dma_start(out=out, in_=in_, **kw)
        self.ops[e].append(('dma', fn, sem))
        self._publish((sem, cnt, 'dma'), reads, writes)
        self.all_sems[id(sem)] = (sem, cnt)
        self.n_instr += 1

    def finish(self):
        sp = self.ops['sp']
        for sid, (sem, val) in self.all_sems.items():
            sp.append(('wait', sem, val))
        for e in ENGS:
            if e != 'sp' and self.cnt[e] > 0:
                sp.append(('wait', self.sem[e], self.cnt[e]))
        self.flush()

    def flush(self):
        nc = self.nc
        ops = self.ops
        self.buf_sem = {}
        self.next_load = 0
        self.ops = {e: [] for e in ENGS}

        def emit(eng, lst):
            for it in lst:
                if it[0] == 'wait':
                    eng.wait_ge(it[1], it[2])
                elif it[0] == 'op':
                    it[1](eng).then_inc(it[2], 1)
                else:
                    it[1](eng).then_inc(it[2], 16)

        with nc.Block() as block:
            @block.tensor
            def _(g):
                emit(g, ops['pe'])

            @block.vector
            def _(g):
                emit(g, ops['dve'])

            @block.scalar
            def _(g):
                emit(g, ops['act'])

            @block.gpsimd
            def _(g):
                emit(g, ops['pool'])

            @block.sync
            def _(g):
                emit(g, ops['sp'])

import math

D = 1024
T = 4096
C = 256
NTOK = T + C
DEPTH = 2
EPS = 1e-6
GW = 64
NEG = -30000.0
DBG = False
STAGES = None


def _groups():
    g = [(0, C)]
    for i in range(T // 512):
        g.append((C + 512 * i, 512))
    return g


def build(dbg=False, stop_after=None):
    nc = bass.Bass("TRN2", target_bir_lowering=False)

    def din(name, shape, dt=F32):
        return nc.dram_tensor(name, list(shape), dt, kind="ExternalInput").ap()

    kind_s = "ExternalOutput" if dbg else "Internal"

    def dscr(name, shape, dt):
        return nc.dram_tensor(name, list(shape), dt, kind=kind_s).ap()

    x_in = din("x", [T, D])
    c_in = din("ctx", [C, D])
    cvec = din("cvec", [128, 8, 2])
    gmix = din("gmix", [DEPTH, 128, 8])
    gffn = din("gffn", [DEPTH, 128, 8])
    gfin = din("gfin", [128, D])
    w_ada = din("w_ada", [DEPTH, D, 6 * D])
    b_adaT = din("b_adaT", [DEPTH, 128, 48])
    w_in = din("w_in", [DEPTH, D, 2816])
    tab_in = din("tab", [DEPTH, 128, 8, 22 * 64])
    hy_swT = din("hy_swT", [DEPTH, 128, 6, 3])
    hy_sbT = din("hy_sbT", [DEPTH, 128, 6])
    hy_f1w = din("hy_f1w", [DEPTH, 33, 64])
    hy_f1b = din("hy_f1b", [DEPTH, 64, 1])
    hy_f2w = din("hy_f2w", [DEPTH, 64, 64])
    hy_f2b = din("hy_f2b", [DEPTH, 64, 1])
    hy_f3w = din("hy_f3w", [DEPTH, 64, 512])
    hy_fq = din("hy_fq", [DEPTH, 64, 2])
    hy_dsk = din("hy_dsk", [DEPTH, 128, 2])
    zT_m = din("zT_m", [33, T])
    zT_c = din("zT_c", [33, C])
    dec_m = din("dec_m", [128, 2, T])
    dec_c = din("dec_c", [128, 2, C])
    rg_cw = din("rg_cw", [DEPTH, 128, 2, 4])
    rg_cb = din("rg_cb", [DEPTH, 128, 2])
    rg_wa = din("rg_wa", [DEPTH, 2, 4, 64, 64])
    rg_wx = din("rg_wx", [DEPTH, 2, 4, 64, 64])
    rg_baT = din("rg_baT", [DEPTH, 128, 2, 2])
    rg_bxT = din("rg_bxT", [DEPTH, 128, 2, 2])
    rg_lamT = din("rg_lamT", [DEPTH, 128, 2, 2])
    w_gate = din("w_gate", [DEPTH, D, 3 * D])
    b_gateT = din("b_gateT", [DEPTH, 128, 24])
    w_bra = din("w_br_a", [DEPTH, 512, D])
    w_brb = din("w_br_b", [DEPTH, 256, D])
    w_brc = din("w_br_c", [DEPTH, 256, D])
    w_out = din("w_out", [DEPTH, D, D])
    p_wq = din("peer_wq", [DEPTH, D, 2048])
    p_keys = din("peer_keys", [DEPTH, 8, 2, 128, 128])
    p_u = din("peer_u", [DEPTH, 16384, D])
    p_v = din("peer_v", [DEPTH, 16384, D])
    ident_in = din("ident", [128, 128])
    iota_in = din("iota16", [128, 16])
    osel_in = din("onesel", [128, 2, 128])
    y_out = nc.dram_tensor("y", [T, D], F32, kind="ExternalOutput").ap()

    xres = dscr("xres", [T, D], F32)
    cres = dscr("cres", [C, D], F32)
    bcd = dscr("bcd", [10, 128, D], F32)
    xnT_d = dscr("xnT_d", [128, 8, NTOK], BF16)
    xntm_d = dscr("xntm_d", [NTOK, D], BF16)
    qkT_d = dscr("qkT_d", [8, 128, NTOK], BF16)
    vp_d = dscr("vp_d", [4, NTOK, 256], BF16)
    phT_d = dscr("phT_d", [6, 128, NTOK], F32)
    prT_d = dscr("prT_d", [4, 128, NTOK], F32)
    yaT_d = dscr("yaT_d", [4, 128, NTOK], BF16)
    ybT_d = dscr("ybT_d", [2, 128, NTOK], BF16)
    ycT_d = dscr("ycT_d", [2, 128, NTOK], BF16)
    ed_m = dscr("ed_m", [256, 2 * T], BF16)
    ed_c = dscr("ed_c", [256, 2 * C], BF16)
    ub_m = dscr("ub_m", [128, 2 * 128 * 32], BF16)
    ub_c = dscr("ub_c", [128, 2 * 128 * 2], BF16)
    x0_d = dscr("x0_d", [2, 128, NTOK], BF16)
    uv_d = nc.dram_tensor("uv_d", [16384, 2 * D], BF16, kind="Internal").ap()
    qT_d = nc.dram_tensor("qT_d", [128, 16, NTOK], BF16, kind="Internal").ap()
    dbg_h = dscr("dbg_h", [2, 64, T], F32) if dbg else None

    ges = ExitStack()
    with ges:
        S = Sched(nc, ges)
        PS = [S.psum("ps%d" % i, [128, 512], F32) for i in range(6)]
        PSB = [S.psum("psb%d" % i, [128, 1024], BF16) for i in range(2)]
        ident_f = S.sbuf("ident_f", [128, 128], F32)
        ident_b = S.sbuf("ident_b", [128, 128], BF16)
        ones_f = S.sbuf("ones_f", [128, 128], F32)
        iota16 = S.sbuf("iota16", [128, 16], F32)
        osel = S.sbuf("osel", [128, 2, 128], BF16)
        osel_f = S.sbuf("osel_f", [128, 2, 128], F32)
        modT = S.sbuf("modT", [128, 48, 2], F32)
        Gs = S.sbuf("Gs", [128, 2, 8, 2], F32)
        S.dma('sp', ident_f.t[:], ident_in, writes=[ident_f])
        S.dma('sp', iota16.t[:], iota_in, writes=[iota16])
        S.dma('sp', osel_f.t[:], osel_in, writes=[osel_f])
        S.op('dve', lambda g: g.tensor_copy(out=ident_b.t[:], in_=ident_f.t[:]), reads=[ident_f], writes=[ident_b])
        S.op('dve', lambda g: g.tensor_copy(out=osel.t[:], in_=osel_f.t[:]), reads=[osel_f], writes=[osel])
        S.op('dve', lambda g: g.memset(ones_f.t[:], 1.0), writes=[ones_f])
        rr = {'n': 0}

        def alt(engs=('dve', 'act')):
            rr['n'] += 1
            return engs[rr['n'] % len(engs)]

        def copy_op(e, out, in_, reads, writes):
            if e == 'act':
                S.op('act', lambda g: g.activation(out=out, in_=in_, func=AF.Copy), reads=reads, writes=writes)
            else:
                S.op(e, lambda g: g.tensor_copy(out=out, in_=in_), reads=reads, writes=writes)

        class Stage:
            def __init__(self, name):
                self.name = name

            def __enter__(self):
                self.es = ExitStack()
                self.es.__enter__()
                S.stage_es = self.es
                return self

            def __exit__(self, *a):
                S.barrier()
                S.flush()
                S.stage_es = None
                self.es.__exit__(None, None, None)
                return False

        GROUPS = _groups()

        def stage_mod(l):
            with Stage("mod"):
                sc = S.sbuf("sc", [128, 8, 2], F32)
                ba = S.sbuf("ba", [128, 48], F32)
                gm = S.sbuf("gm", [128, 2, 8], F32)
                wb = [S.sbuf("wada", [128, 6144], F32) for _ in range(2)]
                diag = [S.sbuf("diag", [128, 128], F32) for _ in range(2)]
                bct = [S.sbuf("bct", [128, D], F32) for _ in range(2)]
                S.dma('sp', sc.t[:], cvec, writes=[sc])
                S.dma('sp', ba.t[:], b_adaT[l], writes=[ba])
                S.dma('sp', gm.t[:, 0, :], gmix[l], writes=[gm])
                S.dma('sp', gm.t[:, 1, :], gffn[l], writes=[gm])
                S.op('act', lambda g: g.activation(out=sc.t[:], in_=sc.t[:], func=AF.Silu), reads=[sc], writes=[sc])
                ps = PS[0]
                S.op('dve', lambda g: g.memset(ps.t[:, 0:96], 0.0), writes=[ps])
                for kc in range(8):
                    w = wb[kc % 2]
                    for hh in range(2):
                        S.dma('sp', w.t[:, hh * 3072:(hh + 1) * 3072], w_ada[l, kc * 128:(kc + 1) * 128, hh * 3072:(hh + 1) * 3072], writes=[w])
                    for j in range(48):
                        S.op('pe', lambda g, w=w, j=j, kc=kc: g.matmul(ps.t[:, 2 * j:2 * j + 2], lhsT=w.t[:, j * 128:(j + 1) * 128], rhs=sc.t[:, kc, :], start=False, stop=False, skip_group_check=True), reads=[w, sc], writes=[ps])
                S.op('dve', lambda g: g.tensor_tensor(out=modT.t[:], in0=ps.t[:, 0:96].rearrange("p (j s) -> p j s", s=2), in1=ba.t[:].unsqueeze(2).broadcast_to([128, 48, 2]), op=ALU.add), reads=[ps, ba], writes=[modT])
                for wh, off in ((0, 8), (1, 32)):
                    S.op('dve', lambda g, wh=wh, off=off: g.tensor_scalar(out=Gs.t[:, wh], in0=modT.t[:, off:off + 8, :], scalar1=1.0, scalar2=None, op0=ALU.add), reads=[modT], writes=[Gs])
                    S.op('dve', lambda g, wh=wh: g.tensor_tensor(out=Gs.t[:, wh], in0=Gs.t[:, wh], in1=gm.t[:, wh, :].unsqueeze(2).broadcast_to([128, 8, 2]), op=ALU.mult), reads=[Gs, gm], writes=[Gs])
                def srcs(idx):
                    if idx < 8:
                        wh, kind, s = idx // 4, (idx // 2) % 2, idx % 2
                        if kind == 0:
                            return lambda kc: Gs.t[:, wh, kc, s:s + 1]
                        off = 0 if wh == 0 else 24
                        return lambda kc: modT.t[:, off + kc, s:s + 1]
                    s = idx - 8
                    return lambda kc: modT.t[:, 40 + kc, s:s + 1]
                n = 0
                for idx in range(10):
                    f = srcs(idx)
                    bt = bct[idx % 2]
                    for half in range(2):
                        pb = PS[1 + (n % 2)]
                        n += 1
                        for k4 in range(4):
                            kc = half * 4 + k4
                            dg = diag[kc % 2]
                            S.op('dve', lambda g, dg=dg, f=f, kc=kc: g.tensor_scalar(out=dg.t[:], in0=ident_f.t[:], scalar1=f(kc), scalar2=None, op0=ALU.mult), reads=[ident_f, Gs, modT], writes=[dg])
                            S.op('pe', lambda g, pb=pb, dg=dg, k4=k4: g.matmul(pb.t[:, k4 * 128:(k4 + 1) * 128], lhsT=ones_f.t[:], rhs=dg.t[:], start=True, stop=True, skip_group_check=True), reads=[ones_f, dg], writes=[pb])
                        copy_op('act', bt.t[:, half * 512:(half + 1) * 512], pb.t[:], [pb], [bt])
                    S.dma('sp', bcd[idx], bt.t[:], reads=[bt])

        def stage_norm(l, wh, first):
            with Stage("norm"):
                Gb = [S.sbuf("Gb", [128, D], F32) for _ in range(2)]
                Sb = [S.sbuf("Sb", [128, D], F32) for _ in range(2)]
                for s in range(2):
                    S.dma('sp', Gb[s].t[:], bcd[wh * 4 + 0 + s], writes=[Gb[s]])
                    S.dma('sp', Sb[s].t[:], bcd[wh * 4 + 2 + s], writes=[Sb[s]])
                xnT = S.sbuf("xnT", [128, 8, NTOK], BF16)
                xin = [S.sbuf("xin", [128, D], F32) for _ in range(3)]
                junk = S.sbuf("junk", [128, D], BF16)
                tmp = [S.sbuf("tmpn", [128, D], F32) for _ in range(2)]
                xnb = [S.sbuf("xnb", [128, D], BF16) for _ in range(2)]
                ss = S.sbuf("ss", [128, 34], F32)
                t1 = S.sbuf("t1", [128, 34], F32)
                rstd = S.sbuf("rstd", [128, 34], F32)
                for tt in range(34):
                    s = 1 if tt < 2 else 0
                    if tt < 2:
                        src = (c_in if first else cres)[tt * 128:(tt + 1) * 128, :]
                    else:
                        src = (x_in if first else xres)[(tt - 2) * 128:(tt - 1) * 128, :]
                    xt = xin[tt % 3]
                    S.dma('sp', xt.t[:], src, writes=[xt])
                    S.op('act', lambda g, xt=xt, tt=tt: g.activation(out=junk.t[:], in_=xt.t[:], func=AF.Square, accum_out=ss.t[:, tt:tt + 1]), reads=[xt], writes=[junk, ss.part(tt)])
                    S.op('dve', lambda g, tt=tt: g.tensor_scalar(out=t1.t[:, tt:tt + 1], in0=ss.t[:, tt:tt + 1], scalar1=1.0 / D, scalar2=EPS, op0=ALU.mult, op1=ALU.add), reads=[ss.part(tt)], writes=[t1.part(tt)])
                    S.op('act', lambda g, tt=tt: g.activation(out=t1.t[:, tt:tt + 1], in_=t1.t[:, tt:tt + 1], func=AF.Sqrt), reads=[t1.part(tt)], writes=[t1.part(tt)])
                    S.op('dve', lambda g, tt=tt: g.reciprocal(out=rstd.t[:, tt:tt + 1], in_=t1.t[:, tt:tt + 1]), reads=[t1.part(tt)], writes=[rstd.part(tt)])
                    tm = tmp[tt % 2]
                    xb = xnb[tt % 2]
                    S.op('dve', lambda g, tm=tm, xt=xt, tt=tt, s=s: g.scalar_tensor_tensor(out=tm.t[:], in0=xt.t[:], scalar=rstd.t[:, tt:tt + 1], in1=Gb[s].t[:], op0=ALU.mult, op1=ALU.mult), reads=[xt, rstd.part(tt), Gb[s]], writes=[tm])
                    S.op('pool', lambda g, tm=tm, xb=xb, s=s: g.tensor_tensor(out=xb.t[:], in0=tm.t[:], in1=Sb[s].t[:], op=ALU.add), reads=[tm, Sb[s]], writes=[xb])
                    S.dma('sp', xntm_d[tt * 128:(tt + 1) * 128, :], xb.t[:], reads=[xb])
                    pb = PSB[tt % 2]
                    for kc in range(8):
                        S.op('pe', lambda g, pb=pb, xb=xb, kc=kc: g.transpose(out=pb.t[:, kc * 128:(kc + 1) * 128], in_=xb.t[:, kc * 128:(kc + 1) * 128], identity=ident_b.t[:]), reads=[xb, ident_b], writes=[pb])
                    copy_op('act', xnT.t[:, :, tt * 128:(tt + 1) * 128], pb.t[:].rearrange("p (k t) -> p k t", k=8), [pb], [xnT.part(tt)])
                for kc in range(8):
                    S.dma('sp', xnT_d[:, kc, :], xnT.t[:, kc, :], reads=[xnT.part(tt) for tt in range(34)])

        def stage_proj(l):
            with Stage("proj"):
                xnT = S.sbuf("xnT", [128, 8, NTOK], BF16)
                for kc in range(8):
                    S.dma('sp', xnT.t[:, kc, :], xnT_d[:, kc, :], writes=[xnT])
                wst = [S.sbuf("wst", [128, 2816], F32) for _ in range(2)]
                wbf = S.sbuf("wbf", [128, 8, 2816], BF16)
                for kc in range(8):
                    ws = wst[kc % 2]
                    S.dma('sp', ws.t[:], w_in[l, kc * 128:(kc + 1) * 128, :], writes=[ws])
                    copy_op(alt(('dve', 'pool')), wbf.t[:, kc, :], ws.t[:], [ws], [wbf.part(kc)])
                wparts = [wbf.part(kc) for kc in range(8)]
                ob = [S.sbuf("ob", [128, 512], BF16) for _ in range(3)]
                of = [S.sbuf("of", [128, 512], F32) for _ in range(3)]
                vpt = [S.sbuf("vpt", [128, 8, 128], BF16) for _ in range(2)]
                for v in vpt:
                    S.op('pool', lambda g, v=v: g.memset(v.t[:], 0.0), writes=[v])
                n = 0
                for (c0, nn) in GROUPS:
                    for ch in list(range(8)) + list(range(12, 22)):
                        ps = PS[n % 4]
                        for kc in range(8):
                            S.op('pe', lambda g, ps=ps, kc=kc, ch=ch, c0=c0, nn=nn: g.matmul(ps.t[:, 0:nn], lhsT=wbf.t[:, kc, ch * 128:(ch + 1) * 128], rhs=xnT.t[:, kc, c0:c0 + nn], start=(kc == 0), stop=(kc == 7)), reads=[xnT] + wparts, writes=[ps])
                        if ch < 8:
                            o = ob[n % 3]
                            if ch < 4:
                                S.op('act', lambda g, o=o, ps=ps, nn=nn: g.activation(out=o.t[:, 0:nn], in_=ps.t[:, 0:nn], func=AF.Copy, scale=0.125), reads=[ps], writes=[o])
                            else:
                                copy_op('dve', o.t[:, 0:nn], ps.t[:, 0:nn], [ps], [o])
                            S.dma('sp', qkT_d[ch, :, c0:c0 + nn], o.t[:, 0:nn], reads=[o])
                        else:
                            o = of[n % 3]
                            copy_op(alt(), o.t[:, 0:nn], ps.t[:, 0:nn], [ps], [o])
                            dst = phT_d[ch - 12] if ch < 18 else prT_d[ch - 18]
                            S.dma('sp', dst[:, c0:c0 + nn], o.t[:, 0:nn], reads=[o])
                        n += 1
                    for t4 in range(nn // 128):
                        tc0 = c0 + t4 * 128
                        ps = PS[4 + (n % 2)]
                        vt = vpt[n % 2]
                        n += 1
                        for kc in range(8):
                            S.op('pe', lambda g, ps=ps, kc=kc, tc0=tc0: g.matmul(ps.t[:], lhsT=xnT.t[:, kc, tc0:tc0 + 128], rhs=wbf.t[:, kc, 1024:1536], start=(kc == 0), stop=(kc == 7)), reads=[xnT] + wparts, writes=[ps])
                        S.op('dve', lambda g, ps=ps, vt=vt: g.tensor_copy(out=bass.AP(vt.t[:].tensor, vt.t[:].offset, [[1024, 128], [256, 4], [192, 2], [1, 64]]), in_=ps.t[:].rearrange("p (j e d) -> p j e d", j=4, e=2)), reads=[ps], writes=[vt])
                        for j in range(4):
                            S.dma('sp', vp_d[j, tc0:tc0 + 128, :], vt.t[:, 2 * j:2 * j + 2, :].rearrange("p h c -> p (h c)"), reads=[vt])

        def attn_ranges(i):
            r0 = 8 * i
            rows = list(range(r0, r0 + 8))
            rs = lambda r: min(max(r - 4, 0), GW - 8)
            amin = min(rs(r) for r in rows)
            amax = max(rs(r) for r in rows) + 7
            a0s = list(range(amin - (amin % 2), amax + 1, 2))
            out = []
            for a0 in a0s:
                hal = []
                for a in (a0, a0 + 1):
                    v = [r for r in rows if rs(r) <= a <= rs(r) + 7] if a < GW else []
                    hal.append((v[0] - r0, v[-1] - r0 + 1) if v else None)
                lo = min(h[0] for h in hal if h)
                hi = max(h[1] for h in hal if h)
                out.append((a0, hal, (lo, hi)))
            return out

        def stage_attn(l):
            with Stage("attn"):
                tab = S.sbuf("tab", [128, 8, 22 * 64], BF16)
                tst = [S.sbuf("tst", [128, 22 * 64], F32) for _ in range(2)]
                for h in range(8):
                    S.dma('sp', tst[h % 2].t[:], tab_in[l, :, h, :], writes=[tst[h % 2]])
                    copy_op(alt(('dve', 'pool')), tab.t[:, h, :], tst[h % 2].t[:], [tst[h % 2]], [tab.part(h)])
                qTs = [S.sbuf("qTs", [128, NTOK], BF16) for _ in range(2)]
                kTs = [S.sbuf("kTs", [128, NTOK], BF16) for _ in range(2)]
                vps = [S.sbuf("vps", [128, 34, 256], BF16) for _ in range(2)]
                pts = [S.sbuf("pt", [128, 512], BF16) for _ in range(3)]
                rec = [S.sbuf("rec", [128, 512], F32) for _ in range(2)]
                yab = [S.sbuf("yab", [128, 512], BF16) for _ in range(2)]
                n = {'s': 0, 'o': 0}
                for j in range(4):
                    qT, kT, vp = qTs[j % 2], kTs[j % 2], vps[j % 2]
                    S.dma('sp', qT.t[:], qkT_d[j], writes=[qT])
                    S.dma('sp', kT.t[:], qkT_d[4 + j], writes=[kT])
                    for q4 in range(2):
                        S.dma('sp', vp.t[:, q4 * 17:(q4 + 1) * 17, :], vp_d[j, q4 * 17 * 128:(q4 + 1) * 17 * 128, :].rearrange("(t p) c -> p t c", p=128), writes=[vp])
                    jobs = []
                    for i in range(-1, 8):
                        if i < 0:
                            qc0, nq = 0, C
                            klist = [('c', 0, None), ('c', 1, None)]
                        else:
                            qc0, nq = C + 512 * i, 512
                            klist = [('c', 0, None), ('c', 1, None)] + [('l', a0, (hal, un)) for (a0, hal, un) in attn_ranges(i)]
                        Dn = PS[5] if (n['o'] % 2) else PS[4]
                        O = PS[2] if (n['o'] % 2) else PS[3]
                        rc = rec[n['o'] % 2]
                        yb_ = yab[n['o'] % 2]
                        n['o'] += 1
                        grp = dict(i=i, qc0=qc0, nq=nq, O=O, Dn=Dn, rc=rc, yb=yb_)
                        cnt = 0
                        tot = 2 * len(klist)
                        for e in range(2):
                            for (kind, a0, info) in klist:
                                jobs.append(dict(g=grp, e=e, kind=kind, a0=a0, info=info, first=(cnt == 0), last=(cnt == tot - 1)))
                                cnt += 1

                    def emit_s(jb, jj=j):
                        g_ = jb['g']
                        i, qc0, nq = g_['i'], g_['qc0'], g_['nq']
                        e = jb['e']
                        h = 2 * jj + e
                        pb = e * 64
                        Sp = PS[n['s'] % 2]
                        pt = pts[n['s'] % 3]
                        n['s'] += 1
                        jb['pt'] = pt
                        a0 = jb['a0']
                        if jb['kind'] == 'c':
                            kc0 = a0 * 128
                            lo, hi = 0, nq
                            S.op('pe', lambda g, Sp=Sp, pb=pb, kc0=kc0, qc0=qc0, nq=nq, kT=kT, qT=qT: g.matmul(Sp.t[:, 0:nq], lhsT=kT.t[pb:pb + 64, kc0:kc0 + 128], rhs=qT.t[pb:pb + 64, qc0:qc0 + nq], start=True, stop=True), reads=[qT, kT], writes=[Sp])
                            S.op('act', lambda g, Sp=Sp, pt=pt, nq=nq: g.activation(out=pt.t[:, 0:nq], in_=Sp.t[:, 0:nq], func=AF.Exp), reads=[Sp], writes=[pt])
                            jb['vtile'] = a0
                        else:
                            hal, (ulo, uhi) = jb['info']
                            kc0 = C + a0 * GW
                            lo, hi = ulo * GW, uhi * GW
                            r0 = 8 * i
                            e0 = (r0 + ulo) - a0 + 10
                            assert 0 <= e0 and e0 + (uhi - ulo) <= 22, (i, a0, e0)
                            S.op('pe', lambda g, Sp=Sp, pb=pb, kc0=kc0, qc0=qc0, lo=lo, hi=hi, kT=kT, qT=qT: g.matmul(Sp.t[:, lo:hi], lhsT=kT.t[pb:pb + 64, kc0:kc0 + 128], rhs=qT.t[pb:pb + 64, qc0 + lo:qc0 + hi], start=True, stop=False), reads=[qT, kT], writes=[Sp])
                            S.op('pe', lambda g, Sp=Sp, lo=lo, hi=hi, h=h, e0=e0: g.matmul(Sp.t[:, lo:hi], lhsT=ident_b.t[:], rhs=tab.t[:, h, e0 * 64:e0 * 64 + (hi - lo)], start=False, stop=True), reads=[tab.part(h), ident_b], writes=[Sp])
                            for hf_, rng in enumerate(hal):
                                p0 = hf_ * 64
                                if rng is None:
                                    S.op('pool', lambda g, pt=pt, p0=p0, lo=lo, hi=hi: g.memset(pt.t[p0:p0 + 64, lo:hi], 0.0), writes=[pt])
                                    continue
                                vlo, vhi = rng[0] * GW, rng[1] * GW
                                S.op('act', lambda g, Sp=Sp, pt=pt, p0=p0, vlo=vlo, vhi=vhi: g.activation(out=pt.t[p0:p0 + 64, vlo:vhi], in_=Sp.t[p0:p0 + 64, vlo:vhi], func=AF.Exp), reads=[Sp], writes=[pt])
                                if vlo > lo:
                                    S.op('pool', lambda g, pt=pt, p0=p0, lo=lo, vlo=vlo: g.memset(pt.t[p0:p0 + 64, lo:vlo], 0.0), writes=[pt])
                                if vhi < hi:
                                    S.op('pool', lambda g, pt=pt, p0=p0, hi=hi, vhi=vhi: g.memset(pt.t[p0:p0 + 64, vhi:hi], 0.0), writes=[pt])
                            jb['vtile'] = 2 + a0 // 2
                        jb['lo'], jb['hi'] = lo, hi

                    def emit_pv(jb, jj=j):
                        g_ = jb['g']
                        O, Dn, nq, qc0 = g_['O'], g_['Dn'], g_['nq'], g_['qc0']
                        pt, lo, hi, e, vtile, first = jb['pt'], jb['lo'], jb['hi'], jb['e'], jb['vtile'], jb['first']
                        S.op('pe', lambda g, vp=vp: g.matmul(O.t[:, lo:hi], lhsT=vp.t[:, vtile, e * 128:(e + 1) * 128], rhs=pt.t[:, lo:hi], start=first, stop=False, skip_group_check=True), reads=[vp, pt], writes=[O])
                        S.op('pe', lambda g: g.matmul(Dn.t[:, lo:hi], lhsT=osel.t[:, e, :], rhs=pt.t[:, lo:hi], start=first, stop=False, skip_group_check=True), reads=[osel, pt], writes=[Dn])
                        if jb['last']:
                            rc, yb_ = g_['rc'], g_['yb']
                            S.op('dve', lambda g: g.reciprocal(out=rc.t[:, 0:nq], in_=Dn.t[:, 0:nq]), reads=[Dn], writes=[rc])
                            S.op('dve', lambda g: g.tensor_tensor(out=yb_.t[:, 0:nq], in0=O.t[:, 0:nq], in1=rc.t[:, 0:nq], op=ALU.mult), reads=[O, rc], writes=[yb_])
                            S.dma('sp', yaT_d[jj, :, qc0:qc0 + nq], yb_.t[:, 0:nq], reads=[yb_])

                    for idx in range(len(jobs) + 1):
                        if idx < len(jobs):
                            emit_s(jobs[idx])
                        if idx >= 1:
                            emit_pv(jobs[idx - 1])

        def sin_layer(ps, n, bias, fq, tmp, tmp2, out, outbuf, xr):
            S.op('dve', lambda g: g.tensor_scalar(out=tmp.t[0:64, 0:n], in0=ps.t[0:64, 0:n], scalar1=bias, scalar2=fq, op0=ALU.add, op1=ALU.mult), reads=[ps] + xr, writes=[tmp])
            MAGIC = 12582912.0
            S.op('dve', lambda g: g.tensor_scalar(out=tmp2.t[0:64, 0:n], in0=tmp.t[0:64, 0:n], scalar1=1.0 / (2 * math.pi), scalar2=MAGIC, op0=ALU.mult, op1=ALU.add), reads=[tmp], writes=[tmp2])
            S.op('dve', lambda g: g.tensor_scalar(out=tmp2.t[0:64, 0:n], in0=tmp2.t[0:64, 0:n], scalar1=MAGIC, scalar2=-2 * math.pi, op0=ALU.subtract, op1=ALU.mult), reads=[tmp2], writes=[tmp2])
            S.op('dve', lambda g: g.tensor_tensor(out=tmp.t[0:64, 0:n], in0=tmp.t[0:64, 0:n], in1=tmp2.t[0:64, 0:n], op=ALU.add), reads=[tmp, tmp2], writes=[tmp])
            S.op('dve', lambda g: g.tensor_scalar(out=tmp.t[0:64, 0:n], in0=tmp.t[0:64, 0:n], scalar1=-3.1415925, scalar2=3.1415925, op0=ALU.max, op1=ALU.min), reads=[tmp], writes=[tmp])
            S.op('act', lambda g: g.activation(out=out, in_=tmp.t[0:64, 0:n], func=AF.Sin), reads=[tmp], writes=[outbuf])

        def stage_hy_filt(l, L, zT, dec, ed):
            with Stage("hyfilt"):
                z = S.sbuf("z", [33, L], F32)
                dc = S.sbuf("dc", [128, 2, L], F32)
                f1w = S.sbuf("f1w", [33, 64], F32)
                f2w = S.sbuf("f2w", [64, 64], F32)
                f3w = S.sbuf("f3w", [64, 512], F32)
                fb = S.sbuf("fb", [64, 2], F32)
                fq = S.sbuf("fq", [64, 2], F32)
                dsk = S.sbuf("dsk", [128, 2], F32)
                h1 = S.sbuf("h1", [64, L], F32)
                h2 = S.sbuf("h2", [64, L], F32)
                hT = [S.sbuf("hT", [128, L], F32) for _ in range(4)]
                tmp = [S.sbuf("stmp", [64, 512], F32) for _ in range(4)]
                et = [S.sbuf("et", [128, 2 * L], BF16) for _ in range(2)]
                S.dma('sp', z.t[:], zT, writes=[z])
                S.dma('sp', dc.t[:], dec, writes=[dc])
                S.dma('sp', f1w.t[:], hy_f1w[l], writes=[f1w])
                S.dma('sp', f2w.t[:], hy_f2w[l], writes=[f2w])
                S.dma('sp', f3w.t[:], hy_f3w[l], writes=[f3w])
                S.dma('sp', fb.t[:, 0:1], hy_f1b[l], writes=[fb])
                S.dma('sp', fb.t[:, 1:2], hy_f2b[l], writes=[fb])
                S.dma('sp', fq.t[:], hy_fq[l], writes=[fq])
                S.dma('sp', dsk.t[:], hy_dsk[l], writes=[dsk])
                n = 0
                for c0 in range(0, L, 512):
                    nn = min(512, L - c0)
                    ps = PS[n % 2]
                    S.op('pe', lambda g, ps=ps, c0=c0, nn=nn: g.matmul(ps.t[0:64, 0:nn], lhsT=f1w.t[:], rhs=z.t[:, c0:c0 + nn], start=True, stop=True), reads=[f1w, z], writes=[ps])
                    sin_layer(ps, nn, fb.t[:, 0:1], fq.t[:, 0:1], tmp[0], tmp[2], h1.t[:, c0:c0 + nn], h1, [fb, fq])
                    ps2 = PS[2 + n % 2]
                    S.op('pe', lambda g, ps2=ps2, c0=c0, nn=nn: g.matmul(ps2.t[0:64, 0:nn], lhsT=f2w.t[:], rhs=h1.t[:, c0:c0 + nn], start=True, stop=True), reads=[f2w, h1], writes=[ps2])
                    sin_layer(ps2, nn, fb.t[:, 1:2], fq.t[:, 1:2], tmp[1], tmp[3], h2.t[:, c0:c0 + nn], h2, [fb, fq])
                    for c4 in range(4):
                        ps3 = PS[4 + c4 % 2]
                        S.op('pe', lambda g, ps3=ps3, c4=c4, c0=c0, nn=nn: g.matmul(ps3.t[:, 0:nn], lhsT=f3w.t[:, c4 * 128:(c4 + 1) * 128], rhs=h2.t[:, c0:c0 + nn], start=True, stop=True), reads=[f3w, h2], writes=[ps3])
                        S.op('dve', lambda g, ps3=ps3, c4=c4, c0=c0, nn=nn: g.tensor_tensor(out=hT[c4].t[:, c0:c0 + nn], in0=ps3.t[:, 0:nn], in1=dc.t[:, c4 % 2, c0:c0 + nn], op=ALU.mult), reads=[ps3, dc], writes=[hT[c4]])
                    n += 1
                if dbg_h is not None and L == T:
                    S.dma('sp', dbg_h[0], h1.t[:], reads=[h1])
                    S.dma('sp', dbg_h[1], h2.t[:], reads=[h2])
                for cc in range(2):
                    e_ = et[cc]
                    S.op('pool', lambda g, e_=e_: g.memset(e_.t[:, 0:1], 0.0), writes=[e_])
                    copy_op('act', e_.t[:, L:2 * L], hT[cc].t[:, :], [hT[cc]], [e_])
                    S.op('dve', lambda g, e_=e_, cc=cc: g.tensor_scalar(out=e_.t[:, L:L + 1], in0=hT[cc].t[:, 0:1], scalar1=dsk.t[:, cc:cc + 1], scalar2=None, op0=ALU.add), reads=[hT[cc], dsk, e_], writes=[e_])
                    S.op('dve', lambda g, e_=e_, cc=cc: g.tensor_copy(out=e_.t[:, 1:L], in_=hT[2 + cc].t[:, L - 1:0:-1]), reads=[hT[2 + cc], e_], writes=[e_])
                    S.dma('sp', ed[cc * 128:(cc + 1) * 128, :], e_.t[:], reads=[e_])

        def stage_hy_sc(l, L, c0, ub):
            nb = L // 128
            with Stage("hysc"):
                sw = S.sbuf("sw", [128, 6, 3], F32)
                sb = S.sbuf("sb", [128, 6], F32)
                S.dma('sp', sw.t[:], hy_swT[l], writes=[sw])
                S.dma('sp', sb.t[:], hy_sbT[l], writes=[sb])
                pin = [S.sbuf("pin", [128, L], F32) for _ in range(2)]
                ta = S.sbuf("ta", [128, L], F32)
                tb = S.sbuf("tb", [128, L], F32)
                vv = S.sbuf("vv", [128, L], F32)
                x0b = [S.sbuf("x0b", [128, L], BF16) for _ in range(2)]
                utr = [S.sbuf("utr", [128, L], BF16) for _ in range(2)]
                ubs = S.sbuf("ubs", [128, 2, 128, nb], BF16)
                n = {'p': 0}

                def conv(c6, out_buf, out_ap_full, final_writes):
                    p = pin[n['p'] % 2]
                    n['p'] += 1
                    S.dma('sp', p.t[:], phT_d[c6, :, c0:c0 + L], writes=[p])
                    S.op('dve', lambda g: g.tensor_scalar(out=ta.t[:], in0=p.t[:], scalar1=sw.t[:, c6, 1:2], scalar2=sb.t[:, c6:c6 + 1], op0=ALU.mult, op1=ALU.add), reads=[p, sw, sb], writes=[ta])
                    S.op('dve', lambda g: g.scalar_tensor_tensor(out=ta.t[:, 1:L], in0=p.t[:, 0:L - 1], scalar=sw.t[:, c6, 0:1], in1=ta.t[:, 1:L], op0=ALU.mult, op1=ALU.add), reads=[p, sw, ta], writes=[ta])
                    S.op('dve', lambda g: g.scalar_tensor_tensor(out=out_ap_full(0, L - 1), in0=p.t[:, 1:L], scalar=sw.t[:, c6, 2:3], in1=ta.t[:, 0:L - 1], op0=ALU.mult, op1=ALU.add), reads=[p, sw, ta], writes=[out_buf])
                    copy_op('dve', out_ap_full(L - 1, L), ta.t[:, L - 1:L], [ta, out_buf], [out_buf])

                for cc in range(2):
                    conv(cc, x0b[cc], lambda a, b, cc=cc: x0b[cc].t[:, a:b], None)
                    S.dma('sp', x0_d[cc, :, c0:c0 + L], x0b[cc].t[:], reads=[x0b[cc]])
                for cc in range(2):
                    conv(2 + cc, tb, lambda a, b: tb.t[:, a:b], None)
                    conv(4 + cc, vv, lambda a, b, vv=vv: vv.t[:, a:b], None)
                    S.op('dve', lambda g, cc=cc, vv=vv: g.tensor_tensor(out=utr[cc].t[:, ::-1], in0=vv.t[:], in1=tb.t[:], op=ALU.mult), reads=[vv, tb], writes=[utr[cc]])
                    for jb in range(0, nb, 8):
                        k = min(8, nb - jb)
                        pb = PSB[(jb // 8) % 2]
                        for q in range(k):
                            S.op('pe', lambda g, pb=pb, q=q, jb=jb, cc=cc: g.transpose(out=pb.t[:, q * 128:(q + 1) * 128], in_=utr[cc].t[:, (jb + q) * 128:(jb + q + 1) * 128], identity=ident_b.t[:]), reads=[utr[cc], ident_b], writes=[pb])
                        base = ubs.t[:, cc, :, :]
                        off = base.offset + (nb - 1 - jb)
                        outap = bass.AP(ubs.t[:].tensor, off, [[2 * 128 * nb, 128], [-1, k], [nb, 128]])
                        S.op('dve', lambda g, pb=pb, k=k, outap=outap: g.tensor_copy(out=outap, in_=pb.t[:, 0:k * 128].rearrange("p (q c) -> p q c", q=k)), reads=[pb], writes=[ubs])
                S.dma('sp', ub, ubs.t[:].rearrange("p a c j -> p (a c j)"), reads=[ubs])

        def stage_hy_toep(l, L, c0, ed, ub):
            nb = L // 128
            W = 2 * L - 127
            with Stage("hytoep"):
                ubs = S.sbuf("ubs", [128, 2, 128, nb], BF16)
                S.dma('sp', ubs.t[:].rearrange("p a c j -> p (a c j)"), ub, writes=[ubs])
                kts = [S.sbuf("kt", [128, W], BF16) for _ in range(3)]
                ysb = S.sbuf("ysb", [128, 128, nb], F32)
                x0b = S.sbuf("x0b", [128, L], BF16)
                ybo = S.sbuf("ybo", [128, L], BF16)
                per_bank = min(512 // nb, 128)
                for cc in range(2):
                    S.dma('sp', x0b.t[:], x0_d[cc, :, c0:c0 + L], writes=[x0b])
                    for c in range(128):
                        ch = cc * 128 + c
                        kt = kts[ch % 3]
                        S.dma('sp', kt.t[:], bass.AP(ed.tensor, ed.offset + ch * 2 * L, [[1, 128], [1, W]]), writes=[kt])
                        bank = PS[(c // per_bank) % 2]
                        col = (c % per_bank) * nb
                        ds = [0] + [d for d in range(-(nb - 1), nb) if d != 0]
                        for d in ds:
                            j0, j1 = max(0, -d), min(nb, nb - d)
                            xo = L - 127 + 128 * d
                            S.op('pe', lambda g, bank=bank, col=col, kt=kt, xo=xo, cc=cc, c=c, j0=j0, j1=j1, d=d: g.matmul(bank.t[:, col + j0 + d:col + j1 + d], lhsT=kt.t[:, xo:xo + 128], rhs=ubs.t[:, cc, c, j0:j1], start=(d == 0), stop=False, skip_group_check=True), reads=[kt, ubs], writes=[bank])
                        if c % per_bank == per_bank - 1:
                            cb = c - per_bank + 1
                            copy_op(alt(), ysb.t[:, cb:c + 1, :], bank.t[:, 0:per_bank * nb].rearrange("p (c j) -> p c j", j=nb), [bank], [ysb])
                    for I0 in range(0, nb, 4):
                        k = min(4, nb - I0)
                        pt_ = PS[2 + (I0 // 4) % 2]
                        for q in range(k):
                            S.op('pe', lambda g, pt_=pt_, q=q, I0=I0: g.transpose(out=pt_.t[:, q * 128:(q + 1) * 128], in_=ysb.t[:, :, I0 + q], identity=ident_f.t[:]), reads=[ysb, ident_f], writes=[pt_])
                        S.op('dve', lambda g, pt_=pt_, k=k, I0=I0: g.tensor_tensor(out=ybo.t[:, I0 * 128:(I0 + k) * 128], in0=pt_.t[:, 0:k * 128], in1=x0b.t[:, I0 * 128:(I0 + k) * 128], op=ALU.mult), reads=[pt_, x0b], writes=[ybo])
                    S.dma('sp', ybT_d[cc, :, c0:c0 + L], ybo.t[:], reads=[ybo])

        def stage_rglru(l):
            with Stage("rglru"):
                cw = S.sbuf("cw", [128, 2, 4], F32)
                cb = S.sbuf("cb", [128, 2], F32)
                baT = S.sbuf("baT", [128, 2, 2], F32)
                bxT = S.sbuf("bxT", [128, 2, 2], F32)
                lam = S.sbuf("lam", [128, 2, 2], F32)
                m8 = S.sbuf("m8", [128, 2, 2], F32)
                m16 = S.sbuf("m16", [128, 2, 2], F32)
                h0 = S.sbuf("h0", [128, 2, 2], F32)
                bdf = S.sbuf("bdf", [128, 8, 128], F32)
                bd = S.sbuf("bd", [128, 8, 128], BF16)
                for (t_, src) in ((cw, rg_cw[l]), (cb, rg_cb[l]), (baT, rg_baT[l]), (bxT, rg_bxT[l]), (lam, rg_lamT[l])):
                    S.dma('sp', t_.t[:], src, writes=[t_])
                S.op('pool', lambda g: g.memset(bdf.t[:], 0.0), writes=[bdf])
                for cc in range(2):
                    for dr in range(2):
                        for ax, wsrc in ((0, rg_wa), (1, rg_wx)):
                            idx = (cc * 2 + dr) * 2 + ax
                            for hb_ in range(2):
                                S.dma('sp', bdf.t[hb_ * 64:(hb_ + 1) * 64, idx, hb_ * 64:(hb_ + 1) * 64], wsrc[l, dr, 2 * cc + hb_], reads=[bdf], writes=[bdf])
                copy_op('dve', bd.t[:], bdf.t[:], [bdf], [bd])
                S.op('act', lambda g: g.activation(out=lam.t[:], in_=lam.t[:], func=AF.Exp, scale=-1.0), reads=[lam], writes=[lam])
                S.op('act', lambda g: g.activation(out=lam.t[:], in_=lam.t[:], func=AF.Ln, bias=1.0), reads=[lam], writes=[lam])
                S.op('dve', lambda g: g.tensor_scalar(out=m8.t[:], in0=lam.t[:], scalar1=-8.0, scalar2=None, op0=ALU.mult), reads=[lam], writes=[m8])
                S.op('dve', lambda g: g.tensor_scalar(out=m16.t[:], in0=lam.t[:], scalar1=-16.0, scalar2=None, op0=ALU.mult), reads=[lam], writes=[m16])
                LM = T
                xin = S.sbuf("rxin", [128, LM], F32)
                xc = S.sbuf("rxc", [128, LM], F32)
                xcb = S.sbuf("rxcb", [128, LM], BF16)
                rb = S.sbuf("rr", [128, LM], F32)
                ib = S.sbuf("ri", [128, LM], F32)
                ab = S.sbuf("ra", [128, LM], F32)
                hh = [S.sbuf("rh", [128, LM], F32) for _ in range(2)]
                yo = S.sbuf("ryo", [128, LM], BF16)
                n = 0
                for (L, c0, isctx) in ((C, 0, True), (T, C, False)):
                    for cc in range(2):
                        S.dma('sp', xin.t[:, 0:L], prT_d[cc, :, c0:c0 + L], writes=[xin])
                        S.op('dve', lambda g, L=L, cc=cc: g.tensor_scalar(out=xc.t[:, 0:L], in0=xin.t[:, 0:L], scalar1=cw.t[:, cc, 2:3], scalar2=cb.t[:, cc:cc + 1], op0=ALU.mult, op1=ALU.add), reads=[xin, cw, cb], writes=[xc])
                        for (k, sh) in ((0, -2), (1, -1), (3, 1)):
                            if sh < 0:
                                oa, ia = (-sh, L), (0, L + sh)
                            else:
                                oa, ia = (0, L - sh), (sh, L)
                            S.op('dve', lambda g, oa=oa, ia=ia, k=k, cc=cc: g.scalar_tensor_tensor(out=xc.t[:, oa[0]:oa[1]], in0=xin.t[:, ia[0]:ia[1]], scalar=cw.t[:, cc, k:k + 1], in1=xc.t[:, oa[0]:oa[1]], op0=ALU.mult, op1=ALU.add), reads=[xin, cw, xc], writes=[xc])
                        copy_op('act', xcb.t[:, 0:L], xc.t[:, 0:L], [xc], [xcb])
                        for dr in range(2):
                            for g0 in range(0, L, 512):
                                nn = min(512, L - g0)
                                for ax, dst, bias in ((0, rb, baT), (1, ib, bxT)):
                                    ps = PS[n % 4]
                                    n += 1
                                    idx = (cc * 2 + dr) * 2 + ax
                                    S.op('pe', lambda g, ps=ps, idx=idx, g0=g0, nn=nn: g.matmul(ps.t[:, 0:nn], lhsT=bd.t[:, idx, :], rhs=xcb.t[:, g0:g0 + nn], start=True, stop=True), reads=[bd, xcb], writes=[ps])
                                    S.op('act', lambda g, ps=ps, dst=dst, bias=bias, g0=g0, nn=nn, cc=cc, dr=dr: g.activation(out=dst.t[:, g0:g0 + nn], in_=ps.t[:, 0:nn], func=AF.Sigmoid, bias=bias.t[:, cc, dr:dr + 1]), reads=[ps, bias], writes=[dst])
                            S.op('act', lambda g, L=L, cc=cc, dr=dr: g.activation(out=ab.t[:, 0:L], in_=rb.t[:, 0:L], func=AF.Exp, scale=m8.t[:, cc, dr:dr + 1]), reads=[rb, m8], writes=[ab])
                            S.op('act', lambda g, L=L, cc=cc, dr=dr: g.activation(out=rb.t[:, 0:L], in_=rb.t[:, 0:L], func=AF.Exp, scale=m16.t[:, cc, dr:dr + 1]), reads=[rb, m16], writes=[rb])
                            S.op('act', lambda g, L=L: g.activation(out=rb.t[:, 0:L], in_=rb.t[:, 0:L], func=AF.Sqrt, scale=-1.0, bias=1.0), reads=[rb], writes=[rb])
                            S.op('dve', lambda g, L=L: g.tensor_tensor(out=ib.t[:, 0:L], in0=ib.t[:, 0:L], in1=xc.t[:, 0:L], op=ALU.mult), reads=[ib, xc], writes=[ib])
                            S.op('dve', lambda g, L=L: g.tensor_tensor(out=ib.t[:, 0:L], in0=ib.t[:, 0:L], in1=rb.t[:, 0:L], op=ALU.mult), reads=[ib, rb], writes=[ib])
                            init = 0.0 if isctx else h0.t[:, cc, dr:dr + 1]
                            ho = hh[dr]
                            if dr == 0:
                                S.op('dve', lambda g, L=L, init=init, ho=ho: g.tensor_tensor_scan(out=ho.t[:, 0:L], data0=ab.t[:, 0:L], data1=ib.t[:, 0:L], initial=init, op0=ALU.mult, op1=ALU.add), reads=[ab, ib, h0], writes=[ho])
                            else:
                                S.op('dve', lambda g, L=L, init=init, ho=ho: g.tensor_tensor_scan(out=ho.t[:, 0:L][:, ::-1], data0=ab.t[:, 0:L][:, ::-1], data1=ib.t[:, 0:L][:, ::-1], initial=init, op0=ALU.mult, op1=ALU.add), reads=[ab, ib, h0], writes=[ho])
                        if isctx:
                            copy_op('dve', h0.t[:, cc, 0:1], hh[0].t[:, L - 1:L], [hh[0], h0], [h0])
                            copy_op('dve', h0.t[:, cc, 1:2], hh[1].t[:, 0:1], [hh[1], h0], [h0])
                        S.dma('sp', xin.t[:, 0:L], prT_d[2 + cc, :, c0:c0 + L], writes=[xin])
                        S.op('act', lambda g, L=L: g.activation(out=xin.t[:, 0:L], in_=xin.t[:, 0:L], func=AF.Gelu), reads=[xin], writes=[xin])
                        S.op('dve', lambda g, L=L: g.tensor_tensor(out=hh[0].t[:, 0:L], in0=hh[0].t[:, 0:L], in1=hh[1].t[:, 0:L], op=ALU.add), reads=[hh[0], hh[1]], writes=[hh[0]])
                        S.op('dve', lambda g, L=L: g.tensor_tensor(out=yo.t[:, 0:L], in0=hh[0].t[:, 0:L], in1=xin.t[:, 0:L], op=ALU.mult), reads=[hh[0], xin], writes=[yo])
                        S.dma('sp', ycT_d[cc, :, c0:c0 + L], yo.t[:, 0:L], reads=[yo])

        def load_w_bf(dst, src_rows_fn, nk, ncols, stg):
            for kc in range(nk):
                st = stg[kc % len(stg)]
                S.dma('sp', st.t[:, 0:ncols], src_rows_fn(kc), writes=[st])
                copy_op(alt(('dve', 'pool')), dst.t[:, kc, :], st.t[:, 0:ncols], [st], [dst])

        def stage_merge(l, first):
            with Stage("merge"):
                stg = [S.sbuf("mstg", [128, 3072], F32)]
                wg = S.sbuf("wg", [128, 8, 3072], BF16)
                wa_ = S.sbuf("wa", [128, 4, D], BF16)
                wb_ = S.sbuf("wb", [128, 2, D], BF16)
                wc_ = S.sbuf("wc", [128, 2, D], BF16)
                wo_ = S.sbuf("wo", [128, 8, D], BF16)
                bg = S.sbuf("bg", [128, 24], F32)
                S.dma('sp', bg.t[:], b_gateT[l], writes=[bg])
                load_w_bf(wg, lambda kc: w_gate[l, kc * 128:(kc + 1) * 128, :], 8, 3072, stg)
                load_w_bf(wa_, lambda kc: w_bra[l, kc * 128:(kc + 1) * 128, :], 4, D, stg)
                load_w_bf(wb_, lambda kc: w_brb[l, kc * 128:(kc + 1) * 128, :], 2, D, stg)
                load_w_bf(wc_, lambda kc: w_brc[l, kc * 128:(kc + 1) * 128, :], 2, D, stg)
                load_w_bf(wo_, lambda kc: w_out[l, kc * 128:(kc + 1) * 128, :], 8, D, stg)
                xg = S.sbuf("xg", [128, 8, 512], BF16)
                ya = S.sbuf("mya", [128, 4, 512], BF16)
                yb = S.sbuf("myb", [128, 2, 512], BF16)
                yc = S.sbuf("myc", [128, 2, 512], BF16)
                mT = S.sbuf("mT", [128, 8, 512], BF16)
                gt = [S.sbuf("gt", [128, 512], BF16) for _ in range(3)]
                t1 = S.sbuf("mt1", [128, 512], F32)
                t2 = S.sbuf("mt2", [128, 512], F32)
                oT = S.sbuf("oT", [128, 8, 512], F32)
                xt = [S.sbuf("mxt", [128, D], F32) for _ in range(2)]
                for (c0, nn) in GROUPS:
                    s = 1 if c0 == 0 else 0
                    for kc in range(8):
                        S.dma('sp', xg.t[:, kc, 0:nn], xnT_d[:, kc, c0:c0 + nn], writes=[xg])
                    for kc in range(4):
                        S.dma('sp', ya.t[:, kc, 0:nn], yaT_d[kc, :, c0:c0 + nn], writes=[ya])
                    for kc in range(2):
                        S.dma('sp', yb.t[:, kc, 0:nn], ybT_d[kc, :, c0:c0 + nn], writes=[yb])
                        S.dma('sp', yc.t[:, kc, 0:nn], ycT_d[kc, :, c0:c0 + nn], writes=[yc])
                    for mc in range(8):
                        brs = ((wa_, ya, 4), (wb_, yb, 2), (wc_, yc, 2))
                        for bi in range(3):
                            pg = PS[bi]
                            for kc in range(8):
                                S.op('pe', lambda g, pg=pg, kc=kc, bi=bi, mc=mc, nn=nn: g.matmul(pg.t[:, 0:nn], lhsT=wg.t[:, kc, bi * D + mc * 128:bi * D + (mc + 1) * 128], rhs=xg.t[:, kc, 0:nn], start=(kc == 0), stop=(kc == 7)), reads=[wg, xg], writes=[pg])
                            S.op('act', lambda g, pg=pg, bi=bi, mc=mc, nn=nn: g.activation(out=gt[bi].t[:, 0:nn], in_=pg.t[:, 0:nn], func=AF.Sigmoid, bias=bg.t[:, bi * 8 + mc:bi * 8 + mc + 1]), reads=[pg, bg], writes=[gt[bi]])
                            pbr = PS[3 + bi]
                            w_, y_, nk = brs[bi]
                            for kc in range(nk):
                                S.op('pe', lambda g, pbr=pbr, kc=kc, w_=w_, y_=y_, nk=nk, mc=mc, nn=nn: g.matmul(pbr.t[:, 0:nn], lhsT=w_.t[:, kc, mc * 128:(mc + 1) * 128], rhs=y_.t[:, kc, 0:nn], start=(kc == 0), stop=(kc == nk - 1)), reads=[w_, y_], writes=[pbr])
                        S.op('dve', lambda g, nn=nn: g.tensor_tensor(out=t1.t[:, 0:nn], in0=PS[3].t[:, 0:nn], in1=gt[0].t[:, 0:nn], op=ALU.mult), reads=[PS[3], gt[0]], writes=[t1])
                        S.op('dve', lambda g, nn=nn: g.tensor_tensor(out=t2.t[:, 0:nn], in0=PS[4].t[:, 0:nn], in1=gt[1].t[:, 0:nn], op=ALU.mult), reads=[PS[4], gt[1]], writes=[t2])
                        S.op('pool', lambda g, nn=nn: g.tensor_tensor(out=t1.t[:, 0:nn], in0=t1.t[:, 0:nn], in1=t2.t[:, 0:nn], op=ALU.add), reads=[t1, t2], writes=[t1])
                        S.op('dve', lambda g, nn=nn: g.tensor_tensor(out=t2.t[:, 0:nn], in0=PS[5].t[:, 0:nn], in1=gt[2].t[:, 0:nn], op=ALU.mult), reads=[PS[5], gt[2]], writes=[t2])
                        S.op('pool', lambda g, nn=nn, mc=mc: g.tensor_tensor(out=mT.t[:, mc, 0:nn], in0=t1.t[:, 0:nn], in1=t2.t[:, 0:nn], op=ALU.add), reads=[t1, t2], writes=[mT])
                    for oc in range(8):
                        po = PS[oc % 2]
                        for mc in range(8):
                            S.op('pe', lambda g, po=po, mc=mc, oc=oc, nn=nn: g.matmul(po.t[:, 0:nn], lhsT=wo_.t[:, mc, oc * 128:(oc + 1) * 128], rhs=mT.t[:, mc, 0:nn], start=(mc == 0), stop=(mc == 7)), reads=[wo_, mT], writes=[po])
                        S.op('act', lambda g, po=po, oc=oc, nn=nn, s=s: g.activation(out=oT.t[:, oc, 0:nn], in_=po.t[:, 0:nn], func=AF.Copy, scale=modT.t[:, 16 + oc, s:s + 1]), reads=[po, modT], writes=[oT])
                    for t4 in range(nn // 128):
                        tok0 = c0 + t4 * 128
                        if c0 == 0:
                            src = (c_in if first else cres)[tok0:tok0 + 128, :]
                            dst = cres[tok0:tok0 + 128, :]
                        else:
                            src = (x_in if first else xres)[tok0 - C:tok0 - C + 128, :]
                            dst = xres[tok0 - C:tok0 - C + 128, :]
                        x_ = xt[t4 % 2]
                        S.dma('sp', x_.t[:], src, writes=[x_])
                        for half in range(2):
                            pt_ = PS[2 + half]
                            for q in range(4):
                                oc = half * 4 + q
                                S.op('pe', lambda g, pt_=pt_, q=q, oc=oc, t4=t4: g.transpose(out=pt_.t[:, q * 128:(q + 1) * 128], in_=oT.t[:, oc, t4 * 128:(t4 + 1) * 128], identity=ident_f.t[:]), reads=[oT, ident_f], writes=[pt_])
                            S.op('dve', lambda g, pt_=pt_, x_=x_, half=half: g.tensor_tensor(out=x_.t[:, half * 512:(half + 1) * 512], in0=pt_.t[:], in1=x_.t[:, half * 512:(half + 1) * 512], op=ALU.add), reads=[pt_, x_], writes=[x_])
                        S.dma('sp', dst, x_.t[:], reads=[x_])

        def top16(src_ap, src_reads, vals, idxs, scr, vparts, iparts):
            S.op('dve', lambda g: g.max(out=vals[:, 0:8], in_=src_ap), reads=src_reads, writes=vparts)
            S.op('dve', lambda g: g.max_index(out=idxs[:, 0:8], in_max=vals[:, 0:8], in_values=src_ap), reads=src_reads + vparts, writes=iparts)
            S.op('dve', lambda g: g.match_replace(out=scr.t[:, 0:src_ap.shape[1]], in_to_replace=vals[:, 0:8], in_values=src_ap, imm_value=-1e30), reads=src_reads + vparts, writes=[scr])
            S.op('dve', lambda g: g.max(out=vals[:, 8:16], in_=scr.t[:, 0:src_ap.shape[1]]), reads=[scr], writes=vparts)
            S.op('dve', lambda g: g.max_index(out=idxs[:, 8:16], in_max=vals[:, 8:16], in_values=scr.t[:, 0:src_ap.shape[1]]), reads=[scr] + vparts, writes=iparts)

        def stage_peer_prep(l):
            with Stage("pprep"):
                R = 4
                uin = [S.sbuf("uin", [128, R, D], F32) for _ in range(2)]
                vin = [S.sbuf("vin", [128, R, D], F32) for _ in range(2)]
                uvo = [S.sbuf("uvo", [128, R, 2 * D], BF16) for _ in range(2)]
                uview = p_u[l].rearrange("(p r) d -> p r d", p=128)
                vview = p_v[l].rearrange("(p r) d -> p r d", p=128)
                oview = uv_d.rearrange("(p r) d -> p r d", p=128)
                nch = 128 // R

                def loads(c):
                    S.dma('sp', uin[c % 2].t[:], uview[:, c * R:(c + 1) * R, :], writes=[uin[c % 2]])
                    S.dma('sp', vin[c % 2].t[:], vview[:, c * R:(c + 1) * R, :], writes=[vin[c % 2]])
                loads(0)
                for c in range(nch):
                    if c + 1 < nch:
                        loads(c + 1)
                    a, b, o = uin[c % 2], vin[c % 2], uvo[c % 2]
                    S.op('dve', lambda g, a=a, o=o: g.tensor_copy(out=o.t[:, :, 0:D], in_=a.t[:]), reads=[a], writes=[o.part(0)])
                    S.op('act', lambda g, b=b, o=o: g.activation(out=o.t[:, :, D:2 * D], in_=b.t[:], func=AF.Copy), reads=[b], writes=[o.part(1)])
                    S.dma('pool', oview[:, c * R:(c + 1) * R, :], o.t[:], reads=[o.part(0), o.part(1)])

        def top16g(src_ap, src_reads, vals, idxs, scr, vparts, iparts):
            n = src_ap.shape[1]
            S.op('dve', lambda g: g.max(out=vals[:, 0:8], in_=src_ap), reads=src_reads, writes=vparts)
            yield
            S.op('dve', lambda g: g.max_index(out=idxs[:, 0:8], in_max=vals[:, 0:8], in_values=src_ap), reads=src_reads + vparts, writes=iparts)
            yield
            S.op('dve', lambda g: g.match_replace(out=scr.t[:, 0:n], in_to_replace=vals[:, 0:8], in_values=src_ap, imm_value=-1e30), reads=src_reads + vparts, writes=[scr])
            yield
            S.op('dve', lambda g: g.max(out=vals[:, 8:16], in_=scr.t[:, 0:n]), reads=[scr], writes=vparts)
            yield
            S.op('dve', lambda g: g.max_index(out=idxs[:, 8:16], in_max=vals[:, 8:16], in_values=scr.t[:, 0:n]), reads=[scr] + vparts, writes=iparts)
            yield

        def stage_peer_q(l):
            with Stage("peerq"):
                stg = [S.sbuf("pstg", [128, 2048], F32) for _ in range(2)]
                wq = S.sbuf("wq", [128, 8, 2048], BF16)
                load_w_bf(wq, lambda kc: p_wq[l, kc * 128:(kc + 1) * 128, :], 8, 2048, stg)
                xgs = [S.sbuf("pxg", [128, 8, 512], BF16) for _ in range(2)]
                qTs = [S.sbuf("pqT", [128, 16, 512], BF16) for _ in range(2)]
                n = 0
                for gi, (c0, nn) in enumerate(GROUPS):
                    xg = xgs[gi % 2]
                    qT = qTs[gi % 2]
                    for kc in range(8):
                        S.dma('sp', xg.t[:, kc, 0:nn], xnT_d[:, kc, c0:c0 + nn], writes=[xg])
                    for hp in range(16):
                        ps = PS[n % 4]
                        n += 1
                        for kc in range(8):
                            S.op('pe', lambda g, ps=ps, kc=kc, hp=hp, nn=nn, xg=xg: g.matmul(ps.t[:, 0:nn], lhsT=wq.t[:, kc, hp * 128:(hp + 1) * 128], rhs=xg.t[:, kc, 0:nn], start=(kc == 0), stop=(kc == 7)), reads=[wq, xg], writes=[ps])
                        copy_op(alt(), qT.t[:, hp, 0:nn], ps.t[:, 0:nn], [ps], [qT])
                    S.dma('pool', qT_d[:, :, c0:c0 + nn], qT.t[:, :, 0:nn], reads=[qT])

        def stage_peer(l):
            with Stage("peer"):
                keysT = S.sbuf("keysT", [128, 16, 128], BF16)
                kst = [S.sbuf("kst", [128, 128], F32) for _ in range(2)]
                for hp in range(16):
                    ks = kst[hp % 2]
                    S.dma('sp', ks.t[:], p_keys[l, hp // 2, hp % 2], writes=[ks])
                    pk = PS[hp % 2]
                    S.op('pe', lambda g, pk=pk, ks=ks: g.transpose(out=pk.t[:, 0:128], in_=ks.t[:], identity=ident_f.t[:]), reads=[ks, ident_f], writes=[pk])
                    copy_op('dve', keysT.t[:, hp, :], pk.t[:, 0:128], [pk], [keysT])
                g5 = [S.sbuf("g5", [128, D], F32) for _ in range(2)]
                for s in range(2):
                    S.dma('sp', g5[s].t[:], bcd[8 + s], writes=[g5[s]])
                qTs = [S.sbuf("pqT", [128, 16, 512], BF16) for _ in range(2)]
                top = S.sbuf("ptop", [128, 16, 16], F32)
                it = S.sbuf("pit", [128, 16, 16], U32)
                itf = S.sbuf("pitf", [128, 16, 16], F32)
                scr = S.sbuf("pscr", [128, 256], F32)
                cand = S.sbuf("pcand", [128, 8, 256], F32)
                eqb = S.sbuf("peq", [128, 8, 256], F32)
                eq = eqb
                best = S.sbuf("pbest", [128, 8, 16], F32)
                pos = S.sbuf("ppos", [128, 8, 16], U32)
                pa = S.sbuf("ppa", [128, 8, 16], U32)
                paf = S.sbuf("ppaf", [128, 2, 8, 16], F32)
                isel = S.sbuf("pisel", [128, 2, 8, 16], F32)
                idxf = S.sbuf("pidxf", [128, 128], F32)
                idxu = [S.sbuf("pidxu", [128, 128], U32) for _ in range(2)]
                gws = [S.sbuf("pgw", [128, 8, 16], F32) for _ in range(2)]
                zs = S.sbuf("pzs", [128, 8], F32)
                dotb = [S.sbuf("pdots", [128, 128], F32) for _ in range(2)]
                actv = [S.sbuf("pact", [128, 128], F32) for _ in range(2)]
                NR = 24
                BT = 4
                uvr = [S.sbuf("uvr", [128, 2 * D], BF16) for _ in range(NR)]
                dg4 = [S.sbuf("pdg4", [128, 4, 128], BF16) for _ in range(3)]
                xn = [S.sbuf("pxn", [128, D], BF16) for _ in range(2)]
                xt = [S.sbuf("pxt", [128, D], F32) for _ in range(2)]
                junk = S.sbuf("pjunk", [128, D], BF16)
                prods = [S.sbuf("pprod", [128, D], BF16) for _ in range(3)]
                junk2 = S.sbuf("pjunk2", [128, D], BF16)
                ytmp = S.sbuf("pytmp", [128, D], F32)
                uvflat = uv_d

                tiles = []
                for gi, (c0, nn) in enumerate(GROUPS):
                    for t4 in range(nn // 128):
                        tiles.append((gi, c0, nn, t4))

                def emit_group_q(gi):
                    c0, nn = GROUPS[gi]
                    qT = qTs[gi % 2]
                    S.dma('sp', qT.t[:, :, 0:nn], qT_d[:, :, c0:c0 + nn], writes=[qT])

                def topk_gen(ti):
                    gi, c0, nn, t4 = tiles[ti]
                    qT = qTs[gi % 2]
                    gw = gws[ti % 2]
                    iu = idxu[ti % 2]
                    for hp in range(16):
                        ps = PS[2 + hp % 2]
                        S.op('pe', lambda g, ps=ps, hp=hp, t4=t4, qT=qT: g.matmul(ps.t[:, 0:128], lhsT=qT.t[:, hp, t4 * 128:(t4 + 1) * 128], rhs=keysT.t[:, hp, :], start=True, stop=True), reads=[qT, keysT], writes=[ps])
                        yield from top16g(ps.t[:, 0:128], [ps], top.t[:, hp, :], it.t[:, hp, :], scr, [top], [it])
                    copy_op('pool', itf.t[:], it.t[:], [it], [itf])
                    tv = top.t[:].rearrange("p (h q) k -> p h q k", q=2)
                    S.op('dve', lambda g, tv=tv: g.tensor_tensor(out=cand.t[:].rearrange("p h (a b) -> p h a b", a=16), in0=tv[:, :, 0, :].unsqueeze(3).broadcast_to([128, 8, 16, 16]), in1=tv[:, :, 1, :].unsqueeze(2).broadcast_to([128, 8, 16, 16]), op=ALU.add), reads=[top], writes=[cand])
                    yield
                    for h in range(8):
                        yield from top16g(cand.t[:, h, :], [cand], best.t[:, h, :], pos.t[:, h, :], scr, [best], [pos])
                    S.op('dve', lambda g: g.tensor_tensor(out=gw.t[:], in0=best.t[:], in1=best.t[:, :, 0:1].broadcast_to([128, 8, 16]), op=ALU.subtract), reads=[best], writes=[gw])
                    yield
                    S.op('act', lambda g: g.activation(out=gw.t[:], in_=gw.t[:], func=AF.Exp), reads=[gw], writes=[gw])
                    S.op('dve', lambda g: g.tensor_reduce(out=zs.t[:], in_=gw.t[:], axis=AX.X, op=ALU.add), reads=[gw], writes=[zs])
                    yield
                    S.op('dve', lambda g: g.reciprocal(out=zs.t[:], in_=zs.t[:]), reads=[zs], writes=[zs])
                    yield
                    S.op('dve', lambda g: g.tensor_tensor(out=gw.t[:], in0=gw.t[:], in1=zs.t[:].unsqueeze(2).broadcast_to([128, 8, 16]), op=ALU.mult), reads=[gw, zs], writes=[gw])
                    yield
                    S.op('dve', lambda g: g.tensor_single_scalar(out=pa.t[:], in_=pos.t[:], scalar=4, op=ALU.logical_shift_right), reads=[pos], writes=[pa])
                    yield
                    copy_op('dve', paf.t[:, 0], pa.t[:], [pa], [paf])
                    yield
                    S.op('dve', lambda g: g.tensor_single_scalar(out=pa.t[:], in_=pos.t[:], scalar=15, op=ALU.bitwise_and), reads=[pos, paf], writes=[pa])
                    yield
                    copy_op('dve', paf.t[:, 1], pa.t[:], [pa], [paf])
                    yield
                    itv = itf.t[:].rearrange("p (h q) k -> p h q k", q=2)
                    eqv = eqb.t[:].rearrange("p h (a b) -> p h a b", a=16)
                    for q in range(2):
                        S.op('dve', lambda g, q=q: g.tensor_tensor(out=eqv, in0=paf.t[:, q].unsqueeze(3).broadcast_to([128, 8, 16, 16]), in1=iota16.t[:].unsqueeze(1).unsqueeze(1).broadcast_to([128, 8, 16, 16]), op=ALU.is_equal), reads=[paf, iota16], writes=[eq])
                        yield
                        S.op('dve', lambda g, q=q, itv=itv: g.tensor_tensor(out=eqv, in0=eqv, in1=itv[:, :, q, :].unsqueeze(2).broadcast_to([128, 8, 16, 16]), op=ALU.mult), reads=[eq, itf], writes=[eq])
                        yield
                        S.op('dve', lambda g, q=q: g.tensor_reduce(out=isel.t[:, q], in_=eqv, axis=AX.X, op=ALU.add), reads=[eq], writes=[isel])
                        yield
                    S.op('dve', lambda g: g.scalar_tensor_tensor(out=idxf.t[:].rearrange("p (h k) -> p h k", h=8), in0=isel.t[:, 0], scalar=128.0, in1=isel.t[:, 1], op0=ALU.mult, op1=ALU.add), reads=[isel], writes=[idxf])
                    yield
                    copy_op('dve', iu.t[:], idxf.t[:], [idxf], [iu])
                    yield

                def finish_batch(b0, av, gw, gwf, py):
                    S.op('dve', lambda g: g.tensor_tensor(out=av.t[:, b0:b0 + BT], in0=av.t[:, b0:b0 + BT], in1=gwf[:, b0:b0 + BT], op=ALU.mult), reads=[av.part(b0 // BT), gw], writes=[av.part(b0 // BT)])
                    dg = dg4[(b0 // BT) % 3]
                    S.op('dve', lambda g: g.tensor_tensor(out=dg.t[:], in0=ident_b.t[:].unsqueeze(1).broadcast_to([128, BT, 128]), in1=av.t[:, b0:b0 + BT].unsqueeze(2).broadcast_to([128, BT, 128]), op=ALU.mult), reads=[ident_b, av.part(b0 // BT)], writes=[dg])
                    for k in range(b0, b0 + BT):
                        rk = uvr[k % NR]
                        for half in range(2):
                            S.op('pe', lambda g, rk=rk, half=half, k=k: g.matmul(py[half].t[:], lhsT=dg.t[:, k - b0, :], rhs=rk.t[:, D + half * 512:D + (half + 1) * 512], start=(k == 0), stop=(k == 127)), reads=[dg, rk], writes=[py[half]])

                emit_group_q(0)
                for _ in topk_gen(0):
                    pass
                py = [PS[4], PS[5]]
                for ti, (gi, c0, nn, t4) in enumerate(tiles):
                    s = 1 if c0 == 0 else 0
                    tok0 = c0 + t4 * 128
                    nxt = None
                    if ti + 1 < len(tiles):
                        if tiles[ti + 1][0] != gi:
                            emit_group_q(gi + 1)
                        nxt = topk_gen(ti + 1)
                    xn_ = xn[ti % 2]
                    x_ = xt[ti % 2]
                    iu = idxu[ti % 2]
                    gw = gws[ti % 2]
                    dots = dotb[ti % 2]
                    av = actv[ti % 2]
                    S.dma('sp', xn_.t[:], xntm_d[tok0:tok0 + 128, :], writes=[xn_])
                    xr = (cres[tok0:tok0 + 128, :] if c0 == 0 else xres[tok0 - C:tok0 - C + 128, :])
                    S.dma('sp', x_.t[:], xr, writes=[x_])
                    gwf = gw.t[:].rearrange("p h k -> p (h k)")
                    for hk in range(128):
                        r_ = uvr[hk % NR]
                        S.dma('pool', None, None, reads=[iu], writes=[r_], fn=lambda g, r_=r_, iu=iu, hk=hk: g.indirect_dma_start(out=r_.t[:], out_offset=None, in_=uvflat, in_offset=bass.IndirectOffsetOnAxis(ap=iu.t[:, hk:hk + 1], axis=0)))
                        if hk % 6 == 5:
                            S.op('dve', lambda g, r_=r_, xn_=xn_, hk=hk, dots=dots: g.scalar_tensor_tensor(out=junk2.t[:], in0=r_.t[:, 0:D], scalar=1.0, in1=xn_.t[:], op0=ALU.mult, op1=ALU.mult, accum_out=dots.t[:, hk:hk + 1]), reads=[r_, xn_], writes=[dots.part(hk // BT)])
                        else:
                            pr_ = prods[hk % 3]
                            S.op('dve', lambda g, r_=r_, xn_=xn_, pr_=pr_: g.tensor_tensor(out=pr_.t[:], in0=r_.t[:, 0:D], in1=xn_.t[:], op=ALU.mult), reads=[r_, xn_], writes=[pr_])
                            S.op('act', lambda g, pr_=pr_, hk=hk, dots=dots: g.activation(out=junk.t[:], in_=pr_.t[:], func=AF.Copy, accum_out=dots.t[:, hk:hk + 1]), reads=[pr_], writes=[dots.part(hk // BT)])
                        if nxt is not None:
                            for _ in range(2):
                                next(nxt, None)
                        if hk % BT == BT - 1:
                            b0 = hk - (BT - 1)
                            if b0 >= BT:
                                finish_batch(b0 - BT, av, gw, gwf, py)
                            else:
                                for _ in range(2):
                                    S.op('act', lambda g: g.activation(out=junk.t[:, 0:8], in_=junk.t[:, 8:16], func=AF.Copy))
                            S.op('act', lambda g, av=av, dots=dots, b0=b0: g.activation(out=av.t[:, b0:b0 + BT], in_=dots.t[:, b0:b0 + BT], func=AF.Gelu), reads=[dots.part(b0 // BT)], writes=[av.part(b0 // BT)])
                    finish_batch(128 - BT, av, gw, gwf, py)
                    if nxt is not None:
                        for _ in nxt:
                            pass
                    for half in range(2):
                        S.op('dve', lambda g, half=half, s=s: g.tensor_tensor(out=ytmp.t[:, half * 512:(half + 1) * 512], in0=py[half].t[:], in1=g5[s].t[:, half * 512:(half + 1) * 512], op=ALU.mult), reads=[py[half], g5[s]], writes=[ytmp])
                    S.op('dve', lambda g, x_=x_: g.tensor_tensor(out=x_.t[:], in0=x_.t[:], in1=ytmp.t[:], op=ALU.add), reads=[x_, ytmp], writes=[x_])
                    S.dma('sp', xr, x_.t[:], reads=[x_])

        def stage_final():
            with Stage("final"):
                gf = S.sbuf("gf", [128, D], F32)
                S.dma('sp', gf.t[:], gfin, writes=[gf])
                xin = [S.sbuf("fx", [128, D], F32) for _ in range(3)]
                junk = S.sbuf("fj", [128, D], BF16)
                yo = [S.sbuf("fy", [128, D], F32) for _ in range(2)]
                ss = S.sbuf("fss", [128, 32], F32)
                t1 = S.sbuf("ft1", [128, 32], F32)
                for tt in range(32):
                    xt = xin[tt % 3]
                    S.dma('sp', xt.t[:], xres[tt * 128:(tt + 1) * 128, :], writes=[xt])
                    S.op('act', lambda g, xt=xt, tt=tt: g.activation(out=junk.t[:], in_=xt.t[:], func=AF.Square, accum_out=ss.t[:, tt:tt + 1]), reads=[xt], writes=[junk, ss.part(tt)])
                    S.op('dve', lambda g, tt=tt: g.tensor_scalar(out=t1.t[:, tt:tt + 1], in0=ss.t[:, tt:tt + 1], scalar1=1.0 / D, scalar2=EPS, op0=ALU.mult, op1=ALU.add), reads=[ss.part(tt)], writes=[t1.part(tt)])
                    S.op('act', lambda g, tt=tt: g.activation(out=t1.t[:, tt:tt + 1], in_=t1.t[:, tt:tt + 1], func=AF.Sqrt), reads=[t1.part(tt)], writes=[t1.part(tt)])
                    S.op('dve', lambda g, tt=tt: g.reciprocal(out=t1.t[:, tt:tt + 1], in_=t1.t[:, tt:tt + 1]), reads=[t1.part(tt)], writes=[t1.part(tt)])
                    y_ = yo[tt % 2]
                    S.op('dve', lambda g, y_=y_, xt=xt, tt=tt: g.scalar_tensor_tensor(out=y_.t[:], in0=xt.t[:], scalar=t1.t[:, tt:tt + 1], in1=gf.t[:], op0=ALU.mult, op1=ALU.mult), reads=[xt, t1.part(tt), gf], writes=[y_])
                    S.dma('sp', y_out[tt * 128:(tt + 1) * 128, :], y_.t[:], reads=[y_])

        S.barrier()
        S.flush()
        plan = []
        for l in range(DEPTH):
            first = (l == 0)
            plan += [("mod", lambda l=l: stage_mod(l)),
                     ("norm1", lambda l=l, first=first: stage_norm(l, 0, first)),
                     ("proj", lambda l=l: stage_proj(l)),
                     ("attn", lambda l=l: stage_attn(l)),
                     ("hyfc", lambda l=l: stage_hy_filt(l, C, zT_c, dec_c, ed_c)),
                     ("hysc_c", lambda l=l: stage_hy_sc(l, C, 0, ub_c)),
                     ("hytc", lambda l=l: stage_hy_toep(l, C, 0, ed_c, ub_c)),
                     ("hyfm", lambda l=l: stage_hy_filt(l, T, zT_m, dec_m, ed_m)),
                     ("hysc_m", lambda l=l: stage_hy_sc(l, T, C, ub_m)),
                     ("hytm", lambda l=l: stage_hy_toep(l, T, C, ed_m, ub_m)),
                     ("rglru", lambda l=l: stage_rglru(l)),
                     ("merge", lambda l=l, first=first: stage_merge(l, first)),
                     ("norm2", lambda l=l: stage_norm(l, 1, False)),
                     ("pprep", lambda l=l: stage_peer_prep(l)),
                     ("peerq", lambda l=l: stage_peer_q(l)),
                     ("peer", lambda l=l: stage_peer(l))]
        plan.append(("final", stage_final))
        for i, (nm, f) in enumerate(plan):
            f()
            if stop_after is not None and i + 1 >= stop_after:
                break
        S.finish()
        build.n_instr = S.n_instr
    return nc


def _chunkT(v, n):
    return np.ascontiguousarray(np.asarray(v, np.float32).reshape(n, 128).T)


def _consts():
    f32 = np.float32
    out = {}
    for nm, L in (("m", T), ("c", C)):
        t = np.linspace(0.0, 1.0, L, dtype=f32)[:, None]
        w = (f32(2.0 * math.pi / L) * np.arange(L, dtype=f32))[:, None]
        bands = np.linspace(1e-4, 15, 16, dtype=f32)[None, :]
        z = np.concatenate([t, np.cos(bands * w), -np.sin(bands * w)], axis=-1).astype(f32)
        deltas = np.linspace(math.log(1e-2) / 1.5, math.log(1e-2) / 0.3, 256, dtype=f32)
        decay = np.exp(-t * np.abs(deltas)[None, :]).astype(f32)
        out["zT_" + nm] = np.ascontiguousarray(z.T)
        out["dec_" + nm] = np.ascontiguousarray(decay.T.reshape(2, 128, L).transpose(1, 0, 2))
    out["ident"] = np.eye(128, dtype=f32)
    out["iota16"] = np.ascontiguousarray(np.broadcast_to(np.arange(16, dtype=f32)[None, :], (128, 16)))
    osel = np.zeros((128, 2, 128), f32)
    osel[:, 0, 0:64] = 1.0
    osel[:, 1, 64:128] = 1.0
    out["onesel"] = osel
    return out


def _rpb_table(rpb):
    wq = np.arange(64)
    start = np.clip(wq - 8, 0, 48)
    wk = np.arange(64)
    colok = (wk[:, None] >= start[None, :]) & (wk[:, None] < start[None, :] + 16)
    dc = np.clip(wk[:, None] - wq[None, :] + 15, 0, 30)
    tab = np.full((DEPTH, 128, 8, 22, 64), NEG, np.float32)
    for half in range(2):
        for ep in range(22):
            e = ep - 3 - half
            if e < 0 or e > 14:
                continue
            dr = 14 - e
            g = rpb[:, :, dr][:, :, dc]
            g = np.where(colok[None, None], g, np.float32(NEG))
            tab[:, half * 64:(half + 1) * 64, :, ep, :] = g.transpose(0, 2, 1, 3)
    return np.ascontiguousarray(tab.reshape(DEPTH, 128, 8, 22 * 64))


_NC_CACHE = {}


def kernel(**inp):
    f32 = np.float32
    g = lambda k: np.asarray(inp[k], f32)
    shared = dict(_consts())
    shared["gmix"] = np.stack([_chunkT(g("norm_mix_g")[l], 8) for l in range(DEPTH)])
    shared["gffn"] = np.stack([_chunkT(g("norm_ffn_g")[l], 8) for l in range(DEPTH)])
    shared["gfin"] = np.ascontiguousarray(np.broadcast_to(g("final_g")[None, :], (128, D)))
    shared["w_ada"] = g("w_ada")
    shared["b_adaT"] = np.stack([_chunkT(g("b_ada")[l], 48) for l in range(DEPTH)])
    shared["w_in"] = g("w_in")
    shared["tab"] = _rpb_table(g("na_rpb"))
    shared["hy_swT"] = np.ascontiguousarray(g("hy_short_w").reshape(DEPTH, 3, 6, 128).transpose(0, 3, 2, 1))
    shared["hy_sbT"] = np.stack([_chunkT(g("hy_short_b")[l], 6) for l in range(DEPTH)])
    shared["hy_f1w"] = g("hy_f1_w")
    shared["hy_f1b"] = g("hy_f1_b")[:, :, None].copy()
    shared["hy_f2w"] = g("hy_f2_w")
    shared["hy_f2b"] = g("hy_f2_b")[:, :, None].copy()
    shared["hy_f3w"] = g("hy_f3_w")
    shared["hy_fq"] = np.ascontiguousarray(g("hy_freq").transpose(0, 2, 1))
    shared["hy_dsk"] = np.stack([_chunkT(g("hy_bias")[l], 2) for l in range(DEPTH)])
    shared["rg_cw"] = np.ascontiguousarray(g("rg_conv_w").reshape(DEPTH, 4, 2, 128).transpose(0, 3, 2, 1))
    shared["rg_cb"] = np.stack([_chunkT(g("rg_conv_b")[l], 2) for l in range(DEPTH)])
    shared["rg_wa"] = g("rg_wa")
    shared["rg_wx"] = g("rg_wx")
    for nm, k in (("rg_baT", "rg_ba"), ("rg_bxT", "rg_bx"), ("rg_lamT", "rg_lambda")):
        shared[nm] = np.ascontiguousarray(g(k).reshape(DEPTH, 2, 2, 128).transpose(0, 3, 2, 1))
    shared["w_gate"] = g("w_gate")
    shared["b_gateT"] = np.stack([_chunkT(g("b_gate")[l], 24) for l in range(DEPTH)])
    shared["w_br_a"] = g("w_br_a")
    shared["w_br_b"] = g("w_br_b")
    shared["w_br_c"] = g("w_br_c")
    shared["w_out"] = g("w_out")
    shared["peer_wq"] = g("peer_wq")
    shared["peer_keys"] = g("peer_keys")
    shared["peer_u"] = g("peer_u")
    shared["peer_v"] = g("peer_v")
    x = g("x")
    ctx = g("ctx")
    c = g("c")
    cc = g("c_ctx")
    nb = x.shape[0]
    in_maps = []
    for b in range(nb):
        m = dict(shared)
        m["x"] = np.ascontiguousarray(x[b])
        m["ctx"] = np.ascontiguousarray(ctx[b])
        m["cvec"] = np.ascontiguousarray(np.stack([_chunkT(c[b], 8), _chunkT(cc, 8)], axis=-1))
        in_maps.append(m)
    if "nc" not in _NC_CACHE:
        _NC_CACHE["nc"] = build()
    nc = _NC_CACHE["nc"]
    res = run_bass_kernel_spmd(nc, in_maps, core_ids=list(range(nb)))
    return np.stack([np.asarray(r["y"], f32) for r in res.results], axis=0)
```

```python
import numpy as np
import concourse.bass as bass
import concourse.mybir as mybir
from concourse.bass_utils import run_bass_kernel_spmd
from contextlib import ExitStack

F32 = mybir.dt.float32
BF16 = mybir.dt.bfloat16
I32 = mybir.dt.int32
U32 = mybir.dt.uint32
U16 = mybir.dt.uint16
AF = mybir.ActivationFunctionType
ALU = mybir.AluOpType
AX = mybir.AxisListType

ENGS = ['pe', 'dve', 'act', 'pool', 'sp']
EPOCH = 30000
N_DMA_SEMS = 8


class Res:
    __slots__ = ('w', 'r')

    def __init__(self):
        self.w = {}
        self.r = {}


class Buf:
    def __init__(self, t):
        self.t = t
        self.res = Res()
        self.parts = {}

    def part(self, key):
        r = self.parts.get(key)
        if r is None:
            r = Res()
            self.parts[key] = r
        return r


def _res(x):
    return x.res if isinstance(x, Buf) else x


class Sched:
    def __init__(self, nc, es):
        self.nc = nc
        self.es = es
        self.ops = {e: [] for e in ENGS}
        self.sem = {}
        self.cnt = {}
        self.known = {e: {} for e in ENGS}
        self.nsem = 0
        for e in ENGS:
            self._new_epoch(e)
        self.dma_sems = [self._mksem("dq%d" % i) for i in range(N_DMA_SEMS)]
        self.dma_cnt = [0] * N_DMA_SEMS
        self.dma_rr = 0
        self.all_sems = {}
        self.n_instr = 0
        self.stage_es = None
        self.load_sems = []
        self.load_cnt = []
        self.buf_sem = {}
        self.next_load = 0

    def _mksem(self, name):
        self.nsem += 1
        return self.es.enter_context(self.nc.semaphore("%s_%d" % (name, self.nsem)))

    def _new_epoch(self, e):
        self.sem[e] = self._mksem("e" + e)
        self.cnt[e] = 0

    def sbuf(self, name, shape, dtype):
        self.nbuf = getattr(self, 'nbuf', 0) + 1
        es = self.stage_es if self.stage_es is not None else self.es
        return Buf(es.enter_context(self.nc.sbuf_tensor("%s_%d" % (name, self.nbuf), shape, dtype)))

    def psum(self, name, shape, dtype):
        return Buf(self.es.enter_context(self.nc.psum_tensor(name, shape, dtype)))

    def barrier(self):
        for e in ENGS:
            kn = self.known[e]
            for sid, (sem, val) in self.all_sems.items():
                if kn.get(sid, 0) < val:
                    kn[sid] = val
                    self.ops[e].append(('wait', sem, val))
            for o in ENGS:
                if o == e or o == 'sp' or self.cnt[o] == 0:
                    continue
                sem = self.sem[o]
                if kn.get(id(sem), 0) < self.cnt[o]:
                    kn[id(sem)] = self.cnt[o]
                    self.ops[e].append(('wait', sem, self.cnt[o]))

    def dram(self, name, shape, dtype):
        return Buf(self.nc.dram_tensor(name, shape, dtype, kind="Internal"))

    def _collect(self, e, reads, writes):
        need = {}

        def mrg(d):
            for s, v in d.items():
                if need.get(s, (None, 0))[1] < v[1]:
                    need[s] = v
        for r in reads:
            mrg(_res(r).w)
        for w in writes:
            w = _res(w)
            mrg(w.w)
            mrg(w.r)
        kn = self.known[e]
        for sid, (sem, val, eng) in need.items():
            if eng == 'pe' and e == 'pe':
                continue
            if kn.get(sid, 0) < val:
                kn[sid] = val
                self.ops[e].append(('wait', sem, val))

    def _publish(self, key, reads, writes):
        sid = id(key[0])
        for w in writes:
            w = _res(w)
            w.w = {sid: key}
            w.r = {}
        for r in reads:
            _res(r).r[sid] = key

    def op(self, e, fn, reads=(), writes=()):
        self._collect(e, reads, writes)
        if self.cnt[e] >= EPOCH:
            self._new_epoch(e)
        self.cnt[e] += 1
        sem = self.sem[e]
        self.ops[e].append(('op', fn, sem))
        self._publish((sem, self.cnt[e], e), reads, writes)
        self.n_instr += 1

    def dma(self, e, out, in_, reads=(), writes=(), fn=None, **kw):
        self._collect(e, reads, writes)
        if writes:
            key = id(_res(writes[0]))
            i = self.buf_sem.get(key)
            if i is None:
                i = self.next_load
                self.next_load += 1
                if i >= len(self.load_sems):
                    self.load_sems.append(self._mksem("dl"))
                    self.load_cnt.append(0)
                self.buf_sem[key] = i
            self.load_cnt[i] += 16
            if self.load_cnt[i] >= EPOCH:
                self.load_sems[i] = self._mksem("dl")
                self.load_cnt[i] = 16
            sem, cnt = self.load_sems[i], self.load_cnt[i]
        else:
            i = self.dma_rr
            self.dma_rr = (i + 1) % N_DMA_SEMS
            self.dma_cnt[i] += 16
            if self.dma_cnt[i] >= EPOCH:
                self.dma_sems[i] = self._mksem("dq")
                self.dma_cnt[i] = 16
            sem, cnt = self.dma_sems[i], self.dma_cnt[i]
        if fn is None:
            def fn(g, out=out, in_=in_, kw=kw):
                return g.dma_start(out=out, in_=in_, **kw)
        self.ops[e].append(('dma', fn, sem))
        self._publish((sem, cnt, 'dma'), reads, writes)
        self.all_sems[id(sem)] = (sem, cnt)
        self.n_instr += 1

    def finish(self):
        sp = self.ops['sp']
        for sid, (sem, val) in self.all_sems.items():
            sp.append(('wait', sem, val))
        for e in ENGS:
            if e != 'sp' and self.cnt[e] > 0:
                sp.append(('wait', self.sem[e], self.cnt[e]))
        self.flush()

    def flush(self):
        nc = self.nc
        ops = self.ops
        self.buf_sem = {}
        self.next_load = 0
        self.ops = {e: [] for e in ENGS}

        def emit(eng, lst):
            for it in lst:
                if it[0] == 'wait':
                    eng.wait_ge(it[1], it[2])
                elif it[0] == 'op':
                    it[1](eng).then_inc(it[2], 1)
                else:
                    it[1](eng).then_inc(it[2], 16)

        with nc.Block() as block:
            @block.tensor
            def _(g):
                emit(g, ops['pe'])

            @block.vector
            def _(g):
                emit(g, ops['dve'])

            @block.scalar
            def _(g):
                emit(g, ops['act'])

            @block.gpsimd
            def _(g):
                emit(g, ops['pool'])

            @block.sync
            def _(g):
                emit(g, ops['sp'])

import math

D = 1024
T = 4096
C = 256
NTOK = T + C
DEPTH = 2
EPS = 1e-6
GW = 64
NEG = -30000.0
DBG = False
STAGES = None


def _groups():
    g = [(0, C)]
    for i in range(T // 512):
        g.append((C + 512 * i, 512))
    return g


def build(dbg=False, stop_after=None):
    nc = bass.Bass("TRN2", target_bir_lowering=False)

    def din(name, shape, dt=F32):
        return nc.dram_tensor(name, list(shape), dt, kind="ExternalInput").ap()

    kind_s = "ExternalOutput" if dbg else "Internal"

    def dscr(name, shape, dt):
        return nc.dram_tensor(name, list(shape), dt, kind=kind_s).ap()

    x_in = din("x", [T, D])
    c_in = din("ctx", [C, D])
    cvec = din("cvec", [128, 8, 2])
    gmix = din("gmix", [DEPTH, 128, 8])
    gffn = din("gffn", [DEPTH, 128, 8])
    gfin = din("gfin", [128, D])
    w_ada = din("w_ada", [DEPTH, D, 6 * D])
    b_adaT = din("b_adaT", [DEPTH, 128, 48])
    w_in = din("w_in", [DEPTH, D, 2816])
    tab_in = din("tab", [DEPTH, 128, 8, 22 * 64])
    hy_swT = din("hy_swT", [DEPTH, 128, 6, 3])
    hy_sbT = din("hy_sbT", [DEPTH, 128, 6])
    hy_f1w = din("hy_f1w", [DEPTH, 33, 64])
    hy_f1b = din("hy_f1b", [DEPTH, 64, 1])
    hy_f2w = din("hy_f2w", [DEPTH, 64, 64])
    hy_f2b = din("hy_f2b", [DEPTH, 64, 1])
    hy_f3w = din("hy_f3w", [DEPTH, 64, 512])
    hy_fq = din("hy_fq", [DEPTH, 64, 2])
    hy_dsk = din("hy_dsk", [DEPTH, 128, 2])
    zT_m = din("zT_m", [33, T])
    zT_c = din("zT_c", [33, C])
    dec_m = din("dec_m", [128, 2, T])
    dec_c = din("dec_c", [128, 2, C])
    rg_cw = din("rg_cw", [DEPTH, 128, 2, 4])
    rg_cb = din("rg_cb", [DEPTH, 128, 2])
    rg_wa = din("rg_wa", [DEPTH, 2, 4, 64, 64])
    rg_wx = din("rg_wx", [DEPTH, 2, 4, 64, 64])
    rg_baT = din("rg_baT", [DEPTH, 128, 2, 2])
    rg_bxT = din("rg_bxT", [DEPTH, 128, 2, 2])
    rg_lamT = din("rg_lamT", [DEPTH, 128, 2, 2])
    w_gate = din("w_gate", [DEPTH, D, 3 * D])
    b_gateT = din("b_gateT", [DEPTH, 128, 24])
    w_bra = din("w_br_a", [DEPTH, 512, D])
    w_brb = din("w_br_b", [DEPTH, 256, D])
    w_brc = din("w_br_c", [DEPTH, 256, D])
    w_out = din("w_out", [DEPTH, D, D])
    p_wq = din("peer_wq", [DEPTH, D, 2048])
    p_keys = din("peer_keys", [DEPTH, 8, 2, 128, 128])
    p_u = din("peer_u", [DEPTH, 16384, D])
    p_v = din("peer_v", [DEPTH, 16384, D])
    ident_in = din("ident", [128, 128])
    iota_in = din("iota16", [128, 16])
    osel_in = din("onesel", [128, 2, 128])
    y_out = nc.dram_tensor("y", [T, D], F32, kind="ExternalOutput").ap()

    xres = dscr("xres", [T, D], F32)
    cres = dscr("cres", [C, D], F32)
    bcd = dscr("bcd", [10, 128, D], F32)
    xnT_d = dscr("xnT_d", [128, 8, NTOK], BF16)
    xntm_d = dscr("xntm_d", [NTOK, D], BF16)
    qkT_d = dscr("qkT_d", [8, 128, NTOK], BF16)
    vp_d = dscr("vp_d", [4, NTOK, 256], BF16)
    phT_d = dscr("phT_d", [6, 128, NTOK], F32)
    prT_d = dscr("prT_d", [4, 128, NTOK], F32)
    yaT_d = dscr("yaT_d", [4, 128, NTOK], BF16)
    ybT_d = dscr("ybT_d", [2, 128, NTOK], BF16)
    ycT_d = dscr("ycT_d", [2, 128, NTOK], BF16)
    ed_m = dscr("ed_m", [256, 2 * T], BF16)
    ed_c = dscr("ed_c", [256, 2 * C], BF16)
    ub_m = dscr("ub_m", [128, 2 * 128 * 32], BF16)
    ub_c = dscr("ub_c", [128, 2 * 128 * 2], BF16)
    x0_d = dscr("x0_d", [2, 128, NTOK], BF16)
    uv_d = nc.dram_tensor("uv_d", [16384, 2 * D], BF16, kind="Internal").ap()
    qT_d = nc.dram_tensor("qT_d", [128, 16, NTOK], BF16, kind="Internal").ap()
    dbg_h = dscr("dbg_h", [2, 64, T], F32) if dbg else None

    ges = ExitStack()
    with ges:
        S = Sched(nc, ges)
        PS = [S.psum("ps%d" % i, [128, 512], F32) for i in range(6)]
        PSB = [S.psum("psb%d" % i, [128, 1024], BF16) for i in range(2)]
        ident_f = S.sbuf("ident_f", [128, 128], F32)
        ident_b = S.sbuf("ident_b", [128, 128], BF16)
        ones_f = S.sbuf("ones_f", [128, 128], F32)
        iota16 = S.sbuf("iota16", [128, 16], F32)
        osel = S.sbuf("osel", [128, 2, 128], BF16)
        osel_f = S.sbuf("osel_f", [128, 2, 128], F32)
        modT = S.sbuf("modT", [128, 48, 2], F32)
        Gs = S.sbuf("Gs", [128, 2, 8, 2], F32)
        S.dma('sp', ident_f.t[:], ident_in, writes=[ident_f])
        S.dma('sp', iota16.t[:], iota_in, writes=[iota16])
        S.dma('sp', osel_f.t[:], osel_in, writes=[osel_f])
        S.op('dve', lambda g: g.tensor_copy(out=ident_b.t[:], in_=ident_f.t[:]), reads=[ident_f], writes=[ident_b])
        S.op('dve', lambda g: g.tensor_copy(out=osel.t[:], in_=osel_f.t[:]), reads=[osel_f], writes=[osel])
        S.op('dve', lambda g: g.memset(ones_f.t[:], 1.0), writes=[ones_f])
        rr = {'n': 0}

        def alt(engs=('dve', 'act')):
            rr['n'] += 1
            return engs[rr['n'] % len(engs)]

        def copy_op(e, out, in_, reads, writes):
            if e == 'act':
                S.op('act', lambda g: g.activation(out=out, in_=in_, func=AF.Copy), reads=reads, writes=writes)
            else:
                S.op(e, lambda g: g.tensor_copy(out=out, in_=in_), reads=reads, writes=writes)

        class Stage:
            def __init__(self, name):
                self.name = name

            def __enter__(self):
                self.es = ExitStack()
                self.es.__enter__()
                S.stage_es = self.es
                return self

            def __exit__(self, *a):
                S.barrier()
                S.flush()
                S.stage_es = None
                self.es.__exit__(None, None, None)
                return False

        GROUPS = _groups()

        def stage_mod(l):
            with Stage("mod"):
                sc = S.sbuf("sc", [128, 8, 2], F32)
                ba = S.sbuf("ba", [128, 48], F32)
                gm = S.sbuf("gm", [128, 2, 8], F32)
                wb = [S.sbuf("wada", [128, 6144], F32) for _ in range(2)]
                diag = [S.sbuf("diag", [128, 128], F32) for _ in range(2)]
                bct = [S.sbuf("bct", [128, D], F32) for _ in range(2)]
                S.dma('sp', sc.t[:], cvec, writes=[sc])
                S.dma('sp', ba.t[:], b_adaT[l], writes=[ba])
                S.dma('sp', gm.t[:, 0, :], gmix[l], writes=[gm])
                S.dma('sp', gm.t[:, 1, :], gffn[l], writes=[gm])
                S.op('act', lambda g: g.activation(out=sc.t[:], in_=sc.t[:], func=AF.Silu), reads=[sc], writes=[sc])
                ps = PS[0]
                S.op('dve', lambda g: g.memset(ps.t[:, 0:96], 0.0), writes=[ps])
                for kc in range(8):
                    w = wb[kc % 2]
                    for hh in range(2):
                        S.dma('sp', w.t[:, hh * 3072:(hh + 1) * 3072], w_ada[l, kc * 128:(kc + 1) * 128, hh * 3072:(hh + 1) * 3072], writes=[w])
                    for j in range(48):
                        S.op('pe', lambda g, w=w, j=j, kc=kc: g.matmul(ps.t[:, 2 * j:2 * j + 2], lhsT=w.t[:, j * 128:(j + 1) * 128], rhs=sc.t[:, kc, :], start=False, stop=False, skip_group_check=True), reads=[w, sc], writes=[ps])
                S.op('dve', lambda g: g.tensor_tensor(out=modT.t[:], in0=ps.t[:, 0:96].rearrange("p (j s) -> p j s", s=2), in1=ba.t[:].unsqueeze(2).broadcast_to([128, 48, 2]), op=ALU.add), reads=[ps, ba], writes=[modT])
                for wh, off in ((0, 8), (1, 32)):
                    S.op('dve', lambda g, wh=wh, off=off: g.tensor_scalar(out=Gs.t[:, wh], in0=modT.t[:, off:off + 8, :], scalar1=1.0, scalar2=None, op0=ALU.add), reads=[modT], writes=[Gs])
                    S.op('dve', lambda g, wh=wh: g.tensor_tensor(out=Gs.t[:, wh], in0=Gs.t[:, wh], in1=gm.t[:, wh, :].unsqueeze(2).broadcast_to([128, 8, 2]), op=ALU.mult), reads=[Gs, gm], writes=[Gs])
                def srcs(idx):
                    if idx < 8:
                        wh, kind, s = idx // 4, (idx // 2) % 2, idx % 2
                        if kind == 0:
                            return lambda kc: Gs.t[:, wh, kc, s:s + 1]
                        off = 0 if wh == 0 else 24
                        return lambda kc: modT.t[:, off + kc, s:s + 1]
                    s = idx - 8
                    return lambda kc: modT.t[:, 40 + kc, s:s + 1]
                n = 0
                for idx in range(10):
                    f = srcs(idx)
                    bt = bct[idx % 2]
                    for half in range(2):
                        pb = PS[1 + (n % 2)]
                        n += 1
                        for k4 in range(4):
                            kc = half * 4 + k4
                            dg = diag[kc % 2]
                            S.op('dve', lambda g, dg=dg, f=f, kc=kc: g.tensor_scalar(out=dg.t[:], in0=ident_f.t[:], scalar1=f(kc), scalar2=None, op0=ALU.mult), reads=[ident_f, Gs, modT], writes=[dg])
                            S.op('pe', lambda g, pb=pb, dg=dg, k4=k4: g.matmul(pb.t[:, k4 * 128:(k4 + 1) * 128], lhsT=ones_f.t[:], rhs=dg.t[:], start=True, stop=True, skip_group_check=True), reads=[ones_f, dg], writes=[pb])
                        copy_op('act', bt.t[:, half * 512:(half + 1) * 512], pb.t[:], [pb], [bt])
                    S.dma('sp', bcd[idx], bt.t[:], reads=[bt])

        def stage_norm(l, wh, first):
            with Stage("norm"):
                Gb = [S.sbuf("Gb", [128, D], F32) for _ in range(2)]
                Sb = [S.sbuf("Sb", [128, D], F32) for _ in range(2)]
                for s in range(2):
                    S.dma('sp', Gb[s].t[:], bcd[wh * 4 + 0 + s], writes=[Gb[s]])
                    S.dma('sp', Sb[s].t[:], bcd[wh * 4 + 2 + s], writes=[Sb[s]])
                xnT = S.sbuf("xnT", [128, 8, NTOK], BF16)
                xin = [S.sbuf("xin", [128, D], F32) for _ in range(3)]
                junk = S.sbuf("junk", [128, D], BF16)
                tmp = [S.sbuf("tmpn", [128, D], F32) for _ in range(2)]
                xnb = [S.sbuf("xnb", [128, D], BF16) for _ in range(2)]
                ss = S.sbuf("ss", [128, 34], F32)
                t1 = S.sbuf("t1", [128, 34], F32)
                rstd = S.sbuf("rstd", [128, 34], F32)
                def ph_a(tt):
                    if tt < 2:
                        src = (c_in if first else cres)[tt * 128:(tt + 1) * 128, :]
                    else:
                        src = (x_in if first else xres)[(tt - 2) * 128:(tt - 1) * 128, :]
                    xt = xin[tt % 3]
                    S.dma('sp', xt.t[:], src, writes=[xt])
                    S.op('act', lambda g: g.activation(out=junk.t[:], in_=xt.t[:], func=AF.Square, accum_out=ss.t[:, tt:tt + 1]), reads=[xt], writes=[ss.part(tt)])
                    S.op('dve', lambda g: g.tensor_scalar(out=t1.t[:, tt:tt + 1], in0=ss.t[:, tt:tt + 1], scalar1=1.0 / D, scalar2=EPS, op0=ALU.mult, op1=ALU.add), reads=[ss.part(tt)], writes=[t1.part(tt)])
                    S.op('act', lambda g: g.activation(out=t1.t[:, tt:tt + 1], in_=t1.t[:, tt:tt + 1], func=AF.Sqrt), reads=[t1.part(tt)], writes=[t1.part(tt)])
                    S.op('dve', lambda g: g.reciprocal(out=rstd.t[:, tt:tt + 1], in_=t1.t[:, tt:tt + 1]), reads=[t1.part(tt)], writes=[rstd.part(tt)])

                def ph_b(tt):
                    s = 1 if tt < 2 else 0
                    xt = xin[tt % 3]
                    tm = tmp[tt % 2]
                    xb = xnb[tt % 2]
                    S.op('dve', lambda g: g.scalar_tensor_tensor(out=tm.t[:], in0=xt.t[:], scalar=rstd.t[:, tt:tt + 1], in1=Gb[s].t[:], op0=ALU.mult, op1=ALU.mult), reads=[xt, rstd.part(tt), Gb[s]], writes=[tm])
                    S.op('pool', lambda g: g.tensor_tensor(out=xb.t[:], in0=tm.t[:], in1=Sb[s].t[:], op=ALU.add), reads=[tm, Sb[s]], writes=[xb])
                    S.dma('pool', xntm_d[tt * 128:(tt + 1) * 128, :], xb.t[:], reads=[xb])

                def ph_c(tt):
                    xb = xnb[tt % 2]
                    pb = PSB[tt % 2]
                    for kc in range(8):
                        S.op('pe', lambda g, kc=kc: g.transpose(out=pb.t[:, kc * 128:(kc + 1) * 128], in_=xb.t[:, kc * 128:(kc + 1) * 128], identity=ident_b.t[:]), reads=[xb, ident_b], writes=[pb])
                    copy_op('act', xnT.t[:, :, tt * 128:(tt + 1) * 128], pb.t[:].rearrange("p (k t) -> p k t", k=8), [pb], [xnT.part(tt)])

                for step in range(34 + 2):
                    if step < 34:
                        ph_a(step)
                    if 0 <= step - 1 < 34:
                        ph_b(step - 1)
                    if 0 <= step - 2 < 34:
                        ph_c(step - 2)
                for kc in range(8):
                    S.dma('sp', xnT_d[:, kc, :], xnT.t[:, kc, :], reads=[xnT.part(tt) for tt in range(34)])

        def stage_proj(l):
            with Stage("proj"):
                xnT = S.sbuf("xnT", [128, 8, NTOK], BF16)
                for kc in range(8):
                    S.dma('sp', xnT.t[:, kc, :], xnT_d[:, kc, :], writes=[xnT])
                wst = [S.sbuf("wst", [128, 2816], F32) for _ in range(2)]
                wbf = S.sbuf("wbf", [128, 8, 2816], BF16)
                for kc in range(8):
                    ws = wst[kc % 2]
                    S.dma('sp', ws.t[:], w_in[l, kc * 128:(kc + 1) * 128, :], writes=[ws])
                    copy_op(alt(('dve', 'pool')), wbf.t[:, kc, :], ws.t[:], [ws], [wbf.part(kc)])
                wparts = [wbf.part(kc) for kc in range(8)]
                ob = [S.sbuf("ob", [128, 512], BF16) for _ in range(3)]
                of = [S.sbuf("of", [128, 512], F32) for _ in range(3)]
                vpt = [S.sbuf("vpt", [128, 8, 128], BF16) for _ in range(2)]
                for v in vpt:
                    S.op('pool', lambda g, v=v: g.memset(v.t[:], 0.0), writes=[v])
                n = 0
                for (c0, nn) in GROUPS:
                    for ch in list(range(8)) + list(range(12, 22)):
                        ps = PS[n % 4]
                        for kc in range(8):
                            S.op('pe', lambda g, ps=ps, kc=kc, ch=ch, c0=c0, nn=nn: g.matmul(ps.t[:, 0:nn], lhsT=wbf.t[:, kc, ch * 128:(ch + 1) * 128], rhs=xnT.t[:, kc, c0:c0 + nn], start=(kc == 0), stop=(kc == 7)), reads=[xnT] + wparts, writes=[ps])
                        if ch < 8:
                            o = ob[n % 3]
                            if ch < 4:
                                S.op('act', lambda g, o=o, ps=ps, nn=nn: g.activation(out=o.t[:, 0:nn], in_=ps.t[:, 0:nn], func=AF.Copy, scale=0.125), reads=[ps], writes=[o])
                            else:
                                copy_op('dve', o.t[:, 0:nn], ps.t[:, 0:nn], [ps], [o])
                            S.dma('sp', qkT_d[ch, :, c0:c0 + nn], o.t[:, 0:nn], reads=[o])
                        else:
                            o = of[n % 3]
                            copy_op(alt(), o.t[:, 0:nn], ps.t[:, 0:nn], [ps], [o])
                            dst = phT_d[ch - 12] if ch < 18 else prT_d[ch - 18]
                            S.dma('sp', dst[:, c0:c0 + nn], o.t[:, 0:nn], reads=[o])
                        n += 1
                    for t4 in range(nn // 128):
                        tc0 = c0 + t4 * 128
                        ps = PS[4 + (n % 2)]
                        vt = vpt[n % 2]
                        n += 1
                        for kc in range(8):
                            S.op('pe', lambda g, ps=ps, kc=kc, tc0=tc0: g.matmul(ps.t[:], lhsT=xnT.t[:, kc, tc0:tc0 + 128], rhs=wbf.t[:, kc, 1024:1536], start=(kc == 0), stop=(kc == 7)), reads=[xnT] + wparts, writes=[ps])
                        S.op('dve', lambda g, ps=ps, vt=vt: g.tensor_copy(out=bass.AP(vt.t[:].tensor, vt.t[:].offset, [[1024, 128], [256, 4], [192, 2], [1, 64]]), in_=ps.t[:].rearrange("p (j e d) -> p j e d", j=4, e=2)), reads=[ps], writes=[vt])
                        for j in range(4):
                            S.dma('sp', vp_d[j, tc0:tc0 + 128, :], vt.t[:, 2 * j:2 * j + 2, :].rearrange("p h c -> p (h c)"), reads=[vt])

        def attn_ranges(i):
            r0 = 8 * i
            rows = list(range(r0, r0 + 8))
            rs = lambda r: min(max(r - 4, 0), GW - 8)
            amin = min(rs(r) for r in rows)
            amax = max(rs(r) for r in rows) + 7
            a0s = list(range(amin - (amin % 2), amax + 1, 2))
            out = []
            for a0 in a0s:
                hal = []
                for a in (a0, a0 + 1):
                    v = [r for r in rows if rs(r) <= a <= rs(r) + 7] if a < GW else []
                    hal.append((v[0] - r0, v[-1] - r0 + 1) if v else None)
                lo = min(h[0] for h in hal if h)
                hi = max(h[1] for h in hal if h)
                out.append((a0, hal, (lo, hi)))
            return out

        def stage_attn(l):
            with Stage("attn"):
                tab = S.sbuf("tab", [128, 8, 22 * 64], BF16)
                tst = [S.sbuf("tst", [128, 22 * 64], F32) for _ in range(2)]
                for h in range(8):
                    S.dma('sp', tst[h % 2].t[:], tab_in[l, :, h, :], writes=[tst[h % 2]])
                    copy_op(alt(('dve', 'pool')), tab.t[:, h, :], tst[h % 2].t[:], [tst[h % 2]], [tab.part(h)])
                qTs = [S.sbuf("qTs", [128, NTOK], BF16) for _ in range(2)]
                kTs = [S.sbuf("kTs", [128, NTOK], BF16) for _ in range(2)]
                vps = [S.sbuf("vps", [128, 34, 256], BF16) for _ in range(2)]
                pts = [S.sbuf("pt", [128, 512], BF16) for _ in range(3)]
                rec = [S.sbuf("rec", [128, 512], F32) for _ in range(2)]
                yab = [S.sbuf("yab", [128, 512], BF16) for _ in range(2)]
                n = {'s': 0, 'o': 0}
                for j in range(4):
                    qT, kT, vp = qTs[j % 2], kTs[j % 2], vps[j % 2]
                    S.dma('sp', qT.t[:], qkT_d[j], writes=[qT])
                    S.dma('sp', kT.t[:], qkT_d[4 + j], writes=[kT])
                    for q4 in range(2):
                        S.dma('sp', vp.t[:, q4 * 17:(q4 + 1) * 17, :], vp_d[j, q4 * 17 * 128:(q4 + 1) * 17 * 128, :].rearrange("(t p) c -> p t c", p=128), writes=[vp])
                    jobs = []
                    for i in range(-1, 8):
                        if i < 0:
                            qc0, nq = 0, C
                            klist = [('c', 0, None), ('c', 1, None)]
                        else:
                            qc0, nq = C + 512 * i, 512
                            klist = [('c', 0, None), ('c', 1, None)] + [('l', a0, (hal, un)) for (a0, hal, un) in attn_ranges(i)]
                        Dn = PS[5] if (n['o'] % 2) else PS[4]
                        O = PS[2] if (n['o'] % 2) else PS[3]
                        rc = rec[n['o'] % 2]
                        yb_ = yab[n['o'] % 2]
                        n['o'] += 1
                        grp = dict(i=i, qc0=qc0, nq=nq, O=O, Dn=Dn, rc=rc, yb=yb_)
                        cnt = 0
                        tot = 2 * len(klist)
                        for e in range(2):
                            for (kind, a0, info) in klist:
                                jobs.append(dict(g=grp, e=e, kind=kind, a0=a0, info=info, first=(cnt == 0), last=(cnt == tot - 1)))
                                cnt += 1

                    def emit_s(jb, jj=j):
                        g_ = jb['g']
                        i, qc0, nq = g_['i'], g_['qc0'], g_['nq']
                        e = jb['e']
                        h = 2 * jj + e
                        pb = e * 64
                        Sp = PS[n['s'] % 2]
                        pt = pts[n['s'] % 3]
                        n['s'] += 1
                        jb['pt'] = pt
                        a0 = jb['a0']
                        if jb['kind'] == 'c':
                            kc0 = a0 * 128
                            lo, hi = 0, nq
                            S.op('pe', lambda g, Sp=Sp, pb=pb, kc0=kc0, qc0=qc0, nq=nq, kT=kT, qT=qT: g.matmul(Sp.t[:, 0:nq], lhsT=kT.t[pb:pb + 64, kc0:kc0 + 128], rhs=qT.t[pb:pb + 64, qc0:qc0 + nq], start=True, stop=True), reads=[qT, kT], writes=[Sp])
                            S.op('act', lambda g, Sp=Sp, pt=pt, nq=nq: g.activation(out=pt.t[:, 0:nq], in_=Sp.t[:, 0:nq], func=AF.Exp), reads=[Sp], writes=[pt])
                            jb['vtile'] = a0
                        else:
                            hal, (ulo, uhi) = jb['info']
                            kc0 = C + a0 * GW
                            lo, hi = ulo * GW, uhi * GW
                            r0 = 8 * i
                            e0 = (r0 + ulo) - a0 + 10
                            assert 0 <= e0 and e0 + (uhi - ulo) <= 22, (i, a0, e0)
                            S.op('pe', lambda g, Sp=Sp, pb=pb, kc0=kc0, qc0=qc0, lo=lo, hi=hi, kT=kT, qT=qT: g.matmul(Sp.t[:, lo:hi], lhsT=kT.t[pb:pb + 64, kc0:kc0 + 128], rhs=qT.t[pb:pb + 64, qc0 + lo:qc0 + hi], start=True, stop=False), reads=[qT, kT], writes=[Sp])
                            S.op('pe', lambda g, Sp=Sp, lo=lo, hi=hi, h=h, e0=e0: g.matmul(Sp.t[:, lo:hi], lhsT=ident_b.t[:], rhs=tab.t[:, h, e0 * 64:e0 * 64 + (hi - lo)], start=False, stop=True), reads=[tab.part(h), ident_b], writes=[Sp])
                            for hf_, rng in enumerate(hal):
                                p0 = hf_ * 64
                                if rng is None:
                                    S.op('pool', lambda g, pt=pt, p0=p0, lo=lo, hi=hi: g.memset(pt.t[p0:p0 + 64, lo:hi], 0.0), writes=[pt])
                                    continue
                                vlo, vhi = rng[0] * GW, rng[1] * GW
                                S.op('act', lambda g, Sp=Sp, pt=pt, p0=p0, vlo=vlo, vhi=vhi: g.activation(out=pt.t[p0:p0 + 64, vlo:vhi], in_=Sp.t[p0:p0 + 64, vlo:vhi], func=AF.Exp), reads=[Sp], writes=[pt])
                                if vlo > lo:
                                    S.op('pool', lambda g, pt=pt, p0=p0, lo=lo, vlo=vlo: g.memset(pt.t[p0:p0 + 64, lo:vlo], 0.0), writes=[pt])
                                if vhi < hi:
                                    S.op('pool', lambda g, pt=pt, p0=p0, hi=hi, vhi=vhi: g.memset(pt.t[p0:p0 + 64, vhi:hi], 0.0), writes=[pt])
                            jb['vtile'] = 2 + a0 // 2
                        jb['lo'], jb['hi'] = lo, hi

                    def emit_pv(jb, jj=j):
                        g_ = jb['g']
                        O, Dn, nq, qc0 = g_['O'], g_['Dn'], g_['nq'], g_['qc0']
                        pt, lo, hi, e, vtile, first = jb['pt'], jb['lo'], jb['hi'], jb['e'], jb['vtile'], jb['first']
                        S.op('pe', lambda g, vp=vp: g.matmul(O.t[:, lo:hi], lhsT=vp.t[:, vtile, e * 128:(e + 1) * 128], rhs=pt.t[:, lo:hi], start=first, stop=False, skip_group_check=True), reads=[vp, pt], writes=[O])
                        S.op('pe', lambda g: g.matmul(Dn.t[:, lo:hi], lhsT=osel.t[:, e, :], rhs=pt.t[:, lo:hi], start=first, stop=False, skip_group_check=True), reads=[osel, pt], writes=[Dn])
                        if jb['last']:
                            rc, yb_ = g_['rc'], g_['yb']
                            S.op('dve', lambda g: g.reciprocal(out=rc.t[:, 0:nq], in_=Dn.t[:, 0:nq]), reads=[Dn], writes=[rc])
                            S.op('dve', lambda g: g.tensor_tensor(out=yb_.t[:, 0:nq], in0=O.t[:, 0:nq], in1=rc.t[:, 0:nq], op=ALU.mult), reads=[O, rc], writes=[yb_])
                            S.dma('sp', yaT_d[jj, :, qc0:qc0 + nq], yb_.t[:, 0:nq], reads=[yb_])

                    for idx in range(len(jobs) + 1):
                        if idx < len(jobs):
                            emit_s(jobs[idx])
                        if idx >= 1:
                            emit_pv(jobs[idx - 1])

        def sin_layer(ps, n, bias, fq, tmp, tmp2, out, outbuf, xr):
            S.op('dve', lambda g: g.tensor_scalar(out=tmp.t[0:64, 0:n], in0=ps.t[0:64, 0:n], scalar1=bias, scalar2=fq, op0=ALU.add, op1=ALU.mult), reads=[ps] + xr, writes=[tmp])
            MAGIC = 12582912.0
            S.op('dve', lambda g: g.tensor_scalar(out=tmp2.t[0:64, 0:n], in0=tmp.t[0:64, 0:n], scalar1=1.0 / (2 * math.pi), scalar2=MAGIC, op0=ALU.mult, op1=ALU.add), reads=[tmp], writes=[tmp2])
            S.op('dve', lambda g: g.tensor_scalar(out=tmp2.t[0:64, 0:n], in0=tmp2.t[0:64, 0:n], scalar1=MAGIC, scalar2=-2 * math.pi, op0=ALU.subtract, op1=ALU.mult), reads=[tmp2], writes=[tmp2])
            S.op('dve', lambda g: g.tensor_tensor(out=tmp.t[0:64, 0:n], in0=tmp.t[0:64, 0:n], in1=tmp2.t[0:64, 0:n], op=ALU.add), reads=[tmp, tmp2], writes=[tmp])
            S.op('dve', lambda g: g.tensor_scalar(out=tmp.t[0:64, 0:n], in0=tmp.t[0:64, 0:n], scalar1=-3.1415925, scalar2=3.1415925, op0=ALU.max, op1=ALU.min), reads=[tmp], writes=[tmp])
            S.op('act', lambda g: g.activation(out=out, in_=tmp.t[0:64, 0:n], func=AF.Sin), reads=[tmp], writes=[outbuf])

        def stage_hy_filt(l, L, zT, dec, ed):
            with Stage("hyfilt"):
                z = S.sbuf("z", [33, L], F32)
                dc = S.sbuf("dc", [128, 2, L], F32)
                f1w = S.sbuf("f1w", [33, 64], F32)
                f2w = S.sbuf("f2w", [64, 64], F32)
                f3w = S.sbuf("f3w", [64, 512], F32)
                fb = S.sbuf("fb", [64, 2], F32)
                fq = S.sbuf("fq", [64, 2], F32)
                dsk = S.sbuf("dsk", [128, 2], F32)
                h1 = S.sbuf("h1", [64, L], F32)
                h2 = S.sbuf("h2", [64, L], F32)
                hT = [S.sbuf("hT", [128, L], F32) for _ in range(4)]
                tmp = [S.sbuf("stmp", [64, 512], F32) for _ in range(4)]
                et = [S.sbuf("et", [128, 2 * L], BF16) for _ in range(2)]
                S.dma('sp', z.t[:], zT, writes=[z])
                S.dma('sp', dc.t[:], dec, writes=[dc])
                S.dma('sp', f1w.t[:], hy_f1w[l], writes=[f1w])
                S.dma('sp', f2w.t[:], hy_f2w[l], writes=[f2w])
                S.dma('sp', f3w.t[:], hy_f3w[l], writes=[f3w])
                S.dma('sp', fb.t[:, 0:1], hy_f1b[l], writes=[fb])
                S.dma('sp', fb.t[:, 1:2], hy_f2b[l], writes=[fb])
                S.dma('sp', fq.t[:], hy_fq[l], writes=[fq])
                S.dma('sp', dsk.t[:], hy_dsk[l], writes=[dsk])
                n = 0
                for c0 in range(0, L, 512):
                    nn = min(512, L - c0)
                    ps = PS[n % 2]
                    S.op('pe', lambda g, ps=ps, c0=c0, nn=nn: g.matmul(ps.t[0:64, 0:nn], lhsT=f1w.t[:], rhs=z.t[:, c0:c0 + nn], start=True, stop=True), reads=[f1w, z], writes=[ps])
                    sin_layer(ps, nn, fb.t[:, 0:1], fq.t[:, 0:1], tmp[0], tmp[2], h1.t[:, c0:c0 + nn], h1, [fb, fq])
                    ps2 = PS[2 + n % 2]
                    S.op('pe', lambda g, ps2=ps2, c0=c0, nn=nn: g.matmul(ps2.t[0:64, 0:nn], lhsT=f2w.t[:], rhs=h1.t[:, c0:c0 + nn], start=True, stop=True), reads=[f2w, h1], writes=[ps2])
                    sin_layer(ps2, nn, fb.t[:, 1:2], fq.t[:, 1:2], tmp[1], tmp[3], h2.t[:, c0:c0 + nn], h2, [fb, fq])
                    for c4 in range(4):
                        ps3 = PS[4 + c4 % 2]
                        S.op('pe', lambda g, ps3=ps3, c4=c4, c0=c0, nn=nn: g.matmul(ps3.t[:, 0:nn], lhsT=f3w.t[:, c4 * 128:(c4 + 1) * 128], rhs=h2.t[:, c0:c0 + nn], start=True, stop=True), reads=[f3w, h2], writes=[ps3])
                        S.op('dve', lambda g, ps3=ps3, c4=c4, c0=c0, nn=nn: g.tensor_tensor(out=hT[c4].t[:, c0:c0 + nn], in0=ps3.t[:, 0:nn], in1=dc.t[:, c4 % 2, c0:c0 + nn], op=ALU.mult), reads=[ps3, dc], writes=[hT[c4]])
                    n += 1
                if dbg_h is not None and L == T:
                    S.dma('sp', dbg_h[0], h1.t[:], reads=[h1])
                    S.dma('sp', dbg_h[1], h2.t[:], reads=[h2])
                for cc in range(2):
                    e_ = et[cc]
                    S.op('pool', lambda g, e_=e_: g.memset(e_.t[:, 0:1], 0.0), writes=[e_])
                    copy_op('act', e_.t[:, L:2 * L], hT[cc].t[:, :], [hT[cc]], [e_])
                    S.op('dve', lambda g, e_=e_, cc=cc: g.tensor_scalar(out=e_.t[:, L:L + 1], in0=hT[cc].t[:, 0:1], scalar1=dsk.t[:, cc:cc + 1], scalar2=None, op0=ALU.add), reads=[hT[cc], dsk, e_], writes=[e_])
                    S.op('dve', lambda g, e_=e_, cc=cc: g.tensor_copy(out=e_.t[:, 1:L], in_=hT[2 + cc].t[:, L - 1:0:-1]), reads=[hT[2 + cc], e_], writes=[e_])
                    S.dma('sp', ed[cc * 128:(cc + 1) * 128, :], e_.t[:], reads=[e_])

        def stage_hy_sc(l, L, c0, ub):
            nb = L // 128
            with Stage("hysc"):
                sw = S.sbuf("sw", [128, 6, 3], F32)
                sb = S.sbuf("sb", [128, 6], F32)
                S.dma('sp', sw.t[:], hy_swT[l], writes=[sw])
                S.dma('sp', sb.t[:], hy_sbT[l], writes=[sb])
                pin = [S.sbuf("pin", [128, L], F32) for _ in range(2)]
                ta = S.sbuf("ta", [128, L], F32)
                tb = S.sbuf("tb", [128, L], F32)
                vv = S.sbuf("vv", [128, L], F32)
                x0b = [S.sbuf("x0b", [128, L], BF16) for _ in range(2)]
                utr = [S.sbuf("utr", [128, L], BF16) for _ in range(2)]
                ubs = S.sbuf("ubs", [128, 2, 128, nb], BF16)
                n = {'p': 0}

                def conv(c6, out_buf, out_ap_full, final_writes):
                    p = pin[n['p'] % 2]
                    n['p'] += 1
                    S.dma('sp', p.t[:], phT_d[c6, :, c0:c0 + L], writes=[p])
                    S.op('dve', lambda g: g.tensor_scalar(out=ta.t[:], in0=p.t[:], scalar1=sw.t[:, c6, 1:2], scalar2=sb.t[:, c6:c6 + 1], op0=ALU.mult, op1=ALU.add), reads=[p, sw, sb], writes=[ta])
                    S.op('dve', lambda g: g.scalar_tensor_tensor(out=ta.t[:, 1:L], in0=p.t[:, 0:L - 1], scalar=sw.t[:, c6, 0:1], in1=ta.t[:, 1:L], op0=ALU.mult, op1=ALU.add), reads=[p, sw, ta], writes=[ta])
                    S.op('dve', lambda g: g.scalar_tensor_tensor(out=out_ap_full(0, L - 1), in0=p.t[:, 1:L], scalar=sw.t[:, c6, 2:3], in1=ta.t[:, 0:L - 1], op0=ALU.mult, op1=ALU.add), reads=[p, sw, ta], writes=[out_buf])
                    copy_op('dve', out_ap_full(L - 1, L), ta.t[:, L - 1:L], [ta, out_buf], [out_buf])

                for cc in range(2):
                    conv(cc, x0b[cc], lambda a, b, cc=cc: x0b[cc].t[:, a:b], None)
                    S.dma('sp', x0_d[cc, :, c0:c0 + L], x0b[cc].t[:], reads=[x0b[cc]])
                for cc in range(2):
                    conv(2 + cc, tb, lambda a, b: tb.t[:, a:b], None)
                    conv(4 + cc, vv, lambda a, b, vv=vv: vv.t[:, a:b], None)
                    S.op('dve', lambda g, cc=cc, vv=vv: g.tensor_tensor(out=utr[cc].t[:, ::-1], in0=vv.t[:], in1=tb.t[:], op=ALU.mult), reads=[vv, tb], writes=[utr[cc]])
                    for jb in range(0, nb, 8):
                        k = min(8, nb - jb)
                        pb = PSB[(jb // 8) % 2]
                        for q in range(k):
                            S.op('pe', lambda g, pb=pb, q=q, jb=jb, cc=cc: g.transpose(out=pb.t[:, q * 128:(q + 1) * 128], in_=utr[cc].t[:, (jb + q) * 128:(jb + q + 1) * 128], identity=ident_b.t[:]), reads=[utr[cc], ident_b], writes=[pb])
                        base = ubs.t[:, cc, :, :]
                        off = base.offset + (nb - 1 - jb)
                        outap = bass.AP(ubs.t[:].tensor, off, [[2 * 128 * nb, 128], [-1, k], [nb, 128]])
                        S.op('dve', lambda g, pb=pb, k=k, outap=outap: g.tensor_copy(out=outap, in_=pb.t[:, 0:k * 128].rearrange("p (q c) -> p q c", q=k)), reads=[pb], writes=[ubs])
                S.dma('sp', ub, ubs.t[:].rearrange("p a c j -> p (a c j)"), reads=[ubs])

        def stage_hy_toep(l, L, c0, ed, ub):
            nb = L // 128
            W = 2 * L - 127
            with Stage("hytoep"):
                ubs = S.sbuf("ubs", [128, 2, 128, nb], BF16)
                S.dma('sp', ubs.t[:].rearrange("p a c j -> p (a c j)"), ub, writes=[ubs])
                kts = [S.sbuf("kt", [128, W], BF16) for _ in range(3)]
                ysb = S.sbuf("ysb", [128, 128, nb], F32)
                x0b = S.sbuf("x0b", [128, L], BF16)
                ybo = S.sbuf("ybo", [128, L], BF16)
                per_bank = min(512 // nb, 128)
                for cc in range(2):
                    S.dma('sp', x0b.t[:], x0_d[cc, :, c0:c0 + L], writes=[x0b])
                    for c in range(128):
                        ch = cc * 128 + c
                        kt = kts[ch % 3]
                        S.dma('sp', kt.t[:], bass.AP(ed.tensor, ed.offset + ch * 2 * L, [[1, 128], [1, W]]), writes=[kt])
                        bank = PS[(c // per_bank) % 2]
                        col = (c % per_bank) * nb
                        ds = [0] + [d for d in range(-(nb - 1), nb) if d != 0]
                        for d in ds:
                            j0, j1 = max(0, -d), min(nb, nb - d)
                            xo = L - 127 + 128 * d
                            S.op('pe', lambda g, bank=bank, col=col, kt=kt, xo=xo, cc=cc, c=c, j0=j0, j1=j1, d=d: g.matmul(bank.t[:, col + j0 + d:col + j1 + d], lhsT=kt.t[:, xo:xo + 128], rhs=ubs.t[:, cc, c, j0:j1], start=(d == 0), stop=False, skip_group_check=True), reads=[kt, ubs], writes=[bank])
                        if c % per_bank == per_bank - 1:
                            cb = c - per_bank + 1
                            copy_op(alt(), ysb.t[:, cb:c + 1, :], bank.t[:, 0:per_bank * nb].rearrange("p (c j) -> p c j", j=nb), [bank], [ysb])
                    for I0 in range(0, nb, 4):
                        k = min(4, nb - I0)
                        pt_ = PS[2 + (I0 // 4) % 2]
                        for q in range(k):
                            S.op('pe', lambda g, pt_=pt_, q=q, I0=I0: g.transpose(out=pt_.t[:, q * 128:(q + 1) * 128], in_=ysb.t[:, :, I0 + q], identity=ident_f.t[:]), reads=[ysb, ident_f], writes=[pt_])
                        S.op('dve', lambda g, pt_=pt_, k=k, I0=I0: g.tensor_tensor(out=ybo.t[:, I0 * 128:(I0 + k) * 128], in0=pt_.t[:, 0:k * 128], in1=x0b.t[:, I0 * 128:(I0 + k) * 128], op=ALU.mult), reads=[pt_, x0b], writes=[ybo])
                    S.dma('sp', ybT_d[cc, :, c0:c0 + L], ybo.t[:], reads=[ybo])

        def stage_rglru(l):
            with Stage("rglru"):
                cw = S.sbuf("cw", [128, 2, 4], F32)
                cb = S.sbuf("cb", [128, 2], F32)
                baT = S.sbuf("baT", [128, 2, 2], F32)
                bxT = S.sbuf("bxT", [128, 2, 2], F32)
                lam = S.sbuf("lam", [128, 2, 2], F32)
                m8 = S.sbuf("m8", [128, 2, 2], F32)
                m16 = S.sbuf("m16", [128, 2, 2], F32)
                h0 = S.sbuf("h0", [128, 2, 2], F32)
                bdf = S.sbuf("bdf", [128, 8, 128], F32)
                bd = S.sbuf("bd", [128, 8, 128], BF16)
                for (t_, src) in ((cw, rg_cw[l]), (cb, rg_cb[l]), (baT, rg_baT[l]), (bxT, rg_bxT[l]), (lam, rg_lamT[l])):
                    S.dma('sp', t_.t[:], src, writes=[t_])
                S.op('pool', lambda g: g.memset(bdf.t[:], 0.0), writes=[bdf])
                for cc in range(2):
                    for dr in range(2):
                        for ax, wsrc in ((0, rg_wa), (1, rg_wx)):
                            idx = (cc * 2 + dr) * 2 + ax
                            for hb_ in range(2):
                                S.dma('sp', bdf.t[hb_ * 64:(hb_ + 1) * 64, idx, hb_ * 64:(hb_ + 1) * 64], wsrc[l, dr, 2 * cc + hb_], reads=[bdf], writes=[bdf])
                copy_op('dve', bd.t[:], bdf.t[:], [bdf], [bd])
                S.op('act', lambda g: g.activation(out=lam.t[:], in_=lam.t[:], func=AF.Exp, scale=-1.0), reads=[lam], writes=[lam])
                S.op('act', lambda g: g.activation(out=lam.t[:], in_=lam.t[:], func=AF.Ln, bias=1.0), reads=[lam], writes=[lam])
                S.op('dve', lambda g: g.tensor_scalar(out=m8.t[:], in0=lam.t[:], scalar1=-8.0, scalar2=None, op0=ALU.mult), reads=[lam], writes=[m8])
                S.op('dve', lambda g: g.tensor_scalar(out=m16.t[:], in0=lam.t[:], scalar1=-16.0, scalar2=None, op0=ALU.mult), reads=[lam], writes=[m16])
                LM = T
                xin = S.sbuf("rxin", [128, LM], F32)
                xc = S.sbuf("rxc", [128, LM], F32)
                xcb = S.sbuf("rxcb", [128, LM], BF16)
                rb = S.sbuf("rr", [128, LM], F32)
                ib = S.sbuf("ri", [128, LM], F32)
                ab = S.sbuf("ra", [128, LM], F32)
                hh = [S.sbuf("rh", [128, LM], F32) for _ in range(2)]
                yo = S.sbuf("ryo", [128, LM], BF16)
                n = 0
                for (L, c0, isctx) in ((C, 0, True), (T, C, False)):
                    for cc in range(2):
                        S.dma('sp', xin.t[:, 0:L], prT_d[cc, :, c0:c0 + L], writes=[xin])
                        S.op('dve', lambda g, L=L, cc=cc: g.tensor_scalar(out=xc.t[:, 0:L], in0=xin.t[:, 0:L], scalar1=cw.t[:, cc, 2:3], scalar2=cb.t[:, cc:cc + 1], op0=ALU.mult, op1=ALU.add), reads=[xin, cw, cb], writes=[xc])
                        for (k, sh) in ((0, -2), (1, -1), (3, 1)):
                            if sh < 0:
                                oa, ia = (-sh, L), (0, L + sh)
                            else:
                                oa, ia = (0, L - sh), (sh, L)
                            S.op('dve', lambda g, oa=oa, ia=ia, k=k, cc=cc: g.scalar_tensor_tensor(out=xc.t[:, oa[0]:oa[1]], in0=xin.t[:, ia[0]:ia[1]], scalar=cw.t[:, cc, k:k + 1], in1=xc.t[:, oa[0]:oa[1]], op0=ALU.mult, op1=ALU.add), reads=[xin, cw, xc], writes=[xc])
                        copy_op('act', xcb.t[:, 0:L], xc.t[:, 0:L], [xc], [xcb])
                        for dr in range(2):
                            for g0 in range(0, L, 512):
                                nn = min(512, L - g0)
                                for ax, dst, bias in ((0, rb, baT), (1, ib, bxT)):
                                    ps = PS[n % 4]
                                    n += 1
                                    idx = (cc * 2 + dr) * 2 + ax
                                    S.op('pe', lambda g, ps=ps, idx=idx, g0=g0, nn=nn: g.matmul(ps.t[:, 0:nn], lhsT=bd.t[:, idx, :], rhs=xcb.t[:, g0:g0 + nn], start=True, stop=True), reads=[bd, xcb], writes=[ps])
                                    S.op('act', lambda g, ps=ps, dst=dst, bias=bias, g0=g0, nn=nn, cc=cc, dr=dr: g.activation(out=dst.t[:, g0:g0 + nn], in_=ps.t[:, 0:nn], func=AF.Sigmoid, bias=bias.t[:, cc, dr:dr + 1]), reads=[ps, bias], writes=[dst])
                            S.op('act', lambda g, L=L, cc=cc, dr=dr: g.activation(out=ab.t[:, 0:L], in_=rb.t[:, 0:L], func=AF.Exp, scale=m8.t[:, cc, dr:dr + 1]), reads=[rb, m8], writes=[ab])
                            S.op('act', lambda g, L=L, cc=cc, dr=dr: g.activation(out=rb.t[:, 0:L], in_=rb.t[:, 0:L], func=AF.Exp, scale=m16.t[:, cc, dr:dr + 1]), reads=[rb, m16], writes=[rb])
                            S.op('act', lambda g, L=L: g.activation(out=rb.t[:, 0:L], in_=rb.t[:, 0:L], func=AF.Sqrt, scale=-1.0, bias=1.0), reads=[rb], writes=[rb])
                            S.op('dve', lambda g, L=L: g.tensor_tensor(out=ib.t[:, 0:L], in0=ib.t[:, 0:L], in1=xc.t[:, 0:L], op=ALU.mult), reads=[ib, xc], writes=[ib])
                            S.op('dve', lambda g, L=L: g.tensor_tensor(out=ib.t[:, 0:L], in0=ib.t[:, 0:L], in1=rb.t[:, 0:L], op=ALU.mult), reads=[ib, rb], writes=[ib])
                            init = 0.0 if isctx else h0.t[:, cc, dr:dr + 1]
                            ho = hh[dr]
                            if dr == 0:
                                S.op('dve', lambda g, L=L, init=init, ho=ho: g.tensor_tensor_scan(out=ho.t[:, 0:L], data0=ab.t[:, 0:L], data1=ib.t[:, 0:L], initial=init, op0=ALU.mult, op1=ALU.add), reads=[ab, ib, h0], writes=[ho])
                            else:
                                S.op('dve', lambda g, L=L, init=init, ho=ho: g.tensor_tensor_scan(out=ho.t[:, 0:L][:, ::-1], data0=ab.t[:, 0:L][:, ::-1], data1=ib.t[:, 0:L][:, ::-1], initial=init, op0=ALU.mult, op1=ALU.add), reads=[ab, ib, h0], writes=[ho])
                        if isctx:
                            copy_op('dve', h0.t[:, cc, 0:1], hh[0].t[:, L - 1:L], [hh[0], h0], [h0])
                            copy_op('dve', h0.t[:, cc, 1:2], hh[1].t[:, 0:1], [hh[1], h0], [h0])
                        S.dma('sp', xin.t[:, 0:L], prT_d[2 + cc, :, c0:c0 + L], writes=[xin])
                        S.op('act', lambda g, L=L: g.activation(out=xin.t[:, 0:L], in_=xin.t[:, 0:L], func=AF.Gelu), reads=[xin], writes=[xin])
                        S.op('dve', lambda g, L=L: g.tensor_tensor(out=hh[0].t[:, 0:L], in0=hh[0].t[:, 0:L], in1=hh[1].t[:, 0:L], op=ALU.add), reads=[hh[0], hh[1]], writes=[hh[0]])
                        S.op('dve', lambda g, L=L: g.tensor_tensor(out=yo.t[:, 0:L], in0=hh[0].t[:, 0:L], in1=xin.t[:, 0:L], op=ALU.mult), reads=[hh[0], xin], writes=[yo])
                        S.dma('sp', ycT_d[cc, :, c0:c0 + L], yo.t[:, 0:L], reads=[yo])

        def load_w_bf(dst, src_rows_fn, nk, ncols, stg):
            for kc in range(nk):
                st = stg[kc % len(stg)]
                S.dma('sp', st.t[:, 0:ncols], src_rows_fn(kc), writes=[st])
                copy_op(alt(('dve', 'pool')), dst.t[:, kc, :], st.t[:, 0:ncols], [st], [dst])

        def stage_merge(l, first):
            with Stage("merge"):
                stg = [S.sbuf("mstg", [128, 3072], F32)]
                wg = S.sbuf("wg", [128, 8, 3072], BF16)
                wa_ = S.sbuf("wa", [128, 4, D], BF16)
                wb_ = S.sbuf("wb", [128, 2, D], BF16)
                wc_ = S.sbuf("wc", [128, 2, D], BF16)
                wo_ = S.sbuf("wo", [128, 8, D], BF16)
                bg = S.sbuf("bg", [128, 24], F32)
                S.dma('sp', bg.t[:], b_gateT[l], writes=[bg])
                load_w_bf(wg, lambda kc: w_gate[l, kc * 128:(kc + 1) * 128, :], 8, 3072, stg)
                load_w_bf(wa_, lambda kc: w_bra[l, kc * 128:(kc + 1) * 128, :], 4, D, stg)
                load_w_bf(wb_, lambda kc: w_brb[l, kc * 128:(kc + 1) * 128, :], 2, D, stg)
                load_w_bf(wc_, lambda kc: w_brc[l, kc * 128:(kc + 1) * 128, :], 2, D, stg)
                load_w_bf(wo_, lambda kc: w_out[l, kc * 128:(kc + 1) * 128, :], 8, D, stg)
                xg = S.sbuf("xg", [128, 8, 512], BF16)
                ya = S.sbuf("mya", [128, 4, 512], BF16)
                yb = S.sbuf("myb", [128, 2, 512], BF16)
                yc = S.sbuf("myc", [128, 2, 512], BF16)
                mT = S.sbuf("mT", [128, 8, 512], BF16)
                gt = [S.sbuf("gt", [128, 512], BF16) for _ in range(3)]
                t1 = S.sbuf("mt1", [128, 512], F32)
                t2 = S.sbuf("mt2", [128, 512], F32)
                oT = S.sbuf("oT", [128, 8, 512], F32)
                xt = [S.sbuf("mxt", [128, D], F32) for _ in range(2)]
                for (c0, nn) in GROUPS:
                    s = 1 if c0 == 0 else 0
                    for kc in range(8):
                        S.dma('sp', xg.t[:, kc, 0:nn], xnT_d[:, kc, c0:c0 + nn], writes=[xg])
                    for kc in range(4):
                        S.dma('sp', ya.t[:, kc, 0:nn], yaT_d[kc, :, c0:c0 + nn], writes=[ya])
                    for kc in range(2):
                        S.dma('sp', yb.t[:, kc, 0:nn], ybT_d[kc, :, c0:c0 + nn], writes=[yb])
                        S.dma('sp', yc.t[:, kc, 0:nn], ycT_d[kc, :, c0:c0 + nn], writes=[yc])
                    for mc in range(8):
                        brs = ((wa_, ya, 4), (wb_, yb, 2), (wc_, yc, 2))
                        for bi in range(3):
                            pg = PS[bi]
                            for kc in range(8):
                                S.op('pe', lambda g, pg=pg, kc=kc, bi=bi, mc=mc, nn=nn: g.matmul(pg.t[:, 0:nn], lhsT=wg.t[:, kc, bi * D + mc * 128:bi * D + (mc + 1) * 128], rhs=xg.t[:, kc, 0:nn], start=(kc == 0), stop=(kc == 7)), reads=[wg, xg], writes=[pg])
                            S.op('act', lambda g, pg=pg, bi=bi, mc=mc, nn=nn: g.activation(out=gt[bi].t[:, 0:nn], in_=pg.t[:, 0:nn], func=AF.Sigmoid, bias=bg.t[:, bi * 8 + mc:bi * 8 + mc + 1]), reads=[pg, bg], writes=[gt[bi]])
                            pbr = PS[3 + bi]
                            w_, y_, nk = brs[bi]
                            for kc in range(nk):
                                S.op('pe', lambda g, pbr=pbr, kc=kc, w_=w_, y_=y_, nk=nk, mc=mc, nn=nn: g.matmul(pbr.t[:, 0:nn], lhsT=w_.t[:, kc, mc * 128:(mc + 1) * 128], rhs=y_.t[:, kc, 0:nn], start=(kc == 0), stop=(kc == nk - 1)), reads=[w_, y_], writes=[pbr])
                        S.op('dve', lambda g, nn=nn: g.tensor_tensor(out=t1.t[:, 0:nn], in0=PS[3].t[:, 0:nn], in1=gt[0].t[:, 0:nn], op=ALU.mult), reads=[PS[3], gt[0]], writes=[t1])
                        S.op('dve', lambda g, nn=nn: g.tensor_tensor(out=t2.t[:, 0:nn], in0=PS[4].t[:, 0:nn], in1=gt[1].t[:, 0:nn], op=ALU.mult), reads=[PS[4], gt[1]], writes=[t2])
                        S.op('pool', lambda g, nn=nn: g.tensor_tensor(out=t1.t[:, 0:nn], in0=t1.t[:, 0:nn], in1=t2.t[:, 0:nn], op=ALU.add), reads=[t1, t2], writes=[t1])
                        S.op('dve', lambda g, nn=nn: g.tensor_tensor(out=t2.t[:, 0:nn], in0=PS[5].t[:, 0:nn], in1=gt[2].t[:, 0:nn], op=ALU.mult), reads=[PS[5], gt[2]], writes=[t2])
                        S.op('pool', lambda g, nn=nn, mc=mc: g.tensor_tensor(out=mT.t[:, mc, 0:nn], in0=t1.t[:, 0:nn], in1=t2.t[:, 0:nn], op=ALU.add), reads=[t1, t2], writes=[mT])
                    for oc in range(8):
                        po = PS[oc % 2]
                        for mc in range(8):
                            S.op('pe', lambda g, po=po, mc=mc, oc=oc, nn=nn: g.matmul(po.t[:, 0:nn], lhsT=wo_.t[:, mc, oc * 128:(oc + 1) * 128], rhs=mT.t[:, mc, 0:nn], start=(mc == 0), stop=(mc == 7)), reads=[wo_, mT], writes=[po])
                        S.op('act', lambda g, po=po, oc=oc, nn=nn, s=s: g.activation(out=oT.t[:, oc, 0:nn], in_=po.t[:, 0:nn], func=AF.Copy, scale=modT.t[:, 16 + oc, s:s + 1]), reads=[po, modT], writes=[oT])
                    for t4 in range(nn // 128):
                        tok0 = c0 + t4 * 128
                        if c0 == 0:
                            src = (c_in if first else cres)[tok0:tok0 + 128, :]
                            dst = cres[tok0:tok0 + 128, :]
                        else:
                            src = (x_in if first else xres)[tok0 - C:tok0 - C + 128, :]
                            dst = xres[tok0 - C:tok0 - C + 128, :]
                        x_ = xt[t4 % 2]
                        S.dma('sp', x_.t[:], src, writes=[x_])
                        for half in range(2):
                            pt_ = PS[2 + half]
                            for q in range(4):
                                oc = half * 4 + q
                                S.op('pe', lambda g, pt_=pt_, q=q, oc=oc, t4=t4: g.transpose(out=pt_.t[:, q * 128:(q + 1) * 128], in_=oT.t[:, oc, t4 * 128:(t4 + 1) * 128], identity=ident_f.t[:]), reads=[oT, ident_f], writes=[pt_])
                            S.op('dve', lambda g, pt_=pt_, x_=x_, half=half: g.tensor_tensor(out=x_.t[:, half * 512:(half + 1) * 512], in0=pt_.t[:], in1=x_.t[:, half * 512:(half + 1) * 512], op=ALU.add), reads=[pt_, x_], writes=[x_])
                        S.dma('pool', dst, x_.t[:], reads=[x_])

        def top16(src_ap, src_reads, vals, idxs, scr, vparts, iparts):
            S.op('dve', lambda g: g.max(out=vals[:, 0:8], in_=src_ap), reads=src_reads, writes=vparts)
            S.op('dve', lambda g: g.max_index(out=idxs[:, 0:8], in_max=vals[:, 0:8], in_values=src_ap), reads=src_reads + vparts, writes=iparts)
            S.op('dve', lambda g: g.match_replace(out=scr.t[:, 0:src_ap.shape[1]], in_to_replace=vals[:, 0:8], in_values=src_ap, imm_value=-1e30), reads=src_reads + vparts, writes=[scr])
            S.op('dve', lambda g: g.max(out=vals[:, 8:16], in_=scr.t[:, 0:src_ap.shape[1]]), reads=[scr], writes=vparts)
            S.op('dve', lambda g: g.max_index(out=idxs[:, 8:16], in_max=vals[:, 8:16], in_values=scr.t[:, 0:src_ap.shape[1]]), reads=[scr] + vparts, writes=iparts)

        def stage_peer_prep(l):
            with Stage("pprep"):
                R = 4
                uin = [S.sbuf("uin", [128, R, D], F32) for _ in range(2)]
                vin = [S.sbuf("vin", [128, R, D], F32) for _ in range(2)]
                uvo = [S.sbuf("uvo", [128, R, 2 * D], BF16) for _ in range(2)]
                uview = p_u[l].rearrange("(p r) d -> p r d", p=128)
                vview = p_v[l].rearrange("(p r) d -> p r d", p=128)
                oview = uv_d.rearrange("(p r) d -> p r d", p=128)
                nch = 128 // R

                def loads(c):
                    S.dma('sp', uin[c % 2].t[:], uview[:, c * R:(c + 1) * R, :], writes=[uin[c % 2]])
                    S.dma('sp', vin[c % 2].t[:], vview[:, c * R:(c + 1) * R, :], writes=[vin[c % 2]])
                loads(0)
                for c in range(nch):
                    if c + 1 < nch:
                        loads(c + 1)
                    a, b, o = uin[c % 2], vin[c % 2], uvo[c % 2]
                    S.op('dve', lambda g, a=a, o=o: g.tensor_copy(out=o.t[:, :, 0:D], in_=a.t[:]), reads=[a], writes=[o.part(0)])
                    S.op('act', lambda g, b=b, o=o: g.activation(out=o.t[:, :, D:2 * D], in_=b.t[:], func=AF.Copy), reads=[b], writes=[o.part(1)])
                    S.dma('pool', oview[:, c * R:(c + 1) * R, :], o.t[:], reads=[o.part(0), o.part(1)])

        def top16g(src_ap, src_reads, vals, idxs, scr, vparts, iparts):
            n = src_ap.shape[1]
            S.op('dve', lambda g: g.max(out=vals[:, 0:8], in_=src_ap), reads=src_reads, writes=vparts)
            yield
            S.op('dve', lambda g: g.max_index(out=idxs[:, 0:8], in_max=vals[:, 0:8], in_values=src_ap), reads=src_reads + vparts, writes=iparts)
            yield
            S.op('dve', lambda g: g.match_replace(out=scr.t[:, 0:n], in_to_replace=vals[:, 0:8], in_values=src_ap, imm_value=-1e30), reads=src_reads + vparts, writes=[scr])
            yield
            S.op('dve', lambda g: g.max(out=vals[:, 8:16], in_=scr.t[:, 0:n]), reads=[scr], writes=vparts)
            yield
            S.op('dve', lambda g: g.max_index(out=idxs[:, 8:16], in_max=vals[:, 8:16], in_values=scr.t[:, 0:n]), reads=[scr] + vparts, writes=iparts)
            yield

        def stage_peer_q(l):
            with Stage("peerq"):
                stg = [S.sbuf("pstg", [128, 2048], F32) for _ in range(2)]
                wq = S.sbuf("wq", [128, 8, 2048], BF16)
                load_w_bf(wq, lambda kc: p_wq[l, kc * 128:(kc + 1) * 128, :], 8, 2048, stg)
                xgs = [S.sbuf("pxg", [128, 8, 512], BF16) for _ in range(2)]
                qTs = [S.sbuf("pqT", [128, 16, 512], BF16) for _ in range(2)]
                n = 0
                for gi, (c0, nn) in enumerate(GROUPS):
                    xg = xgs[gi % 2]
                    qT = qTs[gi % 2]
                    for kc in range(8):
                        S.dma('sp', xg.t[:, kc, 0:nn], xnT_d[:, kc, c0:c0 + nn], writes=[xg])
                    for hp in range(16):
                        ps = PS[n % 4]
                        n += 1
                        for kc in range(8):
                            S.op('pe', lambda g, ps=ps, kc=kc, hp=hp, nn=nn, xg=xg: g.matmul(ps.t[:, 0:nn], lhsT=wq.t[:, kc, hp * 128:(hp + 1) * 128], rhs=xg.t[:, kc, 0:nn], start=(kc == 0), stop=(kc == 7)), reads=[wq, xg], writes=[ps])
                        copy_op(alt(), qT.t[:, hp, 0:nn], ps.t[:, 0:nn], [ps], [qT])
                    S.dma('pool', qT_d[:, :, c0:c0 + nn], qT.t[:, :, 0:nn], reads=[qT])

        def stage_peer(l):
            with Stage("peer"):
                keysT = S.sbuf("keysT", [128, 16, 128], BF16)
                kst = [S.sbuf("kst", [128, 128], F32) for _ in range(2)]
                for hp in range(16):
                    ks = kst[hp % 2]
                    S.dma('sp', ks.t[:], p_keys[l, hp // 2, hp % 2], writes=[ks])
                    pk = PS[hp % 2]
                    S.op('pe', lambda g, pk=pk, ks=ks: g.transpose(out=pk.t[:, 0:128], in_=ks.t[:], identity=ident_f.t[:]), reads=[ks, ident_f], writes=[pk])
                    copy_op('dve', keysT.t[:, hp, :], pk.t[:, 0:128], [pk], [keysT])
                g5 = [S.sbuf("g5", [128, D], F32) for _ in range(2)]
                for s in range(2):
                    S.dma('sp', g5[s].t[:], bcd[8 + s], writes=[g5[s]])
                qTs = [S.sbuf("pqT", [128, 16, 512], BF16) for _ in range(2)]
                top = S.sbuf("ptop", [128, 16, 16], F32)
                it = S.sbuf("pit", [128, 16, 16], U32)
                itf = S.sbuf("pitf", [128, 16, 16], F32)
                scr = S.sbuf("pscr", [128, 256], F32)
                cand = S.sbuf("pcand", [128, 8, 256], F32)
                eqb = S.sbuf("peq", [128, 8, 256], F32)
                eq = eqb
                best = S.sbuf("pbest", [128, 8, 16], F32)
                pos = S.sbuf("ppos", [128, 8, 16], U32)
                pa = S.sbuf("ppa", [128, 8, 16], U32)
                paf = S.sbuf("ppaf", [128, 2, 8, 16], F32)
                isel = S.sbuf("pisel", [128, 2, 8, 16], F32)
                idxf = S.sbuf("pidxf", [128, 128], F32)
                idxu = [S.sbuf("pidxu", [128, 128], U32) for _ in range(2)]
                gws = [S.sbuf("pgw", [128, 8, 16], F32) for _ in range(2)]
                zs = S.sbuf("pzs", [128, 8], F32)
                dotb = [S.sbuf("pdots", [128, 128], F32) for _ in range(2)]
                actv = [S.sbuf("pact", [128, 128], F32) for _ in range(2)]
                NR = 24
                BT = 4
                uvr = [S.sbuf("uvr", [128, 2 * D], BF16) for _ in range(NR)]
                dg4 = [S.sbuf("pdg4", [128, 4, 128], BF16) for _ in range(3)]
                xn = [S.sbuf("pxn", [128, D], BF16) for _ in range(2)]
                xt = [S.sbuf("pxt", [128, D], F32) for _ in range(2)]
                junk = S.sbuf("pjunk", [128, D], BF16)
                prods = [S.sbuf("pprod", [128, D], BF16) for _ in range(3)]
                junk2 = S.sbuf("pjunk2", [128, D], BF16)
                ytmp = S.sbuf("pytmp", [128, D], F32)
                uvflat = uv_d

                tiles = []
                for gi, (c0, nn) in enumerate(GROUPS):
                    for t4 in range(nn // 128):
                        tiles.append((gi, c0, nn, t4))

                def emit_group_q(gi):
                    c0, nn = GROUPS[gi]
                    qT = qTs[gi % 2]
                    S.dma('sp', qT.t[:, :, 0:nn], qT_d[:, :, c0:c0 + nn], writes=[qT])

                def topk_gen(ti):
                    gi, c0, nn, t4 = tiles[ti]
                    qT = qTs[gi % 2]
                    gw = gws[ti % 2]
                    iu = idxu[ti % 2]
                    for hp in range(16):
                        ps = PS[2 + hp % 2]
                        S.op('pe', lambda g, ps=ps, hp=hp, t4=t4, qT=qT: g.matmul(ps.t[:, 0:128], lhsT=qT.t[:, hp, t4 * 128:(t4 + 1) * 128], rhs=keysT.t[:, hp, :], start=True, stop=True), reads=[qT, keysT], writes=[ps])
                        yield from top16g(ps.t[:, 0:128], [ps], top.t[:, hp, :], it.t[:, hp, :], scr, [top], [it])
                    copy_op('pool', itf.t[:], it.t[:], [it], [itf])
                    tv = top.t[:].rearrange("p (h q) k -> p h q k", q=2)
                    S.op('dve', lambda g, tv=tv: g.tensor_tensor(out=cand.t[:].rearrange("p h (a b) -> p h a b", a=16), in0=tv[:, :, 0, :].unsqueeze(3).broadcast_to([128, 8, 16, 16]), in1=tv[:, :, 1, :].unsqueeze(2).broadcast_to([128, 8, 16, 16]), op=ALU.add), reads=[top], writes=[cand])
                    yield
                    for h in range(8):
                        yield from top16g(cand.t[:, h, :], [cand], best.t[:, h, :], pos.t[:, h, :], scr, [best], [pos])
                    S.op('dve', lambda g: g.tensor_tensor(out=gw.t[:], in0=best.t[:], in1=best.t[:, :, 0:1].broadcast_to([128, 8, 16]), op=ALU.subtract), reads=[best], writes=[gw])
                    yield
                    S.op('act', lambda g: g.activation(out=gw.t[:], in_=gw.t[:], func=AF.Exp), reads=[gw], writes=[gw])
                    S.op('dve', lambda g: g.tensor_reduce(out=zs.t[:], in_=gw.t[:], axis=AX.X, op=ALU.add), reads=[gw], writes=[zs])
                    yield
                    S.op('dve', lambda g: g.reciprocal(out=zs.t[:], in_=zs.t[:]), reads=[zs], writes=[zs])
                    yield
                    S.op('dve', lambda g: g.tensor_tensor(out=gw.t[:], in0=gw.t[:], in1=zs.t[:].unsqueeze(2).broadcast_to([128, 8, 16]), op=ALU.mult), reads=[gw, zs], writes=[gw])
                    yield
                    S.op('dve', lambda g: g.tensor_single_scalar(out=pa.t[:], in_=pos.t[:], scalar=4, op=ALU.logical_shift_right), reads=[pos], writes=[pa])
                    yield
                    copy_op('dve', paf.t[:, 0], pa.t[:], [pa], [paf])
                    yield
                    S.op('dve', lambda g: g.tensor_single_scalar(out=pa.t[:], in_=pos.t[:], scalar=15, op=ALU.bitwise_and), reads=[pos, paf], writes=[pa])
                    yield
                    copy_op('dve', paf.t[:, 1], pa.t[:], [pa], [paf])
                    yield
                    itv = itf.t[:].rearrange("p (h q) k -> p h q k", q=2)
                    eqv = eqb.t[:].rearrange("p h (a b) -> p h a b", a=16)
                    for q in range(2):
                        S.op('dve', lambda g, q=q: g.tensor_tensor(out=eqv, in0=paf.t[:, q].unsqueeze(3).broadcast_to([128, 8, 16, 16]), in1=iota16.t[:].unsqueeze(1).unsqueeze(1).broadcast_to([128, 8, 16, 16]), op=ALU.is_equal), reads=[paf, iota16], writes=[eq])
                        yield
                        S.op('dve', lambda g, q=q, itv=itv: g.tensor_tensor(out=eqv, in0=eqv, in1=itv[:, :, q, :].unsqueeze(2).broadcast_to([128, 8, 16, 16]), op=ALU.mult), reads=[eq, itf], writes=[eq])
                        yield
                        S.op('dve', lambda g, q=q: g.tensor_reduce(out=isel.t[:, q], in_=eqv, axis=AX.X, op=ALU.add), reads=[eq], writes=[isel])
                        yield
                    S.op('dve', lambda g: g.scalar_tensor_tensor(out=idxf.t[:].rearrange("p (h k) -> p h k", h=8), in0=isel.t[:, 0], scalar=128.0, in1=isel.t[:, 1], op0=ALU.mult, op1=ALU.add), reads=[isel], writes=[idxf])
                    yield
                    copy_op('dve', iu.t[:], idxf.t[:], [idxf], [iu])
                    yield

                def finish_batch(b0, av, gw, gwf, py):
                    S.op('dve', lambda g: g.tensor_tensor(out=av.t[:, b0:b0 + BT], in0=av.t[:, b0:b0 + BT], in1=gwf[:, b0:b0 + BT], op=ALU.mult), reads=[av.part(b0 // BT), gw], writes=[av.part(b0 // BT)])
                    dg = dg4[(b0 // BT) % 3]
                    S.op('dve', lambda g: g.tensor_tensor(out=dg.t[:], in0=ident_b.t[:].unsqueeze(1).broadcast_to([128, BT, 128]), in1=av.t[:, b0:b0 + BT].unsqueeze(2).broadcast_to([128, BT, 128]), op=ALU.mult), reads=[ident_b, av.part(b0 // BT)], writes=[dg])
                    for k in range(b0, b0 + BT):
                        rk = uvr[k % NR]
                        for half in range(2):
                            S.op('pe', lambda g, rk=rk, half=half, k=k: g.matmul(py[half].t[:], lhsT=dg.t[:, k - b0, :], rhs=rk.t[:, D + half * 512:D + (half + 1) * 512], start=(k == 0), stop=(k == 127)), reads=[dg, rk], writes=[py[half]])

                def load_tile(tj):
                    gj, cj, nj, tj4 = tiles[tj]
                    tk0 = cj + tj4 * 128
                    S.dma('sp', xn[tj % 2].t[:], xntm_d[tk0:tk0 + 128, :], writes=[xn[tj % 2]])
                    xrj = (cres[tk0:tk0 + 128, :] if cj == 0 else xres[tk0 - C:tk0 - C + 128, :])
                    S.dma('sp', xt[tj % 2].t[:], xrj, writes=[xt[tj % 2]])

                emit_group_q(0)
                for _ in topk_gen(0):
                    pass
                py = [PS[4], PS[5]]
                for ti, (gi, c0, nn, t4) in enumerate(tiles):
                    s = 1 if c0 == 0 else 0
                    tok0 = c0 + t4 * 128
                    nxt = None
                    if ti + 1 < len(tiles):
                        if tiles[ti + 1][0] != gi:
                            emit_group_q(gi + 1)
                        nxt = topk_gen(ti + 1)
                    xn_ = xn[ti % 2]
                    x_ = xt[ti % 2]
                    iu = idxu[ti % 2]
                    gw = gws[ti % 2]
                    dots = dotb[ti % 2]
                    av = actv[ti % 2]
                    xr = (cres[tok0:tok0 + 128, :] if c0 == 0 else xres[tok0 - C:tok0 - C + 128, :])
                    if ti == 0:
                        load_tile(0)
                    if ti + 1 < len(tiles):
                        load_tile(ti + 1)
                    gwf = gw.t[:].rearrange("p h k -> p (h k)")
                    for hk in range(128):
                        r_ = uvr[hk % NR]
                        S.dma('pool', None, None, reads=[iu], writes=[r_], fn=lambda g, r_=r_, iu=iu, hk=hk: g.indirect_dma_start(out=r_.t[:], out_offset=None, in_=uvflat, in_offset=bass.IndirectOffsetOnAxis(ap=iu.t[:, hk:hk + 1], axis=0)))
                        if False:
                            S.op('dve', lambda g, r_=r_, xn_=xn_, hk=hk, dots=dots: g.scalar_tensor_tensor(out=junk2.t[:], in0=r_.t[:, 0:D], scalar=1.0, in1=xn_.t[:], op0=ALU.mult, op1=ALU.mult, accum_out=dots.t[:, hk:hk + 1]), reads=[r_, xn_], writes=[dots.part(hk // BT)])
                        else:
                            pr_ = prods[hk % 3]
                            S.op('dve', lambda g, r_=r_, xn_=xn_, pr_=pr_: g.tensor_tensor(out=pr_.t[:], in0=r_.t[:, 0:D], in1=xn_.t[:], op=ALU.mult), reads=[r_, xn_], writes=[pr_])
                            S.op('act', lambda g, pr_=pr_, hk=hk, dots=dots: g.activation(out=junk.t[:], in_=pr_.t[:], func=AF.Copy, accum_out=dots.t[:, hk:hk + 1]), reads=[pr_], writes=[dots.part(hk // BT)])
                        if nxt is not None:
                            for _ in range(2):
                                next(nxt, None)
                        if hk % BT == BT - 1:
                            b0 = hk - (BT - 1)
                            if b0 >= BT:
                                finish_batch(b0 - BT, av, gw, gwf, py)
                            else:
                                for _ in range(2):
                                    S.op('act', lambda g: g.activation(out=junk.t[:, 0:8], in_=junk.t[:, 8:16], func=AF.Copy))
                            S.op('act', lambda g, av=av, dots=dots, b0=b0: g.activation(out=av.t[:, b0:b0 + BT], in_=dots.t[:, b0:b0 + BT], func=AF.Gelu), reads=[dots.part(b0 // BT)], writes=[av.part(b0 // BT)])
                    finish_batch(128 - BT, av, gw, gwf, py)
                    if nxt is not None:
                        for _ in nxt:
                            pass
                    for half in range(2):
                        S.op('dve', lambda g, half=half, s=s: g.tensor_tensor(out=ytmp.t[:, half * 512:(half + 1) * 512], in0=py[half].t[:], in1=g5[s].t[:, half * 512:(half + 1) * 512], op=ALU.mult), reads=[py[half], g5[s]], writes=[ytmp])
                    S.op('dve', lambda g, x_=x_: g.tensor_tensor(out=x_.t[:], in0=x_.t[:], in1=ytmp.t[:], op=ALU.add), reads=[x_, ytmp], writes=[x_])
                    S.dma('sp', xr, x_.t[:], reads=[x_])

        def stage_final():
            with Stage("final"):
                gf = S.sbuf("gf", [128, D], F32)
                S.dma('sp', gf.t[:], gfin, writes=[gf])
                xin = [S.sbuf("fx", [128, D], F32) for _ in range(3)]
                junk = S.sbuf("fj", [128, D], BF16)
                yo = [S.sbuf("fy", [128, D], F32) for _ in range(2)]
                ss = S.sbuf("fss", [128, 32], F32)
                t1 = S.sbuf("ft1", [128, 32], F32)
                for tt in range(32):
                    xt = xin[tt % 3]
                    S.dma('sp', xt.t[:], xres[tt * 128:(tt + 1) * 128, :], writes=[xt])
                    S.op('act', lambda g, xt=xt, tt=tt: g.activation(out=junk.t[:], in_=xt.t[:], func=AF.Square, accum_out=ss.t[:, tt:tt + 1]), reads=[xt], writes=[junk, ss.part(tt)])
                    S.op('dve', lambda g, tt=tt: g.tensor_scalar(out=t1.t[:, tt:tt + 1], in0=ss.t[:, tt:tt + 1], scalar1=1.0 / D, scalar2=EPS, op0=ALU.mult, op1=ALU.add), reads=[ss.part(tt)], writes=[t1.part(tt)])
                    S.op('act', lambda g, tt=tt: g.activation(out=t1.t[:, tt:tt + 1], in_=t1.t[:, tt:tt + 1], func=AF.Sqrt), reads=[t1.part(tt)], writes=[t1.part(tt)])
                    S.op('dve', lambda g, tt=tt: g.reciprocal(out=t1.t[:, tt:tt + 1], in_=t1.t[:, tt:tt + 1]), reads=[t1.part(tt)], writes=[t1.part(tt)])
                    y_ = yo[tt % 2]
                    S.op('dve', lambda g, y_=y_, xt=xt, tt=tt: g.scalar_tensor_tensor(out=y_.t[:], in0=xt.t[:], scalar=t1.t[:, tt:tt + 1], in1=gf.t[:], op0=ALU.mult, op1=ALU.mult), reads=[xt, t1.part(tt), gf], writes=[y_])
                    S.dma('pool', y_out[tt * 128:(tt + 1) * 128, :], y_.t[:], reads=[y_])

        S.barrier()
        S.flush()
        plan = []
        for l in range(DEPTH):
            first = (l == 0)
            plan += [("mod", lambda l=l: stage_mod(l)),
                     ("norm1", lambda l=l, first=first: stage_norm(l, 0, first)),
                     ("proj", lambda l=l: stage_proj(l)),
                     ("attn", lambda l=l: stage_attn(l)),
                     ("hyfc", lambda l=l: stage_hy_filt(l, C, zT_c, dec_c, ed_c)),
                     ("hysc_c", lambda l=l: stage_hy_sc(l, C, 0, ub_c)),
                     ("hytc", lambda l=l: stage_hy_toep(l, C, 0, ed_c, ub_c)),
                     ("hyfm", lambda l=l: stage_hy_filt(l, T, zT_m, dec_m, ed_m)),
                     ("hysc_m", lambda l=l: stage_hy_sc(l, T, C, ub_m)),
                     ("hytm", lambda l=l: stage_hy_toep(l, T, C, ed_m, ub_m)),
                     ("rglru", lambda l=l: stage_rglru(l)),
                     ("merge", lambda l=l, first=first: stage_merge(l, first)),
                     ("norm2", lambda l=l: stage_norm(l, 1, False)),
                     ("pprep", lambda l=l: stage_peer_prep(l)),
                     ("peerq", lambda l=l: stage_peer_q(l)),
                     ("peer", lambda l=l: stage_peer(l))]
        plan.append(("final", stage_final))
        for i, (nm, f) in enumerate(plan):
            f()
            if stop_after is not None and i + 1 >= stop_after:
                break
        S.finish()
        build.n_instr = S.n_instr
    return nc


def _chunkT(v, n):
    return np.ascontiguousarray(np.asarray(v, np.float32).reshape(n, 128).T)


def _consts():
    f32 = np.float32
    out = {}
    for nm, L in (("m", T), ("c", C)):
        t = np.linspace(0.0, 1.0, L, dtype=f32)[:, None]
        w = (f32(2.0 * math.pi / L) * np.arange(L, dtype=f32))[:, None]
        bands = np.linspace(1e-4, 15, 16, dtype=f32)[None, :]
        z = np.concatenate([t, np.cos(bands * w), -np.sin(bands * w)], axis=-1).astype(f32)
        deltas = np.linspace(math.log(1e-2) / 1.5, math.log(1e-2) / 0.3, 256, dtype=f32)
        decay = np.exp(-t * np.abs(deltas)[None, :]).astype(f32)
        out["zT_" + nm] = np.ascontiguousarray(z.T)
        out["dec_" + nm] = np.ascontiguousarray(decay.T.reshape(2, 128, L).transpose(1, 0, 2))
    out["ident"] = np.eye(128, dtype=f32)
    out["iota16"] = np.ascontiguousarray(np.broadcast_to(np.arange(16, dtype=f32)[None, :], (128, 16)))
    osel = np.zeros((128, 2, 128), f32)
    osel[:, 0, 0:64] = 1.0
    osel[:, 1, 64:128] = 1.0
    out["onesel"] = osel
    return out


def _rpb_table(rpb):
    wq = np.arange(64)
    start = np.clip(wq - 8, 0, 48)
    wk = np.arange(64)
    colok = (wk[:, None] >= start[None, :]) & (wk[:, None] < start[None, :] + 16)
    dc = np.clip(wk[:, None] - wq[None, :] + 15, 0, 30)
    tab = np.full((DEPTH, 128, 8, 22, 64), NEG, np.float32)
    for half in range(2):
        for ep in range(22):
            e = ep - 3 - half
            if e < 0 or e > 14:
                continue
            dr = 14 - e
            g = rpb[:, :, dr][:, :, dc]
            g = np.where(colok[None, None], g, np.float32(NEG))
            tab[:, half * 64:(half + 1) * 64, :, ep, :] = g.transpose(0, 2, 1, 3)
    return np.ascontiguousarray(tab.reshape(DEPTH, 128, 8, 22 * 64))


_NC_CACHE = {}


def kernel(**inp):
    f32 = np.float32
    g = lambda k: np.asarray(inp[k], f32)
    shared = dict(_consts())
    shared["gmix"] = np.stack([_chunkT(g("norm_mix_g")[l], 8) for l in range(DEPTH)])
    shared["gffn"] = np.stack([_chunkT(g("norm_ffn_g")[l], 8) for l in range(DEPTH)])
    shared["gfin"] = np.ascontiguousarray(np.broadcast_to(g("final_g")[None, :], (128, D)))
    shared["w_ada"] = g("w_ada")
    shared["b_adaT"] = np.stack([_chunkT(g("b_ada")[l], 48) for l in range(DEPTH)])
    shared["w_in"] = g("w_in")
    shared["tab"] = _rpb_table(g("na_rpb"))
    shared["hy_swT"] = np.ascontiguousarray(g("hy_short_w").reshape(DEPTH, 3, 6, 128).transpose(0, 3, 2, 1))
    shared["hy_sbT"] = np.stack([_chunkT(g("hy_short_b")[l], 6) for l in range(DEPTH)])
    shared["hy_f1w"] = g("hy_f1_w")
    shared["hy_f1b"] = g("hy_f1_b")[:, :, None].copy()
    shared["hy_f2w"] = g("hy_f2_w")
    shared["hy_f2b"] = g("hy_f2_b")[:, :, None].copy()
    shared["hy_f3w"] = g("hy_f3_w")
    shared["hy_fq"] = np.ascontiguousarray(g("hy_freq").transpose(0, 2, 1))
    shared["hy_dsk"] = np.stack([_chunkT(g("hy_bias")[l], 2) for l in range(DEPTH)])
    shared["rg_cw"] = np.ascontiguousarray(g("rg_conv_w").reshape(DEPTH, 4, 2, 128).transpose(0, 3, 2, 1))
    shared["rg_cb"] = np.stack([_chunkT(g("rg_conv_b")[l], 2) for l in range(DEPTH)])
    shared["rg_wa"] = g("rg_wa")
    shared["rg_wx"] = g("rg_wx")
    for nm, k in (("rg_baT", "rg_ba"), ("rg_bxT", "rg_bx"), ("rg_lamT", "rg_lambda")):
        shared[nm] = np.ascontiguousarray(g(k).reshape(DEPTH, 2, 2, 128).transpose(0, 3, 2, 1))
    shared["w_gate"] = g("w_gate")
    shared["b_gateT"] = np.stack([_chunkT(g("b_gate")[l], 24) for l in range(DEPTH)])
    shared["w_br_a"] = g("w_br_a")
    shared["w_br_b"] = g("w_br_b")
    shared["w_br_c"] = g("w_br_c")
    shared["w_out"] = g("w_out")
    shared["peer_wq"] = g("peer_wq")
    shared["peer_keys"] = g("peer_keys")
    shared["peer_u"] = g("peer_u")
    shared["peer_v"] = g("peer_v")
    x = g("x")
    ctx = g("ctx")
    c = g("c")
    cc = g("c_ctx")
    nb = x.shape[0]
    in_maps = []
    for b in range(nb):
        m = dict(shared)
        m["x"] = np.ascontiguousarray(x[b])
        m["ctx"] = np.ascontiguousarray(ctx[b])
        m["cvec"] = np.ascontiguousarray(np.stack([_chunkT(c[b], 8), _chunkT(cc, 8)], axis=-1))
        in_maps.append(m)
    if "nc" not in _NC_CACHE:
        _NC_CACHE["nc"] = build()
    nc = _NC_CACHE["nc"]
    res = run_bass_kernel_spmd(nc, in_maps, core_ids=list(range(nb)))
    return np.stack([np.asarray(r["y"], f32) for r in res.results], axis=0)
```
